# Optimizing a Trainium2 kernel written in Bass

```python
import math
import jax
import jax.numpy as jnp
from jax import lax
import numpy as np

D_MODEL = 1024
BATCH = 16
SEQ = 2048
DEPTH = 2

CHUNK = 64
Q_BLOCK = 128
N_MIXERS = 2
N_HEADS = 16
HEAD_DIM = 64
D_ATT = N_HEADS * HEAD_DIM
SSM_GROUP = 16
D_SSM = D_MODEL
N_GROUPS = D_SSM // SSM_GROUP
STATE = 64
_FF_RAW = -(-8 * D_MODEL // 3)
D_FF = -(-_FF_RAW // 256) * 256
N_FOX = (DEPTH + 1) // 2
N_S5 = DEPTH // 2
EPS = 1e-6
DT_MIN = 1e-3
DT_MAX = 1e-1

kernel_name = 'hybrid_fox_s5_sandwich_trunk'


def rmsnorm(x, gain):
    xf = x.astype(jnp.float32)
    y = xf * lax.rsqrt(jnp.mean(xf * xf, axis=-1, keepdims=True) + EPS)
    return (y * gain.astype(jnp.float32)).astype(x.dtype)


def fox_mixer(h, w_in, b_f, q_gain, k_gain, w_out):
    bsz, seq_len, _ = h.shape
    proj = h @ w_in
    q = proj[..., :D_ATT]
    k = proj[..., D_ATT:2 * D_ATT]
    v = proj[..., 2 * D_ATT:3 * D_ATT]
    gate = proj[..., 3 * D_ATT:4 * D_ATT]
    f_logit = proj[..., 4 * D_ATT:]

    def heads(t):
        return t.reshape(bsz, seq_len, N_HEADS, HEAD_DIM).transpose(0, 2, 1, 3)

    q = rmsnorm(heads(q), q_gain)
    k = rmsnorm(heads(k), k_gain)
    v = heads(v)
    log_f = jax.nn.log_sigmoid(f_logit.astype(jnp.float32) + b_f.astype(jnp.float32))
    c = jnp.cumsum(log_f, axis=1).transpose(0, 2, 1)
    scale = HEAD_DIM ** -0.5

    outs = []
    for blk in range(seq_len // Q_BLOCK):
        q0 = blk * Q_BLOCK
        k_end = q0 + Q_BLOCK
        qb = q[:, :, q0:k_end]
        kb = k[:, :, :k_end]
        vb = v[:, :, :k_end]
        s = jnp.einsum('bhqd,bhkd->bhqk', qb, kb).astype(jnp.float32) * scale
        s = s + c[:, :, q0:k_end, None] - c[:, :, None, :k_end]
        q_pos = q0 + jnp.arange(Q_BLOCK)
        mask = q_pos[:, None] >= jnp.arange(k_end)[None, :]
        s = jnp.where(mask, s, -jnp.inf)
        p = jax.nn.softmax(s, axis=-1).astype(vb.dtype)
        outs.append(jnp.einsum('bhqk,bhkd->bhqd', p, vb))
    o = jnp.concatenate(outs, axis=2)
    o = o.transpose(0, 2, 1, 3).reshape(bsz, seq_len, D_ATT)
    o = o * jax.nn.sigmoid(gate)
    return o @ w_out


def _complex_linear_combine(e1, e2):
    a1r, a1i, b1r, b1i = e1
    a2r, a2i, b2r, b2i = e2
    return (a2r * a1r - a2i * a1i,
            a2r * a1i + a2i * a1r,
            a2r * b1r - a2i * b1i + b2r,
            a2r * b1i + a2i * b1r + b2i)


def s5_mixer(h, w_in, log_dt, lam_re, lam_im, b_re, b_im, c_re, c_im, d_skip, w_glu, w_out):
    f32 = jnp.float32
    bsz, seq_len, _ = h.shape
    u = (h @ w_in).astype(f32)
    lam_re = lam_re.astype(f32)
    lam_im = lam_im.astype(f32)
    dt = jnp.exp(log_dt.astype(f32))[:, None]
    mag = jnp.exp(lam_re * dt)
    a_re = mag * jnp.cos(lam_im * dt)
    a_im = mag * jnp.sin(lam_im * dt)
    den = lam_re * lam_re + lam_im * lam_im
    n_re = a_re - 1.0
    z_re = (n_re * lam_re + a_im * lam_im) / den
    z_im = (a_im * lam_re - n_re * lam_im) / den
    b_re = b_re.astype(f32)
    b_im = b_im.astype(f32)
    bb_re = z_re[..., None] * b_re - z_im[..., None] * b_im
    bb_im = z_re[..., None] * b_im + z_im[..., None] * b_re
    c_re = c_re.astype(f32)
    c_im = c_im.astype(f32)

    n_chunks = seq_len // CHUNK
    u_chunks = u.reshape(bsz, n_chunks, CHUNK, N_GROUPS, SSM_GROUP).transpose(1, 0, 2, 3, 4)

    def step(carry, u_c):
        s_re, s_im = carry
        bu_re = jnp.einsum('gpc,bkgc->bkgp', bb_re, u_c)
        bu_im = jnp.einsum('gpc,bkgc->bkgp', bb_im, u_c)
        bu_re = bu_re.at[:, 0].add(a_re * s_re - a_im * s_im)
        bu_im = bu_im.at[:, 0].add(a_re * s_im + a_im * s_re)
        a_seq_re = jnp.broadcast_to(a_re, bu_re.shape)
        a_seq_im = jnp.broadcast_to(a_im, bu_im.shape)
        _, _, x_re, x_im = lax.associative_scan(
            _complex_linear_combine, (a_seq_re, a_seq_im, bu_re, bu_im), axis=1)
        y = (jnp.einsum('gcp,bkgp->bkgc', c_re, x_re)
             - jnp.einsum('gcp,bkgp->bkgc', c_im, x_im))
        return (x_re[:, -1], x_im[:, -1]), y

    init = (jnp.zeros((bsz, N_GROUPS, STATE), f32), jnp.zeros((bsz, N_GROUPS, STATE), f32))
    _, y = lax.scan(step, init, u_chunks)
    y = y.transpose(1, 0, 2, 3, 4).reshape(bsz, seq_len, D_SSM)
    y = y + d_skip.astype(f32) * u
    y = jax.nn.gelu(y).astype(h.dtype)
    y = y * jax.nn.sigmoid(y @ w_glu)
    return y @ w_out


def swiglu(h, w_gate, w_up, w_down):
    return (jax.nn.silu(h @ w_gate) * (h @ w_up)) @ w_down


def setup_inputs(seed: int = 0) -> dict:
    key = jax.random.key(seed)
    ks = jax.random.split(key, 24)
    f32 = jnp.float32

    def nrm(k, shape, scale):
        return jax.random.normal(k, shape, f32) * scale

    def gain(k, shape):
        return 1.0 + 0.02 * jax.random.normal(k, shape, f32)

    x = jax.random.normal(ks[0], (BATCH, SEQ, D_MODEL), f32)
    fox_w_in = nrm(ks[1], (N_FOX, D_MODEL, 4 * D_ATT + N_HEADS), D_MODEL ** -0.5)
    fox_b_f = jax.random.uniform(ks[2], (N_FOX, N_HEADS), f32, minval=1.0, maxval=5.0)
    fox_q_gain = gain(ks[3], (N_FOX, HEAD_DIM))
    fox_k_gain = gain(ks[4], (N_FOX, HEAD_DIM))
    fox_w_out = nrm(ks[5], (N_FOX, D_ATT, D_MODEL), D_ATT ** -0.5)
    s5_w_in = nrm(ks[6], (N_S5, D_MODEL, D_SSM), D_MODEL ** -0.5)
    s5_log_dt = jax.random.uniform(ks[7], (N_S5, N_GROUPS), f32,
                                   minval=math.log(DT_MIN), maxval=math.log(DT_MAX))
    s5_lam_re = -0.5 + 0.01 * jax.random.normal(ks[8], (N_S5, N_GROUPS, STATE), f32)
    s5_lam_im = (math.pi * jnp.arange(STATE, dtype=f32)
                 + 0.01 * jax.random.normal(ks[9], (N_S5, N_GROUPS, STATE), f32))
    s5_b_re = nrm(ks[10], (N_S5, N_GROUPS, STATE, SSM_GROUP), (2 * SSM_GROUP) ** -0.5)
    s5_b_im = nrm(ks[11], (N_S5, N_GROUPS, STATE, SSM_GROUP), (2 * SSM_GROUP) ** -0.5)
    s5_c_re = nrm(ks[12], (N_S5, N_GROUPS, SSM_GROUP, STATE), (2 * STATE) ** -0.5)
    s5_c_im = nrm(ks[13], (N_S5, N_GROUPS, SSM_GROUP, STATE), (2 * STATE) ** -0.5)
    s5_d = nrm(ks[14], (N_S5, D_SSM), 1.0)
    s5_w_glu = nrm(ks[15], (N_S5, D_SSM, D_SSM), D_SSM ** -0.5)
    s5_w_out = nrm(ks[16], (N_S5, D_SSM, D_MODEL), D_SSM ** -0.5)
    mix_pre_gain = gain(ks[17], (DEPTH, D_MODEL))
    mix_post_gain = gain(ks[18], (DEPTH, D_MODEL))
    ffn_pre_gain = gain(ks[19], (DEPTH, D_MODEL))
    ffn_post_gain = gain(ks[20], (DEPTH, D_MODEL))
    ffn_w_gate = nrm(ks[21], (DEPTH, D_MODEL, D_FF), D_MODEL ** -0.5)
    ffn_w_up = nrm(ks[22], (DEPTH, D_MODEL, D_FF), D_MODEL ** -0.5)
    ffn_w_down = nrm(ks[23], (DEPTH, D_FF, D_MODEL), D_FF ** -0.5)
    return {
        'x': x,
        'fox_w_in': fox_w_in, 'fox_b_f': fox_b_f, 'fox_q_gain': fox_q_gain,
        'fox_k_gain': fox_k_gain, 'fox_w_out': fox_w_out,
        's5_w_in': s5_w_in, 's5_log_dt': s5_log_dt, 's5_lam_re': s5_lam_re,
        's5_lam_im': s5_lam_im, 's5_b_re': s5_b_re, 's5_b_im': s5_b_im,
        's5_c_re': s5_c_re, 's5_c_im': s5_c_im, 's5_d': s5_d,
        's5_w_glu': s5_w_glu, 's5_w_out': s5_w_out,
        'mix_pre_gain': mix_pre_gain, 'mix_post_gain': mix_post_gain,
        'ffn_pre_gain': ffn_pre_gain, 'ffn_post_gain': ffn_post_gain,
        'ffn_w_gate': ffn_w_gate, 'ffn_w_up': ffn_w_up, 'ffn_w_down': ffn_w_down,
    }


def reference(x, fox_w_in, fox_b_f, fox_q_gain, fox_k_gain, fox_w_out,
              s5_w_in, s5_log_dt, s5_lam_re, s5_lam_im, s5_b_re, s5_b_im,
              s5_c_re, s5_c_im, s5_d, s5_w_glu, s5_w_out,
              mix_pre_gain, mix_post_gain, ffn_pre_gain, ffn_post_gain,
              ffn_w_gate, ffn_w_up, ffn_w_down):
    for i in range(DEPTH):
        j = i // N_MIXERS
        h = rmsnorm(x, mix_pre_gain[i])
        if i % N_MIXERS == 0:
            m = fox_mixer(h, fox_w_in[j], fox_b_f[j], fox_q_gain[j], fox_k_gain[j], fox_w_out[j])
        else:
            m = s5_mixer(h, s5_w_in[j], s5_log_dt[j], s5_lam_re[j], s5_lam_im[j],
                         s5_b_re[j], s5_b_im[j], s5_c_re[j], s5_c_im[j], s5_d[j],
                         s5_w_glu[j], s5_w_out[j])
        x = x + rmsnorm(m, mix_post_gain[i])
        h = rmsnorm(x, ffn_pre_gain[i])
        x = x + rmsnorm(swiglu(h, ffn_w_gate[i], ffn_w_up[i], ffn_w_down[i]), ffn_post_gain[i])
    return x
```

```python
import contextlib
import numpy as np
import concourse.bass as bass
import concourse.mybir as mybir
from concourse.bass_utils import run_bass_kernel_spmd

F32 = mybir.dt.float32
BF16 = mybir.dt.bfloat16
I32 = mybir.dt.int32
AF = mybir.ActivationFunctionType
ALU = mybir.AluOpType
AX = mybir.AxisListType


STRICT_SAME_ENGINE = True


class FW:
    def __init__(self, nc, n_dma_sems=6):
        self.nc = nc
        self.stack = contextlib.ExitStack()
        self.E = {}
        self.sems = []
        for name, h in (("pe", nc.tensor), ("act", nc.scalar), ("dve", nc.vector),
                        ("pool", nc.gpsimd), ("sync", nc.sync)):
            e = {"name": name, "h": h, "count": 0, "waited": {}, "dma": [], "dma_i": 0}
            if name != "sync":
                e["sem"] = self._newsem("c_" + name)
            for i in range(n_dma_sems):
                e["dma"].append([self._newsem("d_%s%d" % (name, i)), 0])
            self.E[name] = e
        self.lastw = {}
        self.readers = {}
        self.nwaits = 0

    def _newsem(self, name):
        s = self.stack.enter_context(self.nc.semaphore(name))
        self.sems.append(s)
        return len(self.sems) - 1

    def _wait(self, E, sid, val):
        if E["waited"].get(sid, 0) >= val:
            return
        E["h"].wait_ge(self.sems[sid], val)
        E["waited"][sid] = val
        self.nwaits += 1

    def _sync(self, E, reads, writes, attach=False):
        need = {}

        def add(ev):
            if ev is None:
                return
            sid, val = ev
            if need.get(sid, 0) < val:
                need[sid] = val

        for r in reads:
            add(self.lastw.get(r))
        for w in writes:
            add(self.lastw.get(w))
            for sid, val in self.readers.get(w, {}).items():
                add((sid, val))
        own = E.get("sem")
        todo = []
        for sid, val in need.items():
            if sid == own:
                if E["name"] == "pe":
                    continue
                if STRICT_SAME_ENGINE is False and E["name"] in ("dve", "act") and val < E["count"]:
                    continue
            if E["waited"].get(sid, 0) >= val:
                continue
            todo.append((sid, val))
        held = None
        if attach and todo:
            held = todo.pop()
        for sid, val in todo:
            self._wait(E, sid, val)
        return held

    def _attach(self, E, ins, held):
        if held is not None:
            sid, val = held
            ins._wait_ge(self.sems[sid], val)
            E["waited"][sid] = val
            self.nwaits += 1

    def _record(self, ev, reads, writes):
        sid, val = ev
        for r in reads:
            d = self.readers.setdefault(r, {})
            if d.get(sid, 0) < val:
                d[sid] = val
        for w in writes:
            self.lastw[w] = ev
            self.readers[w] = {}

    def op(self, en, fn, reads=(), writes=()):
        E = self.E[en]
        held = self._sync(E, reads, writes, attach=(en != "pe"))
        ins = fn(E["h"])
        self._attach(E, ins, held)
        E["count"] += 1
        ins.then_inc(self.sems[E["sem"]], 1)
        self._record((E["sem"], E["count"]), reads, writes)
        return ins

    def pe(self, fn, reads=(), writes=()):
        return self.op("pe", fn, reads, writes)

    def act(self, fn, reads=(), writes=()):
        return self.op("act", fn, reads, writes)

    def dve(self, fn, reads=(), writes=()):
        return self.op("dve", fn, reads, writes)

    def pool(self, fn, reads=(), writes=()):
        return self.op("pool", fn, reads, writes)

    def pe_group(self, fns, reads=(), writes=()):
        E = self.E["pe"]
        held = self._sync(E, reads, writes, attach=False)
        ins = None
        for n, fn in enumerate(fns):
            ins = fn(E["h"])
            if n == 0:
                self._attach(E, ins, held)
        E["count"] += 1
        ins.then_inc(self.sems[E["sem"]], 1)
        self._record((E["sem"], E["count"]), reads, writes)

    def dma(self, q, out, in_, reads=(), writes=(), **kw):
        E = self.E[q]
        self._sync(E, reads, writes)
        slot = E["dma"][E["dma_i"] % len(E["dma"])]
        E["dma_i"] += 1
        sid, target = slot
        if target > 0:
            self._wait(E, sid, target)
        E["h"].dma_start(out=out, in_=in_, **kw).then_inc(self.sems[sid], 16)
        slot[1] = target + 16
        self._record((sid, slot[1]), reads, writes)

    def barrier(self):
        evs = []
        for e in self.E.values():
            for sid, target in e["dma"]:
                if target > 0:
                    evs.append((sid, target))
            if "sem" in e and e["count"] > 0:
                evs.append((e["sem"], e["count"]))
        for E in self.E.values():
            for sid, val in evs:
                self._wait(E, sid, val)

    def finish(self):
        S = self.E["sync"]
        for e in self.E.values():
            for sid, target in e["dma"]:
                if target > 0:
                    self._wait(S, sid, target)
            if "sem" in e and e["count"] > 0:
                self._wait(S, e["sem"], e["count"])
        self.stack.close()


EPS = 1e-6
D = 1024
DFF = 2816
NFC = DFF // 128


class Env:
    def __init__(self, nc, fw, st):
        self.nc, self.fw, self.st = nc, fw, st
        self.ps = st.enter_context(nc.psum_tensor("psall", [128, 8, 512], F32))
        self.identb = self.sb("identb", [128, 128], BF16)
        self.identf = self.sb("identf", [128, 128], F32)
        self.mhalf = self.sb("mhalf", [128, 1], F32)
        self.stats = self.sb("stats", [128, 96], F32)
        self.junk = self.sb("junk", [128, 1024], BF16)
        self.si = 0

    def sb(self, name, shape, dt, st=None):
        return (st or self.st).enter_context(self.nc.sbuf_tensor(name, shape, dt))

    def bank(self, i, n=1):
        if n == 1:
            return self.ps[:, i, :]
        return self.ps[:, i:i + n, :]

    def stat(self):
        i = self.si % 96
        self.si += 1
        return self.stats[:, i:i + 1], "st%d" % i

    def init_consts(self, ident_dram):
        fw = self.fw
        fw.dma("sync", self.identf[:], ident_dram, writes=["identf"])
        fw.dve(lambda e: e.tensor_copy(self.identb[:], self.identf[:]), reads=["identf"], writes=["identb"])
        fw.dve(lambda e: e.memset(self.mhalf[:], -0.5), writes=["mhalf"])

    def rstd(self, src_ap, src_key, n):
        fw = self.fw
        P = src_ap.shape[0]
        ss, kss = self.stat()
        var, kvar = self.stat()
        rs, krs = self.stat()
        junk = self.junk[0:P, 0:n]
        sk = list(src_key) if isinstance(src_key, (list, tuple)) else [src_key]
        fw.act(lambda e: e.activation(out=junk, in_=src_ap, func=AF.Square, accum_out=ss[0:P, :]),
               reads=sk, writes=[kss])
        fw.dve(lambda e: e.tensor_scalar(out=var[0:P, :], in0=ss[0:P, :], scalar1=1.0 / n, scalar2=EPS,
                                         op0=ALU.mult, op1=ALU.add), reads=[kss], writes=[kvar])
        fw.pool(lambda e: e.tensor_tensor(out=rs[0:P, :], in0=var[0:P, :], in1=self.mhalf[0:P, :], op=ALU.pow),
                reads=[kvar, "mhalf"], writes=[krs])
        return rs, krs


def load_w_cast(fw, dst_tile, dst_key, w_dram, kchunks, first=False):
    keys = []
    for k in range(kchunks):
        kk = "%s_k%d" % (dst_key, k)
        fw.dma("pool", dst_tile[:, k, :], w_dram[k * 128:(k + 1) * 128, :], writes=[kk])
        keys.append(kk)
    return keys


def load_w_cast_cols(fw, dst_tile, dst_key, w_dram, kchunks, col_groups):
    src = w_dram.rearrange("(k p) c -> p k c", p=128)
    keys = {}
    for gi, (c0, c1) in enumerate(col_groups):
        kk = "%s_c%d" % (dst_key, gi)
        fw.dma("pool", dst_tile[:, 0:kchunks, c0:c1], src[:, :, c0:c1], writes=[kk])
        keys[gi] = kk
    return keys


def norm_transpose(env, x_ap, x_key, g_ap, g_key, hb, hb_key, hT, hT_key, col0, psT_bank):
    fw = env.fw
    rs, krs = env.rstd(x_ap, x_key, D)
    fw.dve(lambda e: e.scalar_tensor_tensor(out=hb, in0=x_ap, scalar=rs, in1=g_ap, op0=ALU.mult, op1=ALU.mult),
           reads=[x_key, krs, g_key], writes=[hb_key])
    transpose_in(env, hb, hb_key, hT, hT_key, col0, psT_bank)


def transpose_in(env, hb, hb_key, hT, hT_key, col0, psT_bank, on="act"):
    fw = env.fw
    pkey = "ps%d" % psT_bank
    psT = env.bank(psT_bank).bitcast(BF16)
    fns = []
    for j in range(8):
        fns.append(lambda e, j=j: e.transpose(psT[:, j * 128:(j + 1) * 128], hb[:, j * 128:(j + 1) * 128],
                                               env.identb[:]))
    fw.pe_group(fns, reads=[hb_key, "identb"], writes=[pkey])
    src = psT.rearrange("p (j c) -> p j c", j=8)
    dst = hT[:, 0:8, col0:col0 + 128]
    if on == "act":
        fw.act(lambda e: e.copy(dst, src), reads=[pkey], writes=[hT_key])
    else:
        fw.dve(lambda e: e.tensor_copy(dst, src), reads=[pkey], writes=[hT_key])


def post_norm_residual(env, ps_ap, ps_key, g_ap, g_key, xr, xr_key, tmp, tmp_key, xo, xo_key):
    fw = env.fw
    pk = list(ps_key) if isinstance(ps_key, (list, tuple)) else [ps_key]
    rs, krs = env.rstd(ps_ap, pk, D)
    fw.dve(lambda e: e.scalar_tensor_tensor(out=tmp, in0=ps_ap, scalar=rs, in1=g_ap, op0=ALU.mult, op1=ALU.mult),
           reads=pk + [krs, g_key], writes=[tmp_key])
    fw.dve(lambda e: e.tensor_tensor(out=xo, in0=tmp, in1=xr, op=ALU.add),
           reads=[tmp_key, xr_key], writes=[xo_key])


def ffn_phase(env, x_in, x_out, wg, wu, wd, pre_g, post_g, ntok, tag):
    nc, fw = env.nc, env.fw
    fw.barrier()
    with contextlib.ExitStack() as st:
        Wg = env.sb(tag + "Wg", [128, 8, DFF], BF16, st)
        Wu = env.sb(tag + "Wu", [128, 8, DFF], BF16, st)
        Wd = env.sb(tag + "Wd", [128, NFC, D], BF16, st)
        gpre = env.sb(tag + "gpre", [128, D], F32, st)
        gpost = env.sb(tag + "gpost", [128, D], F32, st)
        XT = [env.sb(tag + "xt%d" % i, [128, D], F32, st) for i in range(2)]
        XR = [env.sb(tag + "xr%d" % i, [128, D], F32, st) for i in range(2)]
        HB = [env.sb(tag + "hb%d" % i, [128, D], BF16, st) for i in range(2)]
        hT = env.sb(tag + "hT", [128, 8, 512], BF16, st)
        aT = env.sb(tag + "aT", [128, NFC, 512], BF16, st)
        SG = [env.sb(tag + "sg%d" % i, [128, 512], F32, st) for i in range(2)]
        TMP = env.sb(tag + "tmp", [128, D], F32, st)
        fw.dma("sync", gpre[:], pre_g.partition_broadcast(128), writes=[tag + "gpre"])
        fw.dma("sync", gpost[:], post_g.partition_broadcast(128), writes=[tag + "gpost"])
        grp = [(c, min(c + 256, DFF)) for c in range(0, DFF, 256)]
        kWg, kWu = {}, {}
        srcg = wg.rearrange("(k p) c -> p k c", p=128)
        srcu = wu.rearrange("(k p) c -> p k c", p=128)
        for gi, (c0, c1) in enumerate(grp):
            kWg[gi] = "%sWg_c%d" % (tag, gi)
            kWu[gi] = "%sWu_c%d" % (tag, gi)
            fw.dma("pool", Wg[:, :, c0:c1], srcg[:, :, c0:c1], writes=[kWg[gi]])
            fw.dma("pool", Wu[:, :, c0:c1], srcu[:, :, c0:c1], writes=[kWu[gi]])
        kWd = load_w_cast(fw, Wd, tag + "Wd", wd, NFC)
        ntiles = ntok // 512
        it = [0]

        def build_hT(t):
            for s in range(4):
                r0 = t * 512 + s * 128
                i = it[0] % 2
                it[0] += 1
                xt, kx = XT[i], tag + "xt%d" % i
                hb, khb = HB[i], tag + "hb%d" % i
                fw.dma("sync", xt[:], x_in[r0:r0 + 128, :], writes=[kx])
                norm_transpose(env, xt[:], kx, gpre[:], tag + "gpre", hb[:], khb, hT, tag + "hT", s * 128, 0)

        build_hT(0)
        for t in range(ntiles):
            akeys = []
            for fc in range(NFC):
                bg, bu = fc % 2, 2 + fc % 2
                fs = slice(fc * 128, (fc + 1) * 128)
                fw.pe_group([lambda e, k=k: e.matmul(env.bank(bg), Wg[:, k, fs], hT[:, k, :], start=(k == 0),
                                                     stop=(k == 7)) for k in range(8)],
                            reads=[kWg[fc // 2], tag + "hT"], writes=["ps%d" % bg])
                fw.pe_group([lambda e, k=k: e.matmul(env.bank(bu), Wu[:, k, fs], hT[:, k, :], start=(k == 0),
                                                     stop=(k == 7)) for k in range(8)],
                            reads=[kWu[fc // 2], tag + "hT"], writes=["ps%d" % bu])
                sg = SG[fc % 2]
                ksg = tag + "sg%d" % (fc % 2)
                fw.act(lambda e: e.activation(out=sg[:], in_=env.bank(bg), func=AF.Silu),
                       reads=["ps%d" % bg], writes=[ksg])
                ka = tag + "aT%d" % fc
                fw.dve(lambda e: e.tensor_tensor(out=aT[:, fc, :], in0=sg[:], in1=env.bank(bu), op=ALU.mult),
                       reads=[ksg, "ps%d" % bu], writes=[ka])
                akeys.append(ka)
            if t + 1 < ntiles:
                build_hT(t + 1)
            for s in range(4):
                r0 = t * 512 + s * 128
                xr = XR[s % 2]
                kxr = tag + "xr%d" % (s % 2)
                fw.dma("sync", xr[:], x_in[r0:r0 + 128, :], writes=[kxr])
                pb0 = 4 + 2 * (s % 2)
                pso = env.bank(pb0, 2)
                pk = ["ps%d" % pb0, "ps%d" % (pb0 + 1)]
                for hf in range(2):
                    fw.pe_group([lambda e, fc=fc: e.matmul(pso[:, hf, :], aT[:, fc, s * 128:(s + 1) * 128],
                                                           Wd[:, fc, hf * 512:(hf + 1) * 512], start=(fc == 0),
                                                           stop=(fc == NFC - 1)) for fc in range(NFC)],
                                reads=akeys + kWd, writes=pk)
                post_norm_residual(env, pso, pk, gpost[:], tag + "gpost", xr[:], kxr, TMP[:], tag + "tmp",
                                   xr[:], kxr)
                fw.dma("sync", x_out[r0:r0 + 128, :], xr[:], reads=[kxr], writes=[tag + "xout"])


NH = 16
HD = 64
WARM = 0
DATT = 1024
SEQ = 2048
MASKNEG = -240000.0


def fox_phase(env, x_in, x_out, w_in, b_f, q_gain, k_gain, w_out, pre_g, post_g, consts, oT_d, nseq, tag="fx"):
    nc, fw = env.nc, env.fw
    fw.barrier()
    with contextlib.ExitStack() as st:
        T = lambda s: tag + s
        Win = env.sb(tag + "Win", [128, 8, 4112], BF16, st)
        Wo = env.sb(tag + "Wo", [128, 8, D], BF16, st)
        gpre = env.sb(tag + "gpre", [128, D], F32, st)
        gpost = env.sb(tag + "gpost", [128, D], F32, st)
        hT = env.sb(tag + "hT", [128, 8, SEQ], BF16, st)
        XT = [env.sb(tag + "xt%d" % i, [128, D], F32, st) for i in range(2)]
        HB = [env.sb(tag + "hb%d" % i, [128, D], BF16, st) for i in range(2)]
        TMP = env.sb(tag + "tmp", [128, D], F32, st)
        scr = env.sb(tag + "scr", [128, 4096], F32, st)
        lf = scr[0:16, 0:SEQ]
        cT = scr[0:16, SEQ:2 * SEQ]
        QA = [env.sb(tag + "qa%d" % i, [128, SEQ], BF16, st) for i in range(2)]
        KA = [env.sb(tag + "ka%d" % i, [128, SEQ], BF16, st) for i in range(2)]
        QA += [scr[:, 0:1024].bitcast(BF16), scr[:, 1024:2048].bitcast(BF16)]
        KA += [scr[:, 2048:3072].bitcast(BF16), scr[:, 3072:4096].bitcast(BF16)]
        KQ = [[T("qa0")], [T("qa1")], [T("qa2"), T("lf")], [T("qa3"), T("lf")]]
        KK = [[T("ka0")], [T("ka1")], [T("ka2"), T("cT")], [T("ka3"), T("cT")]]
        gT = env.sb(tag + "gT", [128, SEQ], BF16, st)
        V2e = env.sb(tag + "V2e", [128, 16, 128], BF16, st)
        V2o = TMP[:].bitcast(BF16).rearrange("p (a b) -> p a b", a=16)
        PT = [env.sb(tag + "pT%d" % i, [128, 512], BF16, st) for i in range(2)]
        SQ = [env.sb(tag + "sq%d" % i, [128, 512], BF16, st) for i in range(2)]
        RST = [env.sb(tag + "rst%d" % i, [128, 512], F32, st) for i in range(2)]
        bones = env.sb(tag + "bones", [128, 128], BF16, st)
        maskf = env.sb(tag + "maskf", [128, 128], F32, st)
        maskb = env.sb(tag + "maskb", [128, 128], BF16, st)
        qg = env.sb(tag + "qg", [128, 1], F32, st)
        kg = env.sb(tag + "kg", [128, 1], F32, st)
        nbf = env.sb(tag + "nbf", [16, 1], F32, st)
        epsc = env.sb(tag + "epsc", [128, 1], F32, st)
        ones16 = env.sb(tag + "ones16", [16, 512], F32, st)
        csp = env.sb(tag + "csp", [96, SEQ], BF16, st)
        cspt = env.sb(tag + "cspt", [16, SEQ], BF16, st)
        negc = env.sb(tag + "negc", [128, 16, 16], F32, st)
        rl = env.sb(tag + "rl", [128, 512], F32, st)
        og = env.sb(tag + "og", [128, 512], F32, st)
        OTS = [env.sb(tag + "ots%d" % i, [128, 512], BF16, st) for i in range(2)]
        OTL = [env.sb(tag + "oTl0", [128, 8, 128], BF16, st),
               rl[:].bitcast(BF16).rearrange("p (h t) -> p h t", h=8)]

        SER = T("ser")
        fw.dma("sync", gpre[:], pre_g.partition_broadcast(128), writes=[T("gpre"), SER])
        fw.dma("sync", gpost[:], post_g.partition_broadcast(128), writes=[T("gpost"), SER])
        fw.dma("sync", maskf[:], consts, writes=[T("maskf"), SER])
        for half in range(2):
            fw.dma("sync", qg[half * 64:(half + 1) * 64, :], q_gain.rearrange("(p o) -> p o", o=1),
                   writes=[T("qg"), SER])
            fw.dma("sync", kg[half * 64:(half + 1) * 64, :], k_gain.rearrange("(p o) -> p o", o=1),
                   writes=[T("kg"), SER])
        fw.dma("sync", nbf[:], b_f.rearrange("(p o) -> p o", o=1), writes=[T("nbf"), SER])
        fw.dve(lambda e: e.tensor_copy(maskb[:], maskf[:]), reads=[T("maskf"), SER], writes=[T("maskb")])
        fw.dve(lambda e: e.tensor_scalar(out=nbf[:], in0=nbf[:], scalar1=-1.0, scalar2=None, op0=ALU.mult),
               reads=[T("nbf")], writes=[T("nbf")])
        fw.dve(lambda e: e.memset(epsc[:], EPS), writes=[T("epsc")])
        fw.dve(lambda e: e.memset(bones[:], 0.0), writes=[T("bones")])
        fw.dve(lambda e: e.memset(bones[0:64, 0:64], 1.0), writes=[T("bones")])
        fw.dve(lambda e: e.memset(bones[64:128, 64:128], 1.0), writes=[T("bones")])
        fw.dve(lambda e: e.memset(ones16[:], 1.0), writes=[T("ones16")])
        for i in range(4):
            fw.dve(lambda e: e.memset(KA[i][64:128, :], 1.0), writes=KK[i])
            fw.dve(lambda e: e.memset(QA[i][64:128, :], 0.0), writes=KQ[i])
        fw.dve(lambda e: e.memset(V2e[:, :, 64:128], 1.0), writes=[T("V2e")])
        kWin = load_w_cast(fw, Win, T("Win"), w_in, 8)
        kWo = load_w_cast(fw, Wo, T("Wo"), w_out, 8)

        cnt = [0]

        def proj_gen(j):
            sl = [2 * (j % 2), 2 * (j % 2) + 1]
            for (DST, KD, off, gain, kgain) in ((QA, KQ, 0, qg, T("qg")), (KA, KK, 1024, kg, T("kg"))):
                for t in range(4):
                    ts = slice(t * 512, (t + 1) * 512)
                    c = cnt[0]
                    cnt[0] += 1
                    pb = 2 + c % 2
                    sq, ksq = SQ[c % 2], T("sq%d" % (c % 2))
                    rst, krst = RST[c % 2], T("rst%d" % (c % 2))
                    fw.pe_group([lambda e, k=k: e.matmul(env.bank(pb), Win[:, k, off + j * 128:off + (j + 1) * 128],
                                                         hT[:, k, ts], start=(k == 0), stop=(k == 7))
                                 for k in range(8)], reads=kWin + [T("hT")], writes=["ps%d" % pb])
                    fw.act(lambda e: e.activation(out=sq[:], in_=env.bank(pb), func=AF.Square),
                           reads=["ps%d" % pb], writes=[ksq])
                    yield
                    fw.pe(lambda e: e.matmul(env.bank(6), bones[:], sq[:], start=True, stop=True),
                          reads=[T("bones"), ksq], writes=["ps6"])
                    fw.act(lambda e: e.activation(out=rst[:], in_=env.bank(6), func=AF.Ln, bias=epsc[:],
                                                  scale=1.0 / HD), reads=["ps6", T("epsc")], writes=[krst])
                    fw.act(lambda e: e.activation(out=rst[:], in_=rst[:], func=AF.Exp, scale=-0.5),
                           reads=[krst], writes=[krst])
                    for par in range(2):
                        rows = slice(par * 64, (par + 1) * 64)
                        dst = DST[sl[par]]
                        fw.dve(lambda e: e.scalar_tensor_tensor(out=dst[0:64, ts], in0=env.bank(pb)[rows, :],
                                                                scalar=gain[rows, :], in1=rst[rows, :], op0=ALU.mult,
                                                                op1=ALU.mult),
                               reads=["ps%d" % pb, kgain, krst], writes=KD[sl[par]])
                    yield
            for par in range(2):
                h = 2 * j + par
                for l_ in range(3):
                    fw.dma("sync", QA[sl[par]][64 + l_:65 + l_, :], csp[32 * l_ + h:32 * l_ + h + 1, :],
                           reads=[T("csp")], writes=KQ[sl[par]])
            yield

        def gv_emit(j):
            for t in range(4):
                ts = slice(t * 512, (t + 1) * 512)
                pb = 2 + t % 2
                fw.pe_group([lambda e, k=k: e.matmul(env.bank(pb), Win[:, k, 3072 + j * 128:3072 + (j + 1) * 128],
                                                     hT[:, k, ts], start=(k == 0), stop=(k == 7)) for k in range(8)],
                            reads=kWin + [T("hT")], writes=["ps%d" % pb])
                fw.act(lambda e: e.activation(out=gT[:, ts], in_=env.bank(pb), func=AF.Sigmoid),
                       reads=["ps%d" % pb], writes=[T("gT")])
            for g4 in range(4):
                fns = []
                for kk in range(4):
                    kt = g4 * 4 + kk
                    for k in range(8):
                        fns.append(lambda e, k=k, kk=kk, kt=kt: e.matmul(
                            env.bank(7)[:, kk * 128:(kk + 1) * 128], hT[:, k, kt * 128:(kt + 1) * 128],
                            Win[:, k, 2048 + j * 128:2048 + (j + 1) * 128], start=(k == 0), stop=(k == 7)))
                fw.pe_group(fns, reads=kWin + [T("hT")], writes=["ps7"])
                src = env.bank(7).rearrange("p (a b) -> p a b", a=4)
                fw.act(lambda e: e.copy(V2e[:, g4 * 4:(g4 + 1) * 4, 0:64], src[:, :, 0:64]),
                       reads=["ps7"], writes=[T("V2e")])
                fw.act(lambda e: e.copy(V2o[:, g4 * 4:(g4 + 1) * 4, 64:128], src[:, :, 64:128]),
                       reads=["ps7"], writes=[T("tmp")])

        def attention(sq_i, h, nxt, every, exhaust):
            par = h % 2
            slot = 2 * ((h // 2) % 2) + par
            qa, ka = QA[slot], KA[slot]
            kqa, kka = KQ[slot], KK[slot]
            V2, kV2 = (V2e, T("V2e")) if par == 0 else (V2o, T("tmp"))
            orow = slice(par * 64, (par + 1) * 64)
            lrow = slice((1 - par) * 64, (2 - par) * 64)
            items = [(qt, kt) for qt in range(4) for kt in range(4 * qt + 4)]

            def geom(i):
                qt, kt = items[i]
                j = kt - 4 * qt
                return qt, kt, j, max(0, j) * 128

            def S(i):
                qt, kt, j, col0 = geom(i)
                sb_ = i % 2
                q0 = qt * 512
                fns = [lambda e: e.matmul(env.bank(sb_)[:, col0:512], ka[0:67, kt * 128:(kt + 1) * 128],
                                          qa[0:67, q0 + col0:q0 + 512], start=True, stop=(j < 0))]
                if j >= 0:
                    fns.append(lambda e: e.matmul(env.bank(sb_)[:, col0:col0 + 128], env.identb[:], maskb[:],
                                                  start=False, stop=True))
                fw.pe_group(fns, reads=kka + kqa + [T("maskb"), "identb"], writes=["ps%d" % sb_])

            S(0)
            for i in range(len(items)):
                qt, kt, j, col0 = geom(i)
                nkt = 4 * qt + 4
                accb = 4 + (qt % 2)
                sb_ = i % 2
                q0 = qt * 512
                if i + 1 < len(items):
                    S(i + 1)
                pT, kpT = PT[i % 2], T("pT%d" % (i % 2))
                fw.act(lambda e: e.activation(out=pT[:, col0:512], in_=env.bank(sb_)[:, col0:512], func=AF.Exp,
                                              bias=negc[:, kt, h:h + 1], scale=0.125),
                       reads=["ps%d" % sb_, T("negc")], writes=[kpT])
                fw.pe(lambda e: e.matmul(env.bank(accb)[:, col0:512], V2[:, kt, :], pT[:, col0:512],
                                         start=(kt == 0), stop=(kt == nkt - 1)),
                      reads=[kV2, kpT], writes=["ps%d" % accb])
                if kt == nkt - 1:
                    fw.dve(lambda e: e.reciprocal(rl[lrow, :], env.bank(accb)[lrow, :]),
                           reads=["ps%d" % accb], writes=[T("rl")])
                    fw.dve(lambda e: e.tensor_tensor(out=og[orow, :], in0=env.bank(accb)[orow, :], in1=rl[lrow, :],
                                                     op=ALU.mult), reads=["ps%d" % accb, T("rl")], writes=[T("og")])
                    ots, kots = OTS[qt % 2], T("ots%d" % (qt % 2))
                    fw.dve(lambda e: e.tensor_tensor(out=ots[orow, :], in0=og[orow, :], in1=gT[orow, q0:q0 + 512],
                                                     op=ALU.mult), reads=[T("og"), T("gT")], writes=[kots])
                    fw.dma("sync", oT_d[sq_i, h, :, q0:q0 + 512], ots[orow, :], reads=[kots], writes=[T("oTd")])
                if nxt is not None and i % every == every - 1:
                    next(nxt, None)
            if nxt is not None and exhaust:
                for _ in nxt:
                    pass

        it = 0
        for sq_i in range(nseq):
            base = sq_i * SEQ
            for s in range(SEQ // 128):
                r0 = base + s * 128
                xt, kx = XT[it % 2], T("xt%d" % (it % 2))
                hb, khb = HB[it % 2], T("hb%d" % (it % 2))
                fw.dma("sync", xt[:], x_in[r0:r0 + 128, :], writes=[kx])
                norm_transpose(env, xt[:], kx, gpre[:], T("gpre"), hb[:], khb, hT, T("hT"), s * 128, 6)
                it += 1
            for t in range(4):
                ts = slice(t * 512, (t + 1) * 512)
                fw.pe_group([lambda e, k=k: e.matmul(env.bank(7)[0:16, :], Win[:, k, 4096:4112], hT[:, k, ts],
                                                     start=(k == 0), stop=(k == 7)) for k in range(8)],
                            reads=kWin + [T("hT")], writes=["ps7"])
                fw.act(lambda e: e.activation(out=lf[:, ts], in_=env.bank(7)[0:16, :], func=AF.Exp, bias=nbf[:],
                                              scale=-1.0), reads=["ps7", T("nbf")], writes=[T("lf")])
            fw.act(lambda e: e.activation(out=lf, in_=lf, func=AF.Ln, bias=1.0, scale=1.0),
                   reads=[T("lf")], writes=[T("lf")])
            for t in range(4):
                ts = slice(t * 512, (t + 1) * 512)
                init = 0.0 if t == 0 else cT[:, t * 512 - 1:t * 512]
                fw.dve(lambda e: e.tensor_tensor_scan(out=cT[:, ts], data0=ones16[:], data1=lf[:, ts], initial=init,
                                                      op0=ALU.mult, op1=ALU.add),
                       reads=[T("lf"), T("ones16"), T("cT")], writes=[T("cT")])
            for kt in range(16):
                fw.pe(lambda e: e.transpose(env.bank(7)[:, kt * 16:(kt + 1) * 16], cT[:, kt * 128:(kt + 1) * 128],
                                            env.identf[0:16, 0:16]),
                      reads=[T("cT"), "identf"], writes=["ps7"])
            fw.dve(lambda e: e.tensor_copy(negc[:].rearrange("p a b -> p (a b)"), env.bank(7)[:, 0:256]),
                   reads=["ps7"], writes=[T("negc")])
            fw.dve(lambda e: e.tensor_scalar(out=lf, in0=cT, scalar1=-8.0, scalar2=None, op0=ALU.mult),
                   reads=[T("cT"), T("lf")], writes=[T("lf")])
            for lvl in range(3):
                cl = csp[32 * lvl:32 * lvl + 16, :]
                fw.dve(lambda e: e.tensor_copy(cspt[:], lf), reads=[T("lf"), T("cspt")], writes=[T("cspt")])
                fw.act(lambda e: e.copy(cl, cspt[:]), reads=[T("cspt")], writes=[T("csp")])
                if lvl < 2:
                    fw.dve(lambda e: e.tensor_tensor(out=lf, in0=lf, in1=cspt[:], op=ALU.subtract),
                           reads=[T("lf"), T("cspt")], writes=[T("lf")])
            fw.dve(lambda e: e.memset(V2o[:, :, 0:64], 1.0), reads=[T("tmp")], writes=[T("tmp")])

            for _ in proj_gen(0):
                pass
            for j in range(NH // 2):
                gv_emit(j)
                nxt = proj_gen(j + 1) if j + 1 < NH // 2 else None
                attention(sq_i, 2 * j, nxt, 4, False)
                attention(sq_i, 2 * j + 1, nxt, 4, True)
            for s in range(SEQ // 128):
                r0 = base + s * 128
                ol, kol = OTL[s % 2], (T("oTl0") if s % 2 == 0 else T("rl"))
                for two in range(2):
                    fw.dma("sync", ol[two * 64:(two + 1) * 64, :, :],
                           oT_d[sq_i, :, :, s * 128:(s + 1) * 128].rearrange("(hp two) d t -> two d hp t", two=2)[two],
                           reads=[T("oTd")], writes=[kol])
                xr, kxr = XT[s % 2], T("xt%d" % (s % 2))
                fw.dma("sync", xr[:], x_in[r0:r0 + 128, :], writes=[kxr])
                pb0 = 2 if s % 2 == 0 else 6
                pso = env.bank(pb0, 2)
                pk = ["ps%d" % pb0, "ps%d" % (pb0 + 1)]
                for hf in range(2):
                    fw.pe_group([lambda e, h=h: e.matmul(pso[:, hf, :], ol[:, h, :],
                                                         Wo[:, h, hf * 512:(hf + 1) * 512], start=(h == 0),
                                                         stop=(h == 7)) for h in range(8)],
                                reads=[kol] + kWo, writes=pk)
                post_norm_residual(env, pso, pk, gpost[:], T("gpost"), xr[:], kxr, TMP[:], T("tmp"), xr[:], kxr)
                fw.dma("sync", x_out[r0:r0 + 128, :], xr[:], reads=[kxr], writes=[T("xout")])


NG = 64
GP = 32
DBG = {}


def dbgdump(env, name, ap, keys):
    if name in DBG:
        env.fw.dma("sync", DBG[name], ap, reads=keys, writes=["dbg_" + name])

TWO_PI = 6.283185307179586


def s5_phase(env, x_in, x_out, w_in, log_dt, lam_re, lam_im, b_re, b_im, c_re, c_im, d_skip, w_glu, w_out,
             pre_g, post_g, nseq, tag="s5"):
    nc, fw = env.nc, env.fw
    T = lambda s: tag + s
    NTOK = nseq * SEQ
    fw.barrier()
    with contextlib.ExitStack() as st:
        uT = env.sb(T("uT"), [128, 8, NTOK], BF16, st)
        LB = [env.sb(T("LB%d" % i), [128, GP, 128], BF16, st) for i in range(3)]
        LC = [env.sb(T("LC%d" % i), [128, GP, 128], BF16, st) for i in range(3)]
        CM = env.sb(T("CM"), [128, GP, 11], F32, st)
        SM = env.sb(T("SM"), [128, GP, 11], F32, st)
        RR = env.sb(T("RR"), [128, GP], F32, st)
        dvec = env.sb(T("dvec"), [128, 8], F32, st)
        SER = T("ser")

        with contextlib.ExitStack() as s0:
            P = {}
            for nm in ("lre", "lim", "ldt", "dt", "lr", "th", "mag", "f", "s4", "c2", "s2", "sn", "cs", "are", "aim",
                       "den", "nre", "zre", "zim", "t0", "t1"):
                P[nm] = env.sb(T("p_" + nm), [128, GP], F32, s0)
            fi = env.sb(T("p_fi"), [128, GP], I32, s0)
            braw = [env.sb(T("braw%d" % i), [128, GP, 16], F32, s0) for i in range(2)]
            bb = [env.sb(T("bb%d" % i), [128, GP, 16], F32, s0) for i in range(2)]
            Bpad = [env.sb(T("Bpad%d" % i), [128, GP, 128], F32, s0) for i in range(2)]
            Cc = [env.sb(T("Cc%d" % i), [128, 8, 128], F32, s0) for i in range(2)]
            MQ = env.sb(T("MQ"), [128, 4, 128], F32, s0)
            LG = [env.sb(T("LG%d" % i), [64, 128], F32, s0) for i in range(2)]
            LD = env.sb(T("LD"), [128, 64], F32, s0)
            DV = env.sb(T("DV"), [8, 128], F32, s0)
            mhg = env.sb(T("mhg"), [128, GP], F32, s0)
            for i, lsrc in enumerate((lam_re, lam_im)):
                for dup in range(2):
                    fw.dma("sync", LG[i][:, dup * 64:(dup + 1) * 64], lsrc, writes=[T("LG%d" % i), SER])
            fw.dma("sync", LD[:], log_dt.partition_broadcast(128), writes=[T("LD"), SER])
            fw.dma("sync", DV[:], d_skip.rearrange("(blk c) -> blk c", c=128), writes=[T("DV"), SER])
            for gi in range(2):
                rows = slice(gi * 64, (gi + 1) * 64)
                for i, bsrc in enumerate((b_re, b_im)):
                    for g0 in range(0, GP, 8):
                        fw.dma("sync", braw[i][rows, g0:g0 + 8, :],
                               bsrc.rearrange("(gp gi) p c -> gi p gp c", gi=2)[gi][:, g0:g0 + 8, :],
                               writes=[T("braw%d" % i), SER])
                for i, csrc in enumerate((c_re, c_im)):
                    for b0 in range(0, 8, 4):
                        fw.dma("sync", Cc[i][:, b0:b0 + 4, rows],
                               csrc.rearrange("(blk gl) c p -> (gl c) blk p", gl=8)[:, b0:b0 + 4, :],
                               writes=[T("Cc%d" % i), SER])
            fw.dve(lambda e: e.memset(mhg[:], -0.5), reads=[SER], writes=[T("mhg")])
            for i, nm in enumerate(("lre", "lim")):
                fw.pe(lambda e: e.transpose(env.bank(7)[:, 0:64], LG[i][:, :], env.identf[0:64, 0:64]),
                      reads=[T("LG%d" % i), "identf", SER], writes=["ps7"])
                fw.dve(lambda e: e.tensor_copy(P[nm][0:64, :], env.bank(7)[0:64, 0:64:2]), reads=["ps7"],
                       writes=[T(nm)])
                fw.dve(lambda e: e.tensor_copy(P[nm][64:128, :], env.bank(7)[64:128, 1:64:2]), reads=["ps7"],
                       writes=[T(nm)])
            fw.dve(lambda e: e.tensor_copy(P["ldt"][0:64, :], LD[0:64, 0:64:2]), reads=[T("LD"), SER], writes=[T("ldt")])
            fw.dve(lambda e: e.tensor_copy(P["ldt"][64:128, :], LD[64:128, 1:64:2]), reads=[T("LD"), SER],
                   writes=[T("ldt")])
            fw.pe(lambda e: e.transpose(env.bank(7)[:, 64:72], DV[:, :], env.identf[0:8, 0:8]),
                  reads=[T("DV"), "identf", SER], writes=["ps7"])
            fw.dve(lambda e: e.tensor_copy(dvec[:], env.bank(7)[:, 64:72]), reads=["ps7"], writes=[T("dvec")])

            def ew(eng, out, a, b, op, keys):
                getattr(fw, eng)(lambda e: e.tensor_tensor(out=out, in0=a, in1=b, op=op), reads=keys, writes=keys)

            K = [T("setup")]
            dep = [T("lre"), T("lim"), T("ldt"), SER] + K
            fw.act(lambda e: e.activation(out=P["dt"][:], in_=P["ldt"][:], func=AF.Exp), reads=dep, writes=K)
            ew("dve", P["lr"][:], P["lre"][:], P["dt"][:], ALU.mult, dep)
            ew("dve", P["th"][:], P["lim"][:], P["dt"][:], ALU.mult, dep)
            fw.act(lambda e: e.activation(out=P["mag"][:], in_=P["lr"][:], func=AF.Exp), reads=K, writes=K)
            fw.dve(lambda e: e.tensor_scalar(out=P["t0"][:], in0=P["th"][:], scalar1=1.0 / TWO_PI, scalar2=None,
                                             op0=ALU.mult), reads=K, writes=K)
            fw.dve(lambda e: e.tensor_copy(fi[:], P["t0"][:]), reads=K, writes=K)
            fw.dve(lambda e: e.tensor_copy(P["t1"][:], fi[:]), reads=K, writes=K)
            ew("dve", P["f"][:], P["t0"][:], P["t1"][:], ALU.subtract, K)
            fw.act(lambda e: e.activation(out=P["s4"][:], in_=P["f"][:], func=AF.Sin, scale=TWO_PI / 4), reads=K, writes=K)
            fw.act(lambda e: e.activation(out=P["s2"][:], in_=P["f"][:], func=AF.Sin, scale=TWO_PI / 2), reads=K, writes=K)
            ew("dve", P["t0"][:], P["s4"][:], P["s4"][:], ALU.mult, K)
            fw.dve(lambda e: e.tensor_scalar(out=P["c2"][:], in0=P["t0"][:], scalar1=-2.0, scalar2=1.0, op0=ALU.mult,
                                             op1=ALU.add), reads=K, writes=K)
            ew("dve", P["t0"][:], P["s2"][:], P["c2"][:], ALU.mult, K)
            fw.dve(lambda e: e.tensor_scalar(out=P["sn"][:], in0=P["t0"][:], scalar1=2.0, scalar2=None, op0=ALU.mult),
                   reads=K, writes=K)
            ew("dve", P["t0"][:], P["s2"][:], P["s2"][:], ALU.mult, K)
            fw.dve(lambda e: e.tensor_scalar(out=P["cs"][:], in0=P["t0"][:], scalar1=-2.0, scalar2=1.0, op0=ALU.mult,
                                             op1=ALU.add), reads=K, writes=K)
            ew("dve", P["t0"][:], P["cs"][:], P["cs"][:], ALU.mult, K)
            ew("dve", P["t1"][:], P["sn"][:], P["sn"][:], ALU.mult, K)
            ew("dve", P["t0"][:], P["t0"][:], P["t1"][:], ALU.add, K)
            fw.dve(lambda e: e.tensor_scalar(out=P["t0"][:], in0=P["t0"][:], scalar1=-0.5, scalar2=1.5, op0=ALU.mult,
                                             op1=ALU.add), reads=K, writes=K)
            ew("dve", P["cs"][:], P["cs"][:], P["t0"][:], ALU.mult, K)
            ew("dve", P["sn"][:], P["sn"][:], P["t0"][:], ALU.mult, K)
            ew("dve", P["are"][:], P["mag"][:], P["cs"][:], ALU.mult, K)
            ew("dve", P["aim"][:], P["mag"][:], P["sn"][:], ALU.mult, K)
            fw.dve(lambda e: e.tensor_copy(RR[:], P["mag"][:]), reads=K, writes=K + [T("RR")])
            fw.dve(lambda e: e.tensor_copy(CM[:, :, 0], P["cs"][:]), reads=K, writes=K)
            fw.dve(lambda e: e.tensor_copy(SM[:, :, 0], P["sn"][:]), reads=K, writes=K)
            for k in range(1, 11):
                ew("dve", P["t0"][:], SM[:, :, k - 1], CM[:, :, k - 1], ALU.mult, K)
                fw.dve(lambda e: e.tensor_scalar(out=SM[:, :, k], in0=P["t0"][:], scalar1=2.0, scalar2=None,
                                                 op0=ALU.mult), reads=K, writes=K)
                ew("dve", P["t0"][:], SM[:, :, k - 1], SM[:, :, k - 1], ALU.mult, K)
                fw.dve(lambda e: e.tensor_scalar(out=CM[:, :, k], in0=P["t0"][:], scalar1=-2.0, scalar2=1.0,
                                                 op0=ALU.mult, op1=ALU.add), reads=K, writes=K)
            ew("dve", P["den"][:], P["lre"][:], P["lre"][:], ALU.mult, K)
            ew("dve", P["t0"][:], P["lim"][:], P["lim"][:], ALU.mult, K)
            ew("dve", P["den"][:], P["den"][:], P["t0"][:], ALU.add, K)
            fw.dve(lambda e: e.reciprocal(P["den"][:], P["den"][:]), reads=K, writes=K)
            fw.dve(lambda e: e.tensor_scalar(out=P["nre"][:], in0=P["are"][:], scalar1=-1.0, scalar2=None,
                                             op0=ALU.add), reads=K, writes=K)
            ew("dve", P["t0"][:], P["nre"][:], P["lre"][:], ALU.mult, K)
            ew("dve", P["t1"][:], P["aim"][:], P["lim"][:], ALU.mult, K)
            ew("dve", P["t0"][:], P["t0"][:], P["t1"][:], ALU.add, K)
            ew("dve", P["zre"][:], P["t0"][:], P["den"][:], ALU.mult, K)
            ew("dve", P["t0"][:], P["aim"][:], P["lre"][:], ALU.mult, K)
            ew("dve", P["t1"][:], P["nre"][:], P["lim"][:], ALU.mult, K)
            ew("dve", P["t0"][:], P["t0"][:], P["t1"][:], ALU.subtract, K)
            ew("dve", P["zim"][:], P["t0"][:], P["den"][:], ALU.mult, K)
            KB = K + [T("braw0"), T("braw1")]
            for c in range(16):
                ew("dve", P["t0"][:], P["zre"][:], braw[0][:, :, c], ALU.mult, KB)
                ew("dve", P["t1"][:], P["zim"][:], braw[1][:, :, c], ALU.mult, KB)
                ew("dve", bb[0][:, :, c], P["t0"][:], P["t1"][:], ALU.subtract, KB)
                ew("dve", P["t0"][:], P["zre"][:], braw[1][:, :, c], ALU.mult, KB)
                ew("dve", P["t1"][:], P["zim"][:], braw[0][:, :, c], ALU.mult, KB)
                ew("dve", bb[1][:, :, c], P["t0"][:], P["t1"][:], ALU.add, KB)
            for i in range(3):
                if i == 2:
                    fw.dve(lambda e: e.tensor_tensor(out=Bpad[0][:], in0=Bpad[0][:], in1=Bpad[1][:], op=ALU.add),
                           reads=KB + [T("LB")], writes=KB)
                    for gp in range(GP):
                        fw.pe(lambda e: e.transpose(env.bank(gp % 4)[:, 0:128], Bpad[0][:, gp, :], env.identf[:]),
                              reads=KB + ["identf"], writes=["ps%d" % (gp % 4)])
                        fw.act(lambda e: e.copy(LB[2][:, gp, :], env.bank(gp % 4)[:, 0:128]),
                               reads=["ps%d" % (gp % 4)], writes=[T("LB")])
                    break
                fw.dve(lambda e: e.memset(Bpad[i][:], 0.0), reads=KB, writes=KB)
                for gi in range(2):
                    rows = slice(gi * 64, (gi + 1) * 64)
                    for q in range(4):
                        c0 = (2 * q + gi) * 16
                        fw.dve(lambda e: e.tensor_copy(Bpad[i][rows, q::4, c0:c0 + 16], bb[i][rows, q::4, :]),
                               reads=KB, writes=KB)
                for gp in range(GP):
                    fw.pe(lambda e: e.transpose(env.bank(gp % 4)[:, 0:128], Bpad[i][:, gp, :], env.identf[:]),
                          reads=KB + ["identf"], writes=["ps%d" % (gp % 4)])
                    fw.act(lambda e: e.copy(LB[i][:, gp, :], env.bank(gp % 4)[:, 0:128]),
                           reads=["ps%d" % (gp % 4)], writes=[T("LB")])
            dbgdump(env, "are", P["are"][:], K)
            dbgdump(env, "zre", P["zre"][:], K)
            dbgdump(env, "cm", CM[:].rearrange("p a b -> p (a b)"), K)
            dbgdump(env, "bb0", bb[0][:].rearrange("p a b -> p (a b)"), KB)
            dbgdump(env, "braw0", braw[0][:].rearrange("p a b -> p (a b)"), KB)
            dbgdump(env, "zim", P["zim"][:], KB)
            fw.dve(lambda e: e.memset(MQ[:], 0.0), writes=[T("MQ")])
            for q in range(4):
                fw.dve(lambda e: e.memset(MQ[0:64, q, (2 * q) * 16:(2 * q) * 16 + 16], 1.0), writes=[T("MQ")])
                fw.dve(lambda e: e.memset(MQ[64:128, q, (2 * q + 1) * 16:(2 * q + 1) * 16 + 16], 1.0),
                       writes=[T("MQ")])
            fw.dve(lambda e: e.tensor_scalar(out=Cc[1][:], in0=Cc[1][:], scalar1=-1.0, scalar2=None, op0=ALU.mult),
                   reads=[T("Cc1")], writes=[T("Cc1")])
            for i in range(2):
                for blk in range(8):
                    pb = 4 + blk % 2
                    fw.pe(lambda e: e.transpose(env.bank(pb)[:, 0:128], Cc[i][:, blk, :], env.identf[:]),
                          reads=[T("Cc%d" % i), "identf"], writes=["ps%d" % pb])
                    for q in range(4):
                        fw.dve(lambda e: e.tensor_tensor(out=LC[i][:, blk * 4 + q, :], in0=env.bank(pb)[:, 0:128],
                                                         in1=MQ[:, q, :], op=ALU.mult),
                               reads=["ps%d" % pb, T("MQ")], writes=[T("LC")])
                        if i == 0:
                            fw.dve(lambda e: e.scalar_tensor_tensor(out=LC[2][:, blk * 4 + q, :],
                                                                    in0=env.bank(pb)[:, 0:128], scalar=-1.0,
                                                                    in1=MQ[:, q, :], op0=ALU.mult, op1=ALU.mult),
                                   reads=["ps%d" % pb, T("MQ")], writes=[T("LC")])

        dbgdump(env, "LB0", LB[0][:, 0, :], [T("LB")])
        dbgdump(env, "LC0", LC[0][:, 0, :], [T("LC")])
        dbgdump(env, "LC1", LC[1][:, 5, :], [T("LC")])
        fw.barrier()
        with contextlib.ExitStack() as s1:
            Wi = env.sb(T("Wi"), [128, 8, D], BF16, s1)
            gpre = env.sb(T("gpre"), [128, D], F32, s1)
            fw.dma("sync", gpre[:], pre_g.partition_broadcast(128), writes=[T("gpre")])
            HT = [env.sb(T("hT%d" % i), [128, 8, 512], BF16, s1) for i in range(2)]
            XT = [env.sb(T("xt%d" % i), [128, D], F32, s1) for i in range(2)]
            HB = [env.sb(T("hb%d" % i), [128, D], BF16, s1) for i in range(2)]
            kWi = load_w_cast(fw, Wi, T("Wi"), w_in, 8)
            it = 0
            for t in range(NTOK // 512):
                hT, khT = HT[t % 2], T("hT%d" % (t % 2))
                for s in range(4):
                    r0 = t * 512 + s * 128
                    xt, kx = XT[it % 2], T("xt%d" % (it % 2))
                    hb, khb = HB[it % 2], T("hb%d" % (it % 2))
                    fw.dma("sync", xt[:], x_in[r0:r0 + 128, :], writes=[kx])
                    norm_transpose(env, xt[:], kx, gpre[:], T("gpre"), hb[:], khb, hT, khT, s * 128, 6)
                    it += 1
                for blk in range(8):
                    pb = blk % 2
                    fw.pe_group([lambda e, k=k: e.matmul(env.bank(pb), Wi[:, k, blk * 128:(blk + 1) * 128],
                                                         hT[:, k, :], start=(k == 0), stop=(k == 7))
                                 for k in range(8)], reads=kWi + [khT], writes=["ps%d" % pb])
                    fw.act(lambda e: e.copy(uT[:, blk, t * 512:(t + 1) * 512], env.bank(pb)),
                           reads=["ps%d" % pb], writes=[T("uT%d_%d" % (blk, t // 4))])

        fw.barrier()
        with contextlib.ExitStack() as s2:
            TAB = [tuple(env.sb(T("%s%d" % (nm, q)), [128, 512], F32, s2) for nm in ("COS", "SIN", "TM", "TP"))
                   for q in range(4)]
            TT = env.sb(T("TT"), [128, 256], F32, s2)
            RT = [env.sb(T("Rt%d" % q), [128, 512], F32, s2) for q in range(4)]
            ones = env.sb(T("ones"), [128, 512], F32, s2)
            WS = []
            for i in range(2):
                d = {}
                for nm in ("bs", "bre", "bim", "w_re", "w_im"):
                    d[nm] = env.sb(T("w%d_%s" % (i, nm)), [128, 512], F32, s2)
                for nm in ("p1", "p2", "p3", "p4"):
                    d[nm] = env.sb(T("w%d_%s" % (i, nm)), [128, 512], BF16, s2)
                WS.append(d)
            INI = env.sb(T("ini"), [128, 8], F32, s2)
            YV = env.sb(T("yv"), [128, 512], F32, s2)
            G1 = env.sb(T("g1"), [128, 512], F32, s2)
            G2 = env.sb(T("g2"), [128, 512], F32, s2)
            NS9 = env.sb(T("ns9"), [128, GP], F32, s2)
            fw.dve(lambda e: e.memset(ones[:], 1.0), writes=[T("ones")])
            fw.dve(lambda e: e.tensor_scalar(out=NS9[:], in0=SM[:, :, 9], scalar1=-1.0, scalar2=None, op0=ALU.mult),
                   reads=[T("setup")], writes=[T("ns9")])

            def build_table(q, gp):
                COS, SIN, TM, TP = TAB[q]
                KT = [T("tab%d" % q)]
                fw.dve(lambda e: e.memset(COS[:, 0:1], 1.0), reads=KT, writes=KT)
                fw.dve(lambda e: e.memset(SIN[:, 0:1], 0.0), reads=KT, writes=KT)
                for k in range(9):
                    m = 1 << k
                    cm, sm = CM[:, gp, k:k + 1], SM[:, gp, k:k + 1]
                    fw.dve(lambda e: e.tensor_scalar(out=TT[:, 0:m], in0=SIN[:, 0:m], scalar1=sm, scalar2=None,
                                                     op0=ALU.mult), reads=KT + [T("setup"), T("TT")], writes=[T("TT")])
                    fw.dve(lambda e: e.scalar_tensor_tensor(out=COS[:, m:2 * m], in0=COS[:, 0:m], scalar=cm,
                                                            in1=TT[:, 0:m], op0=ALU.mult, op1=ALU.subtract),
                           reads=KT + [T("TT")], writes=KT)
                    fw.dve(lambda e: e.tensor_scalar(out=TT[:, 0:m], in0=COS[:, 0:m], scalar1=sm, scalar2=None,
                                                     op0=ALU.mult), reads=KT + [T("TT")], writes=[T("TT")])
                    fw.dve(lambda e: e.scalar_tensor_tensor(out=SIN[:, m:2 * m], in0=SIN[:, 0:m], scalar=cm,
                                                            in1=TT[:, 0:m], op0=ALU.mult, op1=ALU.add),
                           reads=KT + [T("TT")], writes=KT)
                fw.dve(lambda e: e.tensor_tensor(out=TM[:], in0=COS[:], in1=SIN[:], op=ALU.subtract),
                       reads=KT, writes=KT)
                fw.dve(lambda e: e.tensor_tensor(out=TP[:], in0=COS[:], in1=SIN[:], op=ALU.add),
                       reads=KT, writes=KT)
                fw.dve(lambda e: e.tensor_scalar(out=RT[q][:], in0=ones[:], scalar1=RR[:, gp:gp + 1], scalar2=None,
                                                 op0=ALU.mult), reads=[T("ones"), T("RR"), T("Rt%d" % q)],
                       writes=[T("Rt%d" % q)])

            def stage_a(i, blk, sq_i, q, t):
                gp = blk * 4 + q
                W, kw = WS[i % 2], T("ws%d" % (i % 2))
                COS, SIN, TM, TP = TAB[q]
                KT = [T("tab%d" % q)]
                cols = slice(sq_i * SEQ + t * 512, sq_i * SEQ + (t + 1) * 512)
                ku = T("uT%d_%d" % (blk, sq_i))
                for (bnk, li, dst) in ((0, 2, "bs"), (1, 0, "bre"), (2, 1, "bim")):
                    fw.pe(lambda e: e.matmul(env.bank(bnk), LB[li][:, gp, :], uT[:, blk, cols], start=True,
                                             stop=True), reads=[T("LB"), ku], writes=["ps%d" % bnk])
                    fw.act(lambda e: e.copy(W[dst][:], env.bank(bnk)), reads=["ps%d" % bnk], writes=[kw + dst])
                fw.dve(lambda e: e.tensor_tensor(out=W["bs"][:], in0=W["bs"][:], in1=COS[:], op=ALU.mult),
                       reads=[kw + "bs"] + KT, writes=[kw + "bs"])
                fw.dve(lambda e: e.tensor_tensor(out=W["bim"][:], in0=W["bim"][:], in1=TM[:], op=ALU.mult),
                       reads=[kw + "bim"] + KT, writes=[kw + "bim"])
                fw.dve(lambda e: e.tensor_tensor(out=W["bre"][:], in0=W["bre"][:], in1=TP[:], op=ALU.mult),
                       reads=[kw + "bre"] + KT, writes=[kw + "bre"])
                fw.dve(lambda e: e.tensor_tensor(out=W["bim"][:], in0=W["bs"][:], in1=W["bim"][:], op=ALU.subtract),
                       reads=[kw + "bs", kw + "bim"], writes=[kw + "bim"])
                fw.dve(lambda e: e.tensor_tensor(out=W["bre"][:], in0=W["bs"][:], in1=W["bre"][:], op=ALU.subtract),
                       reads=[kw + "bs", kw + "bre"], writes=[kw + "bre"])

            def stage_b(i, blk, sq_i, q, t):
                gp = blk * 4 + q
                W, kw = WS[i % 2], T("ws%d" % (i % 2))
                Wp, kwp = WS[(i + 1) % 2], T("ws%d" % ((i + 1) % 2))
                COS, SIN, TM, TP = TAB[q]
                KT = [T("tab%d" % q)]
                if t == 0:
                    ire, iim = 0.0, 0.0
                    kini = []
                else:
                    c9, s9, ns9 = CM[:, gp, 9:10], SM[:, gp, 9:10], NS9[:, gp:gp + 1]
                    lre, lim_ = Wp["w_re"][:, 511:512], Wp["w_im"][:, 511:512]
                    o = 4 * (i % 2)
                    kini = [T("ini%d" % (i % 2))]
                    fw.act(lambda e: e.activation(out=INI[:, o:o + 1], in_=lim_, func=AF.Identity, scale=ns9),
                           reads=[kwp + "w_im", T("ns9")] + kini, writes=kini)
                    fw.act(lambda e: e.activation(out=INI[:, o + 1:o + 2], in_=lre, func=AF.Identity, scale=c9,
                                                  bias=INI[:, o:o + 1]), reads=[kwp + "w_re", T("setup")] + kini,
                           writes=kini)
                    fw.act(lambda e: e.activation(out=INI[:, o + 2:o + 3], in_=lre, func=AF.Identity, scale=s9),
                           reads=[kwp + "w_re"] + kini, writes=kini)
                    fw.act(lambda e: e.activation(out=INI[:, o + 3:o + 4], in_=lim_, func=AF.Identity, scale=c9,
                                                  bias=INI[:, o + 2:o + 3]), reads=[kwp + "w_im"] + kini, writes=kini)
                    ire, iim = INI[:, o + 1:o + 2], INI[:, o + 3:o + 4]
                fw.dve(lambda e: e.tensor_tensor_scan(out=W["w_re"][:], data0=RT[q][:], data1=W["bim"][:],
                                                      initial=ire, op0=ALU.mult, op1=ALU.add),
                       reads=[kw + "bim", T("Rt%d" % q)] + kini, writes=[kw + "w_re"])
                fw.dve(lambda e: e.tensor_tensor_scan(out=W["w_im"][:], data0=RT[q][:], data1=W["bre"][:],
                                                      initial=iim, op0=ALU.mult, op1=ALU.add),
                       reads=[kw + "bre", T("Rt%d" % q)] + kini, writes=[kw + "w_im"])
                yb = 4 + t
                plan = (("p1", "w_re", COS, "dve", 0), ("p2", "w_im", SIN, "dve", 2), ("p3", "w_re", SIN, "dve", 1),
                        ("p4", "w_im", COS, "dve", 1))
                for n_, (o_, a_, tab, eng, lc) in enumerate(plan):
                    getattr(fw, eng)(lambda e: e.tensor_tensor(out=W[o_][:], in0=W[a_][:], in1=tab[:], op=ALU.mult),
                                     reads=[kw + a_] + KT, writes=[kw + o_])
                for n_, (o_, a_, tab, eng, lc) in enumerate(plan):
                    fw.pe(lambda e: e.matmul(env.bank(yb), LC[lc][:, gp, :], W[o_][:], start=(q == 0 and n_ == 0),
                                             stop=(q == 3 and n_ == 3)),
                          reads=[T("LC"), kw + o_], writes=["ps%d" % yb])

            def epilogue(blk, sq_i):
                ku = T("uT%d_%d" % (blk, sq_i))
                for t in range(4):
                    cols = slice(sq_i * SEQ + t * 512, sq_i * SEQ + (t + 1) * 512)
                    yb = 4 + t
                    KY = [T("yv")]
                    fw.dve(lambda e: e.scalar_tensor_tensor(out=YV[:], in0=uT[:, blk, cols],
                                                            scalar=dvec[:, blk:blk + 1], in1=env.bank(yb),
                                                            op0=ALU.mult, op1=ALU.add),
                           reads=[ku, T("dvec"), "ps%d" % yb] + KY, writes=KY)
                    fw.dve(lambda e: e.tensor_tensor(out=G1[:], in0=YV[:], in1=YV[:], op=ALU.mult),
                           reads=KY, writes=KY)
                    fw.dve(lambda e: e.tensor_scalar(out=G1[:], in0=G1[:], scalar1=0.044715, scalar2=1.0,
                                                     op0=ALU.mult, op1=ALU.add), reads=KY, writes=KY)
                    fw.dve(lambda e: e.tensor_tensor(out=G1[:], in0=G1[:], in1=YV[:], op=ALU.mult),
                           reads=KY, writes=KY)
                    fw.act(lambda e: e.activation(out=G2[:], in_=G1[:], func=AF.Tanh, scale=0.7978845608028654),
                           reads=KY, writes=KY)
                    fw.dve(lambda e: e.tensor_scalar(out=G2[:], in0=G2[:], scalar1=1.0, scalar2=0.5, op0=ALU.add,
                                                     op1=ALU.mult), reads=KY, writes=KY)
                    fw.dve(lambda e: e.tensor_tensor(out=uT[:, blk, cols], in0=G2[:], in1=YV[:], op=ALU.mult),
                           reads=KY + [ku], writes=KY + [ku])

            gi_ = 0
            for blk in range(8):
                for q in range(4):
                    build_table(q, blk * 4 + q)
                items = [(sq_i, q, t) for sq_i in range(nseq) for q in range(4) for t in range(4)]
                stage_a(gi_, blk, *items[0])
                for n, it_ in enumerate(items):
                    if n + 1 < len(items):
                        stage_a(gi_ + 1, blk, *items[n + 1])
                    stage_b(gi_, blk, *it_)
                    gi_ += 1
                    if it_[1] == 3 and it_[2] == 3:
                        epilogue(blk, it_[0])

        dbgdump(env, "yg0", uT[:, 0, 0:512], [T("uT0")])
        fw.barrier()
        with contextlib.ExitStack() as s3:
            Wg = env.sb(T("Wglu"), [128, 8, D], BF16, s3)
            gpost = env.sb(T("gpost"), [128, D], F32, s3)
            fw.dma("sync", gpost[:], post_g.partition_broadcast(128), writes=[T("gpost")])
            Wo = env.sb(T("Wo"), [128, 8, D], BF16, s3)
            ZT = [env.sb(T("zT%d" % i), [128, 8, 512], BF16, s3) for i in range(2)]
            SG = [env.sb(T("sg%d" % i), [128, 512], F32, s3) for i in range(2)]
            XR = [env.sb(T("xr%d" % i), [128, D], F32, s3) for i in range(2)]
            TMP = env.sb(T("tmp"), [128, D], F32, s3)
            kWglu = load_w_cast(fw, Wg, T("Wglu"), w_glu, 8)
            kWo5 = load_w_cast(fw, Wo, T("Wo"), w_out, 8)
            ukeys = [T("uT%d_%d" % (b, q_)) for b in range(8) for q_ in range(nseq)]
            for t in range(NTOK // 512):
                cols = slice(t * 512, (t + 1) * 512)
                zT, kzT = ZT[t % 2], T("zT%d" % (t % 2))
                for blk in range(8):
                    pb = blk % 2
                    fw.pe_group([lambda e, k=k: e.matmul(env.bank(pb), Wg[:, k, blk * 128:(blk + 1) * 128],
                                                         uT[:, k, cols], start=(k == 0), stop=(k == 7))
                                 for k in range(8)], reads=kWglu + ukeys, writes=["ps%d" % pb])
                    sg, ksg = SG[blk % 2], T("sg%d" % (blk % 2))
                    fw.act(lambda e: e.activation(out=sg[:], in_=env.bank(pb), func=AF.Sigmoid),
                           reads=["ps%d" % pb], writes=[ksg])
                    fw.dve(lambda e: e.tensor_tensor(out=zT[:, blk, :], in0=sg[:], in1=uT[:, blk, cols], op=ALU.mult),
                           reads=[ksg] + ukeys, writes=[kzT])
                for s in range(4):
                    r0 = t * 512 + s * 128
                    xr, kxr = XR[s % 2], T("xr%d" % (s % 2))
                    fw.dma("sync", xr[:], x_in[r0:r0 + 128, :], writes=[kxr])
                    pb0 = 2 + 2 * (s % 2)
                    pso = env.bank(pb0, 2)
                    pk = ["ps%d" % pb0, "ps%d" % (pb0 + 1)]
                    for hf in range(2):
                        fw.pe_group([lambda e, k=k: e.matmul(pso[:, hf, :], zT[:, k, s * 128:(s + 1) * 128],
                                                             Wo[:, k, hf * 512:(hf + 1) * 512], start=(k == 0),
                                                             stop=(k == 7)) for k in range(8)],
                                    reads=[kzT] + kWo5, writes=pk)
                    post_norm_residual(env, pso, pk, gpost[:], T("gpost"), xr[:], kxr, TMP[:], T("tmp"),
                                       xr[:], kxr)
                    fw.dma("sync", x_out[r0:r0 + 128, :], xr[:], reads=[kxr], writes=[T("xout")])


NSEQ_CORE = 2
NCORES = 8
_CACHE = {}


def build_program():
    nc = bass.Bass("TRN2", target_bir_lowering=False)
    ntok = NSEQ_CORE * SEQ

    def din(name, shape):
        return nc.dram_tensor(name, shape, F32, kind="ExternalInput").ap()

    x = din("x", [ntok, D])
    fox_w_in = din("fox_w_in", [D, 4112])
    fox_b_f = din("fox_b_f", [NH])
    fox_q_gain = din("fox_q_gain", [HD])
    fox_k_gain = din("fox_k_gain", [HD])
    fox_w_out = din("fox_w_out", [DATT, D])
    s5_w_in = din("s5_w_in", [D, D])
    s5_log_dt = din("s5_log_dt", [NG])
    s5_lam_re = din("s5_lam_re", [NG, 64])
    s5_lam_im = din("s5_lam_im", [NG, 64])
    s5_b_re = din("s5_b_re", [NG, 64, 16])
    s5_b_im = din("s5_b_im", [NG, 64, 16])
    s5_c_re = din("s5_c_re", [NG, 16, 64])
    s5_c_im = din("s5_c_im", [NG, 16, 64])
    s5_d = din("s5_d", [D])
    s5_w_glu = din("s5_w_glu", [D, D])
    s5_w_out = din("s5_w_out", [D, D])
    mix_pre = din("mix_pre_gain", [2, D])
    mix_post = din("mix_post_gain", [2, D])
    ffn_pre = din("ffn_pre_gain", [2, D])
    ffn_post = din("ffn_post_gain", [2, D])
    ffn_wg = din("ffn_w_gate", [2, D, DFF])
    ffn_wu = din("ffn_w_up", [2, D, DFF])
    ffn_wd = din("ffn_w_down", [2, DFF, D])
    ident = din("c_ident", [128, 128])
    cmask = din("c_mask", [128, 128])
    out = nc.dram_tensor("out", [ntok, D], F32, kind="ExternalOutput").ap()
    xa = nc.dram_tensor("xa", [ntok, D], F32).ap()
    xb = nc.dram_tensor("xb", [ntok, D], F32).ap()
    xc = nc.dram_tensor("xc", [ntok, D], F32).ap()
    oT_d = nc.dram_tensor("oT_d", [NSEQ_CORE, NH, HD, SEQ], BF16).ap()

    fw = FW(nc)
    with contextlib.ExitStack() as st:
        env = Env(nc, fw, st)
        env.init_consts(ident)
        fox_phase(env, x, xa, fox_w_in, fox_b_f, fox_q_gain, fox_k_gain, fox_w_out, mix_pre[0, :], mix_post[0, :],
                  cmask, oT_d, NSEQ_CORE)
        ffn_phase(env, xa, xb, ffn_wg[0], ffn_wu[0], ffn_wd[0], ffn_pre[0, :], ffn_post[0, :], ntok, "f0")
        s5_phase(env, xb, xc, s5_w_in, s5_log_dt, s5_lam_re, s5_lam_im, s5_b_re, s5_b_im, s5_c_re, s5_c_im, s5_d,
                 s5_w_glu, s5_w_out, mix_pre[1, :], mix_post[1, :], NSEQ_CORE)
        ffn_phase(env, xc, out, ffn_wg[1], ffn_wu[1], ffn_wd[1], ffn_pre[1, :], ffn_post[1, :], ntok, "f1")
        fw.finish()
    return nc


def kernel(**inputs):
    f = np.float32
    x = np.ascontiguousarray(inputs["x"], dtype=f)
    B = x.shape[0]
    shared = {}
    for k in ("fox_w_in", "fox_b_f", "fox_q_gain", "fox_k_gain", "fox_w_out", "s5_w_in", "s5_log_dt", "s5_lam_re",
              "s5_lam_im", "s5_b_re", "s5_b_im", "s5_c_re", "s5_c_im", "s5_d", "s5_w_glu", "s5_w_out"):
        shared[k] = np.ascontiguousarray(np.asarray(inputs[k], dtype=f)[0])
    for k in ("mix_pre_gain", "mix_post_gain", "ffn_pre_gain", "ffn_post_gain", "ffn_w_gate", "ffn_w_up",
              "ffn_w_down"):
        shared[k] = np.ascontiguousarray(np.asarray(inputs[k], dtype=f))
    shared["c_ident"] = np.eye(128, dtype=f)
    shared["c_mask"] = np.where(np.arange(128)[None, :] < np.arange(128)[:, None], MASKNEG, 0.0).astype(f)
    if "nc" not in _CACHE:
        _CACHE["nc"] = build_program()
    nc = _CACHE["nc"]
    in_maps = []
    for c in range(NCORES):
        m = dict(shared)
        m["x"] = np.ascontiguousarray(x[c * NSEQ_CORE:(c + 1) * NSEQ_CORE].reshape(NSEQ_CORE * SEQ, D))
        in_maps.append(m)
    res = run_bass_kernel_spmd(nc, in_maps, core_ids=list(range(NCORES)))
    outs = [np.asarray(r["out"]).reshape(NSEQ_CORE, SEQ, D) for r in res.results]
    return np.concatenate(outs, axis=0).astype(f)
```

```python
import contextlib
import numpy as np
import concourse.bass as bass
import concourse.mybir as mybir
from concourse.bass_utils import run_bass_kernel_spmd

F32 = mybir.dt.float32
BF16 = mybir.dt.bfloat16
I32 = mybir.dt.int32
AF = mybir.ActivationFunctionType
ALU = mybir.AluOpType
AX = mybir.AxisListType


STRICT_SAME_ENGINE = True


class FW:
    def __init__(self, nc, n_dma_sems=6):
        self.nc = nc
        self.stack = contextlib.ExitStack()
        self.E = {}
        self.sems = []
        for name, h in (("pe", nc.tensor), ("act", nc.scalar), ("dve", nc.vector),
                        ("pool", nc.gpsimd), ("sync", nc.sync)):
            e = {"name": name, "h": h, "count": 0, "waited": {}, "dma": [], "dma_i": 0}
            if name != "sync":
                e["sem"] = self._newsem("c_" + name)
            for i in range(n_dma_sems):
                e["dma"].append([self._newsem("d_%s%d" % (name, i)), 0])
            self.E[name] = e
        self.lastw = {}
        self.readers = {}
        self.nwaits = 0

    def _newsem(self, name):
        s = self.stack.enter_context(self.nc.semaphore(name))
        self.sems.append(s)
        return len(self.sems) - 1

    def _wait(self, E, sid, val):
        if E["waited"].get(sid, 0) >= val:
            return
        E["h"].wait_ge(self.sems[sid], val)
        E["waited"][sid] = val
        self.nwaits += 1

    def _sync(self, E, reads, writes, attach=False):
        need = {}

        def add(ev):
            if ev is None:
                return
            sid, val = ev
            if need.get(sid, 0) < val:
                need[sid] = val

        for r in reads:
            add(self.lastw.get(r))
        for w in writes:
            add(self.lastw.get(w))
            for sid, val in self.readers.get(w, {}).items():
                add((sid, val))
        own = E.get("sem")
        todo = []
        for sid, val in need.items():
            if sid == own:
                if E["name"] == "pe":
                    continue
                if STRICT_SAME_ENGINE is False and E["name"] in ("dve", "act") and val < E["count"]:
                    continue
            if E["waited"].get(sid, 0) >= val:
                continue
            todo.append((sid, val))
        held = None
        if attach and todo:
            held = todo.pop()
        for sid, val in todo:
            self._wait(E, sid, val)
        return held

    def _attach(self, E, ins, held):
        if held is not None:
            sid, val = held
            ins._wait_ge(self.sems[sid], val)
            E["waited"][sid] = val
            self.nwaits += 1

    def _record(self, ev, reads, writes):
        sid, val = ev
        for r in reads:
            d = self.readers.setdefault(r, {})
            if d.get(sid, 0) < val:
                d[sid] = val
        for w in writes:
            self.lastw[w] = ev
            self.readers[w] = {}

    def op(self, en, fn, reads=(), writes=()):
        E = self.E[en]
        held = self._sync(E, reads, writes, attach=(en != "pe"))
        ins = fn(E["h"])
        self._attach(E, ins, held)
        E["count"] += 1
        ins.then_inc(self.sems[E["sem"]], 1)
        self._record((E["sem"], E["count"]), reads, writes)
        return ins

    def pe(self, fn, reads=(), writes=()):
        return self.op("pe", fn, reads, writes)

    def act(self, fn, reads=(), writes=()):
        return self.op("act", fn, reads, writes)

    def dve(self, fn, reads=(), writes=()):
        return self.op("dve", fn, reads, writes)

    def pool(self, fn, reads=(), writes=()):
        return self.op("pool", fn, reads, writes)

    def pe_group(self, fns, reads=(), writes=()):
        E = self.E["pe"]
        held = self._sync(E, reads, writes, attach=False)
        ins = None
        for n, fn in enumerate(fns):
            ins = fn(E["h"])
            if n == 0:
                self._attach(E, ins, held)
        E["count"] += 1
        ins.then_inc(self.sems[E["sem"]], 1)
        self._record((E["sem"], E["count"]), reads, writes)

    def dma(self, q, out, in_, reads=(), writes=(), **kw):
        E = self.E[q]
        self._sync(E, reads, writes)
        slot = E["dma"][E["dma_i"] % len(E["dma"])]
        E["dma_i"] += 1
        sid, target = slot
        if target > 0:
            self._wait(E, sid, target)
        E["h"].dma_start(out=out, in_=in_, **kw).then_inc(self.sems[sid], 16)
        slot[1] = target + 16
        self._record((sid, slot[1]), reads, writes)

    def barrier(self):
        evs = []
        for e in self.E.values():
            for sid, target in e["dma"]:
                if target > 0:
                    evs.append((sid, target))
            if "sem" in e and e["count"] > 0:
                evs.append((e["sem"], e["count"]))
        for E in self.E.values():
            for sid, val in evs:
                self._wait(E, sid, val)

    def finish(self):
        S = self.E["sync"]
        for e in self.E.values():
            for sid, target in e["dma"]:
                if target > 0:
                    self._wait(S, sid, target)
            if "sem" in e and e["count"] > 0:
                self._wait(S, e["sem"], e["count"])
        self.stack.close()


EPS = 1e-6
D = 1024
DFF = 2816
NFC = DFF // 128


class Env:
    def __init__(self, nc, fw, st):
        self.nc, self.fw, self.st = nc, fw, st
        self.ps = st.enter_context(nc.psum_tensor("psall", [128, 8, 512], F32))
        self.identb = self.sb("identb", [128, 128], BF16)
        self.identf = self.sb("identf", [128, 128], F32)
        self.mhalf = self.sb("mhalf", [128, 1], F32)
        self.stats = self.sb("stats", [128, 96], F32)
        self.junk = self.sb("junk", [128, 1024], BF16)
        self.si = 0

    def sb(self, name, shape, dt, st=None):
        return (st or self.st).enter_context(self.nc.sbuf_tensor(name, shape, dt))

    def bank(self, i, n=1):
        if n == 1:
            return self.ps[:, i, :]
        return self.ps[:, i:i + n, :]

    def stat(self):
        i = self.si % 96
        self.si += 1
        return self.stats[:, i:i + 1], "st%d" % i

    def init_consts(self, ident_dram):
        fw = self.fw
        fw.dma("sync", self.identf[:], ident_dram, writes=["identf"])
        fw.dve(lambda e: e.tensor_copy(self.identb[:], self.identf[:]), reads=["identf"], writes=["identb"])
        fw.dve(lambda e: e.memset(self.mhalf[:], -0.5), writes=["mhalf"])

    def rstd(self, src_ap, src_key, n):
        fw = self.fw
        P = src_ap.shape[0]
        ss, kss = self.stat()
        var, kvar = self.stat()
        rs, krs = self.stat()
        junk = self.junk[0:P, 0:n]
        sk = list(src_key) if isinstance(src_key, (list, tuple)) else [src_key]
        fw.act(lambda e: e.activation(out=junk, in_=src_ap, func=AF.Square, accum_out=ss[0:P, :]),
               reads=sk, writes=[kss])
        fw.dve(lambda e: e.tensor_scalar(out=var[0:P, :], in0=ss[0:P, :], scalar1=1.0 / n, scalar2=EPS,
                                         op0=ALU.mult, op1=ALU.add), reads=[kss], writes=[kvar])
        fw.pool(lambda e: e.tensor_tensor(out=rs[0:P, :], in0=var[0:P, :], in1=self.mhalf[0:P, :], op=ALU.pow),
                reads=[kvar, "mhalf"], writes=[krs])
        return rs, krs


def load_w_cast(fw, dst_tile, dst_key, w_dram, kchunks, first=False):
    keys = []
    for k in range(kchunks):
        kk = "%s_k%d" % (dst_key, k)
        fw.dma("pool", dst_tile[:, k, :], w_dram[k * 128:(k + 1) * 128, :], writes=[kk])
        keys.append(kk)
    return keys


def load_w_cast_cols(fw, dst_tile, dst_key, w_dram, kchunks, col_groups):
    src = w_dram.rearrange("(k p) c -> p k c", p=128)
    keys = {}
    for gi, (c0, c1) in enumerate(col_groups):
        kk = "%s_c%d" % (dst_key, gi)
        fw.dma("pool", dst_tile[:, 0:kchunks, c0:c1], src[:, :, c0:c1], writes=[kk])
        keys[gi] = kk
    return keys


def norm_transpose(env, x_ap, x_key, g_ap, g_key, hb, hb_key, hT, hT_key, col0, psT_bank):
    fw = env.fw
    rs, krs = env.rstd(x_ap, x_key, D)
    fw.dve(lambda e: e.scalar_tensor_tensor(out=hb, in0=x_ap, scalar=rs, in1=g_ap, op0=ALU.mult, op1=ALU.mult),
           reads=[x_key, krs, g_key], writes=[hb_key])
    transpose_in(env, hb, hb_key, hT, hT_key, col0, psT_bank)


def transpose_in(env, hb, hb_key, hT, hT_key, col0, psT_bank, on="act"):
    fw = env.fw
    pkey = "ps%d" % psT_bank
    psT = env.bank(psT_bank).bitcast(BF16)
    fns = []
    for j in range(8):
        fns.append(lambda e, j=j: e.transpose(psT[:, j * 128:(j + 1) * 128], hb[:, j * 128:(j + 1) * 128],
                                               env.identb[:]))
    fw.pe_group(fns, reads=[hb_key, "identb"], writes=[pkey])
    src = psT.rearrange("p (j c) -> p j c", j=8)
    dst = hT[:, 0:8, col0:col0 + 128]
    if on == "act":
        fw.act(lambda e: e.copy(dst, src), reads=[pkey], writes=[hT_key])
    else:
        fw.dve(lambda e: e.tensor_copy(dst, src), reads=[pkey], writes=[hT_key])


def post_norm_residual(env, ps_ap, ps_key, g_ap, g_key, xr, xr_key, tmp, tmp_key, xo, xo_key):
    fw = env.fw
    pk = list(ps_key) if isinstance(ps_key, (list, tuple)) else [ps_key]
    rs, krs = env.rstd(ps_ap, pk, D)
    fw.dve(lambda e: e.scalar_tensor_tensor(out=tmp, in0=ps_ap, scalar=rs, in1=g_ap, op0=ALU.mult, op1=ALU.mult),
           reads=pk + [krs, g_key], writes=[tmp_key])
    fw.dve(lambda e: e.tensor_tensor(out=xo, in0=tmp, in1=xr, op=ALU.add),
           reads=[tmp_key, xr_key], writes=[xo_key])


def ffn_phase(env, x_in, x_out, wg, wu, wd, pre_g, post_g, ntok, tag):
    nc, fw = env.nc, env.fw
    fw.barrier()
    with contextlib.ExitStack() as st:
        Wg = env.sb(tag + "Wg", [128, 8, DFF], BF16, st)
        Wu = env.sb(tag + "Wu", [128, 8, DFF], BF16, st)
        Wd = env.sb(tag + "Wd", [128, NFC, D], BF16, st)
        gpre = env.sb(tag + "gpre", [128, D], F32, st)
        gpost = env.sb(tag + "gpost", [128, D], F32, st)
        XT = [env.sb(tag + "xt%d" % i, [128, D], F32, st) for i in range(2)]
        XR = [env.sb(tag + "xr%d" % i, [128, D], F32, st) for i in range(2)]
        HB = [env.sb(tag + "hb%d" % i, [128, D], BF16, st) for i in range(2)]
        hT = env.sb(tag + "hT", [128, 8, 512], BF16, st)
        aT = env.sb(tag + "aT", [128, NFC, 512], BF16, st)
        SG = [env.sb(tag + "sg%d" % i, [128, 512], F32, st) for i in range(2)]
        TMP = env.sb(tag + "tmp", [128, D], F32, st)
        fw.dma("sync", gpre[:], pre_g.partition_broadcast(128), writes=[tag + "gpre"])
        fw.dma("sync", gpost[:], post_g.partition_broadcast(128), writes=[tag + "gpost"])
        grp = [(c, min(c + 256, DFF)) for c in range(0, DFF, 256)]
        kWg, kWu = {}, {}
        srcg = wg.rearrange("(k p) c -> p k c", p=128)
        srcu = wu.rearrange("(k p) c -> p k c", p=128)
        for gi, (c0, c1) in enumerate(grp):
            kWg[gi] = "%sWg_c%d" % (tag, gi)
            kWu[gi] = "%sWu_c%d" % (tag, gi)
            fw.dma("pool", Wg[:, :, c0:c1], srcg[:, :, c0:c1], writes=[kWg[gi]])
            fw.dma("pool", Wu[:, :, c0:c1], srcu[:, :, c0:c1], writes=[kWu[gi]])
        kWd = load_w_cast(fw, Wd, tag + "Wd", wd, NFC)
        ntiles = ntok // 512
        it = [0]

        def build_hT(t):
            for s in range(4):
                r0 = t * 512 + s * 128
                i = it[0] % 2
                it[0] += 1
                xt, kx = XT[i], tag + "xt%d" % i
                hb, khb = HB[i], tag + "hb%d" % i
                fw.dma("sync", xt[:], x_in[r0:r0 + 128, :], writes=[kx])
                norm_transpose(env, xt[:], kx, gpre[:], tag + "gpre", hb[:], khb, hT, tag + "hT", s * 128, 0)

        build_hT(0)
        for t in range(ntiles):
            akeys = []
            for fc in range(NFC):
                bg, bu = fc % 2, 2 + fc % 2
                fs = slice(fc * 128, (fc + 1) * 128)
                fw.pe_group([lambda e, k=k: e.matmul(env.bank(bg), Wg[:, k, fs], hT[:, k, :], start=(k == 0),
                                                     stop=(k == 7)) for k in range(8)],
                            reads=[kWg[fc // 2], tag + "hT"], writes=["ps%d" % bg])
                fw.pe_group([lambda e, k=k: e.matmul(env.bank(bu), Wu[:, k, fs], hT[:, k, :], start=(k == 0),
                                                     stop=(k == 7)) for k in range(8)],
                            reads=[kWu[fc // 2], tag + "hT"], writes=["ps%d" % bu])
                sg = SG[fc % 2]
                ksg = tag + "sg%d" % (fc % 2)
                fw.act(lambda e: e.activation(out=sg[:], in_=env.bank(bg), func=AF.Silu),
                       reads=["ps%d" % bg], writes=[ksg])
                ka = tag + "aT%d" % fc
                fw.dve(lambda e: e.tensor_tensor(out=aT[:, fc, :], in0=sg[:], in1=env.bank(bu), op=ALU.mult),
                       reads=[ksg, "ps%d" % bu], writes=[ka])
                akeys.append(ka)
            if t + 1 < ntiles:
                build_hT(t + 1)
            for s in range(4):
                r0 = t * 512 + s * 128
                xr = XR[s % 2]
                kxr = tag + "xr%d" % (s % 2)
                fw.dma("sync", xr[:], x_in[r0:r0 + 128, :], writes=[kxr])
                pb0 = 4 + 2 * (s % 2)
                pso = env.bank(pb0, 2)
                pk = ["ps%d" % pb0, "ps%d" % (pb0 + 1)]
                for hf in range(2):
                    fw.pe_group([lambda e, fc=fc: e.matmul(pso[:, hf, :], aT[:, fc, s * 128:(s + 1) * 128],
                                                           Wd[:, fc, hf * 512:(hf + 1) * 512], start=(fc == 0),
                                                           stop=(fc == NFC - 1)) for fc in range(NFC)],
                                reads=akeys + kWd, writes=pk)
                post_norm_residual(env, pso, pk, gpost[:], tag + "gpost", xr[:], kxr, TMP[:], tag + "tmp",
                                   xr[:], kxr)
                fw.dma("sync", x_out[r0:r0 + 128, :], xr[:], reads=[kxr], writes=[])


NH = 16
HD = 64
WARM = 0
DATT = 1024
SEQ = 2048
MASKNEG = -240000.0


def fox_phase(env, x_in, x_out, w_in, b_f, q_gain, k_gain, w_out, pre_g, post_g, consts, oT_d, nseq, tag="fx"):
    nc, fw = env.nc, env.fw
    fw.barrier()
    with contextlib.ExitStack() as st:
        T = lambda s: tag + s
        Win = env.sb(tag + "Win", [128, 8, 4112], BF16, st)
        Wo = env.sb(tag + "Wo", [128, 8, D], BF16, st)
        gpre = env.sb(tag + "gpre", [128, D], F32, st)
        gpost = env.sb(tag + "gpost", [128, D], F32, st)
        hT = env.sb(tag + "hT", [128, 8, SEQ], BF16, st)
        XT = [env.sb(tag + "xt%d" % i, [128, D], F32, st) for i in range(2)]
        HB = [env.sb(tag + "hb%d" % i, [128, D], BF16, st) for i in range(2)]
        TMP = env.sb(tag + "tmp", [128, D], F32, st)
        scr = env.sb(tag + "scr", [128, 4096], F32, st)
        lf = scr[0:16, 0:SEQ]
        cT = scr[0:16, SEQ:2 * SEQ]
        QA = [env.sb(tag + "qa%d" % i, [128, SEQ], BF16, st) for i in range(2)]
        KA = [env.sb(tag + "ka%d" % i, [128, SEQ], BF16, st) for i in range(2)]
        QA += [scr[:, 0:1024].bitcast(BF16), scr[:, 1024:2048].bitcast(BF16)]
        KA += [scr[:, 2048:3072].bitcast(BF16), scr[:, 3072:4096].bitcast(BF16)]
        KQ = [[T("qa0")], [T("qa1")], [T("qa2"), T("lf")], [T("qa3"), T("lf")]]
        KK = [[T("ka0")], [T("ka1")], [T("ka2"), T("cT")], [T("ka3"), T("cT")]]
        gT = env.sb(tag + "gT", [128, SEQ], BF16, st)
        V2e = env.sb(tag + "V2e", [128, 16, 128], BF16, st)
        V2o = TMP[:].bitcast(BF16).rearrange("p (a b) -> p a b", a=16)
        PT = [env.sb(tag + "pT%d" % i, [128, 512], BF16, st) for i in range(2)]
        SQ = [env.sb(tag + "sq%d" % i, [128, 512], BF16, st) for i in range(2)]
        RST = [env.sb(tag + "rst%d" % i, [128, 512], F32, st) for i in range(2)]
        bones = env.sb(tag + "bones", [128, 128], BF16, st)
        maskf = env.sb(tag + "maskf", [128, 128], F32, st)
        maskb = env.sb(tag + "maskb", [128, 128], BF16, st)
        qg = env.sb(tag + "qg", [128, 1], F32, st)
        kg = env.sb(tag + "kg", [128, 1], F32, st)
        nbf = env.sb(tag + "nbf", [16, 1], F32, st)
        epsc = env.sb(tag + "epsc", [128, 1], F32, st)
        ones16 = env.sb(tag + "ones16", [16, 512], F32, st)
        csp = env.sb(tag + "csp", [96, SEQ], BF16, st)
        cspt = env.sb(tag + "cspt", [16, SEQ], BF16, st)
        negc = env.sb(tag + "negc", [128, 16, 16], F32, st)
        rl = env.sb(tag + "rl", [128, 512], F32, st)
        og = env.sb(tag + "og", [128, 512], F32, st)
        OTS = [env.sb(tag + "ots%d" % i, [128, 512], BF16, st) for i in range(2)]
        OTL = [env.sb(tag + "oTl0", [128, 8, 128], BF16, st),
               rl[:].bitcast(BF16).rearrange("p (h t) -> p h t", h=8)]

        SER = T("ser")
        fw.dma("sync", gpre[:], pre_g.partition_broadcast(128), writes=[T("gpre"), SER])
        fw.dma("sync", gpost[:], post_g.partition_broadcast(128), writes=[T("gpost"), SER])
        fw.dma("sync", maskf[:], consts, writes=[T("maskf"), SER])
        for half in range(2):
            fw.dma("sync", qg[half * 64:(half + 1) * 64, :], q_gain.rearrange("(p o) -> p o", o=1),
                   writes=[T("qg"), SER])
            fw.dma("sync", kg[half * 64:(half + 1) * 64, :], k_gain.rearrange("(p o) -> p o", o=1),
                   writes=[T("kg"), SER])
        fw.dma("sync", nbf[:], b_f.rearrange("(p o) -> p o", o=1), writes=[T("nbf"), SER])
        fw.dve(lambda e: e.tensor_copy(maskb[:], maskf[:]), reads=[T("maskf"), SER], writes=[T("maskb")])
        fw.dve(lambda e: e.tensor_scalar(out=nbf[:], in0=nbf[:], scalar1=-1.0, scalar2=None, op0=ALU.mult),
               reads=[T("nbf")], writes=[T("nbf")])
        fw.dve(lambda e: e.memset(epsc[:], EPS), writes=[T("epsc")])
        fw.dve(lambda e: e.memset(bones[:], 0.0), writes=[T("bones")])
        fw.dve(lambda e: e.memset(bones[0:64, 0:64], 1.0), writes=[T("bones")])
        fw.dve(lambda e: e.memset(bones[64:128, 64:128], 1.0), writes=[T("bones")])
        fw.dve(lambda e: e.memset(ones16[:], 1.0), writes=[T("ones16")])
        for i in range(4):
            fw.dve(lambda e: e.memset(KA[i][64:128, :], 1.0), writes=KK[i])
            fw.dve(lambda e: e.memset(QA[i][64:128, :], 0.0), writes=KQ[i])
        fw.dve(lambda e: e.memset(V2e[:, :, 64:128], 1.0), writes=[T("V2e")])
        kWin = load_w_cast(fw, Win, T("Win"), w_in, 8)
        kWo = load_w_cast(fw, Wo, T("Wo"), w_out, 8)

        cnt = [0]

        def proj_gen(j):
            sl = [2 * (j % 2), 2 * (j % 2) + 1]
            for (DST, KD, off, gain, kgain) in ((QA, KQ, 0, qg, T("qg")), (KA, KK, 1024, kg, T("kg"))):
                for t in range(4):
                    ts = slice(t * 512, (t + 1) * 512)
                    c = cnt[0]
                    cnt[0] += 1
                    pb = 2 + c % 2
                    sq, ksq = SQ[c % 2], T("sq%d" % (c % 2))
                    rst, krst = RST[c % 2], T("rst%d" % (c % 2))
                    fw.pe_group([lambda e, k=k: e.matmul(env.bank(pb), Win[:, k, off + j * 128:off + (j + 1) * 128],
                                                         hT[:, k, ts], start=(k == 0), stop=(k == 7))
                                 for k in range(8)], reads=kWin + [T("hT")], writes=["ps%d" % pb])
                    fw.act(lambda e: e.activation(out=sq[:], in_=env.bank(pb), func=AF.Square),
                           reads=["ps%d" % pb], writes=[ksq])
                    yield
                    fw.pe(lambda e: e.matmul(env.bank(6), bones[:], sq[:], start=True, stop=True),
                          reads=[T("bones"), ksq], writes=["ps6"])
                    fw.act(lambda e: e.activation(out=rst[:], in_=env.bank(6), func=AF.Ln, bias=epsc[:],
                                                  scale=1.0 / HD), reads=["ps6", T("epsc")], writes=[krst])
                    fw.act(lambda e: e.activation(out=rst[:], in_=rst[:], func=AF.Exp, scale=-0.5),
                           reads=[krst], writes=[krst])
                    for par in range(2):
                        rows = slice(par * 64, (par + 1) * 64)
                        dst = DST[sl[par]]
                        fw.dve(lambda e: e.scalar_tensor_tensor(out=dst[0:64, ts], in0=env.bank(pb)[rows, :],
                                                                scalar=gain[rows, :], in1=rst[rows, :], op0=ALU.mult,
                                                                op1=ALU.mult),
                               reads=["ps%d" % pb, kgain, krst], writes=KD[sl[par]])
                    yield
            for par in range(2):
                h = 2 * j + par
                for l_ in range(3):
                    fw.dma("sync", QA[sl[par]][64 + l_:65 + l_, :], csp[32 * l_ + h:32 * l_ + h + 1, :],
                           reads=[T("csp")], writes=KQ[sl[par]])
            yield

        def gv_emit(j):
            for t in range(4):
                ts = slice(t * 512, (t + 1) * 512)
                pb = 2 + t % 2
                fw.pe_group([lambda e, k=k: e.matmul(env.bank(pb), Win[:, k, 3072 + j * 128:3072 + (j + 1) * 128],
                                                     hT[:, k, ts], start=(k == 0), stop=(k == 7)) for k in range(8)],
                            reads=kWin + [T("hT")], writes=["ps%d" % pb])
                fw.act(lambda e: e.activation(out=gT[:, ts], in_=env.bank(pb), func=AF.Sigmoid),
                       reads=["ps%d" % pb], writes=[T("gT")])
            for g4 in range(4):
                fns = []
                for kk in range(4):
                    kt = g4 * 4 + kk
                    for k in range(8):
                        fns.append(lambda e, k=k, kk=kk, kt=kt: e.matmul(
                            env.bank(7)[:, kk * 128:(kk + 1) * 128], hT[:, k, kt * 128:(kt + 1) * 128],
                            Win[:, k, 2048 + j * 128:2048 + (j + 1) * 128], start=(k == 0), stop=(k == 7)))
                fw.pe_group(fns, reads=kWin + [T("hT")], writes=["ps7"])
                src = env.bank(7).rearrange("p (a b) -> p a b", a=4)
                fw.act(lambda e: e.copy(V2e[:, g4 * 4:(g4 + 1) * 4, 0:64], src[:, :, 0:64]),
                       reads=["ps7"], writes=[T("V2e")])
                fw.act(lambda e: e.copy(V2o[:, g4 * 4:(g4 + 1) * 4, 64:128], src[:, :, 64:128]),
                       reads=["ps7"], writes=[T("tmp")])

        def attention(sq_i, h, nxt, every, exhaust):
            par = h % 2
            slot = 2 * ((h // 2) % 2) + par
            qa, ka = QA[slot], KA[slot]
            kqa, kka = KQ[slot], KK[slot]
            V2, kV2 = (V2e, T("V2e")) if par == 0 else (V2o, T("tmp"))
            orow = slice(par * 64, (par + 1) * 64)
            lrow = slice((1 - par) * 64, (2 - par) * 64)
            items = [(qt, kt) for qt in range(4) for kt in range(4 * qt + 4)]

            def geom(i):
                qt, kt = items[i]
                j = kt - 4 * qt
                return qt, kt, j, max(0, j) * 128

            def S(i):
                qt, kt, j, col0 = geom(i)
                sb_ = i % 2
                q0 = qt * 512
                fns = [lambda e: e.matmul(env.bank(sb_)[:, col0:512], ka[0:67, kt * 128:(kt + 1) * 128],
                                          qa[0:67, q0 + col0:q0 + 512], start=True, stop=(j < 0))]
                if j >= 0:
                    fns.append(lambda e: e.matmul(env.bank(sb_)[:, col0:col0 + 128], env.identb[:], maskb[:],
                                                  start=False, stop=True))
                fw.pe_group(fns, reads=kka + kqa + [T("maskb"), "identb"], writes=["ps%d" % sb_])

            S(0)
            for i in range(len(items)):
                qt, kt, j, col0 = geom(i)
                nkt = 4 * qt + 4
                accb = 4 + (qt % 2)
                sb_ = i % 2
                q0 = qt * 512
                if i + 1 < len(items):
                    S(i + 1)
                pT, kpT = PT[i % 2], T("pT%d" % (i % 2))
                fw.act(lambda e: e.activation(out=pT[:, col0:512], in_=env.bank(sb_)[:, col0:512], func=AF.Exp,
                                              bias=negc[:, kt, h:h + 1], scale=0.125),
                       reads=["ps%d" % sb_, T("negc")], writes=[kpT])
                fw.pe(lambda e: e.matmul(env.bank(accb)[:, col0:512], V2[:, kt, :], pT[:, col0:512],
                                         start=(kt == 0), stop=(kt == nkt - 1)),
                      reads=[kV2, kpT], writes=["ps%d" % accb])
                if kt == nkt - 1:
                    fw.dve(lambda e: e.reciprocal(rl[lrow, :], env.bank(accb)[lrow, :]),
                           reads=["ps%d" % accb], writes=[T("rl")])
                    fw.dve(lambda e: e.tensor_tensor(out=og[orow, :], in0=env.bank(accb)[orow, :], in1=rl[lrow, :],
                                                     op=ALU.mult), reads=["ps%d" % accb, T("rl")], writes=[T("og")])
                    ots, kots = OTS[qt % 2], T("ots%d" % (qt % 2))
                    fw.dve(lambda e: e.tensor_tensor(out=ots[orow, :], in0=og[orow, :], in1=gT[orow, q0:q0 + 512],
                                                     op=ALU.mult), reads=[T("og"), T("gT")], writes=[kots])
                    fw.dma("sync", oT_d[sq_i, h, :, q0:q0 + 512], ots[orow, :], reads=[kots], writes=[T("oTd")])
                if nxt is not None and i % every == every - 1:
                    next(nxt, None)
            if nxt is not None and exhaust:
                for _ in nxt:
                    pass

        def pre_tile(sq, s, bi):
            r0 = sq * SEQ + s * 128
            xt, kx = XT[bi], T("xt%d" % bi)
            hb, khb = HB[bi], T("hb%d" % bi)
            fw.dma("sync", xt[:], x_in[r0:r0 + 128, :], writes=[kx])
            norm_transpose(env, xt[:], kx, gpre[:], T("gpre"), hb[:], khb, hT, T("hT"), s * 128, 6)

        def out_tile(sq, s, xi):
            r0 = sq * SEQ + s * 128
            ol, kol = OTL[s % 2], (T("oTl0") if s % 2 == 0 else T("rl"))
            for two in range(2):
                fw.dma("sync", ol[two * 64:(two + 1) * 64, :, :],
                       oT_d[sq, :, :, s * 128:(s + 1) * 128].rearrange("(hp two) d t -> two d hp t", two=2)[two],
                       reads=[T("oTd")], writes=[kol])
            xr, kxr = XT[xi], T("xt%d" % xi)
            fw.dma("sync", xr[:], x_in[r0:r0 + 128, :], writes=[kxr])
            pb0 = 2 if s % 2 == 0 else 4
            pso = env.bank(pb0, 2)
            pk = ["ps%d" % pb0, "ps%d" % (pb0 + 1)]
            for hf in range(2):
                fw.pe_group([lambda e, h=h: e.matmul(pso[:, hf, :], ol[:, h, :], Wo[:, h, hf * 512:(hf + 1) * 512],
                                                     start=(h == 0), stop=(h == 7)) for h in range(8)],
                            reads=[kol] + kWo, writes=pk)
            post_norm_residual(env, pso, pk, gpost[:], T("gpost"), xr[:], kxr, TMP[:], T("tmp"), xr[:], kxr)
            fw.dma("sync", x_out[r0:r0 + 128, :], xr[:], reads=[kxr], writes=[])

        for sq_i in range(nseq):
            base = sq_i * SEQ
            if sq_i == 0:
                for s in range(SEQ // 128):
                    pre_tile(0, s, s % 2)
            for t in range(4):
                ts = slice(t * 512, (t + 1) * 512)
                fw.pe_group([lambda e, k=k: e.matmul(env.bank(7)[0:16, :], Win[:, k, 4096:4112], hT[:, k, ts],
                                                     start=(k == 0), stop=(k == 7)) for k in range(8)],
                            reads=kWin + [T("hT")], writes=["ps7"])
                fw.act(lambda e: e.activation(out=lf[:, ts], in_=env.bank(7)[0:16, :], func=AF.Exp, bias=nbf[:],
                                              scale=-1.0), reads=["ps7", T("nbf")], writes=[T("lf")])
            fw.act(lambda e: e.activation(out=lf, in_=lf, func=AF.Ln, bias=1.0, scale=1.0),
                   reads=[T("lf")], writes=[T("lf")])
            for t in range(4):
                ts = slice(t * 512, (t + 1) * 512)
                init = 0.0 if t == 0 else cT[:, t * 512 - 1:t * 512]
                fw.dve(lambda e: e.tensor_tensor_scan(out=cT[:, ts], data0=ones16[:], data1=lf[:, ts], initial=init,
                                                      op0=ALU.mult, op1=ALU.add),
                       reads=[T("lf"), T("ones16"), T("cT")], writes=[T("cT")])
            for kt in range(16):
                fw.pe(lambda e: e.transpose(env.bank(7)[:, kt * 16:(kt + 1) * 16], cT[:, kt * 128:(kt + 1) * 128],
                                            env.identf[0:16, 0:16]),
                      reads=[T("cT"), "identf"], writes=["ps7"])
            fw.dve(lambda e: e.tensor_copy(negc[:].rearrange("p a b -> p (a b)"), env.bank(7)[:, 0:256]),
                   reads=["ps7"], writes=[T("negc")])
            fw.dve(lambda e: e.tensor_scalar(out=lf, in0=cT, scalar1=-8.0, scalar2=None, op0=ALU.mult),
                   reads=[T("cT"), T("lf")], writes=[T("lf")])
            for lvl in range(3):
                cl = csp[32 * lvl:32 * lvl + 16, :]
                fw.dve(lambda e: e.tensor_copy(cspt[:], lf), reads=[T("lf"), T("cspt")], writes=[T("cspt")])
                fw.act(lambda e: e.copy(cl, cspt[:]), reads=[T("cspt")], writes=[T("csp")])
                if lvl < 2:
                    fw.dve(lambda e: e.tensor_tensor(out=lf, in0=lf, in1=cspt[:], op=ALU.subtract),
                           reads=[T("lf"), T("cspt")], writes=[T("lf")])
            fw.dve(lambda e: e.memset(V2o[:, :, 0:64], 1.0), reads=[T("tmp")], writes=[T("tmp")])

            for _ in proj_gen(0):
                pass
            for j in range(NH // 2):
                gv_emit(j)
                nxt = proj_gen(j + 1) if j + 1 < NH // 2 else None
                attention(sq_i, 2 * j, nxt, 4, False)
                attention(sq_i, 2 * j + 1, nxt, 4, True)
            for s in range(SEQ // 128):
                if sq_i + 1 < nseq:
                    out_tile(sq_i, s, 1)
                    pre_tile(sq_i + 1, s, 0)
                else:
                    out_tile(sq_i, s, s % 2)


NG = 64
GP = 32
DBG = {}


def dbgdump(env, name, ap, keys):
    if name in DBG:
        env.fw.dma("sync", DBG[name], ap, reads=keys, writes=["dbg_" + name])

TWO_PI = 6.283185307179586


def s5_phase(env, x_in, x_out, w_in, log_dt, lam_re, lam_im, b_re, b_im, c_re, c_im, d_skip, w_glu, w_out,
             pre_g, post_g, nseq, tag="s5"):
    nc, fw = env.nc, env.fw
    T = lambda s: tag + s
    NTOK = nseq * SEQ
    fw.barrier()
    with contextlib.ExitStack() as st:
        uT = env.sb(T("uT"), [128, 8, NTOK], BF16, st)
        LB = [env.sb(T("LB%d" % i), [128, GP, 128], BF16, st) for i in range(3)]
        LC = [env.sb(T("LC%d" % i), [128, GP, 128], BF16, st) for i in range(3)]
        CM = env.sb(T("CM"), [128, GP, 11], F32, st)
        SM = env.sb(T("SM"), [128, GP, 11], F32, st)
        RR = env.sb(T("RR"), [128, GP], F32, st)
        dvec = env.sb(T("dvec"), [128, 8], F32, st)
        SER = T("ser")

        with contextlib.ExitStack() as s0:
            P = {}
            for nm in ("lre", "lim", "ldt", "dt", "lr", "th", "mag", "f", "s4", "c2", "s2", "sn", "cs", "are", "aim",
                       "den", "nre", "zre", "zim", "t0", "t1"):
                P[nm] = env.sb(T("p_" + nm), [128, GP], F32, s0)
            fi = env.sb(T("p_fi"), [128, GP], I32, s0)
            braw = [env.sb(T("braw%d" % i), [128, GP, 16], F32, s0) for i in range(2)]
            bb = [env.sb(T("bb%d" % i), [128, GP, 16], F32, s0) for i in range(2)]
            Bpad = [env.sb(T("Bpad%d" % i), [128, GP, 128], F32, s0) for i in range(2)]
            Cc = [env.sb(T("Cc%d" % i), [128, 8, 128], F32, s0) for i in range(2)]
            MQ = env.sb(T("MQ"), [128, 4, 128], F32, s0)
            LG = [env.sb(T("LG%d" % i), [64, 128], F32, s0) for i in range(2)]
            LD = env.sb(T("LD"), [128, 64], F32, s0)
            DV = env.sb(T("DV"), [8, 128], F32, s0)
            mhg = env.sb(T("mhg"), [128, GP], F32, s0)
            for i, lsrc in enumerate((lam_re, lam_im)):
                for dup in range(2):
                    fw.dma("sync", LG[i][:, dup * 64:(dup + 1) * 64], lsrc, writes=[T("LG%d" % i), SER])
            fw.dma("sync", LD[:], log_dt.partition_broadcast(128), writes=[T("LD"), SER])
            fw.dma("sync", DV[:], d_skip.rearrange("(blk c) -> blk c", c=128), writes=[T("DV"), SER])
            for gi in range(2):
                rows = slice(gi * 64, (gi + 1) * 64)
                for i, bsrc in enumerate((b_re, b_im)):
                    for g0 in range(0, GP, 8):
                        fw.dma("sync", braw[i][rows, g0:g0 + 8, :],
                               bsrc.rearrange("(gp gi) p c -> gi p gp c", gi=2)[gi][:, g0:g0 + 8, :],
                               writes=[T("braw%d" % i), SER])
                for i, csrc in enumerate((c_re, c_im)):
                    for b0 in range(0, 8, 4):
                        fw.dma("sync", Cc[i][:, b0:b0 + 4, rows],
                               csrc.rearrange("(blk gl) c p -> (gl c) blk p", gl=8)[:, b0:b0 + 4, :],
                               writes=[T("Cc%d" % i), SER])
            fw.dve(lambda e: e.memset(mhg[:], -0.5), reads=[SER], writes=[T("mhg")])
            for i, nm in enumerate(("lre", "lim")):
                fw.pe(lambda e: e.transpose(env.bank(7)[:, 0:64], LG[i][:, :], env.identf[0:64, 0:64]),
                      reads=[T("LG%d" % i), "identf", SER], writes=["ps7"])
                fw.dve(lambda e: e.tensor_copy(P[nm][0:64, :], env.bank(7)[0:64, 0:64:2]), reads=["ps7"],
                       writes=[T(nm)])
                fw.dve(lambda e: e.tensor_copy(P[nm][64:128, :], env.bank(7)[64:128, 1:64:2]), reads=["ps7"],
                       writes=[T(nm)])
            fw.dve(lambda e: e.tensor_copy(P["ldt"][0:64, :], LD[0:64, 0:64:2]), reads=[T("LD"), SER], writes=[T("ldt")])
            fw.dve(lambda e: e.tensor_copy(P["ldt"][64:128, :], LD[64:128, 1:64:2]), reads=[T("LD"), SER],
                   writes=[T("ldt")])
            fw.pe(lambda e: e.transpose(env.bank(7)[:, 64:72], DV[:, :], env.identf[0:8, 0:8]),
                  reads=[T("DV"), "identf", SER], writes=["ps7"])
            fw.dve(lambda e: e.tensor_copy(dvec[:], env.bank(7)[:, 64:72]), reads=["ps7"], writes=[T("dvec")])

            def ew(eng, out, a, b, op, keys):
                getattr(fw, eng)(lambda e: e.tensor_tensor(out=out, in0=a, in1=b, op=op), reads=keys, writes=keys)

            K = [T("setup")]
            dep = [T("lre"), T("lim"), T("ldt"), SER] + K
            fw.act(lambda e: e.activation(out=P["dt"][:], in_=P["ldt"][:], func=AF.Exp), reads=dep, writes=K)
            ew("dve", P["lr"][:], P["lre"][:], P["dt"][:], ALU.mult, dep)
            ew("dve", P["th"][:], P["lim"][:], P["dt"][:], ALU.mult, dep)
            fw.act(lambda e: e.activation(out=P["mag"][:], in_=P["lr"][:], func=AF.Exp), reads=K, writes=K)
            fw.dve(lambda e: e.tensor_scalar(out=P["t0"][:], in0=P["th"][:], scalar1=1.0 / TWO_PI, scalar2=None,
                                             op0=ALU.mult), reads=K, writes=K)
            fw.dve(lambda e: e.tensor_copy(fi[:], P["t0"][:]), reads=K, writes=K)
            fw.dve(lambda e: e.tensor_copy(P["t1"][:], fi[:]), reads=K, writes=K)
            ew("dve", P["f"][:], P["t0"][:], P["t1"][:], ALU.subtract, K)
            fw.act(lambda e: e.activation(out=P["s4"][:], in_=P["f"][:], func=AF.Sin, scale=TWO_PI / 4), reads=K, writes=K)
            fw.act(lambda e: e.activation(out=P["s2"][:], in_=P["f"][:], func=AF.Sin, scale=TWO_PI / 2), reads=K, writes=K)
            ew("dve", P["t0"][:], P["s4"][:], P["s4"][:], ALU.mult, K)
            fw.dve(lambda e: e.tensor_scalar(out=P["c2"][:], in0=P["t0"][:], scalar1=-2.0, scalar2=1.0, op0=ALU.mult,
                                             op1=ALU.add), reads=K, writes=K)
            ew("dve", P["t0"][:], P["s2"][:], P["c2"][:], ALU.mult, K)
            fw.dve(lambda e: e.tensor_scalar(out=P["sn"][:], in0=P["t0"][:], scalar1=2.0, scalar2=None, op0=ALU.mult),
                   reads=K, writes=K)
            ew("dve", P["t0"][:], P["s2"][:], P["s2"][:], ALU.mult, K)
            fw.dve(lambda e: e.tensor_scalar(out=P["cs"][:], in0=P["t0"][:], scalar1=-2.0, scalar2=1.0, op0=ALU.mult,
                                             op1=ALU.add), reads=K, writes=K)
            ew("dve", P["t0"][:], P["cs"][:], P["cs"][:], ALU.mult, K)
            ew("dve", P["t1"][:], P["sn"][:], P["sn"][:], ALU.mult, K)
            ew("dve", P["t0"][:], P["t0"][:], P["t1"][:], ALU.add, K)
            fw.dve(lambda e: e.tensor_scalar(out=P["t0"][:], in0=P["t0"][:], scalar1=-0.5, scalar2=1.5, op0=ALU.mult,
                                             op1=ALU.add), reads=K, writes=K)
            ew("dve", P["cs"][:], P["cs"][:], P["t0"][:], ALU.mult, K)
            ew("dve", P["sn"][:], P["sn"][:], P["t0"][:], ALU.mult, K)
            ew("dve", P["are"][:], P["mag"][:], P["cs"][:], ALU.mult, K)
            ew("dve", P["aim"][:], P["mag"][:], P["sn"][:], ALU.mult, K)
            fw.dve(lambda e: e.tensor_copy(RR[:], P["mag"][:]), reads=K, writes=K + [T("RR")])
            fw.dve(lambda e: e.tensor_copy(CM[:, :, 0], P["cs"][:]), reads=K, writes=K)
            fw.dve(lambda e: e.tensor_copy(SM[:, :, 0], P["sn"][:]), reads=K, writes=K)
            for k in range(1, 11):
                ew("dve", P["t0"][:], SM[:, :, k - 1], CM[:, :, k - 1], ALU.mult, K)
                fw.dve(lambda e: e.tensor_scalar(out=SM[:, :, k], in0=P["t0"][:], scalar1=2.0, scalar2=None,
                                                 op0=ALU.mult), reads=K, writes=K)
                ew("dve", P["t0"][:], SM[:, :, k - 1], SM[:, :, k - 1], ALU.mult, K)
                fw.dve(lambda e: e.tensor_scalar(out=CM[:, :, k], in0=P["t0"][:], scalar1=-2.0, scalar2=1.0,
                                                 op0=ALU.mult, op1=ALU.add), reads=K, writes=K)
            ew("dve", P["den"][:], P["lre"][:], P["lre"][:], ALU.mult, K)
            ew("dve", P["t0"][:], P["lim"][:], P["lim"][:], ALU.mult, K)
            ew("dve", P["den"][:], P["den"][:], P["t0"][:], ALU.add, K)
            fw.dve(lambda e: e.reciprocal(P["den"][:], P["den"][:]), reads=K, writes=K)
            fw.dve(lambda e: e.tensor_scalar(out=P["nre"][:], in0=P["are"][:], scalar1=-1.0, scalar2=None,
                                             op0=ALU.add), reads=K, writes=K)
            ew("dve", P["t0"][:], P["nre"][:], P["lre"][:], ALU.mult, K)
            ew("dve", P["t1"][:], P["aim"][:], P["lim"][:], ALU.mult, K)
            ew("dve", P["t0"][:], P["t0"][:], P["t1"][:], ALU.add, K)
            ew("dve", P["zre"][:], P["t0"][:], P["den"][:], ALU.mult, K)
            ew("dve", P["t0"][:], P["aim"][:], P["lre"][:], ALU.mult, K)
            ew("dve", P["t1"][:], P["nre"][:], P["lim"][:], ALU.mult, K)
            ew("dve", P["t0"][:], P["t0"][:], P["t1"][:], ALU.subtract, K)
            ew("dve", P["zim"][:], P["t0"][:], P["den"][:], ALU.mult, K)
            KB = K + [T("braw0"), T("braw1")]
            for c in range(16):
                ew("dve", P["t0"][:], P["zre"][:], braw[0][:, :, c], ALU.mult, KB)
                ew("dve", P["t1"][:], P["zim"][:], braw[1][:, :, c], ALU.mult, KB)
                ew("dve", bb[0][:, :, c], P["t0"][:], P["t1"][:], ALU.subtract, KB)
                ew("dve", P["t0"][:], P["zre"][:], braw[1][:, :, c], ALU.mult, KB)
                ew("dve", P["t1"][:], P["zim"][:], braw[0][:, :, c], ALU.mult, KB)
                ew("dve", bb[1][:, :, c], P["t0"][:], P["t1"][:], ALU.add, KB)
            for i in range(3):
                if i == 2:
                    fw.dve(lambda e: e.tensor_tensor(out=Bpad[0][:], in0=Bpad[0][:], in1=Bpad[1][:], op=ALU.add),
                           reads=KB + [T("LB")], writes=KB)
                    for gp in range(GP):
                        fw.pe(lambda e: e.transpose(env.bank(gp % 4)[:, 0:128], Bpad[0][:, gp, :], env.identf[:]),
                              reads=KB + ["identf"], writes=["ps%d" % (gp % 4)])
                        fw.act(lambda e: e.copy(LB[2][:, gp, :], env.bank(gp % 4)[:, 0:128]),
                               reads=["ps%d" % (gp % 4)], writes=[T("LB")])
                    break
                fw.dve(lambda e: e.memset(Bpad[i][:], 0.0), reads=KB, writes=KB)
                for gi in range(2):
                    rows = slice(gi * 64, (gi + 1) * 64)
                    for q in range(4):
                        c0 = (2 * q + gi) * 16
                        fw.dve(lambda e: e.tensor_copy(Bpad[i][rows, q::4, c0:c0 + 16], bb[i][rows, q::4, :]),
                               reads=KB, writes=KB)
                for gp in range(GP):
                    fw.pe(lambda e: e.transpose(env.bank(gp % 4)[:, 0:128], Bpad[i][:, gp, :], env.identf[:]),
                          reads=KB + ["identf"], writes=["ps%d" % (gp % 4)])
                    fw.act(lambda e: e.copy(LB[i][:, gp, :], env.bank(gp % 4)[:, 0:128]),
                           reads=["ps%d" % (gp % 4)], writes=[T("LB")])
            dbgdump(env, "are", P["are"][:], K)
            dbgdump(env, "zre", P["zre"][:], K)
            dbgdump(env, "cm", CM[:].rearrange("p a b -> p (a b)"), K)
            dbgdump(env, "bb0", bb[0][:].rearrange("p a b -> p (a b)"), KB)
            dbgdump(env, "braw0", braw[0][:].rearrange("p a b -> p (a b)"), KB)
            dbgdump(env, "zim", P["zim"][:], KB)
            fw.dve(lambda e: e.memset(MQ[:], 0.0), writes=[T("MQ")])
            for q in range(4):
                fw.dve(lambda e: e.memset(MQ[0:64, q, (2 * q) * 16:(2 * q) * 16 + 16], 1.0), writes=[T("MQ")])
                fw.dve(lambda e: e.memset(MQ[64:128, q, (2 * q + 1) * 16:(2 * q + 1) * 16 + 16], 1.0),
                       writes=[T("MQ")])
            fw.dve(lambda e: e.tensor_scalar(out=Cc[1][:], in0=Cc[1][:], scalar1=-1.0, scalar2=None, op0=ALU.mult),
                   reads=[T("Cc1")], writes=[T("Cc1")])
            for i in range(2):
                for blk in range(8):
                    pb = 4 + blk % 2
                    fw.pe(lambda e: e.transpose(env.bank(pb)[:, 0:128], Cc[i][:, blk, :], env.identf[:]),
                          reads=[T("Cc%d" % i), "identf"], writes=["ps%d" % pb])
                    for q in range(4):
                        fw.dve(lambda e: e.tensor_tensor(out=LC[i][:, blk * 4 + q, :], in0=env.bank(pb)[:, 0:128],
                                                         in1=MQ[:, q, :], op=ALU.mult),
                               reads=["ps%d" % pb, T("MQ")], writes=[T("LC")])
                        if i == 0:
                            fw.dve(lambda e: e.scalar_tensor_tensor(out=LC[2][:, blk * 4 + q, :],
                                                                    in0=env.bank(pb)[:, 0:128], scalar=-1.0,
                                                                    in1=MQ[:, q, :], op0=ALU.mult, op1=ALU.mult),
                                   reads=["ps%d" % pb, T("MQ")], writes=[T("LC")])

        dbgdump(env, "LB0", LB[0][:, 0, :], [T("LB")])
        dbgdump(env, "LC0", LC[0][:, 0, :], [T("LC")])
        dbgdump(env, "LC1", LC[1][:, 5, :], [T("LC")])
        fw.barrier()
        with contextlib.ExitStack() as s1:
            Wi = env.sb(T("Wi"), [128, 8, D], BF16, s1)
            gpre = env.sb(T("gpre"), [128, D], F32, s1)
            fw.dma("sync", gpre[:], pre_g.partition_broadcast(128), writes=[T("gpre")])
            HT = [env.sb(T("hT%d" % i), [128, 8, 512], BF16, s1) for i in range(2)]
            XT = [env.sb(T("xt%d" % i), [128, D], F32, s1) for i in range(2)]
            HB = [env.sb(T("hb%d" % i), [128, D], BF16, s1) for i in range(2)]
            kWi = load_w_cast(fw, Wi, T("Wi"), w_in, 8)
            it = 0
            for t in range(NTOK // 512):
                hT, khT = HT[t % 2], T("hT%d" % (t % 2))
                for s in range(4):
                    r0 = t * 512 + s * 128
                    xt, kx = XT[it % 2], T("xt%d" % (it % 2))
                    hb, khb = HB[it % 2], T("hb%d" % (it % 2))
                    fw.dma("sync", xt[:], x_in[r0:r0 + 128, :], writes=[kx])
                    norm_transpose(env, xt[:], kx, gpre[:], T("gpre"), hb[:], khb, hT, khT, s * 128, 6)
                    it += 1
                for blk in range(8):
                    pb = blk % 4
                    fw.pe_group([lambda e, k=k: e.matmul(env.bank(pb), Wi[:, k, blk * 128:(blk + 1) * 128],
                                                         hT[:, k, :], start=(k == 0), stop=(k == 7))
                                 for k in range(8)], reads=kWi + [khT], writes=["ps%d" % pb])
                    fw.act(lambda e: e.copy(uT[:, blk, t * 512:(t + 1) * 512], env.bank(pb)),
                           reads=["ps%d" % pb], writes=[T("uT%d_%d" % (blk, t // 4))])

        fw.barrier()
        with contextlib.ExitStack() as s2:
            TAB = [tuple(env.sb(T("%s%d" % (nm, q)), [128, 512], F32, s2) for nm in ("COS", "SIN", "TM", "TP"))
                   for q in range(4)]
            TT = env.sb(T("TT"), [128, 256], F32, s2)
            RT = [env.sb(T("Rt%d" % q), [128, 512], F32, s2) for q in range(4)]
            ones = env.sb(T("ones"), [128, 512], F32, s2)
            WS = []
            for i in range(2):
                d = {}
                for nm in ("bs", "bre", "bim", "w_re", "w_im"):
                    d[nm] = env.sb(T("w%d_%s" % (i, nm)), [128, 512], F32, s2)
                for nm in ("p1", "p2", "p3", "p4"):
                    d[nm] = env.sb(T("w%d_%s" % (i, nm)), [128, 512], BF16, s2)
                WS.append(d)
            INI = env.sb(T("ini"), [128, 8], F32, s2)
            YV = env.sb(T("yv"), [128, 512], F32, s2)
            G1 = env.sb(T("g1"), [128, 512], F32, s2)
            G2 = env.sb(T("g2"), [128, 512], F32, s2)
            NS9 = env.sb(T("ns9"), [128, GP], F32, s2)
            fw.dve(lambda e: e.memset(ones[:], 1.0), writes=[T("ones")])
            fw.dve(lambda e: e.tensor_scalar(out=NS9[:], in0=SM[:, :, 9], scalar1=-1.0, scalar2=None, op0=ALU.mult),
                   reads=[T("setup")], writes=[T("ns9")])

            def build_table(q, gp):
                COS, SIN, TM, TP = TAB[q]
                KT = [T("tab%d" % q)]
                fw.dve(lambda e: e.memset(COS[:, 0:1], 1.0), reads=KT, writes=KT)
                fw.dve(lambda e: e.memset(SIN[:, 0:1], 0.0), reads=KT, writes=KT)
                for k in range(9):
                    m = 1 << k
                    cm, sm = CM[:, gp, k:k + 1], SM[:, gp, k:k + 1]
                    fw.dve(lambda e: e.tensor_scalar(out=TT[:, 0:m], in0=SIN[:, 0:m], scalar1=sm, scalar2=None,
                                                     op0=ALU.mult), reads=KT + [T("setup"), T("TT")], writes=[T("TT")])
                    fw.dve(lambda e: e.scalar_tensor_tensor(out=COS[:, m:2 * m], in0=COS[:, 0:m], scalar=cm,
                                                            in1=TT[:, 0:m], op0=ALU.mult, op1=ALU.subtract),
                           reads=KT + [T("TT")], writes=KT)
                    fw.dve(lambda e: e.tensor_scalar(out=TT[:, 0:m], in0=COS[:, 0:m], scalar1=sm, scalar2=None,
                                                     op0=ALU.mult), reads=KT + [T("TT")], writes=[T("TT")])
                    fw.dve(lambda e: e.scalar_tensor_tensor(out=SIN[:, m:2 * m], in0=SIN[:, 0:m], scalar=cm,
                                                            in1=TT[:, 0:m], op0=ALU.mult, op1=ALU.add),
                           reads=KT + [T("TT")], writes=KT)
                fw.dve(lambda e: e.tensor_tensor(out=TM[:], in0=COS[:], in1=SIN[:], op=ALU.subtract),
                       reads=KT, writes=KT)
                fw.dve(lambda e: e.tensor_tensor(out=TP[:], in0=COS[:], in1=SIN[:], op=ALU.add),
                       reads=KT, writes=KT)
                fw.dve(lambda e: e.tensor_scalar(out=RT[q][:], in0=ones[:], scalar1=RR[:, gp:gp + 1], scalar2=None,
                                                 op0=ALU.mult), reads=[T("ones"), T("RR"), T("Rt%d" % q)],
                       writes=[T("Rt%d" % q)])

            def stage_a(i, blk, sq_i, q, t):
                gp = blk * 4 + q
                W, kw = WS[i % 2], T("ws%d" % (i % 2))
                COS, SIN, TM, TP = TAB[q]
                KT = [T("tab%d" % q)]
                cols = slice(sq_i * SEQ + t * 512, sq_i * SEQ + (t + 1) * 512)
                ku = T("uT%d_%d" % (blk, sq_i))
                for (bnk, li, dst) in ((0, 2, "bs"), (1, 0, "bre"), (2, 1, "bim")):
                    fw.pe(lambda e: e.matmul(env.bank(bnk), LB[li][:, gp, :], uT[:, blk, cols], start=True,
                                             stop=True), reads=[T("LB"), ku], writes=["ps%d" % bnk])
                    fw.act(lambda e: e.copy(W[dst][:], env.bank(bnk)), reads=["ps%d" % bnk], writes=[kw + dst])
                fw.dve(lambda e: e.tensor_tensor(out=W["bs"][:], in0=W["bs"][:], in1=COS[:], op=ALU.mult),
                       reads=[kw + "bs"] + KT, writes=[kw + "bs"])
                fw.dve(lambda e: e.tensor_tensor(out=W["bim"][:], in0=W["bim"][:], in1=TM[:], op=ALU.mult),
                       reads=[kw + "bim"] + KT, writes=[kw + "bim"])
                fw.dve(lambda e: e.tensor_tensor(out=W["bre"][:], in0=W["bre"][:], in1=TP[:], op=ALU.mult),
                       reads=[kw + "bre"] + KT, writes=[kw + "bre"])
                fw.dve(lambda e: e.tensor_tensor(out=W["bim"][:], in0=W["bs"][:], in1=W["bim"][:], op=ALU.subtract),
                       reads=[kw + "bs", kw + "bim"], writes=[kw + "bim"])
                fw.dve(lambda e: e.tensor_tensor(out=W["bre"][:], in0=W["bs"][:], in1=W["bre"][:], op=ALU.subtract),
                       reads=[kw + "bs", kw + "bre"], writes=[kw + "bre"])

            def stage_b(i, blk, sq_i, q, t):
                gp = blk * 4 + q
                W, kw = WS[i % 2], T("ws%d" % (i % 2))
                Wp, kwp = WS[(i + 1) % 2], T("ws%d" % ((i + 1) % 2))
                COS, SIN, TM, TP = TAB[q]
                KT = [T("tab%d" % q)]
                if t == 0:
                    ire, iim = 0.0, 0.0
                    kini = []
                else:
                    c9, s9, ns9 = CM[:, gp, 9:10], SM[:, gp, 9:10], NS9[:, gp:gp + 1]
                    lre, lim_ = Wp["w_re"][:, 511:512], Wp["w_im"][:, 511:512]
                    o = 4 * (i % 2)
                    kini = [T("ini%d" % (i % 2))]
                    fw.act(lambda e: e.activation(out=INI[:, o:o + 1], in_=lim_, func=AF.Identity, scale=ns9),
                           reads=[kwp + "w_im", T("ns9")] + kini, writes=kini)
                    fw.act(lambda e: e.activation(out=INI[:, o + 1:o + 2], in_=lre, func=AF.Identity, scale=c9,
                                                  bias=INI[:, o:o + 1]), reads=[kwp + "w_re", T("setup")] + kini,
                           writes=kini)
                    fw.act(lambda e: e.activation(out=INI[:, o + 2:o + 3], in_=lre, func=AF.Identity, scale=s9),
                           reads=[kwp + "w_re"] + kini, writes=kini)
                    fw.act(lambda e: e.activation(out=INI[:, o + 3:o + 4], in_=lim_, func=AF.Identity, scale=c9,
                                                  bias=INI[:, o + 2:o + 3]), reads=[kwp + "w_im"] + kini, writes=kini)
                    ire, iim = INI[:, o + 1:o + 2], INI[:, o + 3:o + 4]
                fw.dve(lambda e: e.tensor_tensor_scan(out=W["w_re"][:], data0=RT[q][:], data1=W["bim"][:],
                                                      initial=ire, op0=ALU.mult, op1=ALU.add),
                       reads=[kw + "bim", T("Rt%d" % q)] + kini, writes=[kw + "w_re"])
                fw.dve(lambda e: e.tensor_tensor_scan(out=W["w_im"][:], data0=RT[q][:], data1=W["bre"][:],
                                                      initial=iim, op0=ALU.mult, op1=ALU.add),
                       reads=[kw + "bre", T("Rt%d" % q)] + kini, writes=[kw + "w_im"])
                yb = 4 + t
                plan = (("p1", "w_re", COS, "dve", 0), ("p2", "w_im", SIN, "dve", 2), ("p3", "w_re", SIN, "dve", 1),
                        ("p4", "w_im", COS, "dve", 1))
                for n_, (o_, a_, tab, eng, lc) in enumerate(plan):
                    getattr(fw, eng)(lambda e: e.tensor_tensor(out=W[o_][:], in0=W[a_][:], in1=tab[:], op=ALU.mult),
                                     reads=[kw + a_] + KT, writes=[kw + o_])
                for n_, (o_, a_, tab, eng, lc) in enumerate(plan):
                    fw.pe(lambda e: e.matmul(env.bank(yb), LC[lc][:, gp, :], W[o_][:], start=(q == 0 and n_ == 0),
                                             stop=(q == 3 and n_ == 3)),
                          reads=[T("LC"), kw + o_], writes=["ps%d" % yb])

            def epilogue(blk, sq_i):
                ku = T("uT%d_%d" % (blk, sq_i))
                for t in range(4):
                    cols = slice(sq_i * SEQ + t * 512, sq_i * SEQ + (t + 1) * 512)
                    yb = 4 + t
                    KY = [T("yv")]
                    fw.dve(lambda e: e.scalar_tensor_tensor(out=YV[:], in0=uT[:, blk, cols],
                                                            scalar=dvec[:, blk:blk + 1], in1=env.bank(yb),
                                                            op0=ALU.mult, op1=ALU.add),
                           reads=[ku, T("dvec"), "ps%d" % yb] + KY, writes=KY)
                    fw.act(lambda e: e.activation(out=G1[:], in_=YV[:], func=AF.Square), reads=KY, writes=KY)
                    fw.act(lambda e: e.activation(out=G1[:], in_=G1[:], func=AF.Identity, scale=0.044715, bias=1.0),
                           reads=KY, writes=KY)
                    fw.dve(lambda e: e.tensor_tensor(out=G1[:], in0=G1[:], in1=YV[:], op=ALU.mult),
                           reads=KY, writes=KY)
                    fw.act(lambda e: e.activation(out=G2[:], in_=G1[:], func=AF.Tanh, scale=0.7978845608028654),
                           reads=KY, writes=KY)
                    fw.act(lambda e: e.activation(out=G2[:], in_=G2[:], func=AF.Identity, scale=0.5, bias=0.5),
                           reads=KY, writes=KY)
                    fw.dve(lambda e: e.tensor_tensor(out=uT[:, blk, cols], in0=G2[:], in1=YV[:], op=ALU.mult),
                           reads=KY + [ku], writes=KY + [ku])

            gi_ = 0
            for blk in range(8):
                for q in range(4):
                    build_table(q, blk * 4 + q)
                items = [(sq_i, q, t) for sq_i in range(nseq) for q in range(4) for t in range(4)]
                stage_a(gi_, blk, *items[0])
                for n, it_ in enumerate(items):
                    if n + 1 < len(items):
                        stage_a(gi_ + 1, blk, *items[n + 1])
                    stage_b(gi_, blk, *it_)
                    gi_ += 1
                    if it_[1] == 3 and it_[2] == 3:
                        epilogue(blk, it_[0])

        dbgdump(env, "yg0", uT[:, 0, 0:512], [T("uT0")])
        fw.barrier()
        with contextlib.ExitStack() as s3:
            Wg = env.sb(T("Wglu"), [128, 8, D], BF16, s3)
            gpost = env.sb(T("gpost"), [128, D], F32, s3)
            fw.dma("sync", gpost[:], post_g.partition_broadcast(128), writes=[T("gpost")])
            Wo = env.sb(T("Wo"), [128, 8, D], BF16, s3)
            ZT = [env.sb(T("zT%d" % i), [128, 8, 512], BF16, s3) for i in range(2)]
            SG = [env.sb(T("sg%d" % i), [128, 512], F32, s3) for i in range(4)]
            XR = [env.sb(T("xr%d" % i), [128, D], F32, s3) for i in range(2)]
            TMP = env.sb(T("tmp"), [128, D], F32, s3)
            kWglu = load_w_cast(fw, Wg, T("Wglu"), w_glu, 8)
            kWo5 = load_w_cast(fw, Wo, T("Wo"), w_out, 8)
            ukeys = [T("uT%d_%d" % (b, q_)) for b in range(8) for q_ in range(nseq)]
            for t in range(NTOK // 512):
                cols = slice(t * 512, (t + 1) * 512)
                zT, kzT = ZT[t % 2], T("zT%d" % (t % 2))
                for blk in range(8):
                    pb = (0, 1, 6, 7)[blk % 4]
                    fw.pe_group([lambda e, k=k: e.matmul(env.bank(pb), Wg[:, k, blk * 128:(blk + 1) * 128],
                                                         uT[:, k, cols], start=(k == 0), stop=(k == 7))
                                 for k in range(8)], reads=kWglu + ukeys, writes=["ps%d" % pb])
                    sg, ksg = SG[blk % 4], T("sg%d" % (blk % 4))
                    fw.act(lambda e: e.activation(out=sg[:], in_=env.bank(pb), func=AF.Sigmoid),
                           reads=["ps%d" % pb], writes=[ksg])
                    fw.dve(lambda e: e.tensor_tensor(out=zT[:, blk, :], in0=sg[:], in1=uT[:, blk, cols], op=ALU.mult),
                           reads=[ksg] + ukeys, writes=[kzT])
                for s in range(4):
                    r0 = t * 512 + s * 128
                    xr, kxr = XR[s % 2], T("xr%d" % (s % 2))
                    fw.dma("sync", xr[:], x_in[r0:r0 + 128, :], writes=[kxr])
                    pb0 = 2 + 2 * (s % 2)
                    pso = env.bank(pb0, 2)
                    pk = ["ps%d" % pb0, "ps%d" % (pb0 + 1)]
                    for hf in range(2):
                        fw.pe_group([lambda e, k=k: e.matmul(pso[:, hf, :], zT[:, k, s * 128:(s + 1) * 128],
                                                             Wo[:, k, hf * 512:(hf + 1) * 512], start=(k == 0),
                                                             stop=(k == 7)) for k in range(8)],
                                    reads=[kzT] + kWo5, writes=pk)
                    post_norm_residual(env, pso, pk, gpost[:], T("gpost"), xr[:], kxr, TMP[:], T("tmp"),
                                       xr[:], kxr)
                    fw.dma("sync", x_out[r0:r0 + 128, :], xr[:], reads=[kxr], writes=[])


NSEQ_CORE = 2
NCORES = 8
_CACHE = {}


def build_program():
    nc = bass.Bass("TRN2", target_bir_lowering=False)
    ntok = NSEQ_CORE * SEQ

    def din(name, shape):
        return nc.dram_tensor(name, shape, F32, kind="ExternalInput").ap()

    x = din("x", [ntok, D])
    fox_w_in = din("fox_w_in", [D, 4112])
    fox_b_f = din("fox_b_f", [NH])
    fox_q_gain = din("fox_q_gain", [HD])
    fox_k_gain = din("fox_k_gain", [HD])
    fox_w_out = din("fox_w_out", [DATT, D])
    s5_w_in = din("s5_w_in", [D, D])
    s5_log_dt = din("s5_log_dt", [NG])
    s5_lam_re = din("s5_lam_re", [NG, 64])
    s5_lam_im = din("s5_lam_im", [NG, 64])
    s5_b_re = din("s5_b_re", [NG, 64, 16])
    s5_b_im = din("s5_b_im", [NG, 64, 16])
    s5_c_re = din("s5_c_re", [NG, 16, 64])
    s5_c_im = din("s5_c_im", [NG, 16, 64])
    s5_d = din("s5_d", [D])
    s5_w_glu = din("s5_w_glu", [D, D])
    s5_w_out = din("s5_w_out", [D, D])
    mix_pre = din("mix_pre_gain", [2, D])
    mix_post = din("mix_post_gain", [2, D])
    ffn_pre = din("ffn_pre_gain", [2, D])
    ffn_post = din("ffn_post_gain", [2, D])
    ffn_wg = din("ffn_w_gate", [2, D, DFF])
    ffn_wu = din("ffn_w_up", [2, D, DFF])
    ffn_wd = din("ffn_w_down", [2, DFF, D])
    ident = din("c_ident", [128, 128])
    cmask = din("c_mask", [128, 128])
    out = nc.dram_tensor("out", [ntok, D], F32, kind="ExternalOutput").ap()
    xa = nc.dram_tensor("xa", [ntok, D], F32).ap()
    xb = nc.dram_tensor("xb", [ntok, D], F32).ap()
    xc = nc.dram_tensor("xc", [ntok, D], F32).ap()
    oT_d = nc.dram_tensor("oT_d", [NSEQ_CORE, NH, HD, SEQ], BF16).ap()

    fw = FW(nc)
    with contextlib.ExitStack() as st:
        env = Env(nc, fw, st)
        env.init_consts(ident)
        fox_phase(env, x, xa, fox_w_in, fox_b_f, fox_q_gain, fox_k_gain, fox_w_out, mix_pre[0, :], mix_post[0, :],
                  cmask, oT_d, NSEQ_CORE)
        ffn_phase(env, xa, xb, ffn_wg[0], ffn_wu[0], ffn_wd[0], ffn_pre[0, :], ffn_post[0, :], ntok, "f0")
        s5_phase(env, xb, xc, s5_w_in, s5_log_dt, s5_lam_re, s5_lam_im, s5_b_re, s5_b_im, s5_c_re, s5_c_im, s5_d,
                 s5_w_glu, s5_w_out, mix_pre[1, :], mix_post[1, :], NSEQ_CORE)
        ffn_phase(env, xc, out, ffn_wg[1], ffn_wu[1], ffn_wd[1], ffn_pre[1, :], ffn_post[1, :], ntok, "f1")
        fw.finish()
    return nc


def kernel(**inputs):
    f = np.float32
    x = np.ascontiguousarray(inputs["x"], dtype=f)
    B = x.shape[0]
    shared = {}
    for k in ("fox_w_in", "fox_b_f", "fox_q_gain", "fox_k_gain", "fox_w_out", "s5_w_in", "s5_log_dt", "s5_lam_re",
              "s5_lam_im", "s5_b_re", "s5_b_im", "s5_c_re", "s5_c_im", "s5_d", "s5_w_glu", "s5_w_out"):
        shared[k] = np.ascontiguousarray(np.asarray(inputs[k], dtype=f)[0])
    for k in ("mix_pre_gain", "mix_post_gain", "ffn_pre_gain", "ffn_post_gain", "ffn_w_gate", "ffn_w_up",
              "ffn_w_down"):
        shared[k] = np.ascontiguousarray(np.asarray(inputs[k], dtype=f))
    shared["c_ident"] = np.eye(128, dtype=f)
    shared["c_mask"] = np.where(np.arange(128)[None, :] < np.arange(128)[:, None], MASKNEG, 0.0).astype(f)
    if "nc" not in _CACHE:
        _CACHE["nc"] = build_program()
    nc = _CACHE["nc"]
    in_maps = []
    for c in range(NCORES):
        m = dict(shared)
        m["x"] = np.ascontiguousarray(x[c * NSEQ_CORE:(c + 1) * NSEQ_CORE].reshape(NSEQ_CORE * SEQ, D))
        in_maps.append(m)
    res = run_bass_kernel_spmd(nc, in_maps, core_ids=list(range(NCORES)))
    outs = [np.asarray(r["out"]).reshape(NSEQ_CORE, SEQ, D) for r in res.results]
    return np.concatenate(outs, axis=0).astype(f)
```

```python
import contextlib
import numpy as np
import concourse.bass as bass
import concourse.mybir as mybir
from concourse.bass_utils import run_bass_kernel_spmd

F32 = mybir.dt.float32
BF16 = mybir.dt.bfloat16
I32 = mybir.dt.int32
AF = mybir.ActivationFunctionType
ALU = mybir.AluOpType
AX = mybir.AxisListType


STRICT_SAME_ENGINE = True


class FW:
    def __init__(self, nc, n_dma_sems=6):
        self.nc = nc
        self.stack = contextlib.ExitStack()
        self.E = {}
        self.sems = []
        for name, h in (("pe", nc.tensor), ("act", nc.scalar), ("dve", nc.vector),
                        ("pool", nc.gpsimd), ("sync", nc.sync)):
            e = {"name": name, "h": h, "count": 0, "waited": {}, "dma": [], "dma_i": 0}
            if name != "sync":
                e["sem"] = self._newsem("c_" + name)
            for i in range(n_dma_sems):
                e["dma"].append([self._newsem("d_%s%d" % (name, i)), 0])
            self.E[name] = e
        self.lastw = {}
        self.readers = {}
        self.nwaits = 0

    def _newsem(self, name):
        s = self.stack.enter_context(self.nc.semaphore(name))
        self.sems.append(s)
        return len(self.sems) - 1

    def _wait(self, E, sid, val):
        if E["waited"].get(sid, 0) >= val:
            return
        E["h"].wait_ge(self.sems[sid], val)
        E["waited"][sid] = val
        self.nwaits += 1

    def _sync(self, E, reads, writes, attach=False):
        need = {}

        def add(ev):
            if ev is None:
                return
            sid, val = ev
            if need.get(sid, 0) < val:
                need[sid] = val

        for r in reads:
            add(self.lastw.get(r))
        for w in writes:
            add(self.lastw.get(w))
            for sid, val in self.readers.get(w, {}).items():
                add((sid, val))
        own = E.get("sem")
        todo = []
        for sid, val in need.items():
            if sid == own:
                if E["name"] == "pe":
                    continue
                if STRICT_SAME_ENGINE is False and E["name"] in ("dve", "act") and val < E["count"]:
                    continue
            if E["waited"].get(sid, 0) >= val:
                continue
            todo.append((sid, val))
        held = None
        if attach and todo:
            held = todo.pop()
        for sid, val in todo:
            self._wait(E, sid, val)
        return held

    def _attach(self, E, ins, held):
        if held is not None:
            sid, val = held
            ins._wait_ge(self.sems[sid], val)
            E["waited"][sid] = val
            self.nwaits += 1

    def _record(self, ev, reads, writes):
        sid, val = ev
        for r in reads:
            d = self.readers.setdefault(r, {})
            if d.get(sid, 0) < val:
                d[sid] = val
        for w in writes:
            self.lastw[w] = ev
            self.readers[w] = {}

    def op(self, en, fn, reads=(), writes=()):
        E = self.E[en]
        held = self._sync(E, reads, writes, attach=(en != "pe"))
        ins = fn(E["h"])
        self._attach(E, ins, held)
        E["count"] += 1
        ins.then_inc(self.sems[E["sem"]], 1)
        self._record((E["sem"], E["count"]), reads, writes)
        return ins

    def pe(self, fn, reads=(), writes=()):
        return self.op("pe", fn, reads, writes)

    def act(self, fn, reads=(), writes=()):
        return self.op("act", fn, reads, writes)

    def dve(self, fn, reads=(), writes=()):
        return self.op("dve", fn, reads, writes)

    def pool(self, fn, reads=(), writes=()):
        return self.op("pool", fn, reads, writes)

    def pe_group(self, fns, reads=(), writes=()):
        E = self.E["pe"]
        held = self._sync(E, reads, writes, attach=False)
        ins = None
        for n, fn in enumerate(fns):
            ins = fn(E["h"])
            if n == 0:
                self._attach(E, ins, held)
        E["count"] += 1
        ins.then_inc(self.sems[E["sem"]], 1)
        self._record((E["sem"], E["count"]), reads, writes)

    def dma(self, q, out, in_, reads=(), writes=(), **kw):
        E = self.E[q]
        self._sync(E, reads, writes)
        slot = E["dma"][E["dma_i"] % len(E["dma"])]
        E["dma_i"] += 1
        sid, target = slot
        if target > 0:
            self._wait(E, sid, target)
        E["h"].dma_start(out=out, in_=in_, **kw).then_inc(self.sems[sid], 16)
        slot[1] = target + 16
        self._record((sid, slot[1]), reads, writes)

    def barrier(self):
        evs = []
        for e in self.E.values():
            for sid, target in e["dma"]:
                if target > 0:
                    evs.append((sid, target))
            if "sem" in e and e["count"] > 0:
                evs.append((e["sem"], e["count"]))
        for E in self.E.values():
            for sid, val in evs:
                self._wait(E, sid, val)

    def finish(self):
        S = self.E["sync"]
        for e in self.E.values():
            for sid, target in e["dma"]:
                if target > 0:
                    self._wait(S, sid, target)
            if "sem" in e and e["count"] > 0:
                self._wait(S, e["sem"], e["count"])
        self.stack.close()


EPS = 1e-6
D = 1024
DFF = 2816
NFC = DFF // 128


class Env:
    def __init__(self, nc, fw, st):
        self.nc, self.fw, self.st = nc, fw, st
        self.ps = st.enter_context(nc.psum_tensor("psall", [128, 8, 512], F32))
        self.identb = self.sb("identb", [128, 128], BF16)
        self.identf = self.sb("identf", [128, 128], F32)
        self.mhalf = self.sb("mhalf", [128, 1], F32)
        self.stats = self.sb("stats", [128, 96], F32)
        self.junk = self.sb("junk", [128, 1024], BF16)
        self.si = 0

    def sb(self, name, shape, dt, st=None):
        return (st or self.st).enter_context(self.nc.sbuf_tensor(name, shape, dt))

    def bank(self, i, n=1):
        if n == 1:
            return self.ps[:, i, :]
        return self.ps[:, i:i + n, :]

    def stat(self):
        i = self.si % 96
        self.si += 1
        return self.stats[:, i:i + 1], "st%d" % i

    def init_consts(self, ident_dram):
        fw = self.fw
        fw.dma("sync", self.identf[:], ident_dram, writes=["identf"])
        fw.dve(lambda e: e.tensor_copy(self.identb[:], self.identf[:]), reads=["identf"], writes=["identb"])
        fw.dve(lambda e: e.memset(self.mhalf[:], -0.5), writes=["mhalf"])

    def rstd(self, src_ap, src_key, n):
        fw = self.fw
        P = src_ap.shape[0]
        ss, kss = self.stat()
        var, kvar = self.stat()
        rs, krs = self.stat()
        junk = self.junk[0:P, 0:n]
        sk = list(src_key) if isinstance(src_key, (list, tuple)) else [src_key]
        fw.act(lambda e: e.activation(out=junk, in_=src_ap, func=AF.Square, accum_out=ss[0:P, :]),
               reads=sk, writes=[kss, "junk"])
        fw.dve(lambda e: e.tensor_scalar(out=var[0:P, :], in0=ss[0:P, :], scalar1=1.0 / n, scalar2=EPS,
                                         op0=ALU.mult, op1=ALU.add), reads=[kss], writes=[kvar])
        fw.pool(lambda e: e.tensor_tensor(out=rs[0:P, :], in0=var[0:P, :], in1=self.mhalf[0:P, :], op=ALU.pow),
                reads=[kvar, "mhalf"], writes=[krs])
        return rs, krs


def load_w_cast(fw, dst_tile, dst_key, w_dram, kchunks, first=False):
    keys = []
    for k in range(kchunks):
        kk = "%s_k%d" % (dst_key, k)
        fw.dma("pool", dst_tile[:, k, :], w_dram[k * 128:(k + 1) * 128, :], writes=[kk])
        keys.append(kk)
    return keys


def load_w_cast_cols(fw, dst_tile, dst_key, w_dram, kchunks, col_groups):
    src = w_dram.rearrange("(k p) c -> p k c", p=128)
    keys = {}
    for gi, (c0, c1) in enumerate(col_groups):
        kk = "%s_c%d" % (dst_key, gi)
        fw.dma("pool", dst_tile[:, 0:kchunks, c0:c1], src[:, :, c0:c1], writes=[kk])
        keys[gi] = kk
    return keys


def norm_transpose(env, x_ap, x_key, g_ap, g_key, hb, hb_key, hT, hT_key, col0, psT_bank):
    fw = env.fw
    rs, krs = env.rstd(x_ap, x_key, D)
    fw.dve(lambda e: e.scalar_tensor_tensor(out=hb, in0=x_ap, scalar=rs, in1=g_ap, op0=ALU.mult, op1=ALU.mult),
           reads=[x_key, krs, g_key], writes=[hb_key])
    transpose_in(env, hb, hb_key, hT, hT_key, col0, psT_bank)


def transpose_in(env, hb, hb_key, hT, hT_key, col0, psT_bank, on="act"):
    fw = env.fw
    pkey = "ps%d" % psT_bank
    psT = env.bank(psT_bank).bitcast(BF16)
    fns = []
    for j in range(8):
        fns.append(lambda e, j=j: e.transpose(psT[:, j * 128:(j + 1) * 128], hb[:, j * 128:(j + 1) * 128],
                                               env.identb[:]))
    fw.pe_group(fns, reads=[hb_key, "identb"], writes=[pkey])
    src = psT.rearrange("p (j c) -> p j c", j=8)
    dst = hT[:, 0:8, col0:col0 + 128]
    if on == "act":
        fw.act(lambda e: e.copy(dst, src), reads=[pkey], writes=[hT_key])
    else:
        fw.dve(lambda e: e.tensor_copy(dst, src), reads=[pkey], writes=[hT_key])


def post_norm_residual(env, ps_ap, ps_key, g_ap, g_key, xr, xr_key, tmp, tmp_key, xo, xo_key):
    fw = env.fw
    pk = list(ps_key) if isinstance(ps_key, (list, tuple)) else [ps_key]
    rs, krs = env.rstd(ps_ap, pk, D)
    fw.dve(lambda e: e.scalar_tensor_tensor(out=tmp, in0=ps_ap, scalar=rs, in1=g_ap, op0=ALU.mult, op1=ALU.mult),
           reads=pk + [krs, g_key], writes=[tmp_key])
    fw.dve(lambda e: e.tensor_tensor(out=xo, in0=tmp, in1=xr, op=ALU.add),
           reads=[tmp_key, xr_key], writes=[xo_key])


def ffn_phase(env, x_in, x_out, wg, wu, wd, pre_g, post_g, ntok, tag):
    nc, fw = env.nc, env.fw
    fw.barrier()
    with contextlib.ExitStack() as st:
        Wg = env.sb(tag + "Wg", [128, 8, DFF], BF16, st)
        Wu = env.sb(tag + "Wu", [128, 8, DFF], BF16, st)
        Wd = env.sb(tag + "Wd", [128, NFC, D], BF16, st)
        gpre = env.sb(tag + "gpre", [128, D], F32, st)
        gpost = env.sb(tag + "gpost", [128, D], F32, st)
        XT = [env.sb(tag + "xt%d" % i, [128, D], F32, st) for i in range(2)]
        XR = [env.sb(tag + "xr%d" % i, [128, D], F32, st) for i in range(2)]
        HB = [env.sb(tag + "hb%d" % i, [128, D], BF16, st) for i in range(2)]
        hT = env.sb(tag + "hT", [128, 8, 512], BF16, st)
        aT = env.sb(tag + "aT", [128, NFC, 512], BF16, st)
        SG = [env.sb(tag + "sg%d" % i, [128, 512], F32, st) for i in range(2)]
        TMP = env.sb(tag + "tmp", [128, D], F32, st)
        fw.dma("sync", gpre[:], pre_g.partition_broadcast(128), writes=[tag + "gpre"])
        fw.dma("sync", gpost[:], post_g.partition_broadcast(128), writes=[tag + "gpost"])
        grp = [(c, min(c + 256, DFF)) for c in range(0, DFF, 256)]
        kWg, kWu = {}, {}
        srcg = wg.rearrange("(k p) c -> p k c", p=128)
        srcu = wu.rearrange("(k p) c -> p k c", p=128)
        for gi, (c0, c1) in enumerate(grp):
            kWg[gi] = "%sWg_c%d" % (tag, gi)
            kWu[gi] = "%sWu_c%d" % (tag, gi)
            fw.dma("pool", Wg[:, :, c0:c1], srcg[:, :, c0:c1], writes=[kWg[gi]])
            fw.dma("pool", Wu[:, :, c0:c1], srcu[:, :, c0:c1], writes=[kWu[gi]])
        kWd = load_w_cast(fw, Wd, tag + "Wd", wd, NFC)
        ntiles = ntok // 512
        it = [0]

        def build_hT(t):
            for s in range(4):
                r0 = t * 512 + s * 128
                i = it[0] % 2
                it[0] += 1
                xt, kx = XT[i], tag + "xt%d" % i
                hb, khb = HB[i], tag + "hb%d" % i
                fw.dma("sync", xt[:], x_in[r0:r0 + 128, :], writes=[kx])
                norm_transpose(env, xt[:], kx, gpre[:], tag + "gpre", hb[:], khb, hT, tag + "hT", s * 128, 0)

        build_hT(0)
        for t in range(ntiles):
            akeys = []
            for fc in range(NFC):
                bg, bu = fc % 2, 2 + fc % 2
                fs = slice(fc * 128, (fc + 1) * 128)
                fw.pe_group([lambda e, k=k: e.matmul(env.bank(bg), Wg[:, k, fs], hT[:, k, :], start=(k == 0),
                                                     stop=(k == 7)) for k in range(8)],
                            reads=[kWg[fc // 2], tag + "hT"], writes=["ps%d" % bg])
                fw.pe_group([lambda e, k=k: e.matmul(env.bank(bu), Wu[:, k, fs], hT[:, k, :], start=(k == 0),
                                                     stop=(k == 7)) for k in range(8)],
                            reads=[kWu[fc // 2], tag + "hT"], writes=["ps%d" % bu])
                sg = SG[fc % 2]
                ksg = tag + "sg%d" % (fc % 2)
                fw.act(lambda e: e.activation(out=sg[:], in_=env.bank(bg), func=AF.Silu),
                       reads=["ps%d" % bg], writes=[ksg])
                ka = tag + "aT%d" % fc
                fw.dve(lambda e: e.tensor_tensor(out=aT[:, fc, :], in0=sg[:], in1=env.bank(bu), op=ALU.mult),
                       reads=[ksg, "ps%d" % bu], writes=[ka])
                akeys.append(ka)
            if t + 1 < ntiles:
                build_hT(t + 1)
            for s in range(4):
                r0 = t * 512 + s * 128
                xr = XR[s % 2]
                kxr = tag + "xr%d" % (s % 2)
                fw.dma("sync", xr[:], x_in[r0:r0 + 128, :], writes=[kxr])
                pb0 = 4 + 2 * (s % 2)
                pso = env.bank(pb0, 2)
                pk = ["ps%d" % pb0, "ps%d" % (pb0 + 1)]
                for hf in range(2):
                    fw.pe_group([lambda e, fc=fc: e.matmul(pso[:, hf, :], aT[:, fc, s * 128:(s + 1) * 128],
                                                           Wd[:, fc, hf * 512:(hf + 1) * 512], start=(fc == 0),
                                                           stop=(fc == NFC - 1)) for fc in range(NFC)],
                                reads=akeys + kWd, writes=pk)
                post_norm_residual(env, pso, pk, gpost[:], tag + "gpost", xr[:], kxr, TMP[:], tag + "tmp",
                                   xr[:], kxr)
                fw.dma("sync", x_out[r0:r0 + 128, :], xr[:], reads=[kxr], writes=[])


NH = 16
HD = 64
WARM = 0
DATT = 1024
SEQ = 2048
MASKNEG = -240000.0


def fox_phase(env, x_in, x_out, w_in, b_f, q_gain, k_gain, w_out, pre_g, post_g, consts, oT_d, nseq, tag="fx"):
    nc, fw = env.nc, env.fw
    fw.barrier()
    with contextlib.ExitStack() as st:
        T = lambda s: tag + s
        Win = env.sb(tag + "Win", [128, 8, 4112], BF16, st)
        Wo = env.sb(tag + "Wo", [128, 8, D], BF16, st)
        gpre = env.sb(tag + "gpre", [128, D], F32, st)
        gpost = env.sb(tag + "gpost", [128, D], F32, st)
        hT = env.sb(tag + "hT", [128, 8, SEQ], BF16, st)
        XT = [env.sb(tag + "xt%d" % i, [128, D], F32, st) for i in range(2)]
        HB = [env.sb(tag + "hb%d" % i, [128, D], BF16, st) for i in range(2)]
        TMP = env.sb(tag + "tmp", [128, D], F32, st)
        scr = env.sb(tag + "scr", [128, 4096], F32, st)
        lf = scr[0:16, 0:SEQ]
        cT = scr[0:16, SEQ:2 * SEQ]
        QA = [env.sb(tag + "qa%d" % i, [128, SEQ], BF16, st) for i in range(2)]
        KA = [env.sb(tag + "ka%d" % i, [128, SEQ], BF16, st) for i in range(2)]
        QA += [scr[:, 0:1024].bitcast(BF16), scr[:, 1024:2048].bitcast(BF16)]
        KA += [scr[:, 2048:3072].bitcast(BF16), scr[:, 3072:4096].bitcast(BF16)]
        KQ = [[T("qa0")], [T("qa1")], [T("qa2"), T("lf")], [T("qa3"), T("lf")]]
        KK = [[T("ka0")], [T("ka1")], [T("ka2"), T("cT")], [T("ka3"), T("cT")]]
        gT = env.sb(tag + "gT", [128, SEQ], BF16, st)
        V2e = env.sb(tag + "V2e", [128, 16, 128], BF16, st)
        V2o = TMP[:].bitcast(BF16).rearrange("p (a b) -> p a b", a=16)
        PT = [env.sb(tag + "pT%d" % i, [128, 512], BF16, st) for i in range(2)]
        SQ = [env.sb(tag + "sq%d" % i, [128, 512], BF16, st) for i in range(2)]
        RST = [env.sb(tag + "rst%d" % i, [128, 512], F32, st) for i in range(2)]
        bones = env.sb(tag + "bones", [128, 128], BF16, st)
        maskf = env.sb(tag + "maskf", [128, 128], F32, st)
        maskb = env.sb(tag + "maskb", [128, 128], BF16, st)
        qg = env.sb(tag + "qg", [128, 1], F32, st)
        kg = env.sb(tag + "kg", [128, 1], F32, st)
        nbf = env.sb(tag + "nbf", [16, 1], F32, st)
        epsc = env.sb(tag + "epsc", [128, 1], F32, st)
        ones16 = env.sb(tag + "ones16", [16, 512], F32, st)
        csp = env.sb(tag + "csp", [96, SEQ], BF16, st)
        cspt = env.sb(tag + "cspt", [16, SEQ], BF16, st)
        negc = env.sb(tag + "negc", [128, 16, 16], F32, st)
        rl = env.sb(tag + "rl", [128, 512], F32, st)
        og = env.sb(tag + "og", [128, 512], F32, st)
        OTS = [env.sb(tag + "ots%d" % i, [128, 512], BF16, st) for i in range(2)]
        OTL = [env.sb(tag + "oTl0", [128, 8, 128], BF16, st),
               rl[:].bitcast(BF16).rearrange("p (h t) -> p h t", h=8)]

        SER = T("ser")
        fw.dma("sync", gpre[:], pre_g.partition_broadcast(128), writes=[T("gpre"), SER])
        fw.dma("sync", gpost[:], post_g.partition_broadcast(128), writes=[T("gpost"), SER])
        fw.dma("sync", maskf[:], consts, writes=[T("maskf"), SER])
        for half in range(2):
            fw.dma("sync", qg[half * 64:(half + 1) * 64, :], q_gain.rearrange("(p o) -> p o", o=1),
                   writes=[T("qg"), SER])
            fw.dma("sync", kg[half * 64:(half + 1) * 64, :], k_gain.rearrange("(p o) -> p o", o=1),
                   writes=[T("kg"), SER])
        fw.dma("sync", nbf[:], b_f.rearrange("(p o) -> p o", o=1), writes=[T("nbf"), SER])
        fw.dve(lambda e: e.tensor_copy(maskb[:], maskf[:]), reads=[T("maskf"), SER], writes=[T("maskb")])
        fw.dve(lambda e: e.tensor_scalar(out=nbf[:], in0=nbf[:], scalar1=-1.0, scalar2=None, op0=ALU.mult),
               reads=[T("nbf")], writes=[T("nbf")])
        fw.dve(lambda e: e.memset(epsc[:], EPS), writes=[T("epsc")])
        fw.dve(lambda e: e.memset(bones[:], 0.0), writes=[T("bones")])
        fw.dve(lambda e: e.memset(bones[0:64, 0:64], 1.0), writes=[T("bones")])
        fw.dve(lambda e: e.memset(bones[64:128, 64:128], 1.0), writes=[T("bones")])
        fw.dve(lambda e: e.memset(ones16[:], 1.0), writes=[T("ones16")])
        for i in range(4):
            fw.dve(lambda e: e.memset(KA[i][64:128, :], 1.0), writes=KK[i])
            fw.dve(lambda e: e.memset(QA[i][64:128, :], 0.0), writes=KQ[i])
        fw.dve(lambda e: e.memset(V2e[:, :, 64:128], 1.0), writes=[T("V2e")])
        kWin = load_w_cast(fw, Win, T("Win"), w_in, 8)
        kWo = load_w_cast(fw, Wo, T("Wo"), w_out, 8)

        cnt = [0]

        def proj_gen(j):
            sl = [2 * (j % 2), 2 * (j % 2) + 1]
            for (DST, KD, off, gain, kgain) in ((QA, KQ, 0, qg, T("qg")), (KA, KK, 1024, kg, T("kg"))):
                for t in range(4):
                    ts = slice(t * 512, (t + 1) * 512)
                    c = cnt[0]
                    cnt[0] += 1
                    pb = 2 + c % 2
                    sq, ksq = SQ[c % 2], T("sq%d" % (c % 2))
                    rst, krst = RST[c % 2], T("rst%d" % (c % 2))
                    fw.pe_group([lambda e, k=k: e.matmul(env.bank(pb), Win[:, k, off + j * 128:off + (j + 1) * 128],
                                                         hT[:, k, ts], start=(k == 0), stop=(k == 7))
                                 for k in range(8)], reads=kWin + [T("hT")], writes=["ps%d" % pb])
                    fw.act(lambda e: e.activation(out=sq[:], in_=env.bank(pb), func=AF.Square),
                           reads=["ps%d" % pb], writes=[ksq])
                    yield
                    fw.pe(lambda e: e.matmul(env.bank(6), bones[:], sq[:], start=True, stop=True),
                          reads=[T("bones"), ksq], writes=["ps6"])
                    fw.act(lambda e: e.activation(out=rst[:], in_=env.bank(6), func=AF.Ln, bias=epsc[:],
                                                  scale=1.0 / HD), reads=["ps6", T("epsc")], writes=[krst])
                    fw.act(lambda e: e.activation(out=rst[:], in_=rst[:], func=AF.Exp, scale=-0.5),
                           reads=[krst], writes=[krst])
                    for par in range(2):
                        rows = slice(par * 64, (par + 1) * 64)
                        dst = DST[sl[par]]
                        fw.dve(lambda e: e.scalar_tensor_tensor(out=dst[0:64, ts], in0=env.bank(pb)[rows, :],
                                                                scalar=gain[rows, :], in1=rst[rows, :], op0=ALU.mult,
                                                                op1=ALU.mult),
                               reads=["ps%d" % pb, kgain, krst], writes=KD[sl[par]])
                    yield
            for par in range(2):
                h = 2 * j + par
                for l_ in range(3):
                    fw.dma("sync", QA[sl[par]][64 + l_:65 + l_, :], csp[32 * l_ + h:32 * l_ + h + 1, :],
                           reads=[T("csp")], writes=KQ[sl[par]])
            yield

        def gv_emit(j):
            for t in range(4):
                ts = slice(t * 512, (t + 1) * 512)
                pb = 2 + t % 2
                fw.pe_group([lambda e, k=k: e.matmul(env.bank(pb), Win[:, k, 3072 + j * 128:3072 + (j + 1) * 128],
                                                     hT[:, k, ts], start=(k == 0), stop=(k == 7)) for k in range(8)],
                            reads=kWin + [T("hT")], writes=["ps%d" % pb])
                fw.act(lambda e: e.activation(out=gT[:, ts], in_=env.bank(pb), func=AF.Sigmoid),
                       reads=["ps%d" % pb], writes=[T("gT")])
            for g4 in range(4):
                fns = []
                for kk in range(4):
                    kt = g4 * 4 + kk
                    for k in range(8):
                        fns.append(lambda e, k=k, kk=kk, kt=kt: e.matmul(
                            env.bank(7)[:, kk * 128:(kk + 1) * 128], hT[:, k, kt * 128:(kt + 1) * 128],
                            Win[:, k, 2048 + j * 128:2048 + (j + 1) * 128], start=(k == 0), stop=(k == 7)))
                fw.pe_group(fns, reads=kWin + [T("hT")], writes=["ps7"])
                src = env.bank(7).rearrange("p (a b) -> p a b", a=4)
                fw.act(lambda e: e.copy(V2e[:, g4 * 4:(g4 + 1) * 4, 0:64], src[:, :, 0:64]),
                       reads=["ps7"], writes=[T("V2e")])
                fw.act(lambda e: e.copy(V2o[:, g4 * 4:(g4 + 1) * 4, 64:128], src[:, :, 64:128]),
                       reads=["ps7"], writes=[T("tmp")])

        def attention(sq_i, h, nxt, every, exhaust):
            par = h % 2
            slot = 2 * ((h // 2) % 2) + par
            qa, ka = QA[slot], KA[slot]
            kqa, kka = KQ[slot], KK[slot]
            V2, kV2 = (V2e, T("V2e")) if par == 0 else (V2o, T("tmp"))
            orow = slice(par * 64, (par + 1) * 64)
            lrow = slice((1 - par) * 64, (2 - par) * 64)
            items = [(qt, kt) for qt in range(4) for kt in range(4 * qt + 4)]

            def geom(i):
                qt, kt = items[i]
                j = kt - 4 * qt
                return qt, kt, j, max(0, j) * 128

            def S(i):
                qt, kt, j, col0 = geom(i)
                sb_ = i % 2
                q0 = qt * 512
                fns = [lambda e: e.matmul(env.bank(sb_)[:, col0:512], ka[0:67, kt * 128:(kt + 1) * 128],
                                          qa[0:67, q0 + col0:q0 + 512], start=True, stop=(j < 0))]
                if j >= 0:
                    fns.append(lambda e: e.matmul(env.bank(sb_)[:, col0:col0 + 128], env.identb[:], maskb[:],
                                                  start=False, stop=True))
                fw.pe_group(fns, reads=kka + kqa + [T("maskb"), "identb"], writes=["ps%d" % sb_])

            S(0)
            for i in range(len(items)):
                qt, kt, j, col0 = geom(i)
                nkt = 4 * qt + 4
                accb = 4 + (qt % 2)
                sb_ = i % 2
                q0 = qt * 512
                if i + 1 < len(items):
                    S(i + 1)
                pT, kpT = PT[i % 2], T("pT%d" % (i % 2))
                fw.act(lambda e: e.activation(out=pT[:, col0:512], in_=env.bank(sb_)[:, col0:512], func=AF.Exp,
                                              bias=negc[:, kt, h:h + 1], scale=0.125),
                       reads=["ps%d" % sb_, T("negc")], writes=[kpT])
                fw.pe(lambda e: e.matmul(env.bank(accb)[:, col0:512], V2[:, kt, :], pT[:, col0:512],
                                         start=(kt == 0), stop=(kt == nkt - 1)),
                      reads=[kV2, kpT], writes=["ps%d" % accb])
                if kt == nkt - 1:
                    fw.dve(lambda e: e.reciprocal(rl[lrow, :], env.bank(accb)[lrow, :]),
                           reads=["ps%d" % accb], writes=[T("rl")])
                    fw.dve(lambda e: e.tensor_tensor(out=og[orow, :], in0=env.bank(accb)[orow, :], in1=rl[lrow, :],
                                                     op=ALU.mult), reads=["ps%d" % accb, T("rl")], writes=[T("og")])
                    ots, kots = OTS[qt % 2], T("ots%d" % (qt % 2))
                    fw.dve(lambda e: e.tensor_tensor(out=ots[orow, :], in0=og[orow, :], in1=gT[orow, q0:q0 + 512],
                                                     op=ALU.mult), reads=[T("og"), T("gT")], writes=[kots])
                    fw.dma("sync", oT_d[sq_i, h, :, q0:q0 + 512], ots[orow, :], reads=[kots], writes=[T("oTd")])
                if nxt is not None and i % every == every - 1:
                    next(nxt, None)
            if nxt is not None and exhaust:
                for _ in nxt:
                    pass

        def pre_tile(sq, s, bi):
            r0 = sq * SEQ + s * 128
            xt, kx = XT[bi], T("xt%d" % bi)
            hb, khb = HB[bi], T("hb%d" % bi)
            fw.dma("sync", xt[:], x_in[r0:r0 + 128, :], writes=[kx])
            norm_transpose(env, xt[:], kx, gpre[:], T("gpre"), hb[:], khb, hT, T("hT"), s * 128, 6)

        def out_tile(sq, s, xi):
            r0 = sq * SEQ + s * 128
            ol, kol = OTL[s % 2], (T("oTl0") if s % 2 == 0 else T("rl"))
            for two in range(2):
                fw.dma("sync", ol[two * 64:(two + 1) * 64, :, :],
                       oT_d[sq, :, :, s * 128:(s + 1) * 128].rearrange("(hp two) d t -> two d hp t", two=2)[two],
                       reads=[T("oTd")], writes=[kol])
            xr, kxr = XT[xi], T("xt%d" % xi)
            fw.dma("sync", xr[:], x_in[r0:r0 + 128, :], writes=[kxr])
            pb0 = 2 if s % 2 == 0 else 4
            pso = env.bank(pb0, 2)
            pk = ["ps%d" % pb0, "ps%d" % (pb0 + 1)]
            for hf in range(2):
                fw.pe_group([lambda e, h=h: e.matmul(pso[:, hf, :], ol[:, h, :], Wo[:, h, hf * 512:(hf + 1) * 512],
                                                     start=(h == 0), stop=(h == 7)) for h in range(8)],
                            reads=[kol] + kWo, writes=pk)
            post_norm_residual(env, pso, pk, gpost[:], T("gpost"), xr[:], kxr, TMP[:], T("tmp"), xr[:], kxr)
            fw.dma("sync", x_out[r0:r0 + 128, :], xr[:], reads=[kxr], writes=[])

        for sq_i in range(nseq):
            base = sq_i * SEQ
            if sq_i == 0:
                for s in range(SEQ // 128):
                    pre_tile(0, s, s % 2)
            for t in range(4):
                ts = slice(t * 512, (t + 1) * 512)
                fw.pe_group([lambda e, k=k: e.matmul(env.bank(7)[0:16, :], Win[:, k, 4096:4112], hT[:, k, ts],
                                                     start=(k == 0), stop=(k == 7)) for k in range(8)],
                            reads=kWin + [T("hT")], writes=["ps7"])
                fw.act(lambda e: e.activation(out=lf[:, ts], in_=env.bank(7)[0:16, :], func=AF.Exp, bias=nbf[:],
                                              scale=-1.0), reads=["ps7", T("nbf")], writes=[T("lf")])
            fw.act(lambda e: e.activation(out=lf, in_=lf, func=AF.Ln, bias=1.0, scale=1.0),
                   reads=[T("lf")], writes=[T("lf")])
            for t in range(4):
                ts = slice(t * 512, (t + 1) * 512)
                init = 0.0 if t == 0 else cT[:, t * 512 - 1:t * 512]
                fw.dve(lambda e: e.tensor_tensor_scan(out=cT[:, ts], data0=ones16[:], data1=lf[:, ts], initial=init,
                                                      op0=ALU.mult, op1=ALU.add),
                       reads=[T("lf"), T("ones16"), T("cT")], writes=[T("cT")])
            for kt in range(16):
                fw.pe(lambda e: e.transpose(env.bank(7)[:, kt * 16:(kt + 1) * 16], cT[:, kt * 128:(kt + 1) * 128],
                                            env.identf[0:16, 0:16]),
                      reads=[T("cT"), "identf"], writes=["ps7"])
            fw.dve(lambda e: e.tensor_copy(negc[:].rearrange("p a b -> p (a b)"), env.bank(7)[:, 0:256]),
                   reads=["ps7"], writes=[T("negc")])
            fw.dve(lambda e: e.tensor_scalar(out=lf, in0=cT, scalar1=-8.0, scalar2=None, op0=ALU.mult),
                   reads=[T("cT"), T("lf")], writes=[T("lf")])
            for lvl in range(3):
                cl = csp[32 * lvl:32 * lvl + 16, :]
                fw.dve(lambda e: e.tensor_copy(cspt[:], lf), reads=[T("lf"), T("cspt")], writes=[T("cspt")])
                fw.act(lambda e: e.copy(cl, cspt[:]), reads=[T("cspt")], writes=[T("csp")])
                if lvl < 2:
                    fw.dve(lambda e: e.tensor_tensor(out=lf, in0=lf, in1=cspt[:], op=ALU.subtract),
                           reads=[T("lf"), T("cspt")], writes=[T("lf")])
            fw.dve(lambda e: e.memset(V2o[:, :, 0:64], 1.0), reads=[T("tmp")], writes=[T("tmp")])

            for _ in proj_gen(0):
                pass
            for j in range(NH // 2):
                gv_emit(j)
                nxt = proj_gen(j + 1) if j + 1 < NH // 2 else None
                attention(sq_i, 2 * j, nxt, 4, False)
                attention(sq_i, 2 * j + 1, nxt, 4, True)
            for s in range(SEQ // 128):
                if sq_i + 1 < nseq:
                    out_tile(sq_i, s, 1)
                    pre_tile(sq_i + 1, s, 0)
                else:
                    out_tile(sq_i, s, s % 2)


NG = 64
GP = 32
DBG = {}


def dbgdump(env, name, ap, keys):
    if name in DBG:
        env.fw.dma("sync", DBG[name], ap, reads=keys, writes=["dbg_" + name])

TWO_PI = 6.283185307179586


def s5_phase(env, x_in, x_out, w_in, log_dt, lam_re, lam_im, b_re, b_im, c_re, c_im, d_skip, w_glu, w_out,
             pre_g, post_g, nseq, tag="s5"):
    nc, fw = env.nc, env.fw
    T = lambda s: tag + s
    NTOK = nseq * SEQ
    fw.barrier()
    with contextlib.ExitStack() as st:
        uT = env.sb(T("uT"), [128, 8, NTOK], BF16, st)
        LB = [env.sb(T("LB%d" % i), [128, GP, 128], BF16, st) for i in range(3)]
        LC = [env.sb(T("LC%d" % i), [128, GP, 128], BF16, st) for i in range(3)]
        CM = env.sb(T("CM"), [128, GP, 11], F32, st)
        SM = env.sb(T("SM"), [128, GP, 11], F32, st)
        RR = env.sb(T("RR"), [128, GP], F32, st)
        dvec = env.sb(T("dvec"), [128, 8], F32, st)
        SER = T("ser")

        with contextlib.ExitStack() as s0:
            P = {}
            for nm in ("lre", "lim", "ldt", "dt", "lr", "th", "mag", "f", "s4", "c2", "s2", "sn", "cs", "are", "aim",
                       "den", "nre", "zre", "zim", "t0", "t1"):
                P[nm] = env.sb(T("p_" + nm), [128, GP], F32, s0)
            fi = env.sb(T("p_fi"), [128, GP], I32, s0)
            braw = [env.sb(T("braw%d" % i), [128, GP, 16], F32, s0) for i in range(2)]
            bb = [env.sb(T("bb%d" % i), [128, GP, 16], F32, s0) for i in range(2)]
            Bpad = [env.sb(T("Bpad%d" % i), [128, GP, 128], F32, s0) for i in range(2)]
            Cc = [env.sb(T("Cc%d" % i), [128, 8, 128], F32, s0) for i in range(2)]
            MQ = env.sb(T("MQ"), [128, 4, 128], F32, s0)
            LG = [env.sb(T("LG%d" % i), [64, 128], F32, s0) for i in range(2)]
            LD = env.sb(T("LD"), [128, 64], F32, s0)
            DV = env.sb(T("DV"), [8, 128], F32, s0)
            mhg = env.sb(T("mhg"), [128, GP], F32, s0)
            for i, lsrc in enumerate((lam_re, lam_im)):
                for dup in range(2):
                    fw.dma("sync", LG[i][:, dup * 64:(dup + 1) * 64], lsrc, writes=[T("LG%d" % i), SER])
            fw.dma("sync", LD[:], log_dt.partition_broadcast(128), writes=[T("LD"), SER])
            fw.dma("sync", DV[:], d_skip.rearrange("(blk c) -> blk c", c=128), writes=[T("DV"), SER])
            for gi in range(2):
                rows = slice(gi * 64, (gi + 1) * 64)
                for i, bsrc in enumerate((b_re, b_im)):
                    for g0 in range(0, GP, 8):
                        fw.dma("sync", braw[i][rows, g0:g0 + 8, :],
                               bsrc.rearrange("(gp gi) p c -> gi p gp c", gi=2)[gi][:, g0:g0 + 8, :],
                               writes=[T("braw%d" % i), SER])
                for i, csrc in enumerate((c_re, c_im)):
                    for b0 in range(0, 8, 4):
                        fw.dma("sync", Cc[i][:, b0:b0 + 4, rows],
                               csrc.rearrange("(blk gl) c p -> (gl c) blk p", gl=8)[:, b0:b0 + 4, :],
                               writes=[T("Cc%d" % i), SER])
            fw.dve(lambda e: e.memset(mhg[:], -0.5), reads=[SER], writes=[T("mhg")])
            for i, nm in enumerate(("lre", "lim")):
                fw.pe(lambda e: e.transpose(env.bank(7)[:, 0:64], LG[i][:, :], env.identf[0:64, 0:64]),
                      reads=[T("LG%d" % i), "identf", SER], writes=["ps7"])
                fw.dve(lambda e: e.tensor_copy(P[nm][0:64, :], env.bank(7)[0:64, 0:64:2]), reads=["ps7"],
                       writes=[T(nm)])
                fw.dve(lambda e: e.tensor_copy(P[nm][64:128, :], env.bank(7)[64:128, 1:64:2]), reads=["ps7"],
                       writes=[T(nm)])
            fw.dve(lambda e: e.tensor_copy(P["ldt"][0:64, :], LD[0:64, 0:64:2]), reads=[T("LD"), SER], writes=[T("ldt")])
            fw.dve(lambda e: e.tensor_copy(P["ldt"][64:128, :], LD[64:128, 1:64:2]), reads=[T("LD"), SER],
                   writes=[T("ldt")])
            fw.pe(lambda e: e.transpose(env.bank(7)[:, 64:72], DV[:, :], env.identf[0:8, 0:8]),
                  reads=[T("DV"), "identf", SER], writes=["ps7"])
            fw.dve(lambda e: e.tensor_copy(dvec[:], env.bank(7)[:, 64:72]), reads=["ps7"], writes=[T("dvec")])

            def ew(eng, out, a, b, op, keys):
                getattr(fw, eng)(lambda e: e.tensor_tensor(out=out, in0=a, in1=b, op=op), reads=keys, writes=keys)

            K = [T("setup")]
            dep = [T("lre"), T("lim"), T("ldt"), SER] + K
            fw.act(lambda e: e.activation(out=P["dt"][:], in_=P["ldt"][:], func=AF.Exp), reads=dep, writes=K)
            ew("dve", P["lr"][:], P["lre"][:], P["dt"][:], ALU.mult, dep)
            ew("dve", P["th"][:], P["lim"][:], P["dt"][:], ALU.mult, dep)
            fw.act(lambda e: e.activation(out=P["mag"][:], in_=P["lr"][:], func=AF.Exp), reads=K, writes=K)
            fw.dve(lambda e: e.tensor_scalar(out=P["t0"][:], in0=P["th"][:], scalar1=1.0 / TWO_PI, scalar2=None,
                                             op0=ALU.mult), reads=K, writes=K)
            fw.dve(lambda e: e.tensor_copy(fi[:], P["t0"][:]), reads=K, writes=K)
            fw.dve(lambda e: e.tensor_copy(P["t1"][:], fi[:]), reads=K, writes=K)
            ew("dve", P["f"][:], P["t0"][:], P["t1"][:], ALU.subtract, K)
            fw.act(lambda e: e.activation(out=P["s4"][:], in_=P["f"][:], func=AF.Sin, scale=TWO_PI / 4), reads=K, writes=K)
            fw.act(lambda e: e.activation(out=P["s2"][:], in_=P["f"][:], func=AF.Sin, scale=TWO_PI / 2), reads=K, writes=K)
            ew("dve", P["t0"][:], P["s4"][:], P["s4"][:], ALU.mult, K)
            fw.dve(lambda e: e.tensor_scalar(out=P["c2"][:], in0=P["t0"][:], scalar1=-2.0, scalar2=1.0, op0=ALU.mult,
                                             op1=ALU.add), reads=K, writes=K)
            ew("dve", P["t0"][:], P["s2"][:], P["c2"][:], ALU.mult, K)
            fw.dve(lambda e: e.tensor_scalar(out=P["sn"][:], in0=P["t0"][:], scalar1=2.0, scalar2=None, op0=ALU.mult),
                   reads=K, writes=K)
            ew("dve", P["t0"][:], P["s2"][:], P["s2"][:], ALU.mult, K)
            fw.dve(lambda e: e.tensor_scalar(out=P["cs"][:], in0=P["t0"][:], scalar1=-2.0, scalar2=1.0, op0=ALU.mult,
                                             op1=ALU.add), reads=K, writes=K)
            ew("dve", P["t0"][:], P["cs"][:], P["cs"][:], ALU.mult, K)
            ew("dve", P["t1"][:], P["sn"][:], P["sn"][:], ALU.mult, K)
            ew("dve", P["t0"][:], P["t0"][:], P["t1"][:], ALU.add, K)
            fw.dve(lambda e: e.tensor_scalar(out=P["t0"][:], in0=P["t0"][:], scalar1=-0.5, scalar2=1.5, op0=ALU.mult,
                                             op1=ALU.add), reads=K, writes=K)
            ew("dve", P["cs"][:], P["cs"][:], P["t0"][:], ALU.mult, K)
            ew("dve", P["sn"][:], P["sn"][:], P["t0"][:], ALU.mult, K)
            ew("dve", P["are"][:], P["mag"][:], P["cs"][:], ALU.mult, K)
            ew("dve", P["aim"][:], P["mag"][:], P["sn"][:], ALU.mult, K)
            fw.dve(lambda e: e.tensor_copy(RR[:], P["mag"][:]), reads=K, writes=K + [T("RR")])
            fw.dve(lambda e: e.tensor_copy(CM[:, :, 0], P["cs"][:]), reads=K, writes=K)
            fw.dve(lambda e: e.tensor_copy(SM[:, :, 0], P["sn"][:]), reads=K, writes=K)
            for k in range(1, 11):
                ew("dve", P["t0"][:], SM[:, :, k - 1], CM[:, :, k - 1], ALU.mult, K)
                fw.dve(lambda e: e.tensor_scalar(out=SM[:, :, k], in0=P["t0"][:], scalar1=2.0, scalar2=None,
                                                 op0=ALU.mult), reads=K, writes=K)
                ew("dve", P["t0"][:], SM[:, :, k - 1], SM[:, :, k - 1], ALU.mult, K)
                fw.dve(lambda e: e.tensor_scalar(out=CM[:, :, k], in0=P["t0"][:], scalar1=-2.0, scalar2=1.0,
                                                 op0=ALU.mult, op1=ALU.add), reads=K, writes=K)
            ew("dve", P["den"][:], P["lre"][:], P["lre"][:], ALU.mult, K)
            ew("dve", P["t0"][:], P["lim"][:], P["lim"][:], ALU.mult, K)
            ew("dve", P["den"][:], P["den"][:], P["t0"][:], ALU.add, K)
            fw.dve(lambda e: e.reciprocal(P["den"][:], P["den"][:]), reads=K, writes=K)
            fw.dve(lambda e: e.tensor_scalar(out=P["nre"][:], in0=P["are"][:], scalar1=-1.0, scalar2=None,
                                             op0=ALU.add), reads=K, writes=K)
            ew("dve", P["t0"][:], P["nre"][:], P["lre"][:], ALU.mult, K)
            ew("dve", P["t1"][:], P["aim"][:], P["lim"][:], ALU.mult, K)
            ew("dve", P["t0"][:], P["t0"][:], P["t1"][:], ALU.add, K)
            ew("dve", P["zre"][:], P["t0"][:], P["den"][:], ALU.mult, K)
            ew("dve", P["t0"][:], P["aim"][:], P["lre"][:], ALU.mult, K)
            ew("dve", P["t1"][:], P["nre"][:], P["lim"][:], ALU.mult, K)
            ew("dve", P["t0"][:], P["t0"][:], P["t1"][:], ALU.subtract, K)
            ew("dve", P["zim"][:], P["t0"][:], P["den"][:], ALU.mult, K)
            KB = K + [T("braw0"), T("braw1")]
            for c in range(16):
                ew("dve", P["t0"][:], P["zre"][:], braw[0][:, :, c], ALU.mult, KB)
                ew("dve", P["t1"][:], P["zim"][:], braw[1][:, :, c], ALU.mult, KB)
                ew("dve", bb[0][:, :, c], P["t0"][:], P["t1"][:], ALU.subtract, KB)
                ew("dve", P["t0"][:], P["zre"][:], braw[1][:, :, c], ALU.mult, KB)
                ew("dve", P["t1"][:], P["zim"][:], braw[0][:, :, c], ALU.mult, KB)
                ew("dve", bb[1][:, :, c], P["t0"][:], P["t1"][:], ALU.add, KB)
            for i in range(3):
                if i == 2:
                    fw.dve(lambda e: e.tensor_tensor(out=Bpad[0][:], in0=Bpad[0][:], in1=Bpad[1][:], op=ALU.add),
                           reads=KB + [T("LB")], writes=KB)
                    for gp in range(GP):
                        fw.pe(lambda e: e.transpose(env.bank(gp % 4)[:, 0:128], Bpad[0][:, gp, :], env.identf[:]),
                              reads=KB + ["identf"], writes=["ps%d" % (gp % 4)])
                        fw.act(lambda e: e.copy(LB[2][:, gp, :], env.bank(gp % 4)[:, 0:128]),
                               reads=["ps%d" % (gp % 4)], writes=[T("LB")])
                    break
                fw.dve(lambda e: e.memset(Bpad[i][:], 0.0), reads=KB, writes=KB)
                for gi in range(2):
                    rows = slice(gi * 64, (gi + 1) * 64)
                    for q in range(4):
                        c0 = (2 * q + gi) * 16
                        fw.dve(lambda e: e.tensor_copy(Bpad[i][rows, q::4, c0:c0 + 16], bb[i][rows, q::4, :]),
                               reads=KB, writes=KB)
                for gp in range(GP):
                    fw.pe(lambda e: e.transpose(env.bank(gp % 4)[:, 0:128], Bpad[i][:, gp, :], env.identf[:]),
                          reads=KB + ["identf"], writes=["ps%d" % (gp % 4)])
                    fw.act(lambda e: e.copy(LB[i][:, gp, :], env.bank(gp % 4)[:, 0:128]),
                           reads=["ps%d" % (gp % 4)], writes=[T("LB")])
            dbgdump(env, "are", P["are"][:], K)
            dbgdump(env, "zre", P["zre"][:], K)
            dbgdump(env, "cm", CM[:].rearrange("p a b -> p (a b)"), K)
            dbgdump(env, "bb0", bb[0][:].rearrange("p a b -> p (a b)"), KB)
            dbgdump(env, "braw0", braw[0][:].rearrange("p a b -> p (a b)"), KB)
            dbgdump(env, "zim", P["zim"][:], KB)
            fw.dve(lambda e: e.memset(MQ[:], 0.0), writes=[T("MQ")])
            for q in range(4):
                fw.dve(lambda e: e.memset(MQ[0:64, q, (2 * q) * 16:(2 * q) * 16 + 16], 1.0), writes=[T("MQ")])
                fw.dve(lambda e: e.memset(MQ[64:128, q, (2 * q + 1) * 16:(2 * q + 1) * 16 + 16], 1.0),
                       writes=[T("MQ")])
            fw.dve(lambda e: e.tensor_scalar(out=Cc[1][:], in0=Cc[1][:], scalar1=-1.0, scalar2=None, op0=ALU.mult),
                   reads=[T("Cc1")], writes=[T("Cc1")])
            for i in range(2):
                for blk in range(8):
                    pb = 4 + blk % 2
                    fw.pe(lambda e: e.transpose(env.bank(pb)[:, 0:128], Cc[i][:, blk, :], env.identf[:]),
                          reads=[T("Cc%d" % i), "identf"], writes=["ps%d" % pb])
                    for q in range(4):
                        fw.dve(lambda e: e.tensor_tensor(out=LC[i][:, blk * 4 + q, :], in0=env.bank(pb)[:, 0:128],
                                                         in1=MQ[:, q, :], op=ALU.mult),
                               reads=["ps%d" % pb, T("MQ")], writes=[T("LC")])
                        if i == 0:
                            fw.dve(lambda e: e.scalar_tensor_tensor(out=LC[2][:, blk * 4 + q, :],
                                                                    in0=env.bank(pb)[:, 0:128], scalar=-1.0,
                                                                    in1=MQ[:, q, :], op0=ALU.mult, op1=ALU.mult),
                                   reads=["ps%d" % pb, T("MQ")], writes=[T("LC")])

        dbgdump(env, "LB0", LB[0][:, 0, :], [T("LB")])
        dbgdump(env, "LC0", LC[0][:, 0, :], [T("LC")])
        dbgdump(env, "LC1", LC[1][:, 5, :], [T("LC")])
        fw.barrier()
        with contextlib.ExitStack() as s1:
            Wi = env.sb(T("Wi"), [128, 8, D], BF16, s1)
            gpre = env.sb(T("gpre"), [128, D], F32, s1)
            fw.dma("sync", gpre[:], pre_g.partition_broadcast(128), writes=[T("gpre")])
            HT = [env.sb(T("hT%d" % i), [128, 8, 512], BF16, s1) for i in range(2)]
            XT = [env.sb(T("xt%d" % i), [128, D], F32, s1) for i in range(2)]
            HB = [env.sb(T("hb%d" % i), [128, D], BF16, s1) for i in range(2)]
            kWi = load_w_cast(fw, Wi, T("Wi"), w_in, 8)
            it = 0
            for t in range(NTOK // 512):
                hT, khT = HT[t % 2], T("hT%d" % (t % 2))
                for s in range(4):
                    r0 = t * 512 + s * 128
                    xt, kx = XT[it % 2], T("xt%d" % (it % 2))
                    hb, khb = HB[it % 2], T("hb%d" % (it % 2))
                    fw.dma("sync", xt[:], x_in[r0:r0 + 128, :], writes=[kx])
                    norm_transpose(env, xt[:], kx, gpre[:], T("gpre"), hb[:], khb, hT, khT, s * 128, 6)
                    it += 1
                for blk in range(8):
                    pb = blk % 4
                    fw.pe_group([lambda e, k=k: e.matmul(env.bank(pb), Wi[:, k, blk * 128:(blk + 1) * 128],
                                                         hT[:, k, :], start=(k == 0), stop=(k == 7))
                                 for k in range(8)], reads=kWi + [khT], writes=["ps%d" % pb])
                    fw.act(lambda e: e.copy(uT[:, blk, t * 512:(t + 1) * 512], env.bank(pb)),
                           reads=["ps%d" % pb], writes=[T("uT%d_%d" % (blk, t // 4))])

        fw.barrier()
        with contextlib.ExitStack() as s2:
            TAB = [tuple(env.sb(T("%s%d" % (nm, q)), [128, 512], F32, s2) for nm in ("COS", "SIN", "TM", "TP"))
                   for q in range(4)]
            TT = env.sb(T("TT"), [128, 256], F32, s2)
            RT = [env.sb(T("Rt%d" % q), [128, 512], F32, s2) for q in range(4)]
            ones = env.sb(T("ones"), [128, 512], F32, s2)
            WS = []
            for i in range(2):
                d = {}
                for nm in ("bs", "bre", "bim", "w_re", "w_im"):
                    d[nm] = env.sb(T("w%d_%s" % (i, nm)), [128, 512], F32, s2)
                for nm in ("p1", "p2", "p3", "p4"):
                    d[nm] = env.sb(T("w%d_%s" % (i, nm)), [128, 512], BF16, s2)
                WS.append(d)
            INI = env.sb(T("ini"), [128, 8], F32, s2)
            YV = env.sb(T("yv"), [128, 512], F32, s2)
            G1 = env.sb(T("g1"), [128, 512], F32, s2)
            G2 = env.sb(T("g2"), [128, 512], F32, s2)
            NS9 = env.sb(T("ns9"), [128, GP], F32, s2)
            fw.dve(lambda e: e.memset(ones[:], 1.0), writes=[T("ones")])
            fw.dve(lambda e: e.tensor_scalar(out=NS9[:], in0=SM[:, :, 9], scalar1=-1.0, scalar2=None, op0=ALU.mult),
                   reads=[T("setup")], writes=[T("ns9")])

            def build_table(q, gp):
                COS, SIN, TM, TP = TAB[q]
                KT = [T("tab%d" % q)]
                fw.dve(lambda e: e.memset(COS[:, 0:1], 1.0), reads=KT, writes=KT)
                fw.dve(lambda e: e.memset(SIN[:, 0:1], 0.0), reads=KT, writes=KT)
                for k in range(9):
                    m = 1 << k
                    cm, sm = CM[:, gp, k:k + 1], SM[:, gp, k:k + 1]
                    fw.dve(lambda e: e.tensor_scalar(out=TT[:, 0:m], in0=SIN[:, 0:m], scalar1=sm, scalar2=None,
                                                     op0=ALU.mult), reads=KT + [T("setup"), T("TT")], writes=[T("TT")])
                    fw.dve(lambda e: e.scalar_tensor_tensor(out=COS[:, m:2 * m], in0=COS[:, 0:m], scalar=cm,
                                                            in1=TT[:, 0:m], op0=ALU.mult, op1=ALU.subtract),
                           reads=KT + [T("TT")], writes=KT)
                    fw.dve(lambda e: e.tensor_scalar(out=TT[:, 0:m], in0=COS[:, 0:m], scalar1=sm, scalar2=None,
                                                     op0=ALU.mult), reads=KT + [T("TT")], writes=[T("TT")])
                    fw.dve(lambda e: e.scalar_tensor_tensor(out=SIN[:, m:2 * m], in0=SIN[:, 0:m], scalar=cm,
                                                            in1=TT[:, 0:m], op0=ALU.mult, op1=ALU.add),
                           reads=KT + [T("TT")], writes=KT)
                fw.dve(lambda e: e.tensor_tensor(out=TM[:], in0=COS[:], in1=SIN[:], op=ALU.subtract),
                       reads=KT, writes=KT)
                fw.dve(lambda e: e.tensor_tensor(out=TP[:], in0=COS[:], in1=SIN[:], op=ALU.add),
                       reads=KT, writes=KT)
                fw.dve(lambda e: e.tensor_scalar(out=RT[q][:], in0=ones[:], scalar1=RR[:, gp:gp + 1], scalar2=None,
                                                 op0=ALU.mult), reads=[T("ones"), T("RR"), T("Rt%d" % q)],
                       writes=[T("Rt%d" % q)])

            def stage_a(i, blk, sq_i, q, t):
                gp = blk * 4 + q
                W, kw = WS[i % 2], T("ws%d" % (i % 2))
                COS, SIN, TM, TP = TAB[q]
                KT = [T("tab%d" % q)]
                cols = slice(sq_i * SEQ + t * 512, sq_i * SEQ + (t + 1) * 512)
                ku = T("uT%d_%d" % (blk, sq_i))
                for (bnk, li, dst) in ((0, 2, "bs"), (1, 0, "bre"), (2, 1, "bim")):
                    fw.pe(lambda e: e.matmul(env.bank(bnk), LB[li][:, gp, :], uT[:, blk, cols], start=True,
                                             stop=True), reads=[T("LB"), ku], writes=["ps%d" % bnk])
                    fw.act(lambda e: e.copy(W[dst][:], env.bank(bnk)), reads=["ps%d" % bnk], writes=[kw + dst])
                fw.dve(lambda e: e.tensor_tensor(out=W["bs"][:], in0=W["bs"][:], in1=COS[:], op=ALU.mult),
                       reads=[kw + "bs"] + KT, writes=[kw + "bs"])
                fw.dve(lambda e: e.tensor_tensor(out=W["bim"][:], in0=W["bim"][:], in1=TM[:], op=ALU.mult),
                       reads=[kw + "bim"] + KT, writes=[kw + "bim"])
                fw.dve(lambda e: e.tensor_tensor(out=W["bre"][:], in0=W["bre"][:], in1=TP[:], op=ALU.mult),
                       reads=[kw + "bre"] + KT, writes=[kw + "bre"])
                fw.dve(lambda e: e.tensor_tensor(out=W["bim"][:], in0=W["bs"][:], in1=W["bim"][:], op=ALU.subtract),
                       reads=[kw + "bs", kw + "bim"], writes=[kw + "bim"])
                fw.dve(lambda e: e.tensor_tensor(out=W["bre"][:], in0=W["bs"][:], in1=W["bre"][:], op=ALU.subtract),
                       reads=[kw + "bs", kw + "bre"], writes=[kw + "bre"])

            def stage_b(i, blk, sq_i, q, t):
                gp = blk * 4 + q
                W, kw = WS[i % 2], T("ws%d" % (i % 2))
                Wp, kwp = WS[(i + 1) % 2], T("ws%d" % ((i + 1) % 2))
                COS, SIN, TM, TP = TAB[q]
                KT = [T("tab%d" % q)]
                if t == 0:
                    ire, iim = 0.0, 0.0
                    kini = []
                else:
                    c9, s9, ns9 = CM[:, gp, 9:10], SM[:, gp, 9:10], NS9[:, gp:gp + 1]
                    lre, lim_ = Wp["w_re"][:, 511:512], Wp["w_im"][:, 511:512]
                    o = 4 * (i % 2)
                    kini = [T("ini%d" % (i % 2))]
                    fw.act(lambda e: e.activation(out=INI[:, o:o + 1], in_=lim_, func=AF.Identity, scale=ns9),
                           reads=[kwp + "w_im", T("ns9")] + kini, writes=kini)
                    fw.act(lambda e: e.activation(out=INI[:, o + 1:o + 2], in_=lre, func=AF.Identity, scale=c9,
                                                  bias=INI[:, o:o + 1]), reads=[kwp + "w_re", T("setup")] + kini,
                           writes=kini)
                    fw.act(lambda e: e.activation(out=INI[:, o + 2:o + 3], in_=lre, func=AF.Identity, scale=s9),
                           reads=[kwp + "w_re"] + kini, writes=kini)
                    fw.act(lambda e: e.activation(out=INI[:, o + 3:o + 4], in_=lim_, func=AF.Identity, scale=c9,
                                                  bias=INI[:, o + 2:o + 3]), reads=[kwp + "w_im"] + kini, writes=kini)
                    ire, iim = INI[:, o + 1:o + 2], INI[:, o + 3:o + 4]
                fw.dve(lambda e: e.tensor_tensor_scan(out=W["w_re"][:], data0=RT[q][:], data1=W["bim"][:],
                                                      initial=ire, op0=ALU.mult, op1=ALU.add),
                       reads=[kw + "bim", T("Rt%d" % q)] + kini, writes=[kw + "w_re"])
                fw.dve(lambda e: e.tensor_tensor_scan(out=W["w_im"][:], data0=RT[q][:], data1=W["bre"][:],
                                                      initial=iim, op0=ALU.mult, op1=ALU.add),
                       reads=[kw + "bre", T("Rt%d" % q)] + kini, writes=[kw + "w_im"])
                yb = 4 + t
                plan = (("p1", "w_re", COS, "dve", 0), ("p2", "w_im", SIN, "dve", 2), ("p3", "w_re", SIN, "dve", 1),
                        ("p4", "w_im", COS, "dve", 1))
                for n_, (o_, a_, tab, eng, lc) in enumerate(plan):
                    getattr(fw, eng)(lambda e: e.tensor_tensor(out=W[o_][:], in0=W[a_][:], in1=tab[:], op=ALU.mult),
                                     reads=[kw + a_] + KT, writes=[kw + o_])
                for n_, (o_, a_, tab, eng, lc) in enumerate(plan):
                    fw.pe(lambda e: e.matmul(env.bank(yb), LC[lc][:, gp, :], W[o_][:], start=(q == 0 and n_ == 0),
                                             stop=(q == 3 and n_ == 3)),
                          reads=[T("LC"), kw + o_], writes=["ps%d" % yb])

            def epilogue(blk, sq_i):
                ku = T("uT%d_%d" % (blk, sq_i))
                for t in range(4):
                    cols = slice(sq_i * SEQ + t * 512, sq_i * SEQ + (t + 1) * 512)
                    yb = 4 + t
                    KY = [T("yv")]
                    fw.dve(lambda e: e.scalar_tensor_tensor(out=YV[:], in0=uT[:, blk, cols],
                                                            scalar=dvec[:, blk:blk + 1], in1=env.bank(yb),
                                                            op0=ALU.mult, op1=ALU.add),
                           reads=[ku, T("dvec"), "ps%d" % yb] + KY, writes=KY)
                    fw.act(lambda e: e.activation(out=G1[:], in_=YV[:], func=AF.Square), reads=KY, writes=KY)
                    fw.act(lambda e: e.activation(out=G1[:], in_=G1[:], func=AF.Identity, scale=0.044715, bias=1.0),
                           reads=KY, writes=KY)
                    fw.dve(lambda e: e.tensor_tensor(out=G1[:], in0=G1[:], in1=YV[:], op=ALU.mult),
                           reads=KY, writes=KY)
                    fw.act(lambda e: e.activation(out=G2[:], in_=G1[:], func=AF.Tanh, scale=0.7978845608028654),
                           reads=KY, writes=KY)
                    fw.act(lambda e: e.activation(out=G2[:], in_=G2[:], func=AF.Identity, scale=0.5, bias=0.5),
                           reads=KY, writes=KY)
                    fw.dve(lambda e: e.tensor_tensor(out=uT[:, blk, cols], in0=G2[:], in1=YV[:], op=ALU.mult),
                           reads=KY + [ku], writes=KY + [ku])

            gi_ = 0
            for blk in range(8):
                for q in range(4):
                    build_table(q, blk * 4 + q)
                items = [(sq_i, q, t) for sq_i in range(nseq) for q in range(4) for t in range(4)]
                stage_a(gi_, blk, *items[0])
                for n, it_ in enumerate(items):
                    if n + 1 < len(items):
                        stage_a(gi_ + 1, blk, *items[n + 1])
                    stage_b(gi_, blk, *it_)
                    gi_ += 1
                    if it_[1] == 3 and it_[2] == 3:
                        epilogue(blk, it_[0])

        dbgdump(env, "yg0", uT[:, 0, 0:512], [T("uT0")])
        fw.barrier()
        with contextlib.ExitStack() as s3:
            Wg = env.sb(T("Wglu"), [128, 8, D], BF16, s3)
            gpost = env.sb(T("gpost"), [128, D], F32, s3)
            fw.dma("sync", gpost[:], post_g.partition_broadcast(128), writes=[T("gpost")])
            Wo = env.sb(T("Wo"), [128, 8, D], BF16, s3)
            ZT = [env.sb(T("zT%d" % i), [128, 8, 512], BF16, s3) for i in range(2)]
            SG = [env.sb(T("sg%d" % i), [128, 512], F32, s3) for i in range(4)]
            XR = [env.sb(T("xr%d" % i), [128, D], F32, s3) for i in range(2)]
            TMP = env.sb(T("tmp"), [128, D], F32, s3)
            kWglu = load_w_cast(fw, Wg, T("Wglu"), w_glu, 8)
            kWo5 = load_w_cast(fw, Wo, T("Wo"), w_out, 8)
            ukeys = [T("uT%d_%d" % (b, q_)) for b in range(8) for q_ in range(nseq)]
            for t in range(NTOK // 512):
                cols = slice(t * 512, (t + 1) * 512)
                zT, kzT = ZT[t % 2], T("zT%d" % (t % 2))
                for blk in range(8):
                    pb = (0, 1, 6, 7)[blk % 4]
                    fw.pe_group([lambda e, k=k: e.matmul(env.bank(pb), Wg[:, k, blk * 128:(blk + 1) * 128],
                                                         uT[:, k, cols], start=(k == 0), stop=(k == 7))
                                 for k in range(8)], reads=kWglu + ukeys, writes=["ps%d" % pb])
                    sg, ksg = SG[blk % 4], T("sg%d" % (blk % 4))
                    fw.act(lambda e: e.activation(out=sg[:], in_=env.bank(pb), func=AF.Sigmoid),
                           reads=["ps%d" % pb], writes=[ksg])
                    fw.dve(lambda e: e.tensor_tensor(out=zT[:, blk, :], in0=sg[:], in1=uT[:, blk, cols], op=ALU.mult),
                           reads=[ksg] + ukeys, writes=[kzT])
                for s in range(4):
                    r0 = t * 512 + s * 128
                    xr, kxr = XR[s % 2], T("xr%d" % (s % 2))
                    fw.dma("sync", xr[:], x_in[r0:r0 + 128, :], writes=[kxr])
                    pb0 = 2 + 2 * (s % 2)
                    pso = env.bank(pb0, 2)
                    pk = ["ps%d" % pb0, "ps%d" % (pb0 + 1)]
                    for hf in range(2):
                        fw.pe_group([lambda e, k=k: e.matmul(pso[:, hf, :], zT[:, k, s * 128:(s + 1) * 128],
                                                             Wo[:, k, hf * 512:(hf + 1) * 512], start=(k == 0),
                                                             stop=(k == 7)) for k in range(8)],
                                    reads=[kzT] + kWo5, writes=pk)
                    post_norm_residual(env, pso, pk, gpost[:], T("gpost"), xr[:], kxr, TMP[:], T("tmp"),
                                       xr[:], kxr)
                    fw.dma("sync", x_out[r0:r0 + 128, :], xr[:], reads=[kxr], writes=[])


NSEQ_CORE = 2
NCORES = 8
_CACHE = {}


def build_program():
    nc = bass.Bass("TRN2", target_bir_lowering=False)
    ntok = NSEQ_CORE * SEQ

    def din(name, shape):
        return nc.dram_tensor(name, shape, F32, kind="ExternalInput").ap()

    x = din("x", [ntok, D])
    fox_w_in = din("fox_w_in", [D, 4112])
    fox_b_f = din("fox_b_f", [NH])
    fox_q_gain = din("fox_q_gain", [HD])
    fox_k_gain = din("fox_k_gain", [HD])
    fox_w_out = din("fox_w_out", [DATT, D])
    s5_w_in = din("s5_w_in", [D, D])
    s5_log_dt = din("s5_log_dt", [NG])
    s5_lam_re = din("s5_lam_re", [NG, 64])
    s5_lam_im = din("s5_lam_im", [NG, 64])
    s5_b_re = din("s5_b_re", [NG, 64, 16])
    s5_b_im = din("s5_b_im", [NG, 64, 16])
    s5_c_re = din("s5_c_re", [NG, 16, 64])
    s5_c_im = din("s5_c_im", [NG, 16, 64])
    s5_d = din("s5_d", [D])
    s5_w_glu = din("s5_w_glu", [D, D])
    s5_w_out = din("s5_w_out", [D, D])
    mix_pre = din("mix_pre_gain", [2, D])
    mix_post = din("mix_post_gain", [2, D])
    ffn_pre = din("ffn_pre_gain", [2, D])
    ffn_post = din("ffn_post_gain", [2, D])
    ffn_wg = din("ffn_w_gate", [2, D, DFF])
    ffn_wu = din("ffn_w_up", [2, D, DFF])
    ffn_wd = din("ffn_w_down", [2, DFF, D])
    ident = din("c_ident", [128, 128])
    cmask = din("c_mask", [128, 128])
    out = nc.dram_tensor("out", [ntok, D], F32, kind="ExternalOutput").ap()
    xa = nc.dram_tensor("xa", [ntok, D], F32).ap()
    xb = nc.dram_tensor("xb", [ntok, D], F32).ap()
    xc = nc.dram_tensor("xc", [ntok, D], F32).ap()
    oT_d = nc.dram_tensor("oT_d", [NSEQ_CORE, NH, HD, SEQ], BF16).ap()

    fw = FW(nc)
    with contextlib.ExitStack() as st:
        env = Env(nc, fw, st)
        env.init_consts(ident)
        fox_phase(env, x, xa, fox_w_in, fox_b_f, fox_q_gain, fox_k_gain, fox_w_out, mix_pre[0, :], mix_post[0, :],
                  cmask, oT_d, NSEQ_CORE)
        ffn_phase(env, xa, xb, ffn_wg[0], ffn_wu[0], ffn_wd[0], ffn_pre[0, :], ffn_post[0, :], ntok, "f0")
        s5_phase(env, xb, xc, s5_w_in, s5_log_dt, s5_lam_re, s5_lam_im, s5_b_re, s5_b_im, s5_c_re, s5_c_im, s5_d,
                 s5_w_glu, s5_w_out, mix_pre[1, :], mix_post[1, :], NSEQ_CORE)
        ffn_phase(env, xc, out, ffn_wg[1], ffn_wu[1], ffn_wd[1], ffn_pre[1, :], ffn_post[1, :], ntok, "f1")
        fw.finish()
    return nc


def kernel(**inputs):
    f = np.float32
    x = np.ascontiguousarray(inputs["x"], dtype=f)
    B = x.shape[0]
    shared = {}
    for k in ("fox_w_in", "fox_b_f", "fox_q_gain", "fox_k_gain", "fox_w_out", "s5_w_in", "s5_log_dt", "s5_lam_re",
              "s5_lam_im", "s5_b_re", "s5_b_im", "s5_c_re", "s5_c_im", "s5_d", "s5_w_glu", "s5_w_out"):
        shared[k] = np.ascontiguousarray(np.asarray(inputs[k], dtype=f)[0])
    for k in ("mix_pre_gain", "mix_post_gain", "ffn_pre_gain", "ffn_post_gain", "ffn_w_gate", "ffn_w_up",
              "ffn_w_down"):
        shared[k] = np.ascontiguousarray(np.asarray(inputs[k], dtype=f))
    shared["c_ident"] = np.eye(128, dtype=f)
    shared["c_mask"] = np.where(np.arange(128)[None, :] < np.arange(128)[:, None], MASKNEG, 0.0).astype(f)
    if "nc" not in _CACHE:
        _CACHE["nc"] = build_program()
    nc = _CACHE["nc"]
    in_maps = []
    for c in range(NCORES):
        m = dict(shared)
        m["x"] = np.ascontiguousarray(x[c * NSEQ_CORE:(c + 1) * NSEQ_CORE].reshape(NSEQ_CORE * SEQ, D))
        in_maps.append(m)
    res = run_bass_kernel_spmd(nc, in_maps, core_ids=list(range(NCORES)))
    outs = [np.asarray(r["out"]).reshape(NSEQ_CORE, SEQ, D) for r in res.results]
    return np.concatenate(outs, axis=0).astype(f)
```

```python
import contextlib
import numpy as np
import concourse.bass as bass
import concourse.mybir as mybir
from concourse.bass_utils import run_bass_kernel_spmd

F32 = mybir.dt.float32
BF16 = mybir.dt.bfloat16
I32 = mybir.dt.int32
AF = mybir.ActivationFunctionType
ALU = mybir.AluOpType
AX = mybir.AxisListType


STRICT_SAME_ENGINE = True


class FW:
    def __init__(self, nc, n_dma_sems=6):
        self.nc = nc
        self.stack = contextlib.ExitStack()
        self.E = {}
        self.sems = []
        for name, h in (("pe", nc.tensor), ("act", nc.scalar), ("dve", nc.vector),
                        ("pool", nc.gpsimd), ("sync", nc.sync)):
            e = {"name": name, "h": h, "count": 0, "waited": {}, "dma": [], "dma_i": 0}
            if name != "sync":
                e["sem"] = self._newsem("c_" + name)
            for i in range(n_dma_sems):
                e["dma"].append([self._newsem("d_%s%d" % (name, i)), 0])
            self.E[name] = e
        self.lastw = {}
        self.readers = {}
        self.nwaits = 0

    def _newsem(self, name):
        s = self.stack.enter_context(self.nc.semaphore(name))
        self.sems.append(s)
        return len(self.sems) - 1

    def _wait(self, E, sid, val):
        if E["waited"].get(sid, 0) >= val:
            return
        E["h"].wait_ge(self.sems[sid], val)
        E["waited"][sid] = val
        self.nwaits += 1

    def _sync(self, E, reads, writes, attach=False):
        need = {}

        def add(ev):
            if ev is None:
                return
            sid, val = ev
            if need.get(sid, 0) < val:
                need[sid] = val

        for r in reads:
            add(self.lastw.get(r))
        for w in writes:
            add(self.lastw.get(w))
            for sid, val in self.readers.get(w, {}).items():
                add((sid, val))
        own = E.get("sem")
        todo = []
        for sid, val in need.items():
            if sid == own:
                if E["name"] == "pe":
                    continue
                if STRICT_SAME_ENGINE is False and E["name"] in ("dve", "act") and val < E["count"]:
                    continue
            if E["waited"].get(sid, 0) >= val:
                continue
            todo.append((sid, val))
        held = None
        if attach and todo:
            held = todo.pop()
        for sid, val in todo:
            self._wait(E, sid, val)
        return held

    def _attach(self, E, ins, held):
        if held is not None:
            sid, val = held
            ins._wait_ge(self.sems[sid], val)
            E["waited"][sid] = val
            self.nwaits += 1

    def _record(self, ev, reads, writes):
        sid, val = ev
        for r in reads:
            d = self.readers.setdefault(r, {})
            if d.get(sid, 0) < val:
                d[sid] = val
        for w in writes:
            self.lastw[w] = ev
            self.readers[w] = {}

    def op(self, en, fn, reads=(), writes=()):
        E = self.E[en]
        held = self._sync(E, reads, writes, attach=(en != "pe"))
        ins = fn(E["h"])
        self._attach(E, ins, held)
        E["count"] += 1
        ins.then_inc(self.sems[E["sem"]], 1)
        self._record((E["sem"], E["count"]), reads, writes)
        return ins

    def pe(self, fn, reads=(), writes=()):
        return self.op("pe", fn, reads, writes)

    def act(self, fn, reads=(), writes=()):
        return self.op("act", fn, reads, writes)

    def dve(self, fn, reads=(), writes=()):
        return self.op("dve", fn, reads, writes)

    def pool(self, fn, reads=(), writes=()):
        return self.op("pool", fn, reads, writes)

    def pe_group(self, fns, reads=(), writes=()):
        E = self.E["pe"]
        held = self._sync(E, reads, writes, attach=False)
        ins = None
        for n, fn in enumerate(fns):
            ins = fn(E["h"])
            if n == 0:
                self._attach(E, ins, held)
        E["count"] += 1
        ins.then_inc(self.sems[E["sem"]], 1)
        self._record((E["sem"], E["count"]), reads, writes)

    def dma(self, q, out, in_, reads=(), writes=(), **kw):
        E = self.E[q]
        self._sync(E, reads, writes)
        slot = E["dma"][E["dma_i"] % len(E["dma"])]
        E["dma_i"] += 1
        sid, target = slot
        if target > 0:
            self._wait(E, sid, target)
        E["h"].dma_start(out=out, in_=in_, **kw).then_inc(self.sems[sid], 16)
        slot[1] = target + 16
        self._record((sid, slot[1]), reads, writes)

    def barrier(self):
        evs = []
        for e in self.E.values():
            for sid, target in e["dma"]:
                if target > 0:
                    evs.append((sid, target))
            if "sem" in e and e["count"] > 0:
                evs.append((e["sem"], e["count"]))
        for E in self.E.values():
            for sid, val in evs:
                self._wait(E, sid, val)

    def finish(self):
        S = self.E["sync"]
        for e in self.E.values():
            for sid, target in e["dma"]:
                if target > 0:
                    self._wait(S, sid, target)
            if "sem" in e and e["count"] > 0:
                self._wait(S, e["sem"], e["count"])
        self.stack.close()


EPS = 1e-6
D = 1024
DFF = 2816
NFC = DFF // 128


class Env:
    def __init__(self, nc, fw, st):
        self.nc, self.fw, self.st = nc, fw, st
        self.ps = st.enter_context(nc.psum_tensor("psall", [128, 8, 512], F32))
        self.identb = self.sb("identb", [128, 128], BF16)
        self.identf = self.sb("identf", [128, 128], F32)
        self.mhalf = self.sb("mhalf", [128, 1], F32)
        self.stats = self.sb("stats", [128, 96], F32)
        self.junk = self.sb("junk", [128, 1024], BF16)
        self.si = 0

    def sb(self, name, shape, dt, st=None):
        return (st or self.st).enter_context(self.nc.sbuf_tensor(name, shape, dt))

    def bank(self, i, n=1):
        if n == 1:
            return self.ps[:, i, :]
        return self.ps[:, i:i + n, :]

    def stat(self):
        i = self.si % 96
        self.si += 1
        return self.stats[:, i:i + 1], "st%d" % i

    def init_consts(self, ident_dram):
        fw = self.fw
        fw.dma("sync", self.identf[:], ident_dram, writes=["identf"])
        fw.dve(lambda e: e.tensor_copy(self.identb[:], self.identf[:]), reads=["identf"], writes=["identb"])
        fw.dve(lambda e: e.memset(self.mhalf[:], -0.5), writes=["mhalf"])

    def rstd(self, src_ap, src_key, n):
        fw = self.fw
        P = src_ap.shape[0]
        ss, kss = self.stat()
        var, kvar = self.stat()
        rs, krs = self.stat()
        junk = self.junk[0:P, 0:n]
        sk = list(src_key) if isinstance(src_key, (list, tuple)) else [src_key]
        fw.act(lambda e: e.activation(out=junk, in_=src_ap, func=AF.Square, accum_out=ss[0:P, :]),
               reads=sk, writes=[kss, "junk"])
        fw.dve(lambda e: e.tensor_scalar(out=var[0:P, :], in0=ss[0:P, :], scalar1=1.0 / n, scalar2=EPS,
                                         op0=ALU.mult, op1=ALU.add), reads=[kss], writes=[kvar])
        fw.pool(lambda e: e.tensor_tensor(out=rs[0:P, :], in0=var[0:P, :], in1=self.mhalf[0:P, :], op=ALU.pow),
                reads=[kvar, "mhalf"], writes=[krs])
        return rs, krs


def load_w_cast(fw, dst_tile, dst_key, w_dram, kchunks, first=False):
    keys = []
    for k in range(kchunks):
        kk = "%s_k%d" % (dst_key, k)
        fw.dma("pool", dst_tile[:, k, :], w_dram[k * 128:(k + 1) * 128, :], writes=[kk])
        keys.append(kk)
    return keys


def load_w_cast_cols(fw, dst_tile, dst_key, w_dram, kchunks, col_groups):
    src = w_dram.rearrange("(k p) c -> p k c", p=128)
    keys = {}
    for gi, (c0, c1) in enumerate(col_groups):
        kk = "%s_c%d" % (dst_key, gi)
        fw.dma("pool", dst_tile[:, 0:kchunks, c0:c1], src[:, :, c0:c1], writes=[kk])
        keys[gi] = kk
    return keys


def norm_transpose(env, x_ap, x_key, g_ap, g_key, hb, hb_key, hT, hT_key, col0, psT_bank):
    fw = env.fw
    rs, krs = env.rstd(x_ap, x_key, D)
    fw.dve(lambda e: e.scalar_tensor_tensor(out=hb, in0=x_ap, scalar=rs, in1=g_ap, op0=ALU.mult, op1=ALU.mult),
           reads=[x_key, krs, g_key], writes=[hb_key])
    transpose_in(env, hb, hb_key, hT, hT_key, col0, psT_bank)


def transpose_in(env, hb, hb_key, hT, hT_key, col0, psT_bank, on="act"):
    fw = env.fw
    pkey = "ps%d" % psT_bank
    psT = env.bank(psT_bank).bitcast(BF16)
    fns = []
    for j in range(8):
        fns.append(lambda e, j=j: e.transpose(psT[:, j * 128:(j + 1) * 128], hb[:, j * 128:(j + 1) * 128],
                                               env.identb[:]))
    fw.pe_group(fns, reads=[hb_key, "identb"], writes=[pkey])
    src = psT.rearrange("p (j c) -> p j c", j=8)
    dst = hT[:, 0:8, col0:col0 + 128]
    if on == "act":
        fw.act(lambda e: e.copy(dst, src), reads=[pkey], writes=[hT_key])
    else:
        fw.dve(lambda e: e.tensor_copy(dst, src), reads=[pkey], writes=[hT_key])


def post_norm_residual(env, ps_ap, ps_key, g_ap, g_key, xr, xr_key, tmp, tmp_key, xo, xo_key):
    fw = env.fw
    pk = list(ps_key) if isinstance(ps_key, (list, tuple)) else [ps_key]
    rs, krs = env.rstd(ps_ap, pk, D)
    fw.dve(lambda e: e.scalar_tensor_tensor(out=tmp, in0=ps_ap, scalar=rs, in1=g_ap, op0=ALU.mult, op1=ALU.mult),
           reads=pk + [krs, g_key], writes=[tmp_key])
    fw.dve(lambda e: e.tensor_tensor(out=xo, in0=tmp, in1=xr, op=ALU.add),
           reads=[tmp_key, xr_key], writes=[xo_key])


def ffn_phase(env, x_in, x_out, wg, wu, wd, pre_g, post_g, ntok, tag):
    nc, fw = env.nc, env.fw
    fw.barrier()
    with contextlib.ExitStack() as st:
        Wg = env.sb(tag + "Wg", [128, 8, DFF], BF16, st)
        Wu = env.sb(tag + "Wu", [128, 8, DFF], BF16, st)
        Wd = env.sb(tag + "Wd", [128, NFC, D], BF16, st)
        gpre = env.sb(tag + "gpre", [128, D], F32, st)
        gpost = env.sb(tag + "gpost", [128, D], F32, st)
        XT = [env.sb(tag + "xt%d" % i, [128, D], F32, st) for i in range(2)]
        XR = [env.sb(tag + "xr%d" % i, [128, D], F32, st) for i in range(2)]
        HB = [env.sb(tag + "hb%d" % i, [128, D], BF16, st) for i in range(2)]
        hT = env.sb(tag + "hT", [128, 8, 512], BF16, st)
        aT = env.sb(tag + "aT", [128, NFC, 512], BF16, st)
        SG = [env.sb(tag + "sg%d" % i, [128, 512], F32, st) for i in range(2)]
        TMP = env.sb(tag + "tmp", [128, D], F32, st)
        fw.dma("sync", gpre[:], pre_g.partition_broadcast(128), writes=[tag + "gpre"])
        fw.dma("sync", gpost[:], post_g.partition_broadcast(128), writes=[tag + "gpost"])
        grp = [(c, min(c + 256, DFF)) for c in range(0, DFF, 256)]
        kWg, kWu = {}, {}
        srcg = wg.rearrange("(k p) c -> p k c", p=128)
        srcu = wu.rearrange("(k p) c -> p k c", p=128)
        for gi, (c0, c1) in enumerate(grp):
            kWg[gi] = "%sWg_c%d" % (tag, gi)
            kWu[gi] = "%sWu_c%d" % (tag, gi)
            fw.dma("pool", Wg[:, :, c0:c1], srcg[:, :, c0:c1], writes=[kWg[gi]])
            fw.dma("pool", Wu[:, :, c0:c1], srcu[:, :, c0:c1], writes=[kWu[gi]])
        kWd = load_w_cast(fw, Wd, tag + "Wd", wd, NFC)
        ntiles = ntok // 512
        it = [0]

        def build_hT(t):
            for s in range(4):
                r0 = t * 512 + s * 128
                i = it[0] % 2
                it[0] += 1
                xt, kx = XT[i], tag + "xt%d" % i
                hb, khb = HB[i], tag + "hb%d" % i
                fw.dma("sync", xt[:], x_in[r0:r0 + 128, :], writes=[kx])
                norm_transpose(env, xt[:], kx, gpre[:], tag + "gpre", hb[:], khb, hT, tag + "hT", s * 128, 0)

        build_hT(0)
        for t in range(ntiles):
            akeys = []
            for fc in range(NFC):
                bg, bu = fc % 2, 2 + fc % 2
                fs = slice(fc * 128, (fc + 1) * 128)
                fw.pe_group([lambda e, k=k: e.matmul(env.bank(bg), Wg[:, k, fs], hT[:, k, :], start=(k == 0),
                                                     stop=(k == 7)) for k in range(8)],
                            reads=[kWg[fc // 2], tag + "hT"], writes=["ps%d" % bg])
                fw.pe_group([lambda e, k=k: e.matmul(env.bank(bu), Wu[:, k, fs], hT[:, k, :], start=(k == 0),
                                                     stop=(k == 7)) for k in range(8)],
                            reads=[kWu[fc // 2], tag + "hT"], writes=["ps%d" % bu])
                sg = SG[fc % 2]
                ksg = tag + "sg%d" % (fc % 2)
                fw.act(lambda e: e.activation(out=sg[:], in_=env.bank(bg), func=AF.Silu),
                       reads=["ps%d" % bg], writes=[ksg])
                ka = tag + "aT%d" % fc
                fw.dve(lambda e: e.tensor_tensor(out=aT[:, fc, :], in0=sg[:], in1=env.bank(bu), op=ALU.mult),
                       reads=[ksg, "ps%d" % bu], writes=[ka])
                akeys.append(ka)
            if t + 1 < ntiles:
                build_hT(t + 1)
            for s in range(4):
                r0 = t * 512 + s * 128
                xr = XR[s % 2]
                kxr = tag + "xr%d" % (s % 2)
                fw.dma("sync", xr[:], x_in[r0:r0 + 128, :], writes=[kxr])
                pb0 = 4 + 2 * (s % 2)
                pso = env.bank(pb0, 2)
                pk = ["ps%d" % pb0, "ps%d" % (pb0 + 1)]
                for hf in range(2):
                    fw.pe_group([lambda e, fc=fc: e.matmul(pso[:, hf, :], aT[:, fc, s * 128:(s + 1) * 128],
                                                           Wd[:, fc, hf * 512:(hf + 1) * 512], start=(fc == 0),
                                                           stop=(fc == NFC - 1)) for fc in range(NFC)],
                                reads=akeys + kWd, writes=pk)
                post_norm_residual(env, pso, pk, gpost[:], tag + "gpost", xr[:], kxr, TMP[:], tag + "tmp",
                                   xr[:], kxr)
                fw.dma("sync", x_out[r0:r0 + 128, :], xr[:], reads=[kxr], writes=[])


NH = 16
HD = 64
WARM = 0
DATT = 1024
SEQ = 2048
MASKNEG = -240000.0


def fox_phase(env, x_in, x_out, w_in, b_f, q_gain, k_gain, w_out, pre_g, post_g, consts, oT_d, nseq, tag="fx"):
    nc, fw = env.nc, env.fw
    fw.barrier()
    with contextlib.ExitStack() as st:
        T = lambda s: tag + s
        Win = env.sb(tag + "Win", [128, 8, 4112], BF16, st)
        Wo = env.sb(tag + "Wo", [128, 8, D], BF16, st)
        gpre = env.sb(tag + "gpre", [128, D], F32, st)
        gpost = env.sb(tag + "gpost", [128, D], F32, st)
        hT = env.sb(tag + "hT", [128, 8, SEQ], BF16, st)
        XT = [env.sb(tag + "xt%d" % i, [128, D], F32, st) for i in range(2)]
        HB = [env.sb(tag + "hb%d" % i, [128, D], BF16, st) for i in range(2)]
        TMP = env.sb(tag + "tmp", [128, D], F32, st)
        scr = env.sb(tag + "scr", [128, 4096], F32, st)
        lf = scr[0:16, 0:SEQ]
        cT = scr[0:16, SEQ:2 * SEQ]
        QA = [env.sb(tag + "qa%d" % i, [128, SEQ], BF16, st) for i in range(2)]
        KA = [env.sb(tag + "ka%d" % i, [128, SEQ], BF16, st) for i in range(2)]
        QA += [scr[:, 0:1024].bitcast(BF16), scr[:, 1024:2048].bitcast(BF16)]
        KA += [scr[:, 2048:3072].bitcast(BF16), scr[:, 3072:4096].bitcast(BF16)]
        KQ = [[T("qa0")], [T("qa1")], [T("qa2"), T("lf")], [T("qa3"), T("lf")]]
        KK = [[T("ka0")], [T("ka1")], [T("ka2"), T("cT")], [T("ka3"), T("cT")]]
        gT = env.sb(tag + "gT", [128, SEQ], BF16, st)
        V2e = env.sb(tag + "V2e", [128, 16, 128], BF16, st)
        V2o = TMP[:].bitcast(BF16).rearrange("p (a b) -> p a b", a=16)
        PT = [env.sb(tag + "pT%d" % i, [128, 512], BF16, st) for i in range(2)]
        SQ = [env.sb(tag + "sq%d" % i, [128, 512], BF16, st) for i in range(2)]
        RST = [env.sb(tag + "rst%d" % i, [128, 512], F32, st) for i in range(2)]
        bones = env.sb(tag + "bones", [128, 128], BF16, st)
        maskf = env.sb(tag + "maskf", [128, 128], F32, st)
        maskb = env.sb(tag + "maskb", [128, 128], BF16, st)
        qg = env.sb(tag + "qg", [128, 1], F32, st)
        kg = env.sb(tag + "kg", [128, 1], F32, st)
        nbf = env.sb(tag + "nbf", [16, 1], F32, st)
        epsc = env.sb(tag + "epsc", [128, 1], F32, st)
        ones16 = env.sb(tag + "ones16", [16, 512], F32, st)
        csp = env.sb(tag + "csp", [96, SEQ], BF16, st)
        cspt128 = env.sb(tag + "cspt", [128, SEQ], BF16, st)
        cspt = cspt128[0:16, :]
        negc = env.sb(tag + "negc", [128, 16, 16], F32, st)
        rl = env.sb(tag + "rl", [128, 512], F32, st)
        og = env.sb(tag + "og", [128, 512], F32, st)
        OTS = [env.sb(tag + "ots%d" % i, [128, 512], BF16, st) for i in range(2)]
        OTL = [env.sb(tag + "oTl0", [128, 8, 128], BF16, st),
               rl[:].bitcast(BF16).rearrange("p (h t) -> p h t", h=8)]

        SER = T("ser")
        fw.dma("sync", gpre[:], pre_g.partition_broadcast(128), writes=[T("gpre"), SER])
        fw.dma("sync", gpost[:], post_g.partition_broadcast(128), writes=[T("gpost"), SER])
        fw.dma("sync", maskf[:], consts, writes=[T("maskf"), SER])
        for half in range(2):
            fw.dma("sync", qg[half * 64:(half + 1) * 64, :], q_gain.rearrange("(p o) -> p o", o=1),
                   writes=[T("qg"), SER])
            fw.dma("sync", kg[half * 64:(half + 1) * 64, :], k_gain.rearrange("(p o) -> p o", o=1),
                   writes=[T("kg"), SER])
        fw.dma("sync", nbf[:], b_f.rearrange("(p o) -> p o", o=1), writes=[T("nbf"), SER])
        fw.dve(lambda e: e.tensor_copy(maskb[:], maskf[:]), reads=[T("maskf"), SER], writes=[T("maskb")])
        fw.dve(lambda e: e.tensor_scalar(out=nbf[:], in0=nbf[:], scalar1=-1.0, scalar2=None, op0=ALU.mult),
               reads=[T("nbf")], writes=[T("nbf")])
        fw.dve(lambda e: e.memset(epsc[:], EPS), writes=[T("epsc")])
        fw.dve(lambda e: e.memset(bones[:], 0.0), writes=[T("bones")])
        fw.dve(lambda e: e.memset(bones[0:64, 0:64], 1.0), writes=[T("bones")])
        fw.dve(lambda e: e.memset(bones[64:128, 64:128], 1.0), writes=[T("bones")])
        fw.dve(lambda e: e.memset(ones16[:], 1.0), writes=[T("ones16")])
        for i in range(4):
            fw.dve(lambda e: e.memset(KA[i][64:128, :], 1.0), writes=KK[i])
            fw.dve(lambda e: e.memset(QA[i][64:128, :], 0.0), writes=KQ[i])
        fw.dve(lambda e: e.memset(V2e[:, :, 64:128], 1.0), writes=[T("V2e")])
        kWin = load_w_cast(fw, Win, T("Win"), w_in, 8)
        kWo = load_w_cast(fw, Wo, T("Wo"), w_out, 8)

        cnt = [0]

        def proj_gen(j):
            sl = [2 * (j % 2), 2 * (j % 2) + 1]
            for (DST, KD, off, gain, kgain) in ((QA, KQ, 0, qg, T("qg")), (KA, KK, 1024, kg, T("kg"))):
                for t in range(4):
                    ts = slice(t * 512, (t + 1) * 512)
                    c = cnt[0]
                    cnt[0] += 1
                    pb = 2 + c % 2
                    sq, ksq = SQ[c % 2], T("sq%d" % (c % 2))
                    rst, krst = RST[c % 2], T("rst%d" % (c % 2))
                    fw.pe_group([lambda e, k=k: e.matmul(env.bank(pb), Win[:, k, off + j * 128:off + (j + 1) * 128],
                                                         hT[:, k, ts], start=(k == 0), stop=(k == 7))
                                 for k in range(8)], reads=kWin + [T("hT")], writes=["ps%d" % pb])
                    fw.act(lambda e: e.activation(out=sq[:], in_=env.bank(pb), func=AF.Square),
                           reads=["ps%d" % pb], writes=[ksq])
                    yield
                    fw.pe(lambda e: e.matmul(env.bank(6), bones[:], sq[:], start=True, stop=True),
                          reads=[T("bones"), ksq], writes=["ps6"])
                    fw.act(lambda e: e.activation(out=rst[:], in_=env.bank(6), func=AF.Ln, bias=epsc[:],
                                                  scale=1.0 / HD), reads=["ps6", T("epsc")], writes=[krst])
                    fw.act(lambda e: e.activation(out=rst[:], in_=rst[:], func=AF.Exp, scale=-0.5),
                           reads=[krst], writes=[krst])
                    for par in range(2):
                        rows = slice(par * 64, (par + 1) * 64)
                        dst = DST[sl[par]]
                        fw.dve(lambda e: e.scalar_tensor_tensor(out=dst[0:64, ts], in0=env.bank(pb)[rows, :],
                                                                scalar=gain[rows, :], in1=rst[rows, :], op0=ALU.mult,
                                                                op1=ALU.mult),
                               reads=["ps%d" % pb, kgain, krst], writes=KD[sl[par]])
                    yield
            for par in range(2):
                h = 2 * j + par
                for l_ in range(3):
                    fw.dma("sync", QA[sl[par]][64 + l_:65 + l_, :], csp[32 * l_ + h:32 * l_ + h + 1, :],
                           reads=[T("csp")], writes=KQ[sl[par]])
            yield

        def gv_emit(j):
            for t in range(4):
                ts = slice(t * 512, (t + 1) * 512)
                pb = 2 + t % 2
                fw.pe_group([lambda e, k=k: e.matmul(env.bank(pb), Win[:, k, 3072 + j * 128:3072 + (j + 1) * 128],
                                                     hT[:, k, ts], start=(k == 0), stop=(k == 7)) for k in range(8)],
                            reads=kWin + [T("hT")], writes=["ps%d" % pb])
                fw.act(lambda e: e.activation(out=gT[:, ts], in_=env.bank(pb), func=AF.Sigmoid),
                       reads=["ps%d" % pb], writes=[T("gT")])
            for g4 in range(4):
                fns = []
                for kk in range(4):
                    kt = g4 * 4 + kk
                    for k in range(8):
                        fns.append(lambda e, k=k, kk=kk, kt=kt: e.matmul(
                            env.bank(7)[:, kk * 128:(kk + 1) * 128], hT[:, k, kt * 128:(kt + 1) * 128],
                            Win[:, k, 2048 + j * 128:2048 + (j + 1) * 128], start=(k == 0), stop=(k == 7)))
                fw.pe_group(fns, reads=kWin + [T("hT")], writes=["ps7"])
                src = env.bank(7).rearrange("p (a b) -> p a b", a=4)
                fw.act(lambda e: e.copy(V2e[:, g4 * 4:(g4 + 1) * 4, 0:64], src[:, :, 0:64]),
                       reads=["ps7"], writes=[T("V2e")])
                fw.act(lambda e: e.copy(V2o[:, g4 * 4:(g4 + 1) * 4, 64:128], src[:, :, 64:128]),
                       reads=["ps7"], writes=[T("tmp")])

        def attention(sq_i, h, nxt, every, exhaust):
            par = h % 2
            slot = 2 * ((h // 2) % 2) + par
            qa, ka = QA[slot], KA[slot]
            kqa, kka = KQ[slot], KK[slot]
            V2, kV2 = (V2e, T("V2e")) if par == 0 else (V2o, T("tmp"))
            orow = slice(par * 64, (par + 1) * 64)
            lrow = slice((1 - par) * 64, (2 - par) * 64)
            items = [(qt, kt) for qt in range(4) for kt in range(4 * qt + 4)]

            def geom(i):
                qt, kt = items[i]
                j = kt - 4 * qt
                return qt, kt, j, max(0, j) * 128

            SB = (0, 1, 7)
            PT3 = [PT[0], PT[1], cspt128[:, 0:512]]
            KPT3 = [T("pT0"), T("pT1"), T("cspt")]

            def S(i):
                qt, kt, j, col0 = geom(i)
                sb_ = SB[i % 3]
                q0 = qt * 512
                fns = [lambda e: e.matmul(env.bank(sb_)[:, col0:512], ka[0:67, kt * 128:(kt + 1) * 128],
                                          qa[0:67, q0 + col0:q0 + 512], start=True, stop=(j < 0))]
                if j >= 0:
                    fns.append(lambda e: e.matmul(env.bank(sb_)[:, col0:col0 + 128], env.identb[:], maskb[:],
                                                  start=False, stop=True))
                fw.pe_group(fns, reads=kka + kqa + [T("maskb"), "identb"], writes=["ps%d" % sb_])

            S(0)
            S(1)
            for i in range(len(items)):
                qt, kt, j, col0 = geom(i)
                nkt = 4 * qt + 4
                accb = 4 + (qt % 2)
                sb_ = SB[i % 3]
                q0 = qt * 512
                if i + 2 < len(items):
                    S(i + 2)
                pT, kpT = PT3[i % 3], KPT3[i % 3]
                fw.act(lambda e: e.activation(out=pT[:, col0:512], in_=env.bank(sb_)[:, col0:512], func=AF.Exp,
                                              bias=negc[:, kt, h:h + 1], scale=0.125),
                       reads=["ps%d" % sb_, T("negc")], writes=[kpT])
                fw.pe(lambda e: e.matmul(env.bank(accb)[:, col0:512], V2[:, kt, :], pT[:, col0:512],
                                         start=(kt == 0), stop=(kt == nkt - 1)),
                      reads=[kV2, kpT], writes=["ps%d" % accb])
                if kt == nkt - 1:
                    fw.dve(lambda e: e.reciprocal(rl[lrow, :], env.bank(accb)[lrow, :]),
                           reads=["ps%d" % accb], writes=[T("rl")])
                    fw.dve(lambda e: e.tensor_tensor(out=og[orow, :], in0=env.bank(accb)[orow, :], in1=rl[lrow, :],
                                                     op=ALU.mult), reads=["ps%d" % accb, T("rl")], writes=[T("og")])
                    ots, kots = OTS[qt % 2], T("ots%d" % (qt % 2))
                    fw.dve(lambda e: e.tensor_tensor(out=ots[orow, :], in0=og[orow, :], in1=gT[orow, q0:q0 + 512],
                                                     op=ALU.mult), reads=[T("og"), T("gT")], writes=[kots])
                    fw.dma("sync", oT_d[sq_i, h, :, q0:q0 + 512], ots[orow, :], reads=[kots], writes=[T("oTd")])
                if nxt is not None and i % every == every - 1:
                    next(nxt, None)
            if nxt is not None and exhaust:
                for _ in nxt:
                    pass

        def pre_tile(sq, s, bi):
            r0 = sq * SEQ + s * 128
            xt, kx = XT[bi], T("xt%d" % bi)
            hb, khb = HB[bi], T("hb%d" % bi)
            fw.dma("sync", xt[:], x_in[r0:r0 + 128, :], writes=[kx])
            norm_transpose(env, xt[:], kx, gpre[:], T("gpre"), hb[:], khb, hT, T("hT"), s * 128, 6)

        def out_tile(sq, s, xi):
            r0 = sq * SEQ + s * 128
            ol, kol = OTL[s % 2], (T("oTl0") if s % 2 == 0 else T("rl"))
            for two in range(2):
                fw.dma("sync", ol[two * 64:(two + 1) * 64, :, :],
                       oT_d[sq, :, :, s * 128:(s + 1) * 128].rearrange("(hp two) d t -> two d hp t", two=2)[two],
                       reads=[T("oTd")], writes=[kol])
            xr, kxr = XT[xi], T("xt%d" % xi)
            fw.dma("sync", xr[:], x_in[r0:r0 + 128, :], writes=[kxr])
            pb0 = 2 if s % 2 == 0 else 4
            pso = env.bank(pb0, 2)
            pk = ["ps%d" % pb0, "ps%d" % (pb0 + 1)]
            for hf in range(2):
                fw.pe_group([lambda e, h=h: e.matmul(pso[:, hf, :], ol[:, h, :], Wo[:, h, hf * 512:(hf + 1) * 512],
                                                     start=(h == 0), stop=(h == 7)) for h in range(8)],
                            reads=[kol] + kWo, writes=pk)
            post_norm_residual(env, pso, pk, gpost[:], T("gpost"), xr[:], kxr, TMP[:], T("tmp"), xr[:], kxr)
            fw.dma("sync", x_out[r0:r0 + 128, :], xr[:], reads=[kxr], writes=[])

        for sq_i in range(nseq):
            base = sq_i * SEQ
            if sq_i == 0:
                for s in range(SEQ // 128):
                    pre_tile(0, s, s % 2)
            for t in range(4):
                ts = slice(t * 512, (t + 1) * 512)
                fw.pe_group([lambda e, k=k: e.matmul(env.bank(7)[0:16, :], Win[:, k, 4096:4112], hT[:, k, ts],
                                                     start=(k == 0), stop=(k == 7)) for k in range(8)],
                            reads=kWin + [T("hT")], writes=["ps7"])
                fw.act(lambda e: e.activation(out=lf[:, ts], in_=env.bank(7)[0:16, :], func=AF.Exp, bias=nbf[:],
                                              scale=-1.0), reads=["ps7", T("nbf")], writes=[T("lf")])
            fw.act(lambda e: e.activation(out=lf, in_=lf, func=AF.Ln, bias=1.0, scale=1.0),
                   reads=[T("lf")], writes=[T("lf")])
            for t in range(4):
                ts = slice(t * 512, (t + 1) * 512)
                init = 0.0 if t == 0 else cT[:, t * 512 - 1:t * 512]
                fw.dve(lambda e: e.tensor_tensor_scan(out=cT[:, ts], data0=ones16[:], data1=lf[:, ts], initial=init,
                                                      op0=ALU.mult, op1=ALU.add),
                       reads=[T("lf"), T("ones16"), T("cT")], writes=[T("cT")])
            for kt in range(16):
                fw.pe(lambda e: e.transpose(env.bank(7)[:, kt * 16:(kt + 1) * 16], cT[:, kt * 128:(kt + 1) * 128],
                                            env.identf[0:16, 0:16]),
                      reads=[T("cT"), "identf"], writes=["ps7"])
            fw.dve(lambda e: e.tensor_copy(negc[:].rearrange("p a b -> p (a b)"), env.bank(7)[:, 0:256]),
                   reads=["ps7"], writes=[T("negc")])
            fw.dve(lambda e: e.tensor_scalar(out=lf, in0=cT, scalar1=-8.0, scalar2=None, op0=ALU.mult),
                   reads=[T("cT"), T("lf")], writes=[T("lf")])
            for lvl in range(3):
                cl = csp[32 * lvl:32 * lvl + 16, :]
                fw.dve(lambda e: e.tensor_copy(cspt, lf), reads=[T("lf"), T("cspt")], writes=[T("cspt")])
                fw.act(lambda e: e.copy(cl, cspt), reads=[T("cspt")], writes=[T("csp")])
                if lvl < 2:
                    fw.dve(lambda e: e.tensor_tensor(out=lf, in0=lf, in1=cspt, op=ALU.subtract),
                           reads=[T("lf"), T("cspt")], writes=[T("lf")])
            fw.dve(lambda e: e.memset(V2o[:, :, 0:64], 1.0), reads=[T("tmp")], writes=[T("tmp")])

            for _ in proj_gen(0):
                pass
            for j in range(NH // 2):
                gv_emit(j)
                nxt = proj_gen(j + 1) if j + 1 < NH // 2 else None
                attention(sq_i, 2 * j, nxt, 4, False)
                attention(sq_i, 2 * j + 1, nxt, 4, True)
            for s in range(SEQ // 128):
                if sq_i + 1 < nseq:
                    out_tile(sq_i, s, 1)
                    pre_tile(sq_i + 1, s, 0)
                else:
                    out_tile(sq_i, s, s % 2)


NG = 64
GP = 32
DBG = {}


def dbgdump(env, name, ap, keys):
    if name in DBG:
        env.fw.dma("sync", DBG[name], ap, reads=keys, writes=["dbg_" + name])

TWO_PI = 6.283185307179586


def s5_phase(env, x_in, x_out, w_in, log_dt, lam_re, lam_im, b_re, b_im, c_re, c_im, d_skip, w_glu, w_out,
             pre_g, post_g, nseq, tag="s5"):
    nc, fw = env.nc, env.fw
    T = lambda s: tag + s
    NTOK = nseq * SEQ
    fw.barrier()
    with contextlib.ExitStack() as st:
        uT = env.sb(T("uT"), [128, 8, NTOK], BF16, st)
        LB = [env.sb(T("LB%d" % i), [128, GP, 128], BF16, st) for i in range(3)]
        LC = [env.sb(T("LC%d" % i), [128, GP, 128], BF16, st) for i in range(3)]
        CM = env.sb(T("CM"), [128, GP, 11], F32, st)
        SM = env.sb(T("SM"), [128, GP, 11], F32, st)
        RR = env.sb(T("RR"), [128, GP], F32, st)
        dvec = env.sb(T("dvec"), [128, 8], F32, st)
        SER = T("ser")

        with contextlib.ExitStack() as s0:
            P = {}
            for nm in ("lre", "lim", "ldt", "dt", "lr", "th", "mag", "f", "s4", "c2", "s2", "sn", "cs", "are", "aim",
                       "den", "nre", "zre", "zim", "t0", "t1"):
                P[nm] = env.sb(T("p_" + nm), [128, GP], F32, s0)
            fi = env.sb(T("p_fi"), [128, GP], I32, s0)
            braw = [env.sb(T("braw%d" % i), [128, GP, 16], F32, s0) for i in range(2)]
            bb = [env.sb(T("bb%d" % i), [128, GP, 16], F32, s0) for i in range(2)]
            Bpad = [env.sb(T("Bpad%d" % i), [128, GP, 128], F32, s0) for i in range(2)]
            Cc = [env.sb(T("Cc%d" % i), [128, 8, 128], F32, s0) for i in range(2)]
            MQ = env.sb(T("MQ"), [128, 4, 128], F32, s0)
            LG = [env.sb(T("LG%d" % i), [64, 128], F32, s0) for i in range(2)]
            LD = env.sb(T("LD"), [128, 64], F32, s0)
            DV = env.sb(T("DV"), [8, 128], F32, s0)
            mhg = env.sb(T("mhg"), [128, GP], F32, s0)
            for i, lsrc in enumerate((lam_re, lam_im)):
                for dup in range(2):
                    fw.dma("sync", LG[i][:, dup * 64:(dup + 1) * 64], lsrc, writes=[T("LG%d" % i), SER])
            fw.dma("sync", LD[:], log_dt.partition_broadcast(128), writes=[T("LD"), SER])
            fw.dma("sync", DV[:], d_skip.rearrange("(blk c) -> blk c", c=128), writes=[T("DV"), SER])
            for gi in range(2):
                rows = slice(gi * 64, (gi + 1) * 64)
                for i, bsrc in enumerate((b_re, b_im)):
                    for g0 in range(0, GP, 8):
                        fw.dma("sync", braw[i][rows, g0:g0 + 8, :],
                               bsrc.rearrange("(gp gi) p c -> gi p gp c", gi=2)[gi][:, g0:g0 + 8, :],
                               writes=[T("braw%d" % i), SER])
                for i, csrc in enumerate((c_re, c_im)):
                    for b0 in range(0, 8, 4):
                        fw.dma("sync", Cc[i][:, b0:b0 + 4, rows],
                               csrc.rearrange("(blk gl) c p -> (gl c) blk p", gl=8)[:, b0:b0 + 4, :],
                               writes=[T("Cc%d" % i), SER])
            fw.dve(lambda e: e.memset(mhg[:], -0.5), reads=[SER], writes=[T("mhg")])
            for i, nm in enumerate(("lre", "lim")):
                fw.pe(lambda e: e.transpose(env.bank(7)[:, 0:64], LG[i][:, :], env.identf[0:64, 0:64]),
                      reads=[T("LG%d" % i), "identf", SER], writes=["ps7"])
                fw.dve(lambda e: e.tensor_copy(P[nm][0:64, :], env.bank(7)[0:64, 0:64:2]), reads=["ps7"],
                       writes=[T(nm)])
                fw.dve(lambda e: e.tensor_copy(P[nm][64:128, :], env.bank(7)[64:128, 1:64:2]), reads=["ps7"],
                       writes=[T(nm)])
            fw.dve(lambda e: e.tensor_copy(P["ldt"][0:64, :], LD[0:64, 0:64:2]), reads=[T("LD"), SER], writes=[T("ldt")])
            fw.dve(lambda e: e.tensor_copy(P["ldt"][64:128, :], LD[64:128, 1:64:2]), reads=[T("LD"), SER],
                   writes=[T("ldt")])
            fw.pe(lambda e: e.transpose(env.bank(7)[:, 64:72], DV[:, :], env.identf[0:8, 0:8]),
                  reads=[T("DV"), "identf", SER], writes=["ps7"])
            fw.dve(lambda e: e.tensor_copy(dvec[:], env.bank(7)[:, 64:72]), reads=["ps7"], writes=[T("dvec")])

            def ew(eng, out, a, b, op, keys):
                getattr(fw, eng)(lambda e: e.tensor_tensor(out=out, in0=a, in1=b, op=op), reads=keys, writes=keys)

            K = [T("setup")]
            dep = [T("lre"), T("lim"), T("ldt"), SER] + K
            fw.act(lambda e: e.activation(out=P["dt"][:], in_=P["ldt"][:], func=AF.Exp), reads=dep, writes=K)
            ew("dve", P["lr"][:], P["lre"][:], P["dt"][:], ALU.mult, dep)
            ew("dve", P["th"][:], P["lim"][:], P["dt"][:], ALU.mult, dep)
            fw.act(lambda e: e.activation(out=P["mag"][:], in_=P["lr"][:], func=AF.Exp), reads=K, writes=K)
            fw.dve(lambda e: e.tensor_scalar(out=P["t0"][:], in0=P["th"][:], scalar1=1.0 / TWO_PI, scalar2=None,
                                             op0=ALU.mult), reads=K, writes=K)
            fw.dve(lambda e: e.tensor_copy(fi[:], P["t0"][:]), reads=K, writes=K)
            fw.dve(lambda e: e.tensor_copy(P["t1"][:], fi[:]), reads=K, writes=K)
            ew("dve", P["f"][:], P["t0"][:], P["t1"][:], ALU.subtract, K)
            fw.act(lambda e: e.activation(out=P["s4"][:], in_=P["f"][:], func=AF.Sin, scale=TWO_PI / 4), reads=K, writes=K)
            fw.act(lambda e: e.activation(out=P["s2"][:], in_=P["f"][:], func=AF.Sin, scale=TWO_PI / 2), reads=K, writes=K)
            ew("dve", P["t0"][:], P["s4"][:], P["s4"][:], ALU.mult, K)
            fw.dve(lambda e: e.tensor_scalar(out=P["c2"][:], in0=P["t0"][:], scalar1=-2.0, scalar2=1.0, op0=ALU.mult,
                                             op1=ALU.add), reads=K, writes=K)
            ew("dve", P["t0"][:], P["s2"][:], P["c2"][:], ALU.mult, K)
            fw.dve(lambda e: e.tensor_scalar(out=P["sn"][:], in0=P["t0"][:], scalar1=2.0, scalar2=None, op0=ALU.mult),
                   reads=K, writes=K)
            ew("dve", P["t0"][:], P["s2"][:], P["s2"][:], ALU.mult, K)
            fw.dve(lambda e: e.tensor_scalar(out=P["cs"][:], in0=P["t0"][:], scalar1=-2.0, scalar2=1.0, op0=ALU.mult,
                                             op1=ALU.add), reads=K, writes=K)
            ew("dve", P["t0"][:], P["cs"][:], P["cs"][:], ALU.mult, K)
            ew("dve", P["t1"][:], P["sn"][:], P["sn"][:], ALU.mult, K)
            ew("dve", P["t0"][:], P["t0"][:], P["t1"][:], ALU.add, K)
            fw.dve(lambda e: e.tensor_scalar(out=P["t0"][:], in0=P["t0"][:], scalar1=-0.5, scalar2=1.5, op0=ALU.mult,
                                             op1=ALU.add), reads=K, writes=K)
            ew("dve", P["cs"][:], P["cs"][:], P["t0"][:], ALU.mult, K)
            ew("dve", P["sn"][:], P["sn"][:], P["t0"][:], ALU.mult, K)
            ew("dve", P["are"][:], P["mag"][:], P["cs"][:], ALU.mult, K)
            ew("dve", P["aim"][:], P["mag"][:], P["sn"][:], ALU.mult, K)
            fw.dve(lambda e: e.tensor_copy(RR[:], P["mag"][:]), reads=K, writes=K + [T("RR")])
            fw.dve(lambda e: e.tensor_copy(CM[:, :, 0], P["cs"][:]), reads=K, writes=K)
            fw.dve(lambda e: e.tensor_copy(SM[:, :, 0], P["sn"][:]), reads=K, writes=K)
            for k in range(1, 11):
                ew("dve", P["t0"][:], SM[:, :, k - 1], CM[:, :, k - 1], ALU.mult, K)
                fw.dve(lambda e: e.tensor_scalar(out=SM[:, :, k], in0=P["t0"][:], scalar1=2.0, scalar2=None,
                                                 op0=ALU.mult), reads=K, writes=K)
                ew("dve", P["t0"][:], SM[:, :, k - 1], SM[:, :, k - 1], ALU.mult, K)
                fw.dve(lambda e: e.tensor_scalar(out=CM[:, :, k], in0=P["t0"][:], scalar1=-2.0, scalar2=1.0,
                                                 op0=ALU.mult, op1=ALU.add), reads=K, writes=K)
            ew("dve", P["den"][:], P["lre"][:], P["lre"][:], ALU.mult, K)
            ew("dve", P["t0"][:], P["lim"][:], P["lim"][:], ALU.mult, K)
            ew("dve", P["den"][:], P["den"][:], P["t0"][:], ALU.add, K)
            fw.dve(lambda e: e.reciprocal(P["den"][:], P["den"][:]), reads=K, writes=K)
            fw.dve(lambda e: e.tensor_scalar(out=P["nre"][:], in0=P["are"][:], scalar1=-1.0, scalar2=None,
                                             op0=ALU.add), reads=K, writes=K)
            ew("dve", P["t0"][:], P["nre"][:], P["lre"][:], ALU.mult, K)
            ew("dve", P["t1"][:], P["aim"][:], P["lim"][:], ALU.mult, K)
            ew("dve", P["t0"][:], P["t0"][:], P["t1"][:], ALU.add, K)
            ew("dve", P["zre"][:], P["t0"][:], P["den"][:], ALU.mult, K)
            ew("dve", P["t0"][:], P["aim"][:], P["lre"][:], ALU.mult, K)
            ew("dve", P["t1"][:], P["nre"][:], P["lim"][:], ALU.mult, K)
            ew("dve", P["t0"][:], P["t0"][:], P["t1"][:], ALU.subtract, K)
            ew("dve", P["zim"][:], P["t0"][:], P["den"][:], ALU.mult, K)
            KB = K + [T("braw0"), T("braw1")]
            for c in range(16):
                ew("dve", P["t0"][:], P["zre"][:], braw[0][:, :, c], ALU.mult, KB)
                ew("dve", P["t1"][:], P["zim"][:], braw[1][:, :, c], ALU.mult, KB)
                ew("dve", bb[0][:, :, c], P["t0"][:], P["t1"][:], ALU.subtract, KB)
                ew("dve", P["t0"][:], P["zre"][:], braw[1][:, :, c], ALU.mult, KB)
                ew("dve", P["t1"][:], P["zim"][:], braw[0][:, :, c], ALU.mult, KB)
                ew("dve", bb[1][:, :, c], P["t0"][:], P["t1"][:], ALU.add, KB)
            for i in range(3):
                if i == 2:
                    fw.dve(lambda e: e.tensor_tensor(out=Bpad[0][:], in0=Bpad[0][:], in1=Bpad[1][:], op=ALU.add),
                           reads=KB + [T("LB")], writes=KB)
                    for gp in range(GP):
                        fw.pe(lambda e: e.transpose(env.bank(gp % 4)[:, 0:128], Bpad[0][:, gp, :], env.identf[:]),
                              reads=KB + ["identf"], writes=["ps%d" % (gp % 4)])
                        fw.act(lambda e: e.copy(LB[2][:, gp, :], env.bank(gp % 4)[:, 0:128]),
                               reads=["ps%d" % (gp % 4)], writes=[T("LB")])
                    break
                fw.dve(lambda e: e.memset(Bpad[i][:], 0.0), reads=KB, writes=KB)
                for gi in range(2):
                    rows = slice(gi * 64, (gi + 1) * 64)
                    for q in range(4):
                        c0 = (2 * q + gi) * 16
                        fw.dve(lambda e: e.tensor_copy(Bpad[i][rows, q::4, c0:c0 + 16], bb[i][rows, q::4, :]),
                               reads=KB, writes=KB)
                for gp in range(GP):
                    fw.pe(lambda e: e.transpose(env.bank(gp % 4)[:, 0:128], Bpad[i][:, gp, :], env.identf[:]),
                          reads=KB + ["identf"], writes=["ps%d" % (gp % 4)])
                    fw.act(lambda e: e.copy(LB[i][:, gp, :], env.bank(gp % 4)[:, 0:128]),
                           reads=["ps%d" % (gp % 4)], writes=[T("LB")])
            dbgdump(env, "are", P["are"][:], K)
            dbgdump(env, "zre", P["zre"][:], K)
            dbgdump(env, "cm", CM[:].rearrange("p a b -> p (a b)"), K)
            dbgdump(env, "bb0", bb[0][:].rearrange("p a b -> p (a b)"), KB)
            dbgdump(env, "braw0", braw[0][:].rearrange("p a b -> p (a b)"), KB)
            dbgdump(env, "zim", P["zim"][:], KB)
            fw.dve(lambda e: e.memset(MQ[:], 0.0), writes=[T("MQ")])
            for q in range(4):
                fw.dve(lambda e: e.memset(MQ[0:64, q, (2 * q) * 16:(2 * q) * 16 + 16], 1.0), writes=[T("MQ")])
                fw.dve(lambda e: e.memset(MQ[64:128, q, (2 * q + 1) * 16:(2 * q + 1) * 16 + 16], 1.0),
                       writes=[T("MQ")])
            fw.dve(lambda e: e.tensor_scalar(out=Cc[1][:], in0=Cc[1][:], scalar1=-1.0, scalar2=None, op0=ALU.mult),
                   reads=[T("Cc1")], writes=[T("Cc1")])
            for i in range(2):
                for blk in range(8):
                    pb = 4 + blk % 2
                    fw.pe(lambda e: e.transpose(env.bank(pb)[:, 0:128], Cc[i][:, blk, :], env.identf[:]),
                          reads=[T("Cc%d" % i), "identf"], writes=["ps%d" % pb])
                    for q in range(4):
                        fw.dve(lambda e: e.tensor_tensor(out=LC[i][:, blk * 4 + q, :], in0=env.bank(pb)[:, 0:128],
                                                         in1=MQ[:, q, :], op=ALU.mult),
                               reads=["ps%d" % pb, T("MQ")], writes=[T("LC")])
                        if i == 0:
                            fw.dve(lambda e: e.scalar_tensor_tensor(out=LC[2][:, blk * 4 + q, :],
                                                                    in0=env.bank(pb)[:, 0:128], scalar=-1.0,
                                                                    in1=MQ[:, q, :], op0=ALU.mult, op1=ALU.mult),
                                   reads=["ps%d" % pb, T("MQ")], writes=[T("LC")])

        dbgdump(env, "LB0", LB[0][:, 0, :], [T("LB")])
        dbgdump(env, "LC0", LC[0][:, 0, :], [T("LC")])
        dbgdump(env, "LC1", LC[1][:, 5, :], [T("LC")])
        fw.barrier()
        with contextlib.ExitStack() as s1:
            Wi = env.sb(T("Wi"), [128, 8, D], BF16, s1)
            gpre = env.sb(T("gpre"), [128, D], F32, s1)
            fw.dma("sync", gpre[:], pre_g.partition_broadcast(128), writes=[T("gpre")])
            HT = [env.sb(T("hT%d" % i), [128, 8, 512], BF16, s1) for i in range(2)]
            XT = [env.sb(T("xt%d" % i), [128, D], F32, s1) for i in range(2)]
            HB = [env.sb(T("hb%d" % i), [128, D], BF16, s1) for i in range(2)]
            kWi = load_w_cast(fw, Wi, T("Wi"), w_in, 8)
            it = 0
            for t in range(NTOK // 512):
                hT, khT = HT[t % 2], T("hT%d" % (t % 2))
                for s in range(4):
                    r0 = t * 512 + s * 128
                    xt, kx = XT[it % 2], T("xt%d" % (it % 2))
                    hb, khb = HB[it % 2], T("hb%d" % (it % 2))
                    fw.dma("sync", xt[:], x_in[r0:r0 + 128, :], writes=[kx])
                    norm_transpose(env, xt[:], kx, gpre[:], T("gpre"), hb[:], khb, hT, khT, s * 128, 6)
                    it += 1
                for blk in range(8):
                    pb = blk % 4
                    fw.pe_group([lambda e, k=k: e.matmul(env.bank(pb), Wi[:, k, blk * 128:(blk + 1) * 128],
                                                         hT[:, k, :], start=(k == 0), stop=(k == 7))
                                 for k in range(8)], reads=kWi + [khT], writes=["ps%d" % pb])
                    fw.act(lambda e: e.copy(uT[:, blk, t * 512:(t + 1) * 512], env.bank(pb)),
                           reads=["ps%d" % pb], writes=[T("uT%d_%d" % (blk, t // 4))])

        fw.barrier()
        with contextlib.ExitStack() as s2:
            TAB = [tuple(env.sb(T("%s%d" % (nm, q)), [128, 512], F32, s2) for nm in ("COS", "SIN", "TM", "TP"))
                   for q in range(4)]
            TT = env.sb(T("TT"), [128, 256], F32, s2)
            RT = [env.sb(T("Rt%d" % q), [128, 512], F32, s2) for q in range(4)]
            ones = env.sb(T("ones"), [128, 512], F32, s2)
            WS = []
            for i in range(2):
                d = {}
                for nm in ("bs", "bre", "bim", "w_re", "w_im"):
                    d[nm] = env.sb(T("w%d_%s" % (i, nm)), [128, 512], F32, s2)
                for nm in ("p1", "p2", "p3", "p4"):
                    d[nm] = env.sb(T("w%d_%s" % (i, nm)), [128, 512], BF16, s2)
                WS.append(d)
            INI = env.sb(T("ini"), [128, 8], F32, s2)
            YV = env.sb(T("yv"), [128, 512], F32, s2)
            G1 = env.sb(T("g1"), [128, 512], F32, s2)
            G2 = env.sb(T("g2"), [128, 512], F32, s2)
            NS9 = env.sb(T("ns9"), [128, GP], F32, s2)
            fw.dve(lambda e: e.memset(ones[:], 1.0), writes=[T("ones")])
            fw.dve(lambda e: e.tensor_scalar(out=NS9[:], in0=SM[:, :, 9], scalar1=-1.0, scalar2=None, op0=ALU.mult),
                   reads=[T("setup")], writes=[T("ns9")])

            def build_table(q, gp):
                COS, SIN, TM, TP = TAB[q]
                KT = [T("tab%d" % q)]
                fw.dve(lambda e: e.memset(COS[:, 0:1], 1.0), reads=KT, writes=KT)
                fw.dve(lambda e: e.memset(SIN[:, 0:1], 0.0), reads=KT, writes=KT)
                for k in range(9):
                    m = 1 << k
                    cm, sm = CM[:, gp, k:k + 1], SM[:, gp, k:k + 1]
                    fw.dve(lambda e: e.tensor_scalar(out=TT[:, 0:m], in0=SIN[:, 0:m], scalar1=sm, scalar2=None,
                                                     op0=ALU.mult), reads=KT + [T("setup"), T("TT")], writes=[T("TT")])
                    fw.dve(lambda e: e.scalar_tensor_tensor(out=COS[:, m:2 * m], in0=COS[:, 0:m], scalar=cm,
                                                            in1=TT[:, 0:m], op0=ALU.mult, op1=ALU.subtract),
                           reads=KT + [T("TT")], writes=KT)
                    fw.dve(lambda e: e.tensor_scalar(out=TT[:, 0:m], in0=COS[:, 0:m], scalar1=sm, scalar2=None,
                                                     op0=ALU.mult), reads=KT + [T("TT")], writes=[T("TT")])
                    fw.dve(lambda e: e.scalar_tensor_tensor(out=SIN[:, m:2 * m], in0=SIN[:, 0:m], scalar=cm,
                                                            in1=TT[:, 0:m], op0=ALU.mult, op1=ALU.add),
                           reads=KT + [T("TT")], writes=KT)
                fw.dve(lambda e: e.tensor_tensor(out=TM[:], in0=COS[:], in1=SIN[:], op=ALU.subtract),
                       reads=KT, writes=KT)
                fw.dve(lambda e: e.tensor_tensor(out=TP[:], in0=COS[:], in1=SIN[:], op=ALU.add),
                       reads=KT, writes=KT)
                fw.dve(lambda e: e.tensor_scalar(out=RT[q][:], in0=ones[:], scalar1=RR[:, gp:gp + 1], scalar2=None,
                                                 op0=ALU.mult), reads=[T("ones"), T("RR"), T("Rt%d" % q)],
                       writes=[T("Rt%d" % q)])

            def stage_a(i, blk, sq_i, q, t):
                gp = blk * 4 + q
                W, kw = WS[i % 2], T("ws%d" % (i % 2))
                COS, SIN, TM, TP = TAB[q]
                KT = [T("tab%d" % q)]
                cols = slice(sq_i * SEQ + t * 512, sq_i * SEQ + (t + 1) * 512)
                ku = T("uT%d_%d" % (blk, sq_i))
                for (bnk, li, dst) in ((0, 2, "bs"), (1, 0, "bre"), (2, 1, "bim")):
                    fw.pe(lambda e: e.matmul(env.bank(bnk), LB[li][:, gp, :], uT[:, blk, cols], start=True,
                                             stop=True), reads=[T("LB"), ku], writes=["ps%d" % bnk])
                    fw.act(lambda e: e.copy(W[dst][:], env.bank(bnk)), reads=["ps%d" % bnk], writes=[kw + dst])
                fw.dve(lambda e: e.tensor_tensor(out=W["bs"][:], in0=W["bs"][:], in1=COS[:], op=ALU.mult),
                       reads=[kw + "bs"] + KT, writes=[kw + "bs"])
                fw.dve(lambda e: e.tensor_tensor(out=W["bim"][:], in0=W["bim"][:], in1=TM[:], op=ALU.mult),
                       reads=[kw + "bim"] + KT, writes=[kw + "bim"])
                fw.dve(lambda e: e.tensor_tensor(out=W["bre"][:], in0=W["bre"][:], in1=TP[:], op=ALU.mult),
                       reads=[kw + "bre"] + KT, writes=[kw + "bre"])
                fw.dve(lambda e: e.tensor_tensor(out=W["bim"][:], in0=W["bs"][:], in1=W["bim"][:], op=ALU.subtract),
                       reads=[kw + "bs", kw + "bim"], writes=[kw + "bim"])
                fw.dve(lambda e: e.tensor_tensor(out=W["bre"][:], in0=W["bs"][:], in1=W["bre"][:], op=ALU.subtract),
                       reads=[kw + "bs", kw + "bre"], writes=[kw + "bre"])

            def stage_b(i, blk, sq_i, q, t):
                gp = blk * 4 + q
                W, kw = WS[i % 2], T("ws%d" % (i % 2))
                Wp, kwp = WS[(i + 1) % 2], T("ws%d" % ((i + 1) % 2))
                COS, SIN, TM, TP = TAB[q]
                KT = [T("tab%d" % q)]
                if t == 0:
                    ire, iim = 0.0, 0.0
                    kini = []
                else:
                    c9, s9, ns9 = CM[:, gp, 9:10], SM[:, gp, 9:10], NS9[:, gp:gp + 1]
                    lre, lim_ = Wp["w_re"][:, 511:512], Wp["w_im"][:, 511:512]
                    o = 4 * (i % 2)
                    kini = [T("ini%d" % (i % 2))]
                    fw.act(lambda e: e.activation(out=INI[:, o:o + 1], in_=lim_, func=AF.Identity, scale=ns9),
                           reads=[kwp + "w_im", T("ns9")] + kini, writes=kini)
                    fw.act(lambda e: e.activation(out=INI[:, o + 1:o + 2], in_=lre, func=AF.Identity, scale=c9,
                                                  bias=INI[:, o:o + 1]), reads=[kwp + "w_re", T("setup")] + kini,
                           writes=kini)
                    fw.act(lambda e: e.activation(out=INI[:, o + 2:o + 3], in_=lre, func=AF.Identity, scale=s9),
                           reads=[kwp + "w_re"] + kini, writes=kini)
                    fw.act(lambda e: e.activation(out=INI[:, o + 3:o + 4], in_=lim_, func=AF.Identity, scale=c9,
                                                  bias=INI[:, o + 2:o + 3]), reads=[kwp + "w_im"] + kini, writes=kini)
                    ire, iim = INI[:, o + 1:o + 2], INI[:, o + 3:o + 4]
                fw.dve(lambda e: e.tensor_tensor_scan(out=W["w_re"][:], data0=RT[q][:], data1=W["bim"][:],
                                                      initial=ire, op0=ALU.mult, op1=ALU.add),
                       reads=[kw + "bim", T("Rt%d" % q)] + kini, writes=[kw + "w_re"])
                fw.dve(lambda e: e.tensor_tensor_scan(out=W["w_im"][:], data0=RT[q][:], data1=W["bre"][:],
                                                      initial=iim, op0=ALU.mult, op1=ALU.add),
                       reads=[kw + "bre", T("Rt%d" % q)] + kini, writes=[kw + "w_im"])
                yb = 4 + t
                plan = (("p1", "w_re", COS, "dve", 0), ("p2", "w_im", SIN, "dve", 2), ("p3", "w_re", SIN, "dve", 1),
                        ("p4", "w_im", COS, "dve", 1))
                for n_, (o_, a_, tab, eng, lc) in enumerate(plan):
                    getattr(fw, eng)(lambda e: e.tensor_tensor(out=W[o_][:], in0=W[a_][:], in1=tab[:], op=ALU.mult),
                                     reads=[kw + a_] + KT, writes=[kw + o_])
                for n_, (o_, a_, tab, eng, lc) in enumerate(plan):
                    fw.pe(lambda e: e.matmul(env.bank(yb), LC[lc][:, gp, :], W[o_][:], start=(q == 0 and n_ == 0),
                                             stop=(q == 3 and n_ == 3)),
                          reads=[T("LC"), kw + o_], writes=["ps%d" % yb])

            def epilogue(blk, sq_i):
                ku = T("uT%d_%d" % (blk, sq_i))
                for t in range(4):
                    cols = slice(sq_i * SEQ + t * 512, sq_i * SEQ + (t + 1) * 512)
                    yb = 4 + t
                    KY = [T("yv")]
                    fw.dve(lambda e: e.scalar_tensor_tensor(out=YV[:], in0=uT[:, blk, cols],
                                                            scalar=dvec[:, blk:blk + 1], in1=env.bank(yb),
                                                            op0=ALU.mult, op1=ALU.add),
                           reads=[ku, T("dvec"), "ps%d" % yb] + KY, writes=KY)
                    fw.act(lambda e: e.activation(out=G1[:], in_=YV[:], func=AF.Square), reads=KY, writes=KY)
                    fw.act(lambda e: e.activation(out=G1[:], in_=G1[:], func=AF.Identity, scale=0.044715, bias=1.0),
                           reads=KY, writes=KY)
                    fw.dve(lambda e: e.tensor_tensor(out=G1[:], in0=G1[:], in1=YV[:], op=ALU.mult),
                           reads=KY, writes=KY)
                    fw.act(lambda e: e.activation(out=G2[:], in_=G1[:], func=AF.Tanh, scale=0.7978845608028654),
                           reads=KY, writes=KY)
                    fw.act(lambda e: e.activation(out=G2[:], in_=G2[:], func=AF.Identity, scale=0.5, bias=0.5),
                           reads=KY, writes=KY)
                    fw.dve(lambda e: e.tensor_tensor(out=uT[:, blk, cols], in0=G2[:], in1=YV[:], op=ALU.mult),
                           reads=KY + [ku], writes=KY + [ku])

            gi_ = 0
            for blk in range(8):
                for q in range(4):
                    build_table(q, blk * 4 + q)
                items = [(sq_i, q, t) for sq_i in range(nseq) for q in range(4) for t in range(4)]
                stage_a(gi_, blk, *items[0])
                for n, it_ in enumerate(items):
                    if n + 1 < len(items):
                        stage_a(gi_ + 1, blk, *items[n + 1])
                    stage_b(gi_, blk, *it_)
                    gi_ += 1
                    if it_[1] == 3 and it_[2] == 3:
                        epilogue(blk, it_[0])

        dbgdump(env, "yg0", uT[:, 0, 0:512], [T("uT0")])
        fw.barrier()
        with contextlib.ExitStack() as s3:
            Wg = env.sb(T("Wglu"), [128, 8, D], BF16, s3)
            gpost = env.sb(T("gpost"), [128, D], F32, s3)
            fw.dma("sync", gpost[:], post_g.partition_broadcast(128), writes=[T("gpost")])
            Wo = env.sb(T("Wo"), [128, 8, D], BF16, s3)
            ZT = [env.sb(T("zT%d" % i), [128, 8, 512], BF16, s3) for i in range(2)]
            SG = [env.sb(T("sg%d" % i), [128, 512], F32, s3) for i in range(4)]
            XR = [env.sb(T("xr%d" % i), [128, D], F32, s3) for i in range(2)]
            TMP = env.sb(T("tmp"), [128, D], F32, s3)
            kWglu = load_w_cast(fw, Wg, T("Wglu"), w_glu, 8)
            kWo5 = load_w_cast(fw, Wo, T("Wo"), w_out, 8)
            ukeys = [T("uT%d_%d" % (b, q_)) for b in range(8) for q_ in range(nseq)]
            for t in range(NTOK // 512):
                cols = slice(t * 512, (t + 1) * 512)
                zT, kzT = ZT[t % 2], T("zT%d" % (t % 2))
                for blk in range(8):
                    pb = (0, 1, 6, 7)[blk % 4]
                    fw.pe_group([lambda e, k=k: e.matmul(env.bank(pb), Wg[:, k, blk * 128:(blk + 1) * 128],
                                                         uT[:, k, cols], start=(k == 0), stop=(k == 7))
                                 for k in range(8)], reads=kWglu + ukeys, writes=["ps%d" % pb])
                    sg, ksg = SG[blk % 4], T("sg%d" % (blk % 4))
                    fw.act(lambda e: e.activation(out=sg[:], in_=env.bank(pb), func=AF.Sigmoid),
                           reads=["ps%d" % pb], writes=[ksg])
                    fw.dve(lambda e: e.tensor_tensor(out=zT[:, blk, :], in0=sg[:], in1=uT[:, blk, cols], op=ALU.mult),
                           reads=[ksg] + ukeys, writes=[kzT])
                for s in range(4):
                    r0 = t * 512 + s * 128
                    xr, kxr = XR[s % 2], T("xr%d" % (s % 2))
                    fw.dma("sync", xr[:], x_in[r0:r0 + 128, :], writes=[kxr])
                    pb0 = 2 + 2 * (s % 2)
                    pso = env.bank(pb0, 2)
                    pk = ["ps%d" % pb0, "ps%d" % (pb0 + 1)]
                    for hf in range(2):
                        fw.pe_group([lambda e, k=k: e.matmul(pso[:, hf, :], zT[:, k, s * 128:(s + 1) * 128],
                                                             Wo[:, k, hf * 512:(hf + 1) * 512], start=(k == 0),
                                                             stop=(k == 7)) for k in range(8)],
                                    reads=[kzT] + kWo5, writes=pk)
                    post_norm_residual(env, pso, pk, gpost[:], T("gpost"), xr[:], kxr, TMP[:], T("tmp"),
                                       xr[:], kxr)
                    fw.dma("sync", x_out[r0:r0 + 128, :], xr[:], reads=[kxr], writes=[])


NSEQ_CORE = 2
NCORES = 8
_CACHE = {}


def build_program():
    nc = bass.Bass("TRN2", target_bir_lowering=False)
    ntok = NSEQ_CORE * SEQ

    def din(name, shape):
        return nc.dram_tensor(name, shape, F32, kind="ExternalInput").ap()

    x = din("x", [ntok, D])
    fox_w_in = din("fox_w_in", [D, 4112])
    fox_b_f = din("fox_b_f", [NH])
    fox_q_gain = din("fox_q_gain", [HD])
    fox_k_gain = din("fox_k_gain", [HD])
    fox_w_out = din("fox_w_out", [DATT, D])
    s5_w_in = din("s5_w_in", [D, D])
    s5_log_dt = din("s5_log_dt", [NG])
    s5_lam_re = din("s5_lam_re", [NG, 64])
    s5_lam_im = din("s5_lam_im", [NG, 64])
    s5_b_re = din("s5_b_re", [NG, 64, 16])
    s5_b_im = din("s5_b_im", [NG, 64, 16])
    s5_c_re = din("s5_c_re", [NG, 16, 64])
    s5_c_im = din("s5_c_im", [NG, 16, 64])
    s5_d = din("s5_d", [D])
    s5_w_glu = din("s5_w_glu", [D, D])
    s5_w_out = din("s5_w_out", [D, D])
    mix_pre = din("mix_pre_gain", [2, D])
    mix_post = din("mix_post_gain", [2, D])
    ffn_pre = din("ffn_pre_gain", [2, D])
    ffn_post = din("ffn_post_gain", [2, D])
    ffn_wg = din("ffn_w_gate", [2, D, DFF])
    ffn_wu = din("ffn_w_up", [2, D, DFF])
    ffn_wd = din("ffn_w_down", [2, DFF, D])
    ident = din("c_ident", [128, 128])
    cmask = din("c_mask", [128, 128])
    out = nc.dram_tensor("out", [ntok, D], F32, kind="ExternalOutput").ap()
    xa = nc.dram_tensor("xa", [ntok, D], F32).ap()
    xb = nc.dram_tensor("xb", [ntok, D], F32).ap()
    xc = nc.dram_tensor("xc", [ntok, D], F32).ap()
    oT_d = nc.dram_tensor("oT_d", [NSEQ_CORE, NH, HD, SEQ], BF16).ap()

    fw = FW(nc)
    with contextlib.ExitStack() as st:
        env = Env(nc, fw, st)
        env.init_consts(ident)
        fox_phase(env, x, xa, fox_w_in, fox_b_f, fox_q_gain, fox_k_gain, fox_w_out, mix_pre[0, :], mix_post[0, :],
                  cmask, oT_d, NSEQ_CORE)
        ffn_phase(env, xa, xb, ffn_wg[0], ffn_wu[0], ffn_wd[0], ffn_pre[0, :], ffn_post[0, :], ntok, "f0")
        s5_phase(env, xb, xc, s5_w_in, s5_log_dt, s5_lam_re, s5_lam_im, s5_b_re, s5_b_im, s5_c_re, s5_c_im, s5_d,
                 s5_w_glu, s5_w_out, mix_pre[1, :], mix_post[1, :], NSEQ_CORE)
        ffn_phase(env, xc, out, ffn_wg[1], ffn_wu[1], ffn_wd[1], ffn_pre[1, :], ffn_post[1, :], ntok, "f1")
        fw.finish()
    return nc


def kernel(**inputs):
    f = np.float32
    x = np.ascontiguousarray(inputs["x"], dtype=f)
    B = x.shape[0]
    shared = {}
    for k in ("fox_w_in", "fox_b_f", "fox_q_gain", "fox_k_gain", "fox_w_out", "s5_w_in", "s5_log_dt", "s5_lam_re",
              "s5_lam_im", "s5_b_re", "s5_b_im", "s5_c_re", "s5_c_im", "s5_d", "s5_w_glu", "s5_w_out"):
        shared[k] = np.ascontiguousarray(np.asarray(inputs[k], dtype=f)[0])
    for k in ("mix_pre_gain", "mix_post_gain", "ffn_pre_gain", "ffn_post_gain", "ffn_w_gate", "ffn_w_up",
              "ffn_w_down"):
        shared[k] = np.ascontiguousarray(np.asarray(inputs[k], dtype=f))
    shared["c_ident"] = np.eye(128, dtype=f)
    shared["c_mask"] = np.where(np.arange(128)[None, :] < np.arange(128)[:, None], MASKNEG, 0.0).astype(f)
    if "nc" not in _CACHE:
        _CACHE["nc"] = build_program()
    nc = _CACHE["nc"]
    in_maps = []
    for c in range(NCORES):
        m = dict(shared)
        m["x"] = np.ascontiguousarray(x[c * NSEQ_CORE:(c + 1) * NSEQ_CORE].reshape(NSEQ_CORE * SEQ, D))
        in_maps.append(m)
    res = run_bass_kernel_spmd(nc, in_maps, core_ids=list(range(NCORES)))
    outs = [np.asarray(r["out"]).reshape(NSEQ_CORE, SEQ, D) for r in res.results]
    return np.concatenate(outs, axis=0).astype(f)
```

```python
import contextlib
import numpy as np
import concourse.bass as bass
import concourse.mybir as mybir
from concourse.bass_utils import run_bass_kernel_spmd

F32 = mybir.dt.float32
BF16 = mybir.dt.bfloat16
I32 = mybir.dt.int32
AF = mybir.ActivationFunctionType
ALU = mybir.AluOpType
AX = mybir.AxisListType


STRICT_SAME_ENGINE = True


class FW:
    def __init__(self, nc, n_dma_sems=6):
        self.nc = nc
        self.stack = contextlib.ExitStack()
        self.E = {}
        self.sems = []
        for name, h in (("pe", nc.tensor), ("act", nc.scalar), ("dve", nc.vector),
                        ("pool", nc.gpsimd), ("sync", nc.sync)):
            e = {"name": name, "h": h, "count": 0, "waited": {}, "dma": [], "dma_i": 0}
            if name != "sync":
                e["sem"] = self._newsem("c_" + name)
            for i in range(n_dma_sems):
                e["dma"].append([self._newsem("d_%s%d" % (name, i)), 0])
            self.E[name] = e
        self.lastw = {}
        self.readers = {}
        self.nwaits = 0

    def _newsem(self, name):
        s = self.stack.enter_context(self.nc.semaphore(name))
        self.sems.append(s)
        return len(self.sems) - 1

    def _wait(self, E, sid, val):
        if E["waited"].get(sid, 0) >= val:
            return
        E["h"].wait_ge(self.sems[sid], val)
        E["waited"][sid] = val
        self.nwaits += 1

    def _sync(self, E, reads, writes, attach=False):
        need = {}

        def add(ev):
            if ev is None:
                return
            sid, val = ev
            if need.get(sid, 0) < val:
                need[sid] = val

        for r in reads:
            add(self.lastw.get(r))
        for w in writes:
            add(self.lastw.get(w))
            for sid, val in self.readers.get(w, {}).items():
                add((sid, val))
        own = E.get("sem")
        todo = []
        for sid, val in need.items():
            if sid == own:
                if E["name"] == "pe":
                    continue
                if STRICT_SAME_ENGINE is False and E["name"] in ("dve", "act") and val < E["count"]:
                    continue
            if E["waited"].get(sid, 0) >= val:
                continue
            todo.append((sid, val))
        held = None
        if attach and todo:
            held = todo.pop()
        for sid, val in todo:
            self._wait(E, sid, val)
        return held

    def _attach(self, E, ins, held):
        if held is not None:
            sid, val = held
            ins._wait_ge(self.sems[sid], val)
            E["waited"][sid] = val
            self.nwaits += 1

    def _record(self, ev, reads, writes):
        sid, val = ev
        for r in reads:
            d = self.readers.setdefault(r, {})
            if d.get(sid, 0) < val:
                d[sid] = val
        for w in writes:
            self.lastw[w] = ev
            self.readers[w] = {}

    def op(self, en, fn, reads=(), writes=()):
        E = self.E[en]
        held = self._sync(E, reads, writes, attach=(en != "pe"))
        ins = fn(E["h"])
        self._attach(E, ins, held)
        E["count"] += 1
        ins.then_inc(self.sems[E["sem"]], 1)
        self._record((E["sem"], E["count"]), reads, writes)
        return ins

    def pe(self, fn, reads=(), writes=()):
        return self.op("pe", fn, reads, writes)

    def act(self, fn, reads=(), writes=()):
        return self.op("act", fn, reads, writes)

    def dve(self, fn, reads=(), writes=()):
        return self.op("dve", fn, reads, writes)

    def pool(self, fn, reads=(), writes=()):
        return self.op("pool", fn, reads, writes)

    def pe_group(self, fns, reads=(), writes=()):
        E = self.E["pe"]
        held = self._sync(E, reads, writes, attach=False)
        ins = None
        for n, fn in enumerate(fns):
            ins = fn(E["h"])
            if n == 0:
                self._attach(E, ins, held)
        E["count"] += 1
        ins.then_inc(self.sems[E["sem"]], 1)
        self._record((E["sem"], E["count"]), reads, writes)

    def dma(self, q, out, in_, reads=(), writes=(), **kw):
        E = self.E[q]
        self._sync(E, reads, writes)
        slot = E["dma"][E["dma_i"] % len(E["dma"])]
        E["dma_i"] += 1
        sid, target = slot
        if target > 0:
            self._wait(E, sid, target)
        E["h"].dma_start(out=out, in_=in_, **kw).then_inc(self.sems[sid], 16)
        slot[1] = target + 16
        self._record((sid, slot[1]), reads, writes)

    def barrier(self):
        evs = []
        for e in self.E.values():
            for sid, target in e["dma"]:
                if target > 0:
                    evs.append((sid, target))
            if "sem" in e and e["count"] > 0:
                evs.append((e["sem"], e["count"]))
        for E in self.E.values():
            for sid, val in evs:
                self._wait(E, sid, val)

    def finish(self):
        S = self.E["sync"]
        for e in self.E.values():
            for sid, target in e["dma"]:
                if target > 0:
                    self._wait(S, sid, target)
            if "sem" in e and e["count"] > 0:
                self._wait(S, e["sem"], e["count"])
        self.stack.close()


EPS = 1e-6
D = 1024
DFF = 2816
NFC = DFF // 128


class Env:
    def __init__(self, nc, fw, st):
        self.nc, self.fw, self.st = nc, fw, st
        self.ps = st.enter_context(nc.psum_tensor("psall", [128, 8, 512], F32))
        self.identb = self.sb("identb", [128, 128], BF16)
        self.identf = self.sb("identf", [128, 128], F32)
        self.mhalf = self.sb("mhalf", [128, 1], F32)
        self.stats = self.sb("stats", [128, 96], F32)
        self.junk = self.sb("junk", [128, 1024], BF16)
        self.si = 0

    def sb(self, name, shape, dt, st=None):
        return (st or self.st).enter_context(self.nc.sbuf_tensor(name, shape, dt))

    def bank(self, i, n=1):
        if n == 1:
            return self.ps[:, i, :]
        return self.ps[:, i:i + n, :]

    def stat(self):
        i = self.si % 96
        self.si += 1
        return self.stats[:, i:i + 1], "st%d" % i

    def init_consts(self, ident_dram):
        fw = self.fw
        fw.dma("sync", self.identf[:], ident_dram, writes=["identf"])
        fw.dve(lambda e: e.tensor_copy(self.identb[:], self.identf[:]), reads=["identf"], writes=["identb"])
        fw.dve(lambda e: e.memset(self.mhalf[:], -0.5), writes=["mhalf"])

    def rstd(self, src_ap, src_key, n):
        fw = self.fw
        P = src_ap.shape[0]
        ss, kss = self.stat()
        var, kvar = self.stat()
        rs, krs = self.stat()
        junk = self.junk[0:P, 0:n]
        sk = list(src_key) if isinstance(src_key, (list, tuple)) else [src_key]
        fw.act(lambda e: e.activation(out=junk, in_=src_ap, func=AF.Square, accum_out=ss[0:P, :]),
               reads=sk, writes=[kss, "junk"])
        fw.dve(lambda e: e.tensor_scalar(out=var[0:P, :], in0=ss[0:P, :], scalar1=1.0 / n, scalar2=EPS,
                                         op0=ALU.mult, op1=ALU.add), reads=[kss], writes=[kvar])
        fw.pool(lambda e: e.tensor_tensor(out=rs[0:P, :], in0=var[0:P, :], in1=self.mhalf[0:P, :], op=ALU.pow),
                reads=[kvar, "mhalf"], writes=[krs])
        return rs, krs


def load_w_cast(fw, dst_tile, dst_key, w_dram, kchunks, first=False):
    keys = []
    for k in range(kchunks):
        kk = "%s_k%d" % (dst_key, k)
        fw.dma("pool", dst_tile[:, k, :], w_dram[k * 128:(k + 1) * 128, :], writes=[kk])
        keys.append(kk)
    return keys


def load_w_cast_cols(fw, dst_tile, dst_key, w_dram, kchunks, col_groups):
    src = w_dram.rearrange("(k p) c -> p k c", p=128)
    keys = {}
    for gi, (c0, c1) in enumerate(col_groups):
        kk = "%s_c%d" % (dst_key, gi)
        fw.dma("pool", dst_tile[:, 0:kchunks, c0:c1], src[:, :, c0:c1], writes=[kk])
        keys[gi] = kk
    return keys


def norm_transpose(env, x_ap, x_key, g_ap, g_key, hb, hb_key, hT, hT_key, col0, psT_bank):
    fw = env.fw
    rs, krs = env.rstd(x_ap, x_key, D)
    fw.dve(lambda e: e.scalar_tensor_tensor(out=hb, in0=x_ap, scalar=rs, in1=g_ap, op0=ALU.mult, op1=ALU.mult),
           reads=[x_key, krs, g_key], writes=[hb_key])
    transpose_in(env, hb, hb_key, hT, hT_key, col0, psT_bank)


def transpose_in(env, hb, hb_key, hT, hT_key, col0, psT_bank, on="act"):
    fw = env.fw
    pkey = "ps%d" % psT_bank
    psT = env.bank(psT_bank).bitcast(BF16)
    fns = []
    for j in range(8):
        fns.append(lambda e, j=j: e.transpose(psT[:, j * 128:(j + 1) * 128], hb[:, j * 128:(j + 1) * 128],
                                               env.identb[:]))
    fw.pe_group(fns, reads=[hb_key, "identb"], writes=[pkey])
    src = psT.rearrange("p (j c) -> p j c", j=8)
    dst = hT[:, 0:8, col0:col0 + 128]
    if on == "act":
        fw.act(lambda e: e.copy(dst, src), reads=[pkey], writes=[hT_key])
    else:
        fw.dve(lambda e: e.tensor_copy(dst, src), reads=[pkey], writes=[hT_key])


def post_norm_residual(env, ps_ap, ps_key, g_ap, g_key, xr, xr_key, tmp, tmp_key, xo, xo_key):
    fw = env.fw
    pk = list(ps_key) if isinstance(ps_key, (list, tuple)) else [ps_key]
    rs, krs = env.rstd(ps_ap, pk, D)
    fw.dve(lambda e: e.scalar_tensor_tensor(out=tmp, in0=ps_ap, scalar=rs, in1=g_ap, op0=ALU.mult, op1=ALU.mult),
           reads=pk + [krs, g_key], writes=[tmp_key])
    fw.dve(lambda e: e.tensor_tensor(out=xo, in0=tmp, in1=xr, op=ALU.add),
           reads=[tmp_key, xr_key], writes=[xo_key])


def ffn_phase(env, x_in, x_out, wg, wu, wd, pre_g, post_g, ntok, tag):
    nc, fw = env.nc, env.fw
    fw.barrier()
    with contextlib.ExitStack() as st:
        Wg = env.sb(tag + "Wg", [128, 8, DFF], BF16, st)
        Wu = env.sb(tag + "Wu", [128, 8, DFF], BF16, st)
        Wd = env.sb(tag + "Wd", [128, NFC, D], BF16, st)
        gpre = env.sb(tag + "gpre", [128, D], F32, st)
        gpost = env.sb(tag + "gpost", [128, D], F32, st)
        XT = [env.sb(tag + "xt%d" % i, [128, D], F32, st) for i in range(2)]
        XR = [env.sb(tag + "xr%d" % i, [128, D], F32, st) for i in range(2)]
        HB = [env.sb(tag + "hb%d" % i, [128, D], BF16, st) for i in range(2)]
        hT = env.sb(tag + "hT", [128, 8, 512], BF16, st)
        aT = env.sb(tag + "aT", [128, NFC, 512], BF16, st)
        SG = [env.sb(tag + "sg%d" % i, [128, 512], F32, st) for i in range(2)]
        TMP = env.sb(tag + "tmp", [128, D], F32, st)
        fw.dma("sync", gpre[:], pre_g.partition_broadcast(128), writes=[tag + "gpre"])
        fw.dma("sync", gpost[:], post_g.partition_broadcast(128), writes=[tag + "gpost"])
        grp = [(c, min(c + 256, DFF)) for c in range(0, DFF, 256)]
        kWg, kWu = {}, {}
        srcg = wg.rearrange("(k p) c -> p k c", p=128)
        srcu = wu.rearrange("(k p) c -> p k c", p=128)
        for gi, (c0, c1) in enumerate(grp):
            kWg[gi] = "%sWg_c%d" % (tag, gi)
            kWu[gi] = "%sWu_c%d" % (tag, gi)
            fw.dma("pool", Wg[:, :, c0:c1], srcg[:, :, c0:c1], writes=[kWg[gi]])
            fw.dma("pool", Wu[:, :, c0:c1], srcu[:, :, c0:c1], writes=[kWu[gi]])
        kWd = load_w_cast(fw, Wd, tag + "Wd", wd, NFC)
        ntiles = ntok // 512
        it = [0]

        def build_hT(t):
            for s in range(4):
                r0 = t * 512 + s * 128
                i = it[0] % 2
                it[0] += 1
                xt, kx = XT[i], tag + "xt%d" % i
                hb, khb = HB[i], tag + "hb%d" % i
                fw.dma("sync", xt[:], x_in[r0:r0 + 128, :], writes=[kx])
                norm_transpose(env, xt[:], kx, gpre[:], tag + "gpre", hb[:], khb, hT, tag + "hT", s * 128, 0)

        build_hT(0)
        for t in range(ntiles):
            akeys = []
            for fc in range(NFC):
                bg, bu = fc % 2, 2 + fc % 2
                fs = slice(fc * 128, (fc + 1) * 128)
                fw.pe_group([lambda e, k=k: e.matmul(env.bank(bg), Wg[:, k, fs], hT[:, k, :], start=(k == 0),
                                                     stop=(k == 7)) for k in range(8)],
                            reads=[kWg[fc // 2], tag + "hT"], writes=["ps%d" % bg])
                fw.pe_group([lambda e, k=k: e.matmul(env.bank(bu), Wu[:, k, fs], hT[:, k, :], start=(k == 0),
                                                     stop=(k == 7)) for k in range(8)],
                            reads=[kWu[fc // 2], tag + "hT"], writes=["ps%d" % bu])
                sg = SG[fc % 2]
                ksg = tag + "sg%d" % (fc % 2)
                fw.act(lambda e: e.activation(out=sg[:], in_=env.bank(bg), func=AF.Silu),
                       reads=["ps%d" % bg], writes=[ksg])
                ka = tag + "aT%d" % fc
                fw.dve(lambda e: e.tensor_tensor(out=aT[:, fc, :], in0=sg[:], in1=env.bank(bu), op=ALU.mult),
                       reads=[ksg, "ps%d" % bu], writes=[ka])
                akeys.append(ka)
            if t + 1 < ntiles:
                build_hT(t + 1)
            for s in range(4):
                r0 = t * 512 + s * 128
                xr = XR[s % 2]
                kxr = tag + "xr%d" % (s % 2)
                fw.dma("sync", xr[:], x_in[r0:r0 + 128, :], writes=[kxr])
                pb0 = 4 + 2 * (s % 2)
                pso = env.bank(pb0, 2)
                pk = ["ps%d" % pb0, "ps%d" % (pb0 + 1)]
                for hf in range(2):
                    fw.pe_group([lambda e, fc=fc: e.matmul(pso[:, hf, :], aT[:, fc, s * 128:(s + 1) * 128],
                                                           Wd[:, fc, hf * 512:(hf + 1) * 512], start=(fc == 0),
                                                           stop=(fc == NFC - 1)) for fc in range(NFC)],
                                reads=akeys + kWd, writes=pk)
                post_norm_residual(env, pso, pk, gpost[:], tag + "gpost", xr[:], kxr, TMP[:], tag + "tmp",
                                   xr[:], kxr)
                fw.dma("sync", x_out[r0:r0 + 128, :], xr[:], reads=[kxr], writes=[])


NH = 16
HD = 64
WARM = 0
DATT = 1024
SEQ = 2048
MASKNEG = -240000.0


def fox_phase(env, x_in, x_out, w_in, b_f, q_gain, k_gain, w_out, pre_g, post_g, consts, oT_d, nseq, tag="fx"):
    nc, fw = env.nc, env.fw
    fw.barrier()
    with contextlib.ExitStack() as st:
        T = lambda s: tag + s
        Win = env.sb(tag + "Win", [128, 8, 4112], BF16, st)
        Wo = env.sb(tag + "Wo", [128, 8, D], BF16, st)
        gpre = env.sb(tag + "gpre", [128, D], F32, st)
        gpost = env.sb(tag + "gpost", [128, D], F32, st)
        hT = env.sb(tag + "hT", [128, 8, SEQ], BF16, st)
        XT = [env.sb(tag + "xt%d" % i, [128, D], F32, st) for i in range(2)]
        HB = [env.sb(tag + "hb%d" % i, [128, D], BF16, st) for i in range(2)]
        TMP = env.sb(tag + "tmp", [128, D], F32, st)
        scr = env.sb(tag + "scr", [128, 4096], F32, st)
        lf = scr[0:16, 0:SEQ]
        cT = scr[0:16, SEQ:2 * SEQ]
        QA = [env.sb(tag + "qa%d" % i, [128, SEQ], BF16, st) for i in range(2)]
        KA = [env.sb(tag + "ka%d" % i, [128, SEQ], BF16, st) for i in range(2)]
        QA += [scr[:, 0:1024].bitcast(BF16), scr[:, 1024:2048].bitcast(BF16)]
        KA += [scr[:, 2048:3072].bitcast(BF16), scr[:, 3072:4096].bitcast(BF16)]
        KQ = [[T("qa0")], [T("qa1")], [T("qa2"), T("lf")], [T("qa3"), T("lf")]]
        KK = [[T("ka0")], [T("ka1")], [T("ka2"), T("cT")], [T("ka3"), T("cT")]]
        gT = env.sb(tag + "gT", [128, SEQ], BF16, st)
        V2e = env.sb(tag + "V2e", [128, 16, 128], BF16, st)
        V2o = TMP[:].bitcast(BF16).rearrange("p (a b) -> p a b", a=16)
        PT = [env.sb(tag + "pT%d" % i, [128, 512], BF16, st) for i in range(2)]
        SQ = [env.sb(tag + "sq%d" % i, [128, 512], BF16, st) for i in range(2)]
        RST = [env.sb(tag + "rst%d" % i, [128, 512], F32, st) for i in range(2)]
        bones = env.sb(tag + "bones", [128, 128], BF16, st)
        maskf = env.sb(tag + "maskf", [128, 128], F32, st)
        maskb = env.sb(tag + "maskb", [128, 128], BF16, st)
        qg = env.sb(tag + "qg", [128, 1], F32, st)
        kg = env.sb(tag + "kg", [128, 1], F32, st)
        nbf = env.sb(tag + "nbf", [16, 1], F32, st)
        epsc = env.sb(tag + "epsc", [128, 1], F32, st)
        ones16 = env.sb(tag + "ones16", [16, 512], F32, st)
        csp = env.sb(tag + "csp", [96, SEQ], BF16, st)
        cspt128 = env.sb(tag + "cspt", [128, SEQ], BF16, st)
        cspt = cspt128[0:16, :]
        negc = env.sb(tag + "negc", [128, 16, 16], F32, st)
        rl = env.sb(tag + "rl", [128, 512], F32, st)
        og = env.sb(tag + "og", [128, 512], F32, st)
        OTS = [env.sb(tag + "ots%d" % i, [128, 512], BF16, st) for i in range(2)]
        OTL = [env.sb(tag + "oTl0", [128, 8, 128], BF16, st),
               rl[:].bitcast(BF16).rearrange("p (h t) -> p h t", h=8)]

        SER = T("ser")
        fw.dma("sync", gpre[:], pre_g.partition_broadcast(128), writes=[T("gpre"), SER])
        fw.dma("sync", gpost[:], post_g.partition_broadcast(128), writes=[T("gpost"), SER])
        fw.dma("sync", maskf[:], consts, writes=[T("maskf"), SER])
        for half in range(2):
            fw.dma("sync", qg[half * 64:(half + 1) * 64, :], q_gain.rearrange("(p o) -> p o", o=1),
                   writes=[T("qg"), SER])
            fw.dma("sync", kg[half * 64:(half + 1) * 64, :], k_gain.rearrange("(p o) -> p o", o=1),
                   writes=[T("kg"), SER])
        fw.dma("sync", nbf[:], b_f.rearrange("(p o) -> p o", o=1), writes=[T("nbf"), SER])
        fw.dve(lambda e: e.tensor_copy(maskb[:], maskf[:]), reads=[T("maskf"), SER], writes=[T("maskb")])
        fw.dve(lambda e: e.tensor_scalar(out=nbf[:], in0=nbf[:], scalar1=-1.0, scalar2=None, op0=ALU.mult),
               reads=[T("nbf")], writes=[T("nbf")])
        fw.dve(lambda e: e.memset(epsc[:], EPS), writes=[T("epsc")])
        fw.dve(lambda e: e.memset(bones[:], 0.0), writes=[T("bones")])
        fw.dve(lambda e: e.memset(bones[0:64, 0:64], 1.0), writes=[T("bones")])
        fw.dve(lambda e: e.memset(bones[64:128, 64:128], 1.0), writes=[T("bones")])
        fw.dve(lambda e: e.memset(ones16[:], 1.0), writes=[T("ones16")])
        for i in range(4):
            fw.dve(lambda e: e.memset(KA[i][64:128, :], 1.0), writes=KK[i])
            fw.dve(lambda e: e.memset(QA[i][64:128, :], 0.0), writes=KQ[i])
        fw.dve(lambda e: e.memset(V2e[:, :, 64:128], 1.0), writes=[T("V2e")])
        kWin = load_w_cast(fw, Win, T("Win"), w_in, 8)
        kWo = load_w_cast(fw, Wo, T("Wo"), w_out, 8)

        cnt = [0]

        def proj_gen(j):
            sl = [2 * (j % 2), 2 * (j % 2) + 1]
            for (DST, KD, off, gain, kgain) in ((QA, KQ, 0, qg, T("qg")), (KA, KK, 1024, kg, T("kg"))):
                for t in range(4):
                    ts = slice(t * 512, (t + 1) * 512)
                    c = cnt[0]
                    cnt[0] += 1
                    pb = 2 + c % 2
                    sq, ksq = SQ[c % 2], T("sq%d" % (c % 2))
                    rst, krst = RST[c % 2], T("rst%d" % (c % 2))
                    fw.pe_group([lambda e, k=k: e.matmul(env.bank(pb), Win[:, k, off + j * 128:off + (j + 1) * 128],
                                                         hT[:, k, ts], start=(k == 0), stop=(k == 7))
                                 for k in range(8)], reads=kWin + [T("hT")], writes=["ps%d" % pb])
                    fw.act(lambda e: e.activation(out=sq[:], in_=env.bank(pb), func=AF.Square),
                           reads=["ps%d" % pb], writes=[ksq])
                    yield
                    fw.pe(lambda e: e.matmul(env.bank(6), bones[:], sq[:], start=True, stop=True),
                          reads=[T("bones"), ksq], writes=["ps6"])
                    fw.act(lambda e: e.activation(out=rst[:], in_=env.bank(6), func=AF.Ln, bias=epsc[:],
                                                  scale=1.0 / HD), reads=["ps6", T("epsc")], writes=[krst])
                    fw.act(lambda e: e.activation(out=rst[:], in_=rst[:], func=AF.Exp, scale=-0.5),
                           reads=[krst], writes=[krst])
                    for par in range(2):
                        rows = slice(par * 64, (par + 1) * 64)
                        dst = DST[sl[par]]
                        fw.dve(lambda e: e.scalar_tensor_tensor(out=dst[0:64, ts], in0=env.bank(pb)[rows, :],
                                                                scalar=gain[rows, :], in1=rst[rows, :], op0=ALU.mult,
                                                                op1=ALU.mult),
                               reads=["ps%d" % pb, kgain, krst], writes=KD[sl[par]])
                    yield
            for par in range(2):
                h = 2 * j + par
                for l_ in range(3):
                    fw.dma("sync", QA[sl[par]][64 + l_:65 + l_, :], csp[32 * l_ + h:32 * l_ + h + 1, :],
                           reads=[T("csp")], writes=KQ[sl[par]])
            yield

        def gv_emit(j):
            for t in range(4):
                ts = slice(t * 512, (t + 1) * 512)
                pb = 2 + t % 2
                fw.pe_group([lambda e, k=k: e.matmul(env.bank(pb), Win[:, k, 3072 + j * 128:3072 + (j + 1) * 128],
                                                     hT[:, k, ts], start=(k == 0), stop=(k == 7)) for k in range(8)],
                            reads=kWin + [T("hT")], writes=["ps%d" % pb])
                fw.act(lambda e: e.activation(out=gT[:, ts], in_=env.bank(pb), func=AF.Sigmoid),
                       reads=["ps%d" % pb], writes=[T("gT")])
            for g4 in range(4):
                fns = []
                for kk in range(4):
                    kt = g4 * 4 + kk
                    for k in range(8):
                        fns.append(lambda e, k=k, kk=kk, kt=kt: e.matmul(
                            env.bank(7)[:, kk * 128:(kk + 1) * 128], hT[:, k, kt * 128:(kt + 1) * 128],
                            Win[:, k, 2048 + j * 128:2048 + (j + 1) * 128], start=(k == 0), stop=(k == 7)))
                fw.pe_group(fns, reads=kWin + [T("hT")], writes=["ps7"])
                src = env.bank(7).rearrange("p (a b) -> p a b", a=4)
                fw.act(lambda e: e.copy(V2e[:, g4 * 4:(g4 + 1) * 4, 0:64], src[:, :, 0:64]),
                       reads=["ps7"], writes=[T("V2e")])
                fw.act(lambda e: e.copy(V2o[:, g4 * 4:(g4 + 1) * 4, 64:128], src[:, :, 64:128]),
                       reads=["ps7"], writes=[T("tmp")])

        def attention(sq_i, h, nxt, every, exhaust):
            par = h % 2
            slot = 2 * ((h // 2) % 2) + par
            qa, ka = QA[slot], KA[slot]
            kqa, kka = KQ[slot], KK[slot]
            V2, kV2 = (V2e, T("V2e")) if par == 0 else (V2o, T("tmp"))
            orow = slice(par * 64, (par + 1) * 64)
            lrow = slice((1 - par) * 64, (2 - par) * 64)
            items = [(qt, kt) for qt in range(4) for kt in range(4 * qt + 4)]

            def geom(i):
                qt, kt = items[i]
                j = kt - 4 * qt
                return qt, kt, j, max(0, j) * 128

            SB = (0, 1, 7)
            PT3 = [PT[0], PT[1], cspt128[:, 0:512]]
            KPT3 = [T("pT0"), T("pT1"), T("cspt")]

            def S(i):
                qt, kt, j, col0 = geom(i)
                sb_ = SB[i % 3]
                q0 = qt * 512
                fns = [lambda e: e.matmul(env.bank(sb_)[:, col0:512], ka[0:67, kt * 128:(kt + 1) * 128],
                                          qa[0:67, q0 + col0:q0 + 512], start=True, stop=(j < 0))]
                if j >= 0:
                    fns.append(lambda e: e.matmul(env.bank(sb_)[:, col0:col0 + 128], env.identb[:], maskb[:],
                                                  start=False, stop=True))
                fw.pe_group(fns, reads=kka + kqa + [T("maskb"), "identb"], writes=["ps%d" % sb_])

            S(0)
            S(1)
            for i in range(len(items)):
                qt, kt, j, col0 = geom(i)
                nkt = 4 * qt + 4
                accb = 4 + (qt % 2)
                sb_ = SB[i % 3]
                q0 = qt * 512
                if i + 2 < len(items):
                    S(i + 2)
                pT, kpT = PT3[i % 3], KPT3[i % 3]
                fw.act(lambda e: e.activation(out=pT[:, col0:512], in_=env.bank(sb_)[:, col0:512], func=AF.Exp,
                                              bias=negc[:, kt, h:h + 1], scale=0.125),
                       reads=["ps%d" % sb_, T("negc")], writes=[kpT])
                fw.pe(lambda e: e.matmul(env.bank(accb)[:, col0:512], V2[:, kt, :], pT[:, col0:512],
                                         start=(kt == 0), stop=(kt == nkt - 1)),
                      reads=[kV2, kpT], writes=["ps%d" % accb])
                if kt == nkt - 1:
                    fw.dve(lambda e: e.reciprocal(rl[lrow, :], env.bank(accb)[lrow, :]),
                           reads=["ps%d" % accb], writes=[T("rl")])
                    fw.dve(lambda e: e.tensor_tensor(out=og[orow, :], in0=env.bank(accb)[orow, :], in1=rl[lrow, :],
                                                     op=ALU.mult), reads=["ps%d" % accb, T("rl")], writes=[T("og")])
                    ots, kots = OTS[qt % 2], T("ots%d" % (qt % 2))
                    fw.dve(lambda e: e.tensor_tensor(out=ots[orow, :], in0=og[orow, :], in1=gT[orow, q0:q0 + 512],
                                                     op=ALU.mult), reads=[T("og"), T("gT")], writes=[kots])
                    fw.dma("pool", oT_d[sq_i, h, :, q0:q0 + 512], ots[orow, :], reads=[kots], writes=[T("oTd")])
                if nxt is not None and i % every == every - 1:
                    next(nxt, None)
            if nxt is not None and exhaust:
                for _ in nxt:
                    pass

        def pre_tile(sq, s, bi):
            r0 = sq * SEQ + s * 128
            xt, kx = XT[bi], T("xt%d" % bi)
            hb, khb = HB[bi], T("hb%d" % bi)
            fw.dma("sync", xt[:], x_in[r0:r0 + 128, :], writes=[kx])
            norm_transpose(env, xt[:], kx, gpre[:], T("gpre"), hb[:], khb, hT, T("hT"), s * 128, 6)

        def out_tile(sq, s, xi):
            r0 = sq * SEQ + s * 128
            ol, kol = OTL[s % 2], (T("oTl0") if s % 2 == 0 else T("rl"))
            for two in range(2):
                fw.dma("sync", ol[two * 64:(two + 1) * 64, :, :],
                       oT_d[sq, :, :, s * 128:(s + 1) * 128].rearrange("(hp two) d t -> two d hp t", two=2)[two],
                       reads=[T("oTd")], writes=[kol])
            xr, kxr = XT[xi], T("xt%d" % xi)
            fw.dma("sync", xr[:], x_in[r0:r0 + 128, :], writes=[kxr])
            pb0 = 2 if s % 2 == 0 else 4
            pso = env.bank(pb0, 2)
            pk = ["ps%d" % pb0, "ps%d" % (pb0 + 1)]
            for hf in range(2):
                fw.pe_group([lambda e, h=h: e.matmul(pso[:, hf, :], ol[:, h, :], Wo[:, h, hf * 512:(hf + 1) * 512],
                                                     start=(h == 0), stop=(h == 7)) for h in range(8)],
                            reads=[kol] + kWo, writes=pk)
            post_norm_residual(env, pso, pk, gpost[:], T("gpost"), xr[:], kxr, TMP[:], T("tmp"), xr[:], kxr)
            fw.dma("pool", x_out[r0:r0 + 128, :], xr[:], reads=[kxr], writes=[])

        for sq_i in range(nseq):
            base = sq_i * SEQ
            if sq_i == 0:
                for s in range(SEQ // 128):
                    pre_tile(0, s, s % 2)
            for t in range(4):
                ts = slice(t * 512, (t + 1) * 512)
                fw.pe_group([lambda e, k=k: e.matmul(env.bank(7)[0:16, :], Win[:, k, 4096:4112], hT[:, k, ts],
                                                     start=(k == 0), stop=(k == 7)) for k in range(8)],
                            reads=kWin + [T("hT")], writes=["ps7"])
                fw.act(lambda e: e.activation(out=lf[:, ts], in_=env.bank(7)[0:16, :], func=AF.Exp, bias=nbf[:],
                                              scale=-1.0), reads=["ps7", T("nbf")], writes=[T("lf")])
            fw.act(lambda e: e.activation(out=lf, in_=lf, func=AF.Ln, bias=1.0, scale=1.0),
                   reads=[T("lf")], writes=[T("lf")])
            for t in range(4):
                ts = slice(t * 512, (t + 1) * 512)
                init = 0.0 if t == 0 else cT[:, t * 512 - 1:t * 512]
                fw.dve(lambda e: e.tensor_tensor_scan(out=cT[:, ts], data0=ones16[:], data1=lf[:, ts], initial=init,
                                                      op0=ALU.mult, op1=ALU.add),
                       reads=[T("lf"), T("ones16"), T("cT")], writes=[T("cT")])
            for kt in range(16):
                fw.pe(lambda e: e.transpose(env.bank(7)[:, kt * 16:(kt + 1) * 16], cT[:, kt * 128:(kt + 1) * 128],
                                            env.identf[0:16, 0:16]),
                      reads=[T("cT"), "identf"], writes=["ps7"])
            fw.dve(lambda e: e.tensor_copy(negc[:].rearrange("p a b -> p (a b)"), env.bank(7)[:, 0:256]),
                   reads=["ps7"], writes=[T("negc")])
            fw.dve(lambda e: e.tensor_scalar(out=lf, in0=cT, scalar1=-8.0, scalar2=None, op0=ALU.mult),
                   reads=[T("cT"), T("lf")], writes=[T("lf")])
            for lvl in range(3):
                cl = csp[32 * lvl:32 * lvl + 16, :]
                fw.dve(lambda e: e.tensor_copy(cspt, lf), reads=[T("lf"), T("cspt")], writes=[T("cspt")])
                fw.act(lambda e: e.copy(cl, cspt), reads=[T("cspt")], writes=[T("csp")])
                if lvl < 2:
                    fw.dve(lambda e: e.tensor_tensor(out=lf, in0=lf, in1=cspt, op=ALU.subtract),
                           reads=[T("lf"), T("cspt")], writes=[T("lf")])
            fw.dve(lambda e: e.memset(V2o[:, :, 0:64], 1.0), reads=[T("tmp")], writes=[T("tmp")])

            for _ in proj_gen(0):
                pass
            for j in range(NH // 2):
                gv_emit(j)
                nxt = proj_gen(j + 1) if j + 1 < NH // 2 else None
                attention(sq_i, 2 * j, nxt, 4, False)
                attention(sq_i, 2 * j + 1, nxt, 4, True)
            for s in range(SEQ // 128):
                if sq_i + 1 < nseq:
                    out_tile(sq_i, s, 1)
                    pre_tile(sq_i + 1, s, 0)
                else:
                    out_tile(sq_i, s, s % 2)


NG = 64
GP = 32
DBG = {}


def dbgdump(env, name, ap, keys):
    if name in DBG:
        env.fw.dma("sync", DBG[name], ap, reads=keys, writes=["dbg_" + name])

TWO_PI = 6.283185307179586


def s5_phase(env, x_in, x_out, w_in, log_dt, lam_re, lam_im, b_re, b_im, c_re, c_im, d_skip, w_glu, w_out,
             pre_g, post_g, nseq, tag="s5"):
    nc, fw = env.nc, env.fw
    T = lambda s: tag + s
    NTOK = nseq * SEQ
    fw.barrier()
    with contextlib.ExitStack() as st:
        uT = env.sb(T("uT"), [128, 8, NTOK], BF16, st)
        LB = [env.sb(T("LB%d" % i), [128, GP, 128], BF16, st) for i in range(3)]
        LC = [env.sb(T("LC%d" % i), [128, GP, 128], BF16, st) for i in range(3)]
        CM = env.sb(T("CM"), [128, GP, 11], F32, st)
        SM = env.sb(T("SM"), [128, GP, 11], F32, st)
        RR = env.sb(T("RR"), [128, GP], F32, st)
        dvec = env.sb(T("dvec"), [128, 8], F32, st)
        SER = T("ser")

        with contextlib.ExitStack() as s0:
            P = {}
            for nm in ("lre", "lim", "ldt", "dt", "lr", "th", "mag", "f", "s4", "c2", "s2", "sn", "cs", "are", "aim",
                       "den", "nre", "zre", "zim", "t0", "t1"):
                P[nm] = env.sb(T("p_" + nm), [128, GP], F32, s0)
            fi = env.sb(T("p_fi"), [128, GP], I32, s0)
            braw = [env.sb(T("braw%d" % i), [128, GP, 16], F32, s0) for i in range(2)]
            bb = [env.sb(T("bb%d" % i), [128, GP, 16], F32, s0) for i in range(2)]
            Bpad = [env.sb(T("Bpad%d" % i), [128, GP, 128], F32, s0) for i in range(2)]
            Cc = [env.sb(T("Cc%d" % i), [128, 8, 128], F32, s0) for i in range(2)]
            MQ = env.sb(T("MQ"), [128, 4, 128], F32, s0)
            LG = [env.sb(T("LG%d" % i), [64, 128], F32, s0) for i in range(2)]
            LD = env.sb(T("LD"), [128, 64], F32, s0)
            DV = env.sb(T("DV"), [8, 128], F32, s0)
            mhg = env.sb(T("mhg"), [128, GP], F32, s0)
            for i, lsrc in enumerate((lam_re, lam_im)):
                for dup in range(2):
                    fw.dma("sync", LG[i][:, dup * 64:(dup + 1) * 64], lsrc, writes=[T("LG%d" % i), SER])
            fw.dma("sync", LD[:], log_dt.partition_broadcast(128), writes=[T("LD"), SER])
            fw.dma("sync", DV[:], d_skip.rearrange("(blk c) -> blk c", c=128), writes=[T("DV"), SER])
            for gi in range(2):
                rows = slice(gi * 64, (gi + 1) * 64)
                for i, bsrc in enumerate((b_re, b_im)):
                    for g0 in range(0, GP, 8):
                        fw.dma("sync", braw[i][rows, g0:g0 + 8, :],
                               bsrc.rearrange("(gp gi) p c -> gi p gp c", gi=2)[gi][:, g0:g0 + 8, :],
                               writes=[T("braw%d" % i), SER])
                for i, csrc in enumerate((c_re, c_im)):
                    for b0 in range(0, 8, 4):
                        fw.dma("sync", Cc[i][:, b0:b0 + 4, rows],
                               csrc.rearrange("(blk gl) c p -> (gl c) blk p", gl=8)[:, b0:b0 + 4, :],
                               writes=[T("Cc%d" % i), SER])
            fw.dve(lambda e: e.memset(mhg[:], -0.5), reads=[SER], writes=[T("mhg")])
            for i, nm in enumerate(("lre", "lim")):
                fw.pe(lambda e: e.transpose(env.bank(7)[:, 0:64], LG[i][:, :], env.identf[0:64, 0:64]),
                      reads=[T("LG%d" % i), "identf", SER], writes=["ps7"])
                fw.dve(lambda e: e.tensor_copy(P[nm][0:64, :], env.bank(7)[0:64, 0:64:2]), reads=["ps7"],
                       writes=[T(nm)])
                fw.dve(lambda e: e.tensor_copy(P[nm][64:128, :], env.bank(7)[64:128, 1:64:2]), reads=["ps7"],
                       writes=[T(nm)])
            fw.dve(lambda e: e.tensor_copy(P["ldt"][0:64, :], LD[0:64, 0:64:2]), reads=[T("LD"), SER], writes=[T("ldt")])
            fw.dve(lambda e: e.tensor_copy(P["ldt"][64:128, :], LD[64:128, 1:64:2]), reads=[T("LD"), SER],
                   writes=[T("ldt")])
            fw.pe(lambda e: e.transpose(env.bank(7)[:, 64:72], DV[:, :], env.identf[0:8, 0:8]),
                  reads=[T("DV"), "identf", SER], writes=["ps7"])
            fw.dve(lambda e: e.tensor_copy(dvec[:], env.bank(7)[:, 64:72]), reads=["ps7"], writes=[T("dvec")])

            def ew(eng, out, a, b, op, keys):
                getattr(fw, eng)(lambda e: e.tensor_tensor(out=out, in0=a, in1=b, op=op), reads=keys, writes=keys)

            K = [T("setup")]
            dep = [T("lre"), T("lim"), T("ldt"), SER] + K
            fw.act(lambda e: e.activation(out=P["dt"][:], in_=P["ldt"][:], func=AF.Exp), reads=dep, writes=K)
            ew("dve", P["lr"][:], P["lre"][:], P["dt"][:], ALU.mult, dep)
            ew("dve", P["th"][:], P["lim"][:], P["dt"][:], ALU.mult, dep)
            fw.act(lambda e: e.activation(out=P["mag"][:], in_=P["lr"][:], func=AF.Exp), reads=K, writes=K)
            fw.dve(lambda e: e.tensor_scalar(out=P["t0"][:], in0=P["th"][:], scalar1=1.0 / TWO_PI, scalar2=None,
                                             op0=ALU.mult), reads=K, writes=K)
            fw.dve(lambda e: e.tensor_copy(fi[:], P["t0"][:]), reads=K, writes=K)
            fw.dve(lambda e: e.tensor_copy(P["t1"][:], fi[:]), reads=K, writes=K)
            ew("dve", P["f"][:], P["t0"][:], P["t1"][:], ALU.subtract, K)
            fw.act(lambda e: e.activation(out=P["s4"][:], in_=P["f"][:], func=AF.Sin, scale=TWO_PI / 4), reads=K, writes=K)
            fw.act(lambda e: e.activation(out=P["s2"][:], in_=P["f"][:], func=AF.Sin, scale=TWO_PI / 2), reads=K, writes=K)
            ew("dve", P["t0"][:], P["s4"][:], P["s4"][:], ALU.mult, K)
            fw.dve(lambda e: e.tensor_scalar(out=P["c2"][:], in0=P["t0"][:], scalar1=-2.0, scalar2=1.0, op0=ALU.mult,
                                             op1=ALU.add), reads=K, writes=K)
            ew("dve", P["t0"][:], P["s2"][:], P["c2"][:], ALU.mult, K)
            fw.dve(lambda e: e.tensor_scalar(out=P["sn"][:], in0=P["t0"][:], scalar1=2.0, scalar2=None, op0=ALU.mult),
                   reads=K, writes=K)
            ew("dve", P["t0"][:], P["s2"][:], P["s2"][:], ALU.mult, K)
            fw.dve(lambda e: e.tensor_scalar(out=P["cs"][:], in0=P["t0"][:], scalar1=-2.0, scalar2=1.0, op0=ALU.mult,
                                             op1=ALU.add), reads=K, writes=K)
            ew("dve", P["t0"][:], P["cs"][:], P["cs"][:], ALU.mult, K)
            ew("dve", P["t1"][:], P["sn"][:], P["sn"][:], ALU.mult, K)
            ew("dve", P["t0"][:], P["t0"][:], P["t1"][:], ALU.add, K)
            fw.dve(lambda e: e.tensor_scalar(out=P["t0"][:], in0=P["t0"][:], scalar1=-0.5, scalar2=1.5, op0=ALU.mult,
                                             op1=ALU.add), reads=K, writes=K)
            ew("dve", P["cs"][:], P["cs"][:], P["t0"][:], ALU.mult, K)
            ew("dve", P["sn"][:], P["sn"][:], P["t0"][:], ALU.mult, K)
            ew("dve", P["are"][:], P["mag"][:], P["cs"][:], ALU.mult, K)
            ew("dve", P["aim"][:], P["mag"][:], P["sn"][:], ALU.mult, K)
            fw.dve(lambda e: e.tensor_copy(RR[:], P["mag"][:]), reads=K, writes=K + [T("RR")])
            fw.dve(lambda e: e.tensor_copy(CM[:, :, 0], P["cs"][:]), reads=K, writes=K)
            fw.dve(lambda e: e.tensor_copy(SM[:, :, 0], P["sn"][:]), reads=K, writes=K)
            for k in range(1, 11):
                ew("dve", P["t0"][:], SM[:, :, k - 1], CM[:, :, k - 1], ALU.mult, K)
                fw.dve(lambda e: e.tensor_scalar(out=SM[:, :, k], in0=P["t0"][:], scalar1=2.0, scalar2=None,
                                                 op0=ALU.mult), reads=K, writes=K)
                ew("dve", P["t0"][:], SM[:, :, k - 1], SM[:, :, k - 1], ALU.mult, K)
                fw.dve(lambda e: e.tensor_scalar(out=CM[:, :, k], in0=P["t0"][:], scalar1=-2.0, scalar2=1.0,
                                                 op0=ALU.mult, op1=ALU.add), reads=K, writes=K)
            ew("dve", P["den"][:], P["lre"][:], P["lre"][:], ALU.mult, K)
            ew("dve", P["t0"][:], P["lim"][:], P["lim"][:], ALU.mult, K)
            ew("dve", P["den"][:], P["den"][:], P["t0"][:], ALU.add, K)
            fw.dve(lambda e: e.reciprocal(P["den"][:], P["den"][:]), reads=K, writes=K)
            fw.dve(lambda e: e.tensor_scalar(out=P["nre"][:], in0=P["are"][:], scalar1=-1.0, scalar2=None,
                                             op0=ALU.add), reads=K, writes=K)
            ew("dve", P["t0"][:], P["nre"][:], P["lre"][:], ALU.mult, K)
            ew("dve", P["t1"][:], P["aim"][:], P["lim"][:], ALU.mult, K)
            ew("dve", P["t0"][:], P["t0"][:], P["t1"][:], ALU.add, K)
            ew("dve", P["zre"][:], P["t0"][:], P["den"][:], ALU.mult, K)
            ew("dve", P["t0"][:], P["aim"][:], P["lre"][:], ALU.mult, K)
            ew("dve", P["t1"][:], P["nre"][:], P["lim"][:], ALU.mult, K)
            ew("dve", P["t0"][:], P["t0"][:], P["t1"][:], ALU.subtract, K)
            ew("dve", P["zim"][:], P["t0"][:], P["den"][:], ALU.mult, K)
            KB = K + [T("braw0"), T("braw1")]
            for c in range(16):
                ew("dve", P["t0"][:], P["zre"][:], braw[0][:, :, c], ALU.mult, KB)
                ew("dve", P["t1"][:], P["zim"][:], braw[1][:, :, c], ALU.mult, KB)
                ew("dve", bb[0][:, :, c], P["t0"][:], P["t1"][:], ALU.subtract, KB)
                ew("dve", P["t0"][:], P["zre"][:], braw[1][:, :, c], ALU.mult, KB)
                ew("dve", P["t1"][:], P["zim"][:], braw[0][:, :, c], ALU.mult, KB)
                ew("dve", bb[1][:, :, c], P["t0"][:], P["t1"][:], ALU.add, KB)
            for i in range(3):
                if i == 2:
                    fw.dve(lambda e: e.tensor_tensor(out=Bpad[0][:], in0=Bpad[0][:], in1=Bpad[1][:], op=ALU.add),
                           reads=KB + [T("LB")], writes=KB)
                    for gp in range(GP):
                        fw.pe(lambda e: e.transpose(env.bank(gp % 4)[:, 0:128], Bpad[0][:, gp, :], env.identf[:]),
                              reads=KB + ["identf"], writes=["ps%d" % (gp % 4)])
                        fw.act(lambda e: e.copy(LB[2][:, gp, :], env.bank(gp % 4)[:, 0:128]),
                               reads=["ps%d" % (gp % 4)], writes=[T("LB")])
                    break
                fw.dve(lambda e: e.memset(Bpad[i][:], 0.0), reads=KB, writes=KB)
                for gi in range(2):
                    rows = slice(gi * 64, (gi + 1) * 64)
                    for q in range(4):
                        c0 = (2 * q + gi) * 16
                        fw.dve(lambda e: e.tensor_copy(Bpad[i][rows, q::4, c0:c0 + 16], bb[i][rows, q::4, :]),
                               reads=KB, writes=KB)
                for gp in range(GP):
                    fw.pe(lambda e: e.transpose(env.bank(gp % 4)[:, 0:128], Bpad[i][:, gp, :], env.identf[:]),
                          reads=KB + ["identf"], writes=["ps%d" % (gp % 4)])
                    fw.act(lambda e: e.copy(LB[i][:, gp, :], env.bank(gp % 4)[:, 0:128]),
                           reads=["ps%d" % (gp % 4)], writes=[T("LB")])
            dbgdump(env, "are", P["are"][:], K)
            dbgdump(env, "zre", P["zre"][:], K)
            dbgdump(env, "cm", CM[:].rearrange("p a b -> p (a b)"), K)
            dbgdump(env, "bb0", bb[0][:].rearrange("p a b -> p (a b)"), KB)
            dbgdump(env, "braw0", braw[0][:].rearrange("p a b -> p (a b)"), KB)
            dbgdump(env, "zim", P["zim"][:], KB)
            fw.dve(lambda e: e.memset(MQ[:], 0.0), writes=[T("MQ")])
            for q in range(4):
                fw.dve(lambda e: e.memset(MQ[0:64, q, (2 * q) * 16:(2 * q) * 16 + 16], 1.0), writes=[T("MQ")])
                fw.dve(lambda e: e.memset(MQ[64:128, q, (2 * q + 1) * 16:(2 * q + 1) * 16 + 16], 1.0),
                       writes=[T("MQ")])
            fw.dve(lambda e: e.tensor_scalar(out=Cc[1][:], in0=Cc[1][:], scalar1=-1.0, scalar2=None, op0=ALU.mult),
                   reads=[T("Cc1")], writes=[T("Cc1")])
            for i in range(2):
                for blk in range(8):
                    pb = 4 + blk % 2
                    fw.pe(lambda e: e.transpose(env.bank(pb)[:, 0:128], Cc[i][:, blk, :], env.identf[:]),
                          reads=[T("Cc%d" % i), "identf"], writes=["ps%d" % pb])
                    for q in range(4):
                        fw.dve(lambda e: e.tensor_tensor(out=LC[i][:, blk * 4 + q, :], in0=env.bank(pb)[:, 0:128],
                                                         in1=MQ[:, q, :], op=ALU.mult),
                               reads=["ps%d" % pb, T("MQ")], writes=[T("LC")])
                        if i == 0:
                            fw.dve(lambda e: e.scalar_tensor_tensor(out=LC[2][:, blk * 4 + q, :],
                                                                    in0=env.bank(pb)[:, 0:128], scalar=-1.0,
                                                                    in1=MQ[:, q, :], op0=ALU.mult, op1=ALU.mult),
                                   reads=["ps%d" % pb, T("MQ")], writes=[T("LC")])

        dbgdump(env, "LB0", LB[0][:, 0, :], [T("LB")])
        dbgdump(env, "LC0", LC[0][:, 0, :], [T("LC")])
        dbgdump(env, "LC1", LC[1][:, 5, :], [T("LC")])
        fw.barrier()
        with contextlib.ExitStack() as s1:
            Wi = env.sb(T("Wi"), [128, 8, D], BF16, s1)
            gpre = env.sb(T("gpre"), [128, D], F32, s1)
            fw.dma("sync", gpre[:], pre_g.partition_broadcast(128), writes=[T("gpre")])
            HT = [env.sb(T("hT%d" % i), [128, 8, 512], BF16, s1) for i in range(2)]
            XT = [env.sb(T("xt%d" % i), [128, D], F32, s1) for i in range(2)]
            HB = [env.sb(T("hb%d" % i), [128, D], BF16, s1) for i in range(2)]
            kWi = load_w_cast(fw, Wi, T("Wi"), w_in, 8)
            it = 0
            for t in range(NTOK // 512):
                hT, khT = HT[t % 2], T("hT%d" % (t % 2))
                for s in range(4):
                    r0 = t * 512 + s * 128
                    xt, kx = XT[it % 2], T("xt%d" % (it % 2))
                    hb, khb = HB[it % 2], T("hb%d" % (it % 2))
                    fw.dma("sync", xt[:], x_in[r0:r0 + 128, :], writes=[kx])
                    norm_transpose(env, xt[:], kx, gpre[:], T("gpre"), hb[:], khb, hT, khT, s * 128, 6)
                    it += 1
                for blk in range(8):
                    pb = blk % 4
                    fw.pe_group([lambda e, k=k: e.matmul(env.bank(pb), Wi[:, k, blk * 128:(blk + 1) * 128],
                                                         hT[:, k, :], start=(k == 0), stop=(k == 7))
                                 for k in range(8)], reads=kWi + [khT], writes=["ps%d" % pb])
                    fw.act(lambda e: e.copy(uT[:, blk, t * 512:(t + 1) * 512], env.bank(pb)),
                           reads=["ps%d" % pb], writes=[T("uT%d_%d" % (blk, t // 4))])

        fw.barrier()
        with contextlib.ExitStack() as s2:
            TAB = [tuple(env.sb(T("%s%d" % (nm, q)), [128, 512], F32, s2) for nm in ("COS", "SIN", "TM", "TP"))
                   for q in range(4)]
            TT = env.sb(T("TT"), [128, 256], F32, s2)
            RT = [env.sb(T("Rt%d" % q), [128, 512], F32, s2) for q in range(4)]
            ones = env.sb(T("ones"), [128, 512], F32, s2)
            WS = []
            for i in range(2):
                d = {}
                for nm in ("bs", "bre", "bim", "w_re", "w_im"):
                    d[nm] = env.sb(T("w%d_%s" % (i, nm)), [128, 512], F32, s2)
                for nm in ("p1", "p2", "p3", "p4"):
                    d[nm] = env.sb(T("w%d_%s" % (i, nm)), [128, 512], BF16, s2)
                WS.append(d)
            INI = env.sb(T("ini"), [128, 8], F32, s2)
            YV = env.sb(T("yv"), [128, 512], F32, s2)
            G1 = env.sb(T("g1"), [128, 512], F32, s2)
            G2 = env.sb(T("g2"), [128, 512], F32, s2)
            NS9 = env.sb(T("ns9"), [128, GP], F32, s2)
            fw.dve(lambda e: e.memset(ones[:], 1.0), writes=[T("ones")])
            fw.dve(lambda e: e.tensor_scalar(out=NS9[:], in0=SM[:, :, 9], scalar1=-1.0, scalar2=None, op0=ALU.mult),
                   reads=[T("setup")], writes=[T("ns9")])

            def build_table(q, gp):
                COS, SIN, TM, TP = TAB[q]
                KT = [T("tab%d" % q)]
                fw.dve(lambda e: e.memset(COS[:, 0:1], 1.0), reads=KT, writes=KT)
                fw.dve(lambda e: e.memset(SIN[:, 0:1], 0.0), reads=KT, writes=KT)
                for k in range(9):
                    m = 1 << k
                    cm, sm = CM[:, gp, k:k + 1], SM[:, gp, k:k + 1]
                    fw.dve(lambda e: e.tensor_scalar(out=TT[:, 0:m], in0=SIN[:, 0:m], scalar1=sm, scalar2=None,
                                                     op0=ALU.mult), reads=KT + [T("setup"), T("TT")], writes=[T("TT")])
                    fw.dve(lambda e: e.scalar_tensor_tensor(out=COS[:, m:2 * m], in0=COS[:, 0:m], scalar=cm,
                                                            in1=TT[:, 0:m], op0=ALU.mult, op1=ALU.subtract),
                           reads=KT + [T("TT")], writes=KT)
                    fw.dve(lambda e: e.tensor_scalar(out=TT[:, 0:m], in0=COS[:, 0:m], scalar1=sm, scalar2=None,
                                                     op0=ALU.mult), reads=KT + [T("TT")], writes=[T("TT")])
                    fw.dve(lambda e: e.scalar_tensor_tensor(out=SIN[:, m:2 * m], in0=SIN[:, 0:m], scalar=cm,
                                                            in1=TT[:, 0:m], op0=ALU.mult, op1=ALU.add),
                           reads=KT + [T("TT")], writes=KT)
                fw.dve(lambda e: e.tensor_tensor(out=TM[:], in0=COS[:], in1=SIN[:], op=ALU.subtract),
                       reads=KT, writes=KT)
                fw.dve(lambda e: e.tensor_tensor(out=TP[:], in0=COS[:], in1=SIN[:], op=ALU.add),
                       reads=KT, writes=KT)
                fw.dve(lambda e: e.tensor_scalar(out=RT[q][:], in0=ones[:], scalar1=RR[:, gp:gp + 1], scalar2=None,
                                                 op0=ALU.mult), reads=[T("ones"), T("RR"), T("Rt%d" % q)],
                       writes=[T("Rt%d" % q)])

            def stage_a(i, blk, sq_i, q, t):
                gp = blk * 4 + q
                W, kw = WS[i % 2], T("ws%d" % (i % 2))
                COS, SIN, TM, TP = TAB[q]
                KT = [T("tab%d" % q)]
                cols = slice(sq_i * SEQ + t * 512, sq_i * SEQ + (t + 1) * 512)
                ku = T("uT%d_%d" % (blk, sq_i))
                for (bnk, li, dst) in ((0, 2, "bs"), (1, 0, "bre"), (2, 1, "bim")):
                    fw.pe(lambda e: e.matmul(env.bank(bnk), LB[li][:, gp, :], uT[:, blk, cols], start=True,
                                             stop=True), reads=[T("LB"), ku], writes=["ps%d" % bnk])
                    fw.act(lambda e: e.copy(W[dst][:], env.bank(bnk)), reads=["ps%d" % bnk], writes=[kw + dst])
                fw.dve(lambda e: e.tensor_tensor(out=W["bs"][:], in0=W["bs"][:], in1=COS[:], op=ALU.mult),
                       reads=[kw + "bs"] + KT, writes=[kw + "bs"])
                fw.dve(lambda e: e.tensor_tensor(out=W["bim"][:], in0=W["bim"][:], in1=TM[:], op=ALU.mult),
                       reads=[kw + "bim"] + KT, writes=[kw + "bim"])
                fw.dve(lambda e: e.tensor_tensor(out=W["bre"][:], in0=W["bre"][:], in1=TP[:], op=ALU.mult),
                       reads=[kw + "bre"] + KT, writes=[kw + "bre"])
                fw.dve(lambda e: e.tensor_tensor(out=W["bim"][:], in0=W["bs"][:], in1=W["bim"][:], op=ALU.subtract),
                       reads=[kw + "bs", kw + "bim"], writes=[kw + "bim"])
                fw.dve(lambda e: e.tensor_tensor(out=W["bre"][:], in0=W["bs"][:], in1=W["bre"][:], op=ALU.subtract),
                       reads=[kw + "bs", kw + "bre"], writes=[kw + "bre"])

            def stage_b(i, blk, sq_i, q, t):
                gp = blk * 4 + q
                W, kw = WS[i % 2], T("ws%d" % (i % 2))
                Wp, kwp = WS[(i + 1) % 2], T("ws%d" % ((i + 1) % 2))
                COS, SIN, TM, TP = TAB[q]
                KT = [T("tab%d" % q)]
                if t == 0:
                    ire, iim = 0.0, 0.0
                    kini = []
                else:
                    c9, s9, ns9 = CM[:, gp, 9:10], SM[:, gp, 9:10], NS9[:, gp:gp + 1]
                    lre, lim_ = Wp["w_re"][:, 511:512], Wp["w_im"][:, 511:512]
                    o = 4 * (i % 2)
                    kini = [T("ini%d" % (i % 2))]
                    fw.act(lambda e: e.activation(out=INI[:, o:o + 1], in_=lim_, func=AF.Identity, scale=ns9),
                           reads=[kwp + "w_im", T("ns9")] + kini, writes=kini)
                    fw.act(lambda e: e.activation(out=INI[:, o + 1:o + 2], in_=lre, func=AF.Identity, scale=c9,
                                                  bias=INI[:, o:o + 1]), reads=[kwp + "w_re", T("setup")] + kini,
                           writes=kini)
                    fw.act(lambda e: e.activation(out=INI[:, o + 2:o + 3], in_=lre, func=AF.Identity, scale=s9),
                           reads=[kwp + "w_re"] + kini, writes=kini)
                    fw.act(lambda e: e.activation(out=INI[:, o + 3:o + 4], in_=lim_, func=AF.Identity, scale=c9,
                                                  bias=INI[:, o + 2:o + 3]), reads=[kwp + "w_im"] + kini, writes=kini)
                    ire, iim = INI[:, o + 1:o + 2], INI[:, o + 3:o + 4]
                fw.dve(lambda e: e.tensor_tensor_scan(out=W["w_re"][:], data0=RT[q][:], data1=W["bim"][:],
                                                      initial=ire, op0=ALU.mult, op1=ALU.add),
                       reads=[kw + "bim", T("Rt%d" % q)] + kini, writes=[kw + "w_re"])
                fw.dve(lambda e: e.tensor_tensor_scan(out=W["w_im"][:], data0=RT[q][:], data1=W["bre"][:],
                                                      initial=iim, op0=ALU.mult, op1=ALU.add),
                       reads=[kw + "bre", T("Rt%d" % q)] + kini, writes=[kw + "w_im"])
                yb = 4 + t
                plan = (("p1", "w_re", COS, "dve", 0), ("p2", "w_im", SIN, "dve", 2), ("p3", "w_re", SIN, "dve", 1),
                        ("p4", "w_im", COS, "dve", 1))
                for n_, (o_, a_, tab, eng, lc) in enumerate(plan):
                    getattr(fw, eng)(lambda e: e.tensor_tensor(out=W[o_][:], in0=W[a_][:], in1=tab[:], op=ALU.mult),
                                     reads=[kw + a_] + KT, writes=[kw + o_])
                for n_, (o_, a_, tab, eng, lc) in enumerate(plan):
                    fw.pe(lambda e: e.matmul(env.bank(yb), LC[lc][:, gp, :], W[o_][:], start=(q == 0 and n_ == 0),
                                             stop=(q == 3 and n_ == 3)),
                          reads=[T("LC"), kw + o_], writes=["ps%d" % yb])

            def epilogue(blk, sq_i):
                ku = T("uT%d_%d" % (blk, sq_i))
                for t in range(4):
                    cols = slice(sq_i * SEQ + t * 512, sq_i * SEQ + (t + 1) * 512)
                    yb = 4 + t
                    KY = [T("yv")]
                    fw.dve(lambda e: e.scalar_tensor_tensor(out=YV[:], in0=uT[:, blk, cols],
                                                            scalar=dvec[:, blk:blk + 1], in1=env.bank(yb),
                                                            op0=ALU.mult, op1=ALU.add),
                           reads=[ku, T("dvec"), "ps%d" % yb] + KY, writes=KY)
                    fw.act(lambda e: e.activation(out=G1[:], in_=YV[:], func=AF.Square), reads=KY, writes=KY)
                    fw.act(lambda e: e.activation(out=G1[:], in_=G1[:], func=AF.Identity, scale=0.044715, bias=1.0),
                           reads=KY, writes=KY)
                    fw.dve(lambda e: e.tensor_tensor(out=G1[:], in0=G1[:], in1=YV[:], op=ALU.mult),
                           reads=KY, writes=KY)
                    fw.act(lambda e: e.activation(out=G2[:], in_=G1[:], func=AF.Tanh, scale=0.7978845608028654),
                           reads=KY, writes=KY)
                    fw.act(lambda e: e.activation(out=G2[:], in_=G2[:], func=AF.Identity, scale=0.5, bias=0.5),
                           reads=KY, writes=KY)
                    fw.dve(lambda e: e.tensor_tensor(out=uT[:, blk, cols], in0=G2[:], in1=YV[:], op=ALU.mult),
                           reads=KY + [ku], writes=KY + [ku])

            gi_ = 0
            for blk in range(8):
                for q in range(4):
                    build_table(q, blk * 4 + q)
                items = [(sq_i, q, t) for sq_i in range(nseq) for q in range(4) for t in range(4)]
                stage_a(gi_, blk, *items[0])
                for n, it_ in enumerate(items):
                    if n + 1 < len(items):
                        stage_a(gi_ + 1, blk, *items[n + 1])
                    stage_b(gi_, blk, *it_)
                    gi_ += 1
                    if it_[1] == 3 and it_[2] == 3:
                        epilogue(blk, it_[0])

        dbgdump(env, "yg0", uT[:, 0, 0:512], [T("uT0")])
        fw.barrier()
        with contextlib.ExitStack() as s3:
            Wg = env.sb(T("Wglu"), [128, 8, D], BF16, s3)
            gpost = env.sb(T("gpost"), [128, D], F32, s3)
            fw.dma("sync", gpost[:], post_g.partition_broadcast(128), writes=[T("gpost")])
            Wo = env.sb(T("Wo"), [128, 8, D], BF16, s3)
            ZT = [env.sb(T("zT%d" % i), [128, 8, 512], BF16, s3) for i in range(2)]
            SG = [env.sb(T("sg%d" % i), [128, 512], F32, s3) for i in range(4)]
            XR = [env.sb(T("xr%d" % i), [128, D], F32, s3) for i in range(2)]
            TMP = env.sb(T("tmp"), [128, D], F32, s3)
            kWglu = load_w_cast(fw, Wg, T("Wglu"), w_glu, 8)
            kWo5 = load_w_cast(fw, Wo, T("Wo"), w_out, 8)
            ukeys = [T("uT%d_%d" % (b, q_)) for b in range(8) for q_ in range(nseq)]
            for t in range(NTOK // 512):
                cols = slice(t * 512, (t + 1) * 512)
                zT, kzT = ZT[t % 2], T("zT%d" % (t % 2))
                for blk in range(8):
                    pb = (0, 1, 6, 7)[blk % 4]
                    fw.pe_group([lambda e, k=k: e.matmul(env.bank(pb), Wg[:, k, blk * 128:(blk + 1) * 128],
                                                         uT[:, k, cols], start=(k == 0), stop=(k == 7))
                                 for k in range(8)], reads=kWglu + ukeys, writes=["ps%d" % pb])
                    sg, ksg = SG[blk % 4], T("sg%d" % (blk % 4))
                    fw.act(lambda e: e.activation(out=sg[:], in_=env.bank(pb), func=AF.Sigmoid),
                           reads=["ps%d" % pb], writes=[ksg])
                    fw.dve(lambda e: e.tensor_tensor(out=zT[:, blk, :], in0=sg[:], in1=uT[:, blk, cols], op=ALU.mult),
                           reads=[ksg] + ukeys, writes=[kzT])
                for s in range(4):
                    r0 = t * 512 + s * 128
                    xr, kxr = XR[s % 2], T("xr%d" % (s % 2))
                    fw.dma("sync", xr[:], x_in[r0:r0 + 128, :], writes=[kxr])
                    pb0 = 2 + 2 * (s % 2)
                    pso = env.bank(pb0, 2)
                    pk = ["ps%d" % pb0, "ps%d" % (pb0 + 1)]
                    for hf in range(2):
                        fw.pe_group([lambda e, k=k: e.matmul(pso[:, hf, :], zT[:, k, s * 128:(s + 1) * 128],
                                                             Wo[:, k, hf * 512:(hf + 1) * 512], start=(k == 0),
                                                             stop=(k == 7)) for k in range(8)],
                                    reads=[kzT] + kWo5, writes=pk)
                    post_norm_residual(env, pso, pk, gpost[:], T("gpost"), xr[:], kxr, TMP[:], T("tmp"),
                                       xr[:], kxr)
                    fw.dma("pool", x_out[r0:r0 + 128, :], xr[:], reads=[kxr], writes=[])


NSEQ_CORE = 2
NCORES = 8
_CACHE = {}


def build_program():
    nc = bass.Bass("TRN2", target_bir_lowering=False)
    ntok = NSEQ_CORE * SEQ

    def din(name, shape):
        return nc.dram_tensor(name, shape, F32, kind="ExternalInput").ap()

    x = din("x", [ntok, D])
    fox_w_in = din("fox_w_in", [D, 4112])
    fox_b_f = din("fox_b_f", [NH])
    fox_q_gain = din("fox_q_gain", [HD])
    fox_k_gain = din("fox_k_gain", [HD])
    fox_w_out = din("fox_w_out", [DATT, D])
    s5_w_in = din("s5_w_in", [D, D])
    s5_log_dt = din("s5_log_dt", [NG])
    s5_lam_re = din("s5_lam_re", [NG, 64])
    s5_lam_im = din("s5_lam_im", [NG, 64])
    s5_b_re = din("s5_b_re", [NG, 64, 16])
    s5_b_im = din("s5_b_im", [NG, 64, 16])
    s5_c_re = din("s5_c_re", [NG, 16, 64])
    s5_c_im = din("s5_c_im", [NG, 16, 64])
    s5_d = din("s5_d", [D])
    s5_w_glu = din("s5_w_glu", [D, D])
    s5_w_out = din("s5_w_out", [D, D])
    mix_pre = din("mix_pre_gain", [2, D])
    mix_post = din("mix_post_gain", [2, D])
    ffn_pre = din("ffn_pre_gain", [2, D])
    ffn_post = din("ffn_post_gain", [2, D])
    ffn_wg = din("ffn_w_gate", [2, D, DFF])
    ffn_wu = din("ffn_w_up", [2, D, DFF])
    ffn_wd = din("ffn_w_down", [2, DFF, D])
    ident = din("c_ident", [128, 128])
    cmask = din("c_mask", [128, 128])
    out = nc.dram_tensor("out", [ntok, D], F32, kind="ExternalOutput").ap()
    xa = nc.dram_tensor("xa", [ntok, D], F32).ap()
    xb = nc.dram_tensor("xb", [ntok, D], F32).ap()
    xc = nc.dram_tensor("xc", [ntok, D], F32).ap()
    oT_d = nc.dram_tensor("oT_d", [NSEQ_CORE, NH, HD, SEQ], BF16).ap()

    fw = FW(nc)
    with contextlib.ExitStack() as st:
        env = Env(nc, fw, st)
        env.init_consts(ident)
        fox_phase(env, x, xa, fox_w_in, fox_b_f, fox_q_gain, fox_k_gain, fox_w_out, mix_pre[0, :], mix_post[0, :],
                  cmask, oT_d, NSEQ_CORE)
        ffn_phase(env, xa, xb, ffn_wg[0], ffn_wu[0], ffn_wd[0], ffn_pre[0, :], ffn_post[0, :], ntok, "f0")
        s5_phase(env, xb, xc, s5_w_in, s5_log_dt, s5_lam_re, s5_lam_im, s5_b_re, s5_b_im, s5_c_re, s5_c_im, s5_d,
                 s5_w_glu, s5_w_out, mix_pre[1, :], mix_post[1, :], NSEQ_CORE)
        ffn_phase(env, xc, out, ffn_wg[1], ffn_wu[1], ffn_wd[1], ffn_pre[1, :], ffn_post[1, :], ntok, "f1")
        fw.finish()
    return nc


def kernel(**inputs):
    f = np.float32
    x = np.ascontiguousarray(inputs["x"], dtype=f)
    B = x.shape[0]
    shared = {}
    for k in ("fox_w_in", "fox_b_f", "fox_q_gain", "fox_k_gain", "fox_w_out", "s5_w_in", "s5_log_dt", "s5_lam_re",
              "s5_lam_im", "s5_b_re", "s5_b_im", "s5_c_re", "s5_c_im", "s5_d", "s5_w_glu", "s5_w_out"):
        shared[k] = np.ascontiguousarray(np.asarray(inputs[k], dtype=f)[0])
    for k in ("mix_pre_gain", "mix_post_gain", "ffn_pre_gain", "ffn_post_gain", "ffn_w_gate", "ffn_w_up",
              "ffn_w_down"):
        shared[k] = np.ascontiguousarray(np.asarray(inputs[k], dtype=f))
    shared["c_ident"] = np.eye(128, dtype=f)
    shared["c_mask"] = np.where(np.arange(128)[None, :] < np.arange(128)[:, None], MASKNEG, 0.0).astype(f)
    if "nc" not in _CACHE:
        _CACHE["nc"] = build_program()
    nc = _CACHE["nc"]
    in_maps = []
    for c in range(NCORES):
        m = dict(shared)
        m["x"] = np.ascontiguousarray(x[c * NSEQ_CORE:(c + 1) * NSEQ_CORE].reshape(NSEQ_CORE * SEQ, D))
        in_maps.append(m)
    res = run_bass_kernel_spmd(nc, in_maps, core_ids=list(range(NCORES)))
    outs = [np.asarray(r["out"]).reshape(NSEQ_CORE, SEQ, D) for r in res.results]
    return np.concatenate(outs, axis=0).astype(f)
```

```python
import contextlib
import numpy as np
import concourse.bass as bass
import concourse.mybir as mybir
from concourse.bass_utils import run_bass_kernel_spmd

F32 = mybir.dt.float32
BF16 = mybir.dt.bfloat16
I32 = mybir.dt.int32
AF = mybir.ActivationFunctionType
ALU = mybir.AluOpType
AX = mybir.AxisListType


STRICT_SAME_ENGINE = True


class FW:
    def __init__(self, nc, n_dma_sems=6):
        self.nc = nc
        self.stack = contextlib.ExitStack()
        self.E = {}
        self.sems = []
        for name, h in (("pe", nc.tensor), ("act", nc.scalar), ("dve", nc.vector),
                        ("pool", nc.gpsimd), ("sync", nc.sync)):
            e = {"name": name, "h": h, "count": 0, "waited": {}, "dma": [], "dma_i": 0}
            if name != "sync":
                e["sem"] = self._newsem("c_" + name)
            for i in range(n_dma_sems):
                e["dma"].append([self._newsem("d_%s%d" % (name, i)), 0])
            self.E[name] = e
        self.lastw = {}
        self.readers = {}
        self.nwaits = 0

    def _newsem(self, name):
        s = self.stack.enter_context(self.nc.semaphore(name))
        self.sems.append(s)
        return len(self.sems) - 1

    def _wait(self, E, sid, val):
        if E["waited"].get(sid, 0) >= val:
            return
        E["h"].wait_ge(self.sems[sid], val)
        E["waited"][sid] = val
        self.nwaits += 1

    def _sync(self, E, reads, writes, attach=False):
        need = {}

        def add(ev):
            if ev is None:
                return
            sid, val = ev
            if need.get(sid, 0) < val:
                need[sid] = val

        for r in reads:
            add(self.lastw.get(r))
        for w in writes:
            add(self.lastw.get(w))
            for sid, val in self.readers.get(w, {}).items():
                add((sid, val))
        own = E.get("sem")
        todo = []
        for sid, val in need.items():
            if sid == own:
                if E["name"] == "pe":
                    continue
                if STRICT_SAME_ENGINE is False and E["name"] in ("dve", "act") and val < E["count"]:
                    continue
            if E["waited"].get(sid, 0) >= val:
                continue
            todo.append((sid, val))
        held = None
        if attach and todo:
            held = todo.pop()
        for sid, val in todo:
            self._wait(E, sid, val)
        return held

    def _attach(self, E, ins, held):
        if held is not None:
            sid, val = held
            ins._wait_ge(self.sems[sid], val)
            E["waited"][sid] = val
            self.nwaits += 1

    def _record(self, ev, reads, writes):
        sid, val = ev
        for r in reads:
            d = self.readers.setdefault(r, {})
            if d.get(sid, 0) < val:
                d[sid] = val
        for w in writes:
            self.lastw[w] = ev
            self.readers[w] = {}

    def op(self, en, fn, reads=(), writes=()):
        E = self.E[en]
        held = self._sync(E, reads, writes, attach=(en != "pe"))
        ins = fn(E["h"])
        self._attach(E, ins, held)
        E["count"] += 1
        ins.then_inc(self.sems[E["sem"]], 1)
        self._record((E["sem"], E["count"]), reads, writes)
        return ins

    def pe(self, fn, reads=(), writes=()):
        return self.op("pe", fn, reads, writes)

    def act(self, fn, reads=(), writes=()):
        return self.op("act", fn, reads, writes)

    def dve(self, fn, reads=(), writes=()):
        return self.op("dve", fn, reads, writes)

    def pool(self, fn, reads=(), writes=()):
        return self.op("pool", fn, reads, writes)

    def pe_group(self, fns, reads=(), writes=()):
        E = self.E["pe"]
        held = self._sync(E, reads, writes, attach=False)
        ins = None
        for n, fn in enumerate(fns):
            ins = fn(E["h"])
            if n == 0:
                self._attach(E, ins, held)
        E["count"] += 1
        ins.then_inc(self.sems[E["sem"]], 1)
        self._record((E["sem"], E["count"]), reads, writes)

    def dma(self, q, out, in_, reads=(), writes=(), **kw):
        E = self.E[q]
        self._sync(E, reads, writes)
        slot = E["dma"][E["dma_i"] % len(E["dma"])]
        E["dma_i"] += 1
        sid, target = slot
        if target > 0:
            self._wait(E, sid, target)
        E["h"].dma_start(out=out, in_=in_, **kw).then_inc(self.sems[sid], 16)
        slot[1] = target + 16
        self._record((sid, slot[1]), reads, writes)

    def barrier(self):
        evs = []
        for e in self.E.values():
            for sid, target in e["dma"]:
                if target > 0:
                    evs.append((sid, target))
            if "sem" in e and e["count"] > 0:
                evs.append((e["sem"], e["count"]))
        for E in self.E.values():
            for sid, val in evs:
                self._wait(E, sid, val)

    def finish(self):
        S = self.E["sync"]
        for e in self.E.values():
            for sid, target in e["dma"]:
                if target > 0:
                    self._wait(S, sid, target)
            if "sem" in e and e["count"] > 0:
                self._wait(S, e["sem"], e["count"])
        self.stack.close()


EPS = 1e-6
D = 1024
DFF = 2816
NFC = DFF // 128


class Env:
    def __init__(self, nc, fw, st):
        self.nc, self.fw, self.st = nc, fw, st
        self.ps = st.enter_context(nc.psum_tensor("psall", [128, 8, 512], F32))
        self.identb = self.sb("identb", [128, 128], BF16)
        self.identf = self.sb("identf", [128, 128], F32)
        self.mhalf = self.sb("mhalf", [128, 1], F32)
        self.stats = self.sb("stats", [128, 96], F32)
        self.junk = self.sb("junk", [128, 1024], BF16)
        self.si = 0

    def sb(self, name, shape, dt, st=None):
        return (st or self.st).enter_context(self.nc.sbuf_tensor(name, shape, dt))

    def bank(self, i, n=1):
        if n == 1:
            return self.ps[:, i, :]
        return self.ps[:, i:i + n, :]

    def stat(self):
        i = self.si % 96
        self.si += 1
        return self.stats[:, i:i + 1], "st%d" % i

    def init_consts(self, ident_dram):
        fw = self.fw
        fw.dma("sync", self.identf[:], ident_dram, writes=["identf"])
        fw.dve(lambda e: e.tensor_copy(self.identb[:], self.identf[:]), reads=["identf"], writes=["identb"])
        fw.dve(lambda e: e.memset(self.mhalf[:], -0.5), writes=["mhalf"])

    def rstd(self, src_ap, src_key, n):
        fw = self.fw
        P = src_ap.shape[0]
        ss, kss = self.stat()
        var, kvar = self.stat()
        rs, krs = self.stat()
        junk = self.junk[0:P, 0:n]
        sk = list(src_key) if isinstance(src_key, (list, tuple)) else [src_key]
        fw.act(lambda e: e.activation(out=junk, in_=src_ap, func=AF.Square, accum_out=ss[0:P, :]),
               reads=sk, writes=[kss, "junk"])
        fw.dve(lambda e: e.tensor_scalar(out=var[0:P, :], in0=ss[0:P, :], scalar1=1.0 / n, scalar2=EPS,
                                         op0=ALU.mult, op1=ALU.add), reads=[kss], writes=[kvar])
        fw.pool(lambda e: e.tensor_tensor(out=rs[0:P, :], in0=var[0:P, :], in1=self.mhalf[0:P, :], op=ALU.pow),
                reads=[kvar, "mhalf"], writes=[krs])
        return rs, krs


def load_w_cast(fw, dst_tile, dst_key, w_dram, kchunks, first=False):
    keys = []
    for k in range(kchunks):
        kk = "%s_k%d" % (dst_key, k)
        fw.dma("pool", dst_tile[:, k, :], w_dram[k * 128:(k + 1) * 128, :], writes=[kk])
        keys.append(kk)
    return keys


def load_w_cast_cols(fw, dst_tile, dst_key, w_dram, kchunks, col_groups):
    src = w_dram.rearrange("(k p) c -> p k c", p=128)
    keys = {}
    for gi, (c0, c1) in enumerate(col_groups):
        kk = "%s_c%d" % (dst_key, gi)
        fw.dma("pool", dst_tile[:, 0:kchunks, c0:c1], src[:, :, c0:c1], writes=[kk])
        keys[gi] = kk
    return keys


def norm_transpose(env, x_ap, x_key, g_ap, g_key, hb, hb_key, hT, hT_key, col0, psT_bank):
    fw = env.fw
    rs, krs = env.rstd(x_ap, x_key, D)
    fw.dve(lambda e: e.scalar_tensor_tensor(out=hb, in0=x_ap, scalar=rs, in1=g_ap, op0=ALU.mult, op1=ALU.mult),
           reads=[x_key, krs, g_key], writes=[hb_key])
    transpose_in(env, hb, hb_key, hT, hT_key, col0, psT_bank)


def transpose_in(env, hb, hb_key, hT, hT_key, col0, psT_bank, on="act"):
    fw = env.fw
    pkey = "ps%d" % psT_bank
    psT = env.bank(psT_bank).bitcast(BF16)
    fns = []
    for j in range(8):
        fns.append(lambda e, j=j: e.transpose(psT[:, j * 128:(j + 1) * 128], hb[:, j * 128:(j + 1) * 128],
                                               env.identb[:]))
    fw.pe_group(fns, reads=[hb_key, "identb"], writes=[pkey])
    src = psT.rearrange("p (j c) -> p j c", j=8)
    dst = hT[:, 0:8, col0:col0 + 128]
    if on == "act":
        fw.act(lambda e: e.copy(dst, src), reads=[pkey], writes=[hT_key])
    else:
        fw.dve(lambda e: e.tensor_copy(dst, src), reads=[pkey], writes=[hT_key])


def post_norm_residual(env, ps_ap, ps_key, g_ap, g_key, xr, xr_key, tmp, tmp_key, xo, xo_key):
    fw = env.fw
    pk = list(ps_key) if isinstance(ps_key, (list, tuple)) else [ps_key]
    rs, krs = env.rstd(ps_ap, pk, D)
    fw.dve(lambda e: e.scalar_tensor_tensor(out=tmp, in0=ps_ap, scalar=rs, in1=g_ap, op0=ALU.mult, op1=ALU.mult),
           reads=pk + [krs, g_key], writes=[tmp_key])
    fw.dve(lambda e: e.tensor_tensor(out=xo, in0=tmp, in1=xr, op=ALU.add),
           reads=[tmp_key, xr_key], writes=[xo_key])


def ffn_phase(env, x_in, x_out, wg, wu, wd, pre_g, post_g, ntok, tag):
    nc, fw = env.nc, env.fw
    fw.barrier()
    with contextlib.ExitStack() as st:
        Wg = env.sb(tag + "Wg", [128, 8, DFF], BF16, st)
        Wu = env.sb(tag + "Wu", [128, 8, DFF], BF16, st)
        Wd = env.sb(tag + "Wd", [128, NFC, D], BF16, st)
        gpre = env.sb(tag + "gpre", [128, D], F32, st)
        gpost = env.sb(tag + "gpost", [128, D], F32, st)
        XT = [env.sb(tag + "xt%d" % i, [128, D], F32, st) for i in range(2)]
        XR = [env.sb(tag + "xr%d" % i, [128, D], F32, st) for i in range(2)]
        HB = [env.sb(tag + "hb%d" % i, [128, D], BF16, st) for i in range(2)]
        hT = env.sb(tag + "hT", [128, 8, 512], BF16, st)
        aT = env.sb(tag + "aT", [128, NFC, 512], BF16, st)
        SG = [env.sb(tag + "sg%d" % i, [128, 512], F32, st) for i in range(2)]
        TMP = env.sb(tag + "tmp", [128, D], F32, st)
        fw.dma("sync", gpre[:], pre_g.partition_broadcast(128), writes=[tag + "gpre"])
        fw.dma("sync", gpost[:], post_g.partition_broadcast(128), writes=[tag + "gpost"])
        grp = [(c, min(c + 256, DFF)) for c in range(0, DFF, 256)]
        kWg, kWu = {}, {}
        srcg = wg.rearrange("(k p) c -> p k c", p=128)
        srcu = wu.rearrange("(k p) c -> p k c", p=128)
        for gi, (c0, c1) in enumerate(grp):
            kWg[gi] = "%sWg_c%d" % (tag, gi)
            kWu[gi] = "%sWu_c%d" % (tag, gi)
            fw.dma("pool", Wg[:, :, c0:c1], srcg[:, :, c0:c1], writes=[kWg[gi]])
            fw.dma("pool", Wu[:, :, c0:c1], srcu[:, :, c0:c1], writes=[kWu[gi]])
        kWd = load_w_cast(fw, Wd, tag + "Wd", wd, NFC)
        ntiles = ntok // 512
        it = [0]

        def build_hT(t):
            for s in range(4):
                r0 = t * 512 + s * 128
                i = it[0] % 2
                it[0] += 1
                xt, kx = XT[i], tag + "xt%d" % i
                hb, khb = HB[i], tag + "hb%d" % i
                fw.dma("sync", xt[:], x_in[r0:r0 + 128, :], writes=[kx])
                norm_transpose(env, xt[:], kx, gpre[:], tag + "gpre", hb[:], khb, hT, tag + "hT", s * 128, 0)

        build_hT(0)
        for t in range(ntiles):
            akeys = []
            for fc in range(NFC):
                bg, bu = fc % 2, 2 + fc % 2
                fs = slice(fc * 128, (fc + 1) * 128)
                fw.pe_group([lambda e, k=k: e.matmul(env.bank(bg), Wg[:, k, fs], hT[:, k, :], start=(k == 0),
                                                     stop=(k == 7)) for k in range(8)],
                            reads=[kWg[fc // 2], tag + "hT"], writes=["ps%d" % bg])
                fw.pe_group([lambda e, k=k: e.matmul(env.bank(bu), Wu[:, k, fs], hT[:, k, :], start=(k == 0),
                                                     stop=(k == 7)) for k in range(8)],
                            reads=[kWu[fc // 2], tag + "hT"], writes=["ps%d" % bu])
                sg = SG[fc % 2]
                ksg = tag + "sg%d" % (fc % 2)
                fw.act(lambda e: e.activation(out=sg[:], in_=env.bank(bg), func=AF.Silu),
                       reads=["ps%d" % bg], writes=[ksg])
                ka = tag + "aT%d" % fc
                fw.dve(lambda e: e.tensor_tensor(out=aT[:, fc, :], in0=sg[:], in1=env.bank(bu), op=ALU.mult),
                       reads=[ksg, "ps%d" % bu], writes=[ka])
                akeys.append(ka)
            if t + 1 < ntiles:
                build_hT(t + 1)
            for s in range(4):
                r0 = t * 512 + s * 128
                xr = XR[s % 2]
                kxr = tag + "xr%d" % (s % 2)
                fw.dma("sync", xr[:], x_in[r0:r0 + 128, :], writes=[kxr])
                pb0 = 4 + 2 * (s % 2)
                pso = env.bank(pb0, 2)
                pk = ["ps%d" % pb0, "ps%d" % (pb0 + 1)]
                for hf in range(2):
                    fw.pe_group([lambda e, fc=fc: e.matmul(pso[:, hf, :], aT[:, fc, s * 128:(s + 1) * 128],
                                                           Wd[:, fc, hf * 512:(hf + 1) * 512], start=(fc == 0),
                                                           stop=(fc == NFC - 1)) for fc in range(NFC)],
                                reads=akeys + kWd, writes=pk)
                post_norm_residual(env, pso, pk, gpost[:], tag + "gpost", xr[:], kxr, TMP[:], tag + "tmp",
                                   xr[:], kxr)
                fw.dma("sync", x_out[r0:r0 + 128, :], xr[:], reads=[kxr], writes=[])


NH = 16
HD = 64
WARM = 0
DATT = 1024
SEQ = 2048
MASKNEG = -240000.0


def fox_phase(env, x_in, x_out, w_in, b_f, q_gain, k_gain, w_out, pre_g, post_g, consts, oT_d, nseq, tag="fx"):
    nc, fw = env.nc, env.fw
    fw.barrier()
    with contextlib.ExitStack() as st:
        T = lambda s: tag + s
        Win = env.sb(tag + "Win", [128, 8, 4112], BF16, st)
        Wo = env.sb(tag + "Wo", [128, 8, D], BF16, st)
        gpre = env.sb(tag + "gpre", [128, D], F32, st)
        gpost = env.sb(tag + "gpost", [128, D], F32, st)
        hT = env.sb(tag + "hT", [128, 8, SEQ], BF16, st)
        XT = [env.sb(tag + "xt%d" % i, [128, D], F32, st) for i in range(2)]
        HB = [env.sb(tag + "hb%d" % i, [128, D], BF16, st) for i in range(2)]
        TMP = env.sb(tag + "tmp", [128, D], F32, st)
        scr = env.sb(tag + "scr", [128, 4096], F32, st)
        lf = scr[0:16, 0:SEQ]
        cT = scr[0:16, SEQ:2 * SEQ]
        QA = [env.sb(tag + "qa%d" % i, [128, SEQ], BF16, st) for i in range(2)]
        KA = [env.sb(tag + "ka%d" % i, [128, SEQ], BF16, st) for i in range(2)]
        QA += [scr[:, 0:1024].bitcast(BF16), scr[:, 1024:2048].bitcast(BF16)]
        KA += [scr[:, 2048:3072].bitcast(BF16), scr[:, 3072:4096].bitcast(BF16)]
        KQ = [[T("qa0")], [T("qa1")], [T("qa2"), T("lf")], [T("qa3"), T("lf")]]
        KK = [[T("ka0")], [T("ka1")], [T("ka2"), T("cT")], [T("ka3"), T("cT")]]
        gT = env.sb(tag + "gT", [128, SEQ], BF16, st)
        V2e = env.sb(tag + "V2e", [128, 16, 128], BF16, st)
        V2o = TMP[:].bitcast(BF16).rearrange("p (a b) -> p a b", a=16)
        PT = [env.sb(tag + "pT%d" % i, [128, 512], BF16, st) for i in range(2)]
        SQ = [env.sb(tag + "sq%d" % i, [128, 512], BF16, st) for i in range(2)]
        RST = [env.sb(tag + "rst%d" % i, [128, 512], F32, st) for i in range(2)]
        bones = env.sb(tag + "bones", [128, 128], BF16, st)
        maskf = env.sb(tag + "maskf", [128, 128], F32, st)
        maskb = env.sb(tag + "maskb", [128, 128], BF16, st)
        qg = env.sb(tag + "qg", [128, 1], F32, st)
        kg = env.sb(tag + "kg", [128, 1], F32, st)
        nbf = env.sb(tag + "nbf", [16, 1], F32, st)
        epsc = env.sb(tag + "epsc", [128, 1], F32, st)
        ones16 = env.sb(tag + "ones16", [16, 512], F32, st)
        csp = env.sb(tag + "csp", [96, SEQ], BF16, st)
        cspt128 = env.sb(tag + "cspt", [128, SEQ], BF16, st)
        cspt = cspt128[0:16, :]
        negc = env.sb(tag + "negc", [128, 16, 16], F32, st)
        rl = env.sb(tag + "rl", [128, 512], F32, st)
        og = env.sb(tag + "og", [128, 512], F32, st)
        OTS = [env.sb(tag + "ots%d" % i, [128, 512], BF16, st) for i in range(2)]
        OTL = [env.sb(tag + "oTl0", [128, 8, 128], BF16, st),
               rl[:].bitcast(BF16).rearrange("p (h t) -> p h t", h=8)]

        SER = T("ser")
        fw.dma("sync", gpre[:], pre_g.partition_broadcast(128), writes=[T("gpre"), SER])
        fw.dma("sync", gpost[:], post_g.partition_broadcast(128), writes=[T("gpost"), SER])
        fw.dma("sync", maskf[:], consts, writes=[T("maskf"), SER])
        for half in range(2):
            fw.dma("sync", qg[half * 64:(half + 1) * 64, :], q_gain.rearrange("(p o) -> p o", o=1),
                   writes=[T("qg"), SER])
            fw.dma("sync", kg[half * 64:(half + 1) * 64, :], k_gain.rearrange("(p o) -> p o", o=1),
                   writes=[T("kg"), SER])
        fw.dma("sync", nbf[:], b_f.rearrange("(p o) -> p o", o=1), writes=[T("nbf"), SER])
        fw.dve(lambda e: e.tensor_copy(maskb[:], maskf[:]), reads=[T("maskf"), SER], writes=[T("maskb")])
        fw.dve(lambda e: e.tensor_scalar(out=nbf[:], in0=nbf[:], scalar1=-1.0, scalar2=None, op0=ALU.mult),
               reads=[T("nbf")], writes=[T("nbf")])
        fw.dve(lambda e: e.memset(epsc[:], EPS), writes=[T("epsc")])
        fw.dve(lambda e: e.memset(bones[:], 0.0), writes=[T("bones")])
        fw.dve(lambda e: e.memset(bones[0:64, 0:64], 1.0), writes=[T("bones")])
        fw.dve(lambda e: e.memset(bones[64:128, 64:128], 1.0), writes=[T("bones")])
        fw.dve(lambda e: e.memset(ones16[:], 1.0), writes=[T("ones16")])
        for i in range(4):
            fw.dve(lambda e: e.memset(KA[i][64:128, :], 1.0), writes=KK[i])
            fw.dve(lambda e: e.memset(QA[i][64:128, :], 0.0), writes=KQ[i])
        fw.dve(lambda e: e.memset(V2e[:, :, 64:128], 1.0), writes=[T("V2e")])
        kWin = load_w_cast(fw, Win, T("Win"), w_in, 8)
        kWo = load_w_cast(fw, Wo, T("Wo"), w_out, 8)

        cnt = [0]

        def proj_gen(j):
            sl = [2 * (j % 2), 2 * (j % 2) + 1]
            for (DST, KD, off, gain, kgain) in ((QA, KQ, 0, qg, T("qg")), (KA, KK, 1024, kg, T("kg"))):
                for t in range(4):
                    ts = slice(t * 512, (t + 1) * 512)
                    c = cnt[0]
                    cnt[0] += 1
                    pb = 2 + c % 2
                    sq, ksq = SQ[c % 2], T("sq%d" % (c % 2))
                    rst, krst = RST[c % 2], T("rst%d" % (c % 2))
                    fw.pe_group([lambda e, k=k: e.matmul(env.bank(pb), Win[:, k, off + j * 128:off + (j + 1) * 128],
                                                         hT[:, k, ts], start=(k == 0), stop=(k == 7))
                                 for k in range(8)], reads=kWin + [T("hT")], writes=["ps%d" % pb])
                    fw.act(lambda e: e.activation(out=sq[:], in_=env.bank(pb), func=AF.Square),
                           reads=["ps%d" % pb], writes=[ksq])
                    yield
                    fw.pe(lambda e: e.matmul(env.bank(6), bones[:], sq[:], start=True, stop=True),
                          reads=[T("bones"), ksq], writes=["ps6"])
                    fw.act(lambda e: e.activation(out=rst[:], in_=env.bank(6), func=AF.Ln, bias=epsc[:],
                                                  scale=1.0 / HD), reads=["ps6", T("epsc")], writes=[krst])
                    fw.act(lambda e: e.activation(out=rst[:], in_=rst[:], func=AF.Exp, scale=-0.5),
                           reads=[krst], writes=[krst])
                    for par in range(2):
                        rows = slice(par * 64, (par + 1) * 64)
                        dst = DST[sl[par]]
                        fw.dve(lambda e: e.scalar_tensor_tensor(out=dst[0:64, ts], in0=env.bank(pb)[rows, :],
                                                                scalar=gain[rows, :], in1=rst[rows, :], op0=ALU.mult,
                                                                op1=ALU.mult),
                               reads=["ps%d" % pb, kgain, krst], writes=KD[sl[par]])
                    yield
            for par in range(2):
                h = 2 * j + par
                for l_ in range(3):
                    fw.dma("sync", QA[sl[par]][64 + l_:65 + l_, :], csp[32 * l_ + h:32 * l_ + h + 1, :],
                           reads=[T("csp")], writes=KQ[sl[par]])
            yield

        def gv_emit(j):
            for t in range(4):
                ts = slice(t * 512, (t + 1) * 512)
                pb = 2 + t % 2
                fw.pe_group([lambda e, k=k: e.matmul(env.bank(pb), Win[:, k, 3072 + j * 128:3072 + (j + 1) * 128],
                                                     hT[:, k, ts], start=(k == 0), stop=(k == 7)) for k in range(8)],
                            reads=kWin + [T("hT")], writes=["ps%d" % pb])
                fw.act(lambda e: e.activation(out=gT[:, ts], in_=env.bank(pb), func=AF.Sigmoid),
                       reads=["ps%d" % pb], writes=[T("gT")])
            for g4 in range(4):
                fns = []
                for kk in range(4):
                    kt = g4 * 4 + kk
                    for k in range(8):
                        fns.append(lambda e, k=k, kk=kk, kt=kt: e.matmul(
                            env.bank(7)[:, kk * 128:(kk + 1) * 128], hT[:, k, kt * 128:(kt + 1) * 128],
                            Win[:, k, 2048 + j * 128:2048 + (j + 1) * 128], start=(k == 0), stop=(k == 7)))
                fw.pe_group(fns, reads=kWin + [T("hT")], writes=["ps7"])
                src = env.bank(7).rearrange("p (a b) -> p a b", a=4)
                fw.act(lambda e: e.copy(V2e[:, g4 * 4:(g4 + 1) * 4, 0:64], src[:, :, 0:64]),
                       reads=["ps7"], writes=[T("V2e")])
                fw.act(lambda e: e.copy(V2o[:, g4 * 4:(g4 + 1) * 4, 64:128], src[:, :, 64:128]),
                       reads=["ps7"], writes=[T("tmp")])

        def attention(sq_i, h, nxt, every, exhaust):
            par = h % 2
            slot = 2 * ((h // 2) % 2) + par
            qa, ka = QA[slot], KA[slot]
            kqa, kka = KQ[slot], KK[slot]
            V2, kV2 = (V2e, T("V2e")) if par == 0 else (V2o, T("tmp"))
            orow = slice(par * 64, (par + 1) * 64)
            lrow = slice((1 - par) * 64, (2 - par) * 64)
            items = [(qt, kt) for qt in range(4) for kt in range(4 * qt + 4)]

            def geom(i):
                qt, kt = items[i]
                j = kt - 4 * qt
                return qt, kt, j, max(0, j) * 128

            SB = (0, 1, 7, 2, 3)
            PT3 = [PT[0], PT[1], cspt128[:, 0:512], cspt128[:, 512:1024], cspt128[:, 1024:1536]]
            KPT3 = [T("pT0"), T("pT1"), T("cspt"), T("cspt2"), T("cspt3")]

            def S(i):
                qt, kt, j, col0 = geom(i)
                sb_ = SB[i % 5]
                q0 = qt * 512
                fns = [lambda e: e.matmul(env.bank(sb_)[:, col0:512], ka[0:67, kt * 128:(kt + 1) * 128],
                                          qa[0:67, q0 + col0:q0 + 512], start=True, stop=(j < 0))]
                if j >= 0:
                    fns.append(lambda e: e.matmul(env.bank(sb_)[:, col0:col0 + 128], env.identb[:], maskb[:],
                                                  start=False, stop=True))
                fw.pe_group(fns, reads=kka + kqa + [T("maskb"), "identb"], writes=["ps%d" % sb_])

            for i0 in range(4):
                S(i0)
            for i in range(len(items)):
                qt, kt, j, col0 = geom(i)
                nkt = 4 * qt + 4
                accb = 4 + (qt % 2)
                sb_ = SB[i % 5]
                q0 = qt * 512
                if i + 4 < len(items):
                    S(i + 4)
                pT, kpT = PT3[i % 5], KPT3[i % 5]
                fw.act(lambda e: e.activation(out=pT[:, col0:512], in_=env.bank(sb_)[:, col0:512], func=AF.Exp,
                                              bias=negc[:, kt, h:h + 1], scale=0.125),
                       reads=["ps%d" % sb_, T("negc")], writes=[kpT])
                fw.pe(lambda e: e.matmul(env.bank(accb)[:, col0:512], V2[:, kt, :], pT[:, col0:512],
                                         start=(kt == 0), stop=(kt == nkt - 1)),
                      reads=[kV2, kpT], writes=["ps%d" % accb])
                if kt == nkt - 1:
                    fw.dve(lambda e: e.reciprocal(rl[lrow, :], env.bank(accb)[lrow, :]),
                           reads=["ps%d" % accb], writes=[T("rl")])
                    fw.dve(lambda e: e.tensor_tensor(out=og[orow, :], in0=env.bank(accb)[orow, :], in1=rl[lrow, :],
                                                     op=ALU.mult), reads=["ps%d" % accb, T("rl")], writes=[T("og")])
                    ots, kots = OTS[qt % 2], T("ots%d" % (qt % 2))
                    fw.dve(lambda e: e.tensor_tensor(out=ots[orow, :], in0=og[orow, :], in1=gT[orow, q0:q0 + 512],
                                                     op=ALU.mult), reads=[T("og"), T("gT")], writes=[kots])
                    fw.dma("pool", oT_d[sq_i, h, :, q0:q0 + 512], ots[orow, :], reads=[kots], writes=[T("oTd")])
                if nxt is not None and i % every == every - 1:
                    next(nxt, None)
            if nxt is not None and exhaust:
                for _ in nxt:
                    pass

        def pre_tile(sq, s, bi):
            r0 = sq * SEQ + s * 128
            xt, kx = XT[bi], T("xt%d" % bi)
            hb, khb = HB[bi], T("hb%d" % bi)
            fw.dma("sync", xt[:], x_in[r0:r0 + 128, :], writes=[kx])
            norm_transpose(env, xt[:], kx, gpre[:], T("gpre"), hb[:], khb, hT, T("hT"), s * 128, 6)

        def out_tile(sq, s, xi):
            r0 = sq * SEQ + s * 128
            ol, kol = OTL[s % 2], (T("oTl0") if s % 2 == 0 else T("rl"))
            for two in range(2):
                fw.dma("sync", ol[two * 64:(two + 1) * 64, :, :],
                       oT_d[sq, :, :, s * 128:(s + 1) * 128].rearrange("(hp two) d t -> two d hp t", two=2)[two],
                       reads=[T("oTd")], writes=[kol])
            xr, kxr = XT[xi], T("xt%d" % xi)
            fw.dma("sync", xr[:], x_in[r0:r0 + 128, :], writes=[kxr])
            pb0 = 2 if s % 2 == 0 else 4
            pso = env.bank(pb0, 2)
            pk = ["ps%d" % pb0, "ps%d" % (pb0 + 1)]
            for hf in range(2):
                fw.pe_group([lambda e, h=h: e.matmul(pso[:, hf, :], ol[:, h, :], Wo[:, h, hf * 512:(hf + 1) * 512],
                                                     start=(h == 0), stop=(h == 7)) for h in range(8)],
                            reads=[kol] + kWo, writes=pk)
            post_norm_residual(env, pso, pk, gpost[:], T("gpost"), xr[:], kxr, TMP[:], T("tmp"), xr[:], kxr)
            fw.dma("pool", x_out[r0:r0 + 128, :], xr[:], reads=[kxr], writes=[])

        for sq_i in range(nseq):
            base = sq_i * SEQ
            if sq_i == 0:
                for s in range(SEQ // 128):
                    pre_tile(0, s, s % 2)
            for t in range(4):
                ts = slice(t * 512, (t + 1) * 512)
                fw.pe_group([lambda e, k=k: e.matmul(env.bank(7)[0:16, :], Win[:, k, 4096:4112], hT[:, k, ts],
                                                     start=(k == 0), stop=(k == 7)) for k in range(8)],
                            reads=kWin + [T("hT")], writes=["ps7"])
                fw.act(lambda e: e.activation(out=lf[:, ts], in_=env.bank(7)[0:16, :], func=AF.Exp, bias=nbf[:],
                                              scale=-1.0), reads=["ps7", T("nbf")], writes=[T("lf")])
            fw.act(lambda e: e.activation(out=lf, in_=lf, func=AF.Ln, bias=1.0, scale=1.0),
                   reads=[T("lf")], writes=[T("lf")])
            for t in range(4):
                ts = slice(t * 512, (t + 1) * 512)
                init = 0.0 if t == 0 else cT[:, t * 512 - 1:t * 512]
                fw.dve(lambda e: e.tensor_tensor_scan(out=cT[:, ts], data0=ones16[:], data1=lf[:, ts], initial=init,
                                                      op0=ALU.mult, op1=ALU.add),
                       reads=[T("lf"), T("ones16"), T("cT")], writes=[T("cT")])
            for kt in range(16):
                fw.pe(lambda e: e.transpose(env.bank(7)[:, kt * 16:(kt + 1) * 16], cT[:, kt * 128:(kt + 1) * 128],
                                            env.identf[0:16, 0:16]),
                      reads=[T("cT"), "identf"], writes=["ps7"])
            fw.dve(lambda e: e.tensor_copy(negc[:].rearrange("p a b -> p (a b)"), env.bank(7)[:, 0:256]),
                   reads=["ps7"], writes=[T("negc")])
            fw.dve(lambda e: e.tensor_scalar(out=lf, in0=cT, scalar1=-8.0, scalar2=None, op0=ALU.mult),
                   reads=[T("cT"), T("lf")], writes=[T("lf")])
            for lvl in range(3):
                cl = csp[32 * lvl:32 * lvl + 16, :]
                fw.dve(lambda e: e.tensor_copy(cspt, lf), reads=[T("lf"), T("cspt"), T("cspt2"), T("cspt3")],
                       writes=[T("cspt"), T("cspt2"), T("cspt3")])
                fw.act(lambda e: e.copy(cl, cspt), reads=[T("cspt")], writes=[T("csp")])
                if lvl < 2:
                    fw.dve(lambda e: e.tensor_tensor(out=lf, in0=lf, in1=cspt, op=ALU.subtract),
                           reads=[T("lf"), T("cspt")], writes=[T("lf")])
            fw.dve(lambda e: e.memset(V2o[:, :, 0:64], 1.0), reads=[T("tmp")], writes=[T("tmp")])

            for _ in proj_gen(0):
                pass
            for j in range(NH // 2):
                gv_emit(j)
                nxt = proj_gen(j + 1) if j + 1 < NH // 2 else None
                attention(sq_i, 2 * j, nxt, 1000, False)
                attention(sq_i, 2 * j + 1, nxt, 1000, True)
            for s in range(SEQ // 128):
                if sq_i + 1 < nseq:
                    out_tile(sq_i, s, 1)
                    pre_tile(sq_i + 1, s, 0)
                else:
                    out_tile(sq_i, s, s % 2)


NG = 64
GP = 32
DBG = {}


def dbgdump(env, name, ap, keys):
    if name in DBG:
        env.fw.dma("sync", DBG[name], ap, reads=keys, writes=["dbg_" + name])

TWO_PI = 6.283185307179586


def s5_phase(env, x_in, x_out, w_in, log_dt, lam_re, lam_im, b_re, b_im, c_re, c_im, d_skip, w_glu, w_out,
             pre_g, post_g, nseq, tag="s5"):
    nc, fw = env.nc, env.fw
    T = lambda s: tag + s
    NTOK = nseq * SEQ
    fw.barrier()
    with contextlib.ExitStack() as st:
        uT = env.sb(T("uT"), [128, 8, NTOK], BF16, st)
        LB = [env.sb(T("LB%d" % i), [128, GP, 128], BF16, st) for i in range(3)]
        LC = [env.sb(T("LC%d" % i), [128, GP, 128], BF16, st) for i in range(3)]
        CM = env.sb(T("CM"), [128, GP, 11], F32, st)
        SM = env.sb(T("SM"), [128, GP, 11], F32, st)
        RR = env.sb(T("RR"), [128, GP], F32, st)
        dvec = env.sb(T("dvec"), [128, 8], F32, st)
        SER = T("ser")

        with contextlib.ExitStack() as s0:
            P = {}
            for nm in ("lre", "lim", "ldt", "dt", "lr", "th", "mag", "f", "s4", "c2", "s2", "sn", "cs", "are", "aim",
                       "den", "nre", "zre", "zim", "t0", "t1"):
                P[nm] = env.sb(T("p_" + nm), [128, GP], F32, s0)
            fi = env.sb(T("p_fi"), [128, GP], I32, s0)
            braw = [env.sb(T("braw%d" % i), [128, GP, 16], F32, s0) for i in range(2)]
            bb = [env.sb(T("bb%d" % i), [128, GP, 16], F32, s0) for i in range(2)]
            Bpad = [env.sb(T("Bpad%d" % i), [128, GP, 128], F32, s0) for i in range(2)]
            Cc = [env.sb(T("Cc%d" % i), [128, 8, 128], F32, s0) for i in range(2)]
            MQ = env.sb(T("MQ"), [128, 4, 128], F32, s0)
            LG = [env.sb(T("LG%d" % i), [64, 128], F32, s0) for i in range(2)]
            LD = env.sb(T("LD"), [128, 64], F32, s0)
            DV = env.sb(T("DV"), [8, 128], F32, s0)
            mhg = env.sb(T("mhg"), [128, GP], F32, s0)
            for i, lsrc in enumerate((lam_re, lam_im)):
                for dup in range(2):
                    fw.dma("sync", LG[i][:, dup * 64:(dup + 1) * 64], lsrc, writes=[T("LG%d" % i), SER])
            fw.dma("sync", LD[:], log_dt.partition_broadcast(128), writes=[T("LD"), SER])
            fw.dma("sync", DV[:], d_skip.rearrange("(blk c) -> blk c", c=128), writes=[T("DV"), SER])
            for gi in range(2):
                rows = slice(gi * 64, (gi + 1) * 64)
                for i, bsrc in enumerate((b_re, b_im)):
                    for g0 in range(0, GP, 8):
                        fw.dma("sync", braw[i][rows, g0:g0 + 8, :],
                               bsrc.rearrange("(gp gi) p c -> gi p gp c", gi=2)[gi][:, g0:g0 + 8, :],
                               writes=[T("braw%d" % i), SER])
                for i, csrc in enumerate((c_re, c_im)):
                    for b0 in range(0, 8, 4):
                        fw.dma("sync", Cc[i][:, b0:b0 + 4, rows],
                               csrc.rearrange("(blk gl) c p -> (gl c) blk p", gl=8)[:, b0:b0 + 4, :],
                               writes=[T("Cc%d" % i), SER])
            fw.dve(lambda e: e.memset(mhg[:], -0.5), reads=[SER], writes=[T("mhg")])
            for i, nm in enumerate(("lre", "lim")):
                fw.pe(lambda e: e.transpose(env.bank(7)[:, 0:64], LG[i][:, :], env.identf[0:64, 0:64]),
                      reads=[T("LG%d" % i), "identf", SER], writes=["ps7"])
                fw.dve(lambda e: e.tensor_copy(P[nm][0:64, :], env.bank(7)[0:64, 0:64:2]), reads=["ps7"],
                       writes=[T(nm)])
                fw.dve(lambda e: e.tensor_copy(P[nm][64:128, :], env.bank(7)[64:128, 1:64:2]), reads=["ps7"],
                       writes=[T(nm)])
            fw.dve(lambda e: e.tensor_copy(P["ldt"][0:64, :], LD[0:64, 0:64:2]), reads=[T("LD"), SER], writes=[T("ldt")])
            fw.dve(lambda e: e.tensor_copy(P["ldt"][64:128, :], LD[64:128, 1:64:2]), reads=[T("LD"), SER],
                   writes=[T("ldt")])
            fw.pe(lambda e: e.transpose(env.bank(7)[:, 64:72], DV[:, :], env.identf[0:8, 0:8]),
                  reads=[T("DV"), "identf", SER], writes=["ps7"])
            fw.dve(lambda e: e.tensor_copy(dvec[:], env.bank(7)[:, 64:72]), reads=["ps7"], writes=[T("dvec")])

            def ew(eng, out, a, b, op, keys):
                getattr(fw, eng)(lambda e: e.tensor_tensor(out=out, in0=a, in1=b, op=op), reads=keys, writes=keys)

            K = [T("setup")]
            dep = [T("lre"), T("lim"), T("ldt"), SER] + K
            fw.act(lambda e: e.activation(out=P["dt"][:], in_=P["ldt"][:], func=AF.Exp), reads=dep, writes=K)
            ew("dve", P["lr"][:], P["lre"][:], P["dt"][:], ALU.mult, dep)
            ew("dve", P["th"][:], P["lim"][:], P["dt"][:], ALU.mult, dep)
            fw.act(lambda e: e.activation(out=P["mag"][:], in_=P["lr"][:], func=AF.Exp), reads=K, writes=K)
            fw.dve(lambda e: e.tensor_scalar(out=P["t0"][:], in0=P["th"][:], scalar1=1.0 / TWO_PI, scalar2=None,
                                             op0=ALU.mult), reads=K, writes=K)
            fw.dve(lambda e: e.tensor_copy(fi[:], P["t0"][:]), reads=K, writes=K)
            fw.dve(lambda e: e.tensor_copy(P["t1"][:], fi[:]), reads=K, writes=K)
            ew("dve", P["f"][:], P["t0"][:], P["t1"][:], ALU.subtract, K)
            fw.act(lambda e: e.activation(out=P["s4"][:], in_=P["f"][:], func=AF.Sin, scale=TWO_PI / 4), reads=K, writes=K)
            fw.act(lambda e: e.activation(out=P["s2"][:], in_=P["f"][:], func=AF.Sin, scale=TWO_PI / 2), reads=K, writes=K)
            ew("dve", P["t0"][:], P["s4"][:], P["s4"][:], ALU.mult, K)
            fw.dve(lambda e: e.tensor_scalar(out=P["c2"][:], in0=P["t0"][:], scalar1=-2.0, scalar2=1.0, op0=ALU.mult,
                                             op1=ALU.add), reads=K, writes=K)
            ew("dve", P["t0"][:], P["s2"][:], P["c2"][:], ALU.mult, K)
            fw.dve(lambda e: e.tensor_scalar(out=P["sn"][:], in0=P["t0"][:], scalar1=2.0, scalar2=None, op0=ALU.mult),
                   reads=K, writes=K)
            ew("dve", P["t0"][:], P["s2"][:], P["s2"][:], ALU.mult, K)
            fw.dve(lambda e: e.tensor_scalar(out=P["cs"][:], in0=P["t0"][:], scalar1=-2.0, scalar2=1.0, op0=ALU.mult,
                                             op1=ALU.add), reads=K, writes=K)
            ew("dve", P["t0"][:], P["cs"][:], P["cs"][:], ALU.mult, K)
            ew("dve", P["t1"][:], P["sn"][:], P["sn"][:], ALU.mult, K)
            ew("dve", P["t0"][:], P["t0"][:], P["t1"][:], ALU.add, K)
            fw.dve(lambda e: e.tensor_scalar(out=P["t0"][:], in0=P["t0"][:], scalar1=-0.5, scalar2=1.5, op0=ALU.mult,
                                             op1=ALU.add), reads=K, writes=K)
            ew("dve", P["cs"][:], P["cs"][:], P["t0"][:], ALU.mult, K)
            ew("dve", P["sn"][:], P["sn"][:], P["t0"][:], ALU.mult, K)
            ew("dve", P["are"][:], P["mag"][:], P["cs"][:], ALU.mult, K)
            ew("dve", P["aim"][:], P["mag"][:], P["sn"][:], ALU.mult, K)
            fw.dve(lambda e: e.tensor_copy(RR[:], P["mag"][:]), reads=K, writes=K + [T("RR")])
            fw.dve(lambda e: e.tensor_copy(CM[:, :, 0], P["cs"][:]), reads=K, writes=K)
            fw.dve(lambda e: e.tensor_copy(SM[:, :, 0], P["sn"][:]), reads=K, writes=K)
            for k in range(1, 11):
                ew("dve", P["t0"][:], SM[:, :, k - 1], CM[:, :, k - 1], ALU.mult, K)
                fw.dve(lambda e: e.tensor_scalar(out=SM[:, :, k], in0=P["t0"][:], scalar1=2.0, scalar2=None,
                                                 op0=ALU.mult), reads=K, writes=K)
                ew("dve", P["t0"][:], SM[:, :, k - 1], SM[:, :, k - 1], ALU.mult, K)
                fw.dve(lambda e: e.tensor_scalar(out=CM[:, :, k], in0=P["t0"][:], scalar1=-2.0, scalar2=1.0,
                                                 op0=ALU.mult, op1=ALU.add), reads=K, writes=K)
            ew("dve", P["den"][:], P["lre"][:], P["lre"][:], ALU.mult, K)
            ew("dve", P["t0"][:], P["lim"][:], P["lim"][:], ALU.mult, K)
            ew("dve", P["den"][:], P["den"][:], P["t0"][:], ALU.add, K)
            fw.dve(lambda e: e.reciprocal(P["den"][:], P["den"][:]), reads=K, writes=K)
            fw.dve(lambda e: e.tensor_scalar(out=P["nre"][:], in0=P["are"][:], scalar1=-1.0, scalar2=None,
                                             op0=ALU.add), reads=K, writes=K)
            ew("dve", P["t0"][:], P["nre"][:], P["lre"][:], ALU.mult, K)
            ew("dve", P["t1"][:], P["aim"][:], P["lim"][:], ALU.mult, K)
            ew("dve", P["t0"][:], P["t0"][:], P["t1"][:], ALU.add, K)
            ew("dve", P["zre"][:], P["t0"][:], P["den"][:], ALU.mult, K)
            ew("dve", P["t0"][:], P["aim"][:], P["lre"][:], ALU.mult, K)
            ew("dve", P["t1"][:], P["nre"][:], P["lim"][:], ALU.mult, K)
            ew("dve", P["t0"][:], P["t0"][:], P["t1"][:], ALU.subtract, K)
            ew("dve", P["zim"][:], P["t0"][:], P["den"][:], ALU.mult, K)
            KB = K + [T("braw0"), T("braw1")]
            for c in range(16):
                ew("dve", P["t0"][:], P["zre"][:], braw[0][:, :, c], ALU.mult, KB)
                ew("dve", P["t1"][:], P["zim"][:], braw[1][:, :, c], ALU.mult, KB)
                ew("dve", bb[0][:, :, c], P["t0"][:], P["t1"][:], ALU.subtract, KB)
                ew("dve", P["t0"][:], P["zre"][:], braw[1][:, :, c], ALU.mult, KB)
                ew("dve", P["t1"][:], P["zim"][:], braw[0][:, :, c], ALU.mult, KB)
                ew("dve", bb[1][:, :, c], P["t0"][:], P["t1"][:], ALU.add, KB)
            for i in range(3):
                if i == 2:
                    fw.dve(lambda e: e.tensor_tensor(out=Bpad[0][:], in0=Bpad[0][:], in1=Bpad[1][:], op=ALU.add),
                           reads=KB + [T("LB")], writes=KB)
                    for gp in range(GP):
                        fw.pe(lambda e: e.transpose(env.bank(gp % 4)[:, 0:128], Bpad[0][:, gp, :], env.identf[:]),
                              reads=KB + ["identf"], writes=["ps%d" % (gp % 4)])
                        fw.act(lambda e: e.copy(LB[2][:, gp, :], env.bank(gp % 4)[:, 0:128]),
                               reads=["ps%d" % (gp % 4)], writes=[T("LB")])
                    break
                fw.dve(lambda e: e.memset(Bpad[i][:], 0.0), reads=KB, writes=KB)
                for gi in range(2):
                    rows = slice(gi * 64, (gi + 1) * 64)
                    for q in range(4):
                        c0 = (2 * q + gi) * 16
                        fw.dve(lambda e: e.tensor_copy(Bpad[i][rows, q::4, c0:c0 + 16], bb[i][rows, q::4, :]),
                               reads=KB, writes=KB)
                for gp in range(GP):
                    fw.pe(lambda e: e.transpose(env.bank(gp % 4)[:, 0:128], Bpad[i][:, gp, :], env.identf[:]),
                          reads=KB + ["identf"], writes=["ps%d" % (gp % 4)])
                    fw.act(lambda e: e.copy(LB[i][:, gp, :], env.bank(gp % 4)[:, 0:128]),
                           reads=["ps%d" % (gp % 4)], writes=[T("LB")])
            dbgdump(env, "are", P["are"][:], K)
            dbgdump(env, "zre", P["zre"][:], K)
            dbgdump(env, "cm", CM[:].rearrange("p a b -> p (a b)"), K)
            dbgdump(env, "bb0", bb[0][:].rearrange("p a b -> p (a b)"), KB)
            dbgdump(env, "braw0", braw[0][:].rearrange("p a b -> p (a b)"), KB)
            dbgdump(env, "zim", P["zim"][:], KB)
            fw.dve(lambda e: e.memset(MQ[:], 0.0), writes=[T("MQ")])
            for q in range(4):
                fw.dve(lambda e: e.memset(MQ[0:64, q, (2 * q) * 16:(2 * q) * 16 + 16], 1.0), writes=[T("MQ")])
                fw.dve(lambda e: e.memset(MQ[64:128, q, (2 * q + 1) * 16:(2 * q + 1) * 16 + 16], 1.0),
                       writes=[T("MQ")])
            fw.dve(lambda e: e.tensor_scalar(out=Cc[1][:], in0=Cc[1][:], scalar1=-1.0, scalar2=None, op0=ALU.mult),
                   reads=[T("Cc1")], writes=[T("Cc1")])
            for i in range(2):
                for blk in range(8):
                    pb = 4 + blk % 2
                    fw.pe(lambda e: e.transpose(env.bank(pb)[:, 0:128], Cc[i][:, blk, :], env.identf[:]),
                          reads=[T("Cc%d" % i), "identf"], writes=["ps%d" % pb])
                    for q in range(4):
                        fw.dve(lambda e: e.tensor_tensor(out=LC[i][:, blk * 4 + q, :], in0=env.bank(pb)[:, 0:128],
                                                         in1=MQ[:, q, :], op=ALU.mult),
                               reads=["ps%d" % pb, T("MQ")], writes=[T("LC")])
                        if i == 0:
                            fw.dve(lambda e: e.scalar_tensor_tensor(out=LC[2][:, blk * 4 + q, :],
                                                                    in0=env.bank(pb)[:, 0:128], scalar=-1.0,
                                                                    in1=MQ[:, q, :], op0=ALU.mult, op1=ALU.mult),
                                   reads=["ps%d" % pb, T("MQ")], writes=[T("LC")])

        dbgdump(env, "LB0", LB[0][:, 0, :], [T("LB")])
        dbgdump(env, "LC0", LC[0][:, 0, :], [T("LC")])
        dbgdump(env, "LC1", LC[1][:, 5, :], [T("LC")])
        fw.barrier()
        with contextlib.ExitStack() as s1:
            Wi = env.sb(T("Wi"), [128, 8, D], BF16, s1)
            gpre = env.sb(T("gpre"), [128, D], F32, s1)
            fw.dma("sync", gpre[:], pre_g.partition_broadcast(128), writes=[T("gpre")])
            HT = [env.sb(T("hT%d" % i), [128, 8, 512], BF16, s1) for i in range(2)]
            XT = [env.sb(T("xt%d" % i), [128, D], F32, s1) for i in range(2)]
            HB = [env.sb(T("hb%d" % i), [128, D], BF16, s1) for i in range(2)]
            kWi = load_w_cast(fw, Wi, T("Wi"), w_in, 8)
            it = 0
            for t in range(NTOK // 512):
                hT, khT = HT[t % 2], T("hT%d" % (t % 2))
                for s in range(4):
                    r0 = t * 512 + s * 128
                    xt, kx = XT[it % 2], T("xt%d" % (it % 2))
                    hb, khb = HB[it % 2], T("hb%d" % (it % 2))
                    fw.dma("sync", xt[:], x_in[r0:r0 + 128, :], writes=[kx])
                    norm_transpose(env, xt[:], kx, gpre[:], T("gpre"), hb[:], khb, hT, khT, s * 128, 6)
                    it += 1
                for blk in range(8):
                    pb = blk % 4
                    fw.pe_group([lambda e, k=k: e.matmul(env.bank(pb), Wi[:, k, blk * 128:(blk + 1) * 128],
                                                         hT[:, k, :], start=(k == 0), stop=(k == 7))
                                 for k in range(8)], reads=kWi + [khT], writes=["ps%d" % pb])
                    fw.act(lambda e: e.copy(uT[:, blk, t * 512:(t + 1) * 512], env.bank(pb)),
                           reads=["ps%d" % pb], writes=[T("uT%d_%d" % (blk, t // 4))])

        fw.barrier()
        with contextlib.ExitStack() as s2:
            TAB = [tuple(env.sb(T("%s%d" % (nm, q)), [128, 512], F32, s2) for nm in ("COS", "SIN", "TM", "TP"))
                   for q in range(4)]
            TT = env.sb(T("TT"), [128, 256], F32, s2)
            RT = [env.sb(T("Rt%d" % q), [128, 512], F32, s2) for q in range(4)]
            ones = env.sb(T("ones"), [128, 512], F32, s2)
            WS = []
            for i in range(2):
                d = {}
                for nm in ("bs", "bre", "bim", "w_re", "w_im"):
                    d[nm] = env.sb(T("w%d_%s" % (i, nm)), [128, 512], F32, s2)
                for nm in ("p1", "p2", "p3", "p4"):
                    d[nm] = env.sb(T("w%d_%s" % (i, nm)), [128, 512], BF16, s2)
                WS.append(d)
            INI = env.sb(T("ini"), [128, 8], F32, s2)
            YV = env.sb(T("yv"), [128, 512], F32, s2)
            G1 = env.sb(T("g1"), [128, 512], F32, s2)
            G2 = env.sb(T("g2"), [128, 512], F32, s2)
            NS9 = env.sb(T("ns9"), [128, GP], F32, s2)
            fw.dve(lambda e: e.memset(ones[:], 1.0), writes=[T("ones")])
            fw.dve(lambda e: e.tensor_scalar(out=NS9[:], in0=SM[:, :, 9], scalar1=-1.0, scalar2=None, op0=ALU.mult),
                   reads=[T("setup")], writes=[T("ns9")])

            def build_table(q, gp):
                COS, SIN, TM, TP = TAB[q]
                KT = [T("tab%d" % q)]
                fw.dve(lambda e: e.memset(COS[:, 0:1], 1.0), reads=KT, writes=KT)
                fw.dve(lambda e: e.memset(SIN[:, 0:1], 0.0), reads=KT, writes=KT)
                for k in range(9):
                    m = 1 << k
                    cm, sm = CM[:, gp, k:k + 1], SM[:, gp, k:k + 1]
                    fw.dve(lambda e: e.tensor_scalar(out=TT[:, 0:m], in0=SIN[:, 0:m], scalar1=sm, scalar2=None,
                                                     op0=ALU.mult), reads=KT + [T("setup"), T("TT")], writes=[T("TT")])
                    fw.dve(lambda e: e.scalar_tensor_tensor(out=COS[:, m:2 * m], in0=COS[:, 0:m], scalar=cm,
                                                            in1=TT[:, 0:m], op0=ALU.mult, op1=ALU.subtract),
                           reads=KT + [T("TT")], writes=KT)
                    fw.dve(lambda e: e.tensor_scalar(out=TT[:, 0:m], in0=COS[:, 0:m], scalar1=sm, scalar2=None,
                                                     op0=ALU.mult), reads=KT + [T("TT")], writes=[T("TT")])
                    fw.dve(lambda e: e.scalar_tensor_tensor(out=SIN[:, m:2 * m], in0=SIN[:, 0:m], scalar=cm,
                                                            in1=TT[:, 0:m], op0=ALU.mult, op1=ALU.add),
                           reads=KT + [T("TT")], writes=KT)
                fw.dve(lambda e: e.tensor_tensor(out=TM[:], in0=COS[:], in1=SIN[:], op=ALU.subtract),
                       reads=KT, writes=KT)
                fw.dve(lambda e: e.tensor_tensor(out=TP[:], in0=COS[:], in1=SIN[:], op=ALU.add),
                       reads=KT, writes=KT)
                fw.dve(lambda e: e.tensor_scalar(out=RT[q][:], in0=ones[:], scalar1=RR[:, gp:gp + 1], scalar2=None,
                                                 op0=ALU.mult), reads=[T("ones"), T("RR"), T("Rt%d" % q)],
                       writes=[T("Rt%d" % q)])

            def stage_a(i, blk, sq_i, q, t):
                gp = blk * 4 + q
                W, kw = WS[i % 2], T("ws%d" % (i % 2))
                COS, SIN, TM, TP = TAB[q]
                KT = [T("tab%d" % q)]
                cols = slice(sq_i * SEQ + t * 512, sq_i * SEQ + (t + 1) * 512)
                ku = T("uT%d_%d" % (blk, sq_i))
                for (bnk, li, dst) in ((0, 2, "bs"), (1, 0, "bre"), (2, 1, "bim")):
                    fw.pe(lambda e: e.matmul(env.bank(bnk), LB[li][:, gp, :], uT[:, blk, cols], start=True,
                                             stop=True), reads=[T("LB"), ku], writes=["ps%d" % bnk])
                    fw.act(lambda e: e.copy(W[dst][:], env.bank(bnk)), reads=["ps%d" % bnk], writes=[kw + dst])
                fw.dve(lambda e: e.tensor_tensor(out=W["bs"][:], in0=W["bs"][:], in1=COS[:], op=ALU.mult),
                       reads=[kw + "bs"] + KT, writes=[kw + "bs"])
                fw.dve(lambda e: e.tensor_tensor(out=W["bim"][:], in0=W["bim"][:], in1=TM[:], op=ALU.mult),
                       reads=[kw + "bim"] + KT, writes=[kw + "bim"])
                fw.dve(lambda e: e.tensor_tensor(out=W["bre"][:], in0=W["bre"][:], in1=TP[:], op=ALU.mult),
                       reads=[kw + "bre"] + KT, writes=[kw + "bre"])
                fw.dve(lambda e: e.tensor_tensor(out=W["bim"][:], in0=W["bs"][:], in1=W["bim"][:], op=ALU.subtract),
                       reads=[kw + "bs", kw + "bim"], writes=[kw + "bim"])
                fw.dve(lambda e: e.tensor_tensor(out=W["bre"][:], in0=W["bs"][:], in1=W["bre"][:], op=ALU.subtract),
                       reads=[kw + "bs", kw + "bre"], writes=[kw + "bre"])

            def stage_b(i, blk, sq_i, q, t):
                gp = blk * 4 + q
                W, kw = WS[i % 2], T("ws%d" % (i % 2))
                Wp, kwp = WS[(i + 1) % 2], T("ws%d" % ((i + 1) % 2))
                COS, SIN, TM, TP = TAB[q]
                KT = [T("tab%d" % q)]
                if t == 0:
                    ire, iim = 0.0, 0.0
                    kini = []
                else:
                    c9, s9, ns9 = CM[:, gp, 9:10], SM[:, gp, 9:10], NS9[:, gp:gp + 1]
                    lre, lim_ = Wp["w_re"][:, 511:512], Wp["w_im"][:, 511:512]
                    o = 4 * (i % 2)
                    kini = [T("ini%d" % (i % 2))]
                    fw.act(lambda e: e.activation(out=INI[:, o:o + 1], in_=lim_, func=AF.Identity, scale=ns9),
                           reads=[kwp + "w_im", T("ns9")] + kini, writes=kini)
                    fw.act(lambda e: e.activation(out=INI[:, o + 1:o + 2], in_=lre, func=AF.Identity, scale=c9,
                                                  bias=INI[:, o:o + 1]), reads=[kwp + "w_re", T("setup")] + kini,
                           writes=kini)
                    fw.act(lambda e: e.activation(out=INI[:, o + 2:o + 3], in_=lre, func=AF.Identity, scale=s9),
                           reads=[kwp + "w_re"] + kini, writes=kini)
                    fw.act(lambda e: e.activation(out=INI[:, o + 3:o + 4], in_=lim_, func=AF.Identity, scale=c9,
                                                  bias=INI[:, o + 2:o + 3]), reads=[kwp + "w_im"] + kini, writes=kini)
                    ire, iim = INI[:, o + 1:o + 2], INI[:, o + 3:o + 4]
                fw.dve(lambda e: e.tensor_tensor_scan(out=W["w_re"][:], data0=RT[q][:], data1=W["bim"][:],
                                                      initial=ire, op0=ALU.mult, op1=ALU.add),
                       reads=[kw + "bim", T("Rt%d" % q)] + kini, writes=[kw + "w_re"])
                fw.dve(lambda e: e.tensor_tensor_scan(out=W["w_im"][:], data0=RT[q][:], data1=W["bre"][:],
                                                      initial=iim, op0=ALU.mult, op1=ALU.add),
                       reads=[kw + "bre", T("Rt%d" % q)] + kini, writes=[kw + "w_im"])
                yb = 4 + t
                plan = (("p1", "w_re", COS, "dve", 0), ("p2", "w_im", SIN, "dve", 2), ("p3", "w_re", SIN, "dve", 1),
                        ("p4", "w_im", COS, "dve", 1))
                for n_, (o_, a_, tab, eng, lc) in enumerate(plan):
                    getattr(fw, eng)(lambda e: e.tensor_tensor(out=W[o_][:], in0=W[a_][:], in1=tab[:], op=ALU.mult),
                                     reads=[kw + a_] + KT, writes=[kw + o_])
                for n_, (o_, a_, tab, eng, lc) in enumerate(plan):
                    fw.pe(lambda e: e.matmul(env.bank(yb), LC[lc][:, gp, :], W[o_][:], start=(q == 0 and n_ == 0),
                                             stop=(q == 3 and n_ == 3)),
                          reads=[T("LC"), kw + o_], writes=["ps%d" % yb])

            def epilogue(blk, sq_i):
                ku = T("uT%d_%d" % (blk, sq_i))
                for t in range(4):
                    cols = slice(sq_i * SEQ + t * 512, sq_i * SEQ + (t + 1) * 512)
                    yb = 4 + t
                    KY = [T("yv")]
                    fw.dve(lambda e: e.scalar_tensor_tensor(out=YV[:], in0=uT[:, blk, cols],
                                                            scalar=dvec[:, blk:blk + 1], in1=env.bank(yb),
                                                            op0=ALU.mult, op1=ALU.add),
                           reads=[ku, T("dvec"), "ps%d" % yb] + KY, writes=KY)
                    fw.act(lambda e: e.activation(out=G1[:], in_=YV[:], func=AF.Square), reads=KY, writes=KY)
                    fw.act(lambda e: e.activation(out=G1[:], in_=G1[:], func=AF.Identity, scale=0.044715, bias=1.0),
                           reads=KY, writes=KY)
                    fw.dve(lambda e: e.tensor_tensor(out=G1[:], in0=G1[:], in1=YV[:], op=ALU.mult),
                           reads=KY, writes=KY)
                    fw.act(lambda e: e.activation(out=G2[:], in_=G1[:], func=AF.Tanh, scale=0.7978845608028654),
                           reads=KY, writes=KY)
                    fw.act(lambda e: e.activation(out=G2[:], in_=G2[:], func=AF.Identity, scale=0.5, bias=0.5),
                           reads=KY, writes=KY)
                    fw.dve(lambda e: e.tensor_tensor(out=uT[:, blk, cols], in0=G2[:], in1=YV[:], op=ALU.mult),
                           reads=KY + [ku], writes=KY + [ku])

            gi_ = 0
            for blk in range(8):
                for q in range(4):
                    build_table(q, blk * 4 + q)
                items = [(sq_i, q, t) for sq_i in range(nseq) for q in range(4) for t in range(4)]
                stage_a(gi_, blk, *items[0])
                for n, it_ in enumerate(items):
                    if n + 1 < len(items):
                        stage_a(gi_ + 1, blk, *items[n + 1])
                    stage_b(gi_, blk, *it_)
                    gi_ += 1
                    if it_[1] == 3 and it_[2] == 3:
                        epilogue(blk, it_[0])

        dbgdump(env, "yg0", uT[:, 0, 0:512], [T("uT0")])
        fw.barrier()
        with contextlib.ExitStack() as s3:
            Wg = env.sb(T("Wglu"), [128, 8, D], BF16, s3)
            gpost = env.sb(T("gpost"), [128, D], F32, s3)
            fw.dma("sync", gpost[:], post_g.partition_broadcast(128), writes=[T("gpost")])
            Wo = env.sb(T("Wo"), [128, 8, D], BF16, s3)
            ZT = [env.sb(T("zT%d" % i), [128, 8, 512], BF16, s3) for i in range(2)]
            SG = [env.sb(T("sg%d" % i), [128, 512], F32, s3) for i in range(4)]
            XR = [env.sb(T("xr%d" % i), [128, D], F32, s3) for i in range(2)]
            TMP = env.sb(T("tmp"), [128, D], F32, s3)
            kWglu = load_w_cast(fw, Wg, T("Wglu"), w_glu, 8)
            kWo5 = load_w_cast(fw, Wo, T("Wo"), w_out, 8)
            ukeys = [T("uT%d_%d" % (b, q_)) for b in range(8) for q_ in range(nseq)]
            for t in range(NTOK // 512):
                cols = slice(t * 512, (t + 1) * 512)
                zT, kzT = ZT[t % 2], T("zT%d" % (t % 2))
                for blk in range(8):
                    pb = (0, 1, 6, 7)[blk % 4]
                    fw.pe_group([lambda e, k=k: e.matmul(env.bank(pb), Wg[:, k, blk * 128:(blk + 1) * 128],
                                                         uT[:, k, cols], start=(k == 0), stop=(k == 7))
                                 for k in range(8)], reads=kWglu + ukeys, writes=["ps%d" % pb])
                    sg, ksg = SG[blk % 4], T("sg%d" % (blk % 4))
                    fw.act(lambda e: e.activation(out=sg[:], in_=env.bank(pb), func=AF.Sigmoid),
                           reads=["ps%d" % pb], writes=[ksg])
                    fw.dve(lambda e: e.tensor_tensor(out=zT[:, blk, :], in0=sg[:], in1=uT[:, blk, cols], op=ALU.mult),
                           reads=[ksg] + ukeys, writes=[kzT])
                for s in range(4):
                    r0 = t * 512 + s * 128
                    xr, kxr = XR[s % 2], T("xr%d" % (s % 2))
                    fw.dma("sync", xr[:], x_in[r0:r0 + 128, :], writes=[kxr])
                    pb0 = 2 + 2 * (s % 2)
                    pso = env.bank(pb0, 2)
                    pk = ["ps%d" % pb0, "ps%d" % (pb0 + 1)]
                    for hf in range(2):
                        fw.pe_group([lambda e, k=k: e.matmul(pso[:, hf, :], zT[:, k, s * 128:(s + 1) * 128],
                                                             Wo[:, k, hf * 512:(hf + 1) * 512], start=(k == 0),
                                                             stop=(k == 7)) for k in range(8)],
                                    reads=[kzT] + kWo5, writes=pk)
                    post_norm_residual(env, pso, pk, gpost[:], T("gpost"), xr[:], kxr, TMP[:], T("tmp"),
                                       xr[:], kxr)
                    fw.dma("pool", x_out[r0:r0 + 128, :], xr[:], reads=[kxr], writes=[])


NSEQ_CORE = 2
NCORES = 8
_CACHE = {}


def build_program():
    nc = bass.Bass("TRN2", target_bir_lowering=False)
    ntok = NSEQ_CORE * SEQ

    def din(name, shape):
        return nc.dram_tensor(name, shape, F32, kind="ExternalInput").ap()

    x = din("x", [ntok, D])
    fox_w_in = din("fox_w_in", [D, 4112])
    fox_b_f = din("fox_b_f", [NH])
    fox_q_gain = din("fox_q_gain", [HD])
    fox_k_gain = din("fox_k_gain", [HD])
    fox_w_out = din("fox_w_out", [DATT, D])
    s5_w_in = din("s5_w_in", [D, D])
    s5_log_dt = din("s5_log_dt", [NG])
    s5_lam_re = din("s5_lam_re", [NG, 64])
    s5_lam_im = din("s5_lam_im", [NG, 64])
    s5_b_re = din("s5_b_re", [NG, 64, 16])
    s5_b_im = din("s5_b_im", [NG, 64, 16])
    s5_c_re = din("s5_c_re", [NG, 16, 64])
    s5_c_im = din("s5_c_im", [NG, 16, 64])
    s5_d = din("s5_d", [D])
    s5_w_glu = din("s5_w_glu", [D, D])
    s5_w_out = din("s5_w_out", [D, D])
    mix_pre = din("mix_pre_gain", [2, D])
    mix_post = din("mix_post_gain", [2, D])
    ffn_pre = din("ffn_pre_gain", [2, D])
    ffn_post = din("ffn_post_gain", [2, D])
    ffn_wg = din("ffn_w_gate", [2, D, DFF])
    ffn_wu = din("ffn_w_up", [2, D, DFF])
    ffn_wd = din("ffn_w_down", [2, DFF, D])
    ident = din("c_ident", [128, 128])
    cmask = din("c_mask", [128, 128])
    out = nc.dram_tensor("out", [ntok, D], F32, kind="ExternalOutput").ap()
    xa = nc.dram_tensor("xa", [ntok, D], F32).ap()
    xb = nc.dram_tensor("xb", [ntok, D], F32).ap()
    xc = nc.dram_tensor("xc", [ntok, D], F32).ap()
    oT_d = nc.dram_tensor("oT_d", [NSEQ_CORE, NH, HD, SEQ], BF16).ap()

    fw = FW(nc)
    with contextlib.ExitStack() as st:
        env = Env(nc, fw, st)
        env.init_consts(ident)
        fox_phase(env, x, xa, fox_w_in, fox_b_f, fox_q_gain, fox_k_gain, fox_w_out, mix_pre[0, :], mix_post[0, :],
                  cmask, oT_d, NSEQ_CORE)
        ffn_phase(env, xa, xb, ffn_wg[0], ffn_wu[0], ffn_wd[0], ffn_pre[0, :], ffn_post[0, :], ntok, "f0")
        s5_phase(env, xb, xc, s5_w_in, s5_log_dt, s5_lam_re, s5_lam_im, s5_b_re, s5_b_im, s5_c_re, s5_c_im, s5_d,
                 s5_w_glu, s5_w_out, mix_pre[1, :], mix_post[1, :], NSEQ_CORE)
        ffn_phase(env, xc, out, ffn_wg[1], ffn_wu[1], ffn_wd[1], ffn_pre[1, :], ffn_post[1, :], ntok, "f1")
        fw.finish()
    return nc


def kernel(**inputs):
    f = np.float32
    x = np.ascontiguousarray(inputs["x"], dtype=f)
    B = x.shape[0]
    shared = {}
    for k in ("fox_w_in", "fox_b_f", "fox_q_gain", "fox_k_gain", "fox_w_out", "s5_w_in", "s5_log_dt", "s5_lam_re",
              "s5_lam_im", "s5_b_re", "s5_b_im", "s5_c_re", "s5_c_im", "s5_d", "s5_w_glu", "s5_w_out"):
        shared[k] = np.ascontiguousarray(np.asarray(inputs[k], dtype=f)[0])
    for k in ("mix_pre_gain", "mix_post_gain", "ffn_pre_gain", "ffn_post_gain", "ffn_w_gate", "ffn_w_up",
              "ffn_w_down"):
        shared[k] = np.ascontiguousarray(np.asarray(inputs[k], dtype=f))
    shared["c_ident"] = np.eye(128, dtype=f)
    shared["c_mask"] = np.where(np.arange(128)[None, :] < np.arange(128)[:, None], MASKNEG, 0.0).astype(f)
    if "nc" not in _CACHE:
        _CACHE["nc"] = build_program()
    nc = _CACHE["nc"]
    in_maps = []
    for c in range(NCORES):
        m = dict(shared)
        m["x"] = np.ascontiguousarray(x[c * NSEQ_CORE:(c + 1) * NSEQ_CORE].reshape(NSEQ_CORE * SEQ, D))
        in_maps.append(m)
    res = run_bass_kernel_spmd(nc, in_maps, core_ids=list(range(NCORES)))
    outs = [np.asarray(r["out"]).reshape(NSEQ_CORE, SEQ, D) for r in res.results]
    return np.concatenate(outs, axis=0).astype(f)
```
